# Optimizing a Trainium2 kernel written in Bass

```python
import jax, jax.numpy as jnp
from jax import lax
import numpy as np

D_MODEL = 1024
BATCH = 8
SEQ = 2048
DEPTH = 2
DEC_BATCH = 128
DEC_SEQ = 1
PAST_LEN = 16384
PAGE_SIZE = 128

N_EVEN = (DEPTH + 1) // 2
N_ODD = DEPTH // 2
CONV_CH = D_MODEL // 2
CONV_K = 31
DN_HEAD_DIM = 128
DN_HEADS = (D_MODEL // 2) // DN_HEAD_DIM
DN_WIDTH = DN_HEADS * DN_HEAD_DIM
QKV_CH = 3 * DN_WIDTH
SC_K = 4
DN_CHUNK = 64
EVEN_IN = 3 * CONV_CH + QKV_CH + DN_WIDTH + 2 * DN_HEADS
EVEN_MIX = CONV_CH + DN_WIDTH
EVEN_SPLITS = (CONV_CH, 2 * CONV_CH, 3 * CONV_CH, 3 * CONV_CH + QKV_CH,
               3 * CONV_CH + QKV_CH + DN_WIDTH, 3 * CONV_CH + QKV_CH + DN_WIDTH + DN_HEADS)
POOL_WINDOWS = (2, 4, 8, 16)
POOL_WIDTH = D_MODEL
POOL_GROUP = POOL_WIDTH // len(POOL_WINDOWS)
POOL_BUF = max(POOL_WINDOWS) - 1
N_MEM = 256
XA_HEADS = 4
XA_HEAD_DIM = D_MODEL // XA_HEADS
XA_WIDTH = XA_HEADS * XA_HEAD_DIM
EPS = 1e-6

kernel_name = 'hybrid_conformer_deltanet_pool_decoder_step'


def rms_norm(x, g):
    xf = x.astype(jnp.float32)
    y = xf * lax.rsqrt(jnp.mean(xf * xf, axis=-1, keepdims=True) + EPS)
    return y.astype(x.dtype) * g.astype(x.dtype)


def layer_norm(x, g, b):
    xf = x.astype(jnp.float32)
    xc = xf - jnp.mean(xf, axis=-1, keepdims=True)
    y = xc * lax.rsqrt(jnp.mean(xc * xc, axis=-1, keepdims=True) + EPS)
    return y.astype(x.dtype) * g.astype(x.dtype) + b.astype(x.dtype)


def l2_normalize(x):
    return x * lax.rsqrt(jnp.sum(x * x, axis=-1, keepdims=True) + EPS)


def causal_depthwise_conv(x, buf, w):
    k_len, ch = w.shape
    xp = jnp.concatenate([buf.astype(x.dtype), x], axis=1)
    y = lax.conv_general_dilated(xp, w.astype(x.dtype)[:, None, :], window_strides=(1,), padding='VALID',
                                 dimension_numbers=('NWC', 'WIO', 'NWC'), feature_group_count=ch)
    return y, xp[:, xp.shape[1] - (k_len - 1):]


def gated_delta_chunked(q, k, v, g, beta, s0):
    b, t, h, dk = q.shape
    dv = v.shape[-1]
    c = DN_CHUNK
    n = t // c

    def blocks(a):
        return jnp.moveaxis(a.reshape((b, n, c, h) + a.shape[3:]), 3, 1)

    q, k, v, g, beta = blocks(q * dk ** -0.5), blocks(k), blocks(v), blocks(g), blocks(beta)
    gc = jnp.cumsum(g, axis=-1)
    incl = jnp.tril(jnp.ones((c, c), dtype=bool))
    strict = jnp.tril(jnp.ones((c, c), dtype=bool), -1)
    diff = gc[..., :, None] - gc[..., None, :]
    decay = jnp.where(incl, jnp.exp(jnp.where(incl, diff, 0.0)), 0.0)
    kb = k * beta[..., None]
    a_low = jnp.where(strict, jnp.einsum('bhnid,bhnjd->bhnij', kb, k) * decay, 0.0)
    rhs = jnp.concatenate([v * beta[..., None], kb * jnp.exp(gc)[..., None]], axis=-1)
    sol = lax.linalg.triangular_solve(a_low + jnp.eye(c, dtype=a_low.dtype), rhs, left_side=True,
                                      lower=True, unit_diagonal=True)
    u, w = sol[..., :dv], sol[..., dv:]
    qk = jnp.where(incl, jnp.einsum('bhnid,bhnjd->bhnij', q, k) * decay, 0.0)
    q_dec = q * jnp.exp(gc)[..., None]
    k_dec = k * jnp.exp(gc[..., -1:] - gc)[..., None]
    g_last = jnp.exp(gc[..., -1])

    def step(s, xs):
        u_n, w_n, qd_n, kd_n, qk_n, gl_n = xs
        v_new = u_n - jnp.einsum('bhcd,bhde->bhce', w_n, s)
        o_n = jnp.einsum('bhcd,bhde->bhce', qd_n, s) + jnp.einsum('bhcj,bhje->bhce', qk_n, v_new)
        s = s * gl_n[..., None, None] + jnp.einsum('bhcd,bhce->bhde', kd_n, v_new)
        return s, o_n

    xs = tuple(jnp.moveaxis(a, 2, 0) for a in (u, w, q_dec, k_dec, qk, g_last))
    s_fin, o = lax.scan(step, s0, xs)
    o = jnp.transpose(o, (1, 0, 3, 2, 4)).reshape(b, t, h, dv)
    return o, s_fin


def gated_delta_recurrent(q, k, v, g, beta, s0):
    q = q * q.shape[-1] ** -0.5

    def step(s, xs):
        q_t, k_t, v_t, g_t, b_t = xs
        s = s * jnp.exp(g_t)[..., None, None]
        v_new = (v_t - jnp.einsum('bhd,bhde->bhe', k_t, s)) * b_t[..., None]
        s = s + jnp.einsum('bhd,bhe->bhde', k_t, v_new)
        return s, jnp.einsum('bhd,bhde->bhe', q_t, s)

    xs = tuple(jnp.moveaxis(a, 1, 0) for a in (q, k, v, g, beta))
    s_fin, o = lax.scan(step, s0, xs)
    return jnp.moveaxis(o, 0, 1), s_fin


def even_mixer(h, conv_buf, qkv_buf, s0, w_in, dw_w, dw_b, ln_g, ln_b, sc_w, a_log, dt_bias, dn_g, w_out,
               recurrent):
    b, t, _ = h.shape
    p = h @ w_in
    glu_val, glu_gate, gate_a, qkv, z, beta_l, a_l = jnp.split(p, EVEN_SPLITS, axis=-1)
    c, conv_buf_new = causal_depthwise_conv(glu_val * jax.nn.sigmoid(glu_gate), conv_buf, dw_w)
    c = jax.nn.silu(layer_norm(c + dw_b.astype(c.dtype), ln_g, ln_b))
    a_out = c * jax.nn.silu(gate_a)
    qkv_c, qkv_buf_new = causal_depthwise_conv(qkv, qkv_buf, sc_w)
    qkv_c = jax.nn.silu(qkv_c).astype(jnp.float32).reshape(b, t, 3, DN_HEADS, DN_HEAD_DIM)
    q = l2_normalize(qkv_c[:, :, 0])
    k = l2_normalize(qkv_c[:, :, 1])
    v = qkv_c[:, :, 2]
    beta = jax.nn.sigmoid(beta_l.astype(jnp.float32))
    g = -jnp.exp(a_log.astype(jnp.float32)) * jax.nn.softplus(a_l.astype(jnp.float32) + dt_bias.astype(jnp.float32))
    delta = gated_delta_recurrent if recurrent else gated_delta_chunked
    o, s_new = delta(q, k, v, g, beta, s0.astype(jnp.float32))
    o = rms_norm(o, dn_g) * jax.nn.silu(z.astype(jnp.float32).reshape(b, t, DN_HEADS, DN_HEAD_DIM))
    b_out = o.reshape(b, t, DN_WIDTH).astype(h.dtype)
    y = jnp.concatenate([a_out, b_out], axis=-1) @ w_out
    return y, conv_buf_new, qkv_buf_new, s_new.astype(s0.dtype)


def causal_multiscale_pool(u, buf, start_pos):
    b, t, ch = u.shape
    up = jnp.concatenate([buf.astype(u.dtype), u], axis=1)
    cs = jnp.cumsum(up.astype(jnp.float32), axis=1)
    cs = jnp.concatenate([jnp.zeros((b, 1, ch), jnp.float32), cs], axis=1)
    pos = start_pos + jnp.arange(t)
    end = cs[:, POOL_BUF + 1:POOL_BUF + 1 + t]
    means = []
    for gi, win in enumerate(POOL_WINDOWS):
        sl = slice(gi * POOL_GROUP, (gi + 1) * POOL_GROUP)
        begin = cs[:, POOL_BUF + 1 - win:POOL_BUF + 1 - win + t, sl]
        cnt = jnp.minimum(pos + 1, win).astype(jnp.float32)
        means.append((end[..., sl] - begin) / cnt[None, :, None])
    pooled = jnp.concatenate(means, axis=-1) - u.astype(jnp.float32)
    return pooled.astype(u.dtype), up[:, up.shape[1] - POOL_BUF:]


def odd_mixer(h, pool_buf, w_in, w_pool, b_pool, scale, w_out, start_pos):
    b, t, _ = h.shape
    u, gate = jnp.split(h @ w_in, 2, axis=-1)
    pooled, buf_new = causal_multiscale_pool(u, pool_buf, start_pos)
    z = jnp.einsum('btgc,gcd->btgd', pooled.reshape(b, t, len(POOL_WINDOWS), POOL_GROUP), w_pool) + b_pool
    z = z.reshape(b, t, POOL_WIDTH) * scale * jax.nn.silu(gate)
    return z @ w_out, buf_new


def memory_cross_attention(h, mk, mv, wq, wo):
    b, t, _ = h.shape
    q = (h @ wq).reshape(b, t, XA_HEADS, XA_HEAD_DIM)
    s = jnp.einsum('bthd,bmhd->bhtm', q, mk.astype(q.dtype)).astype(jnp.float32) * XA_HEAD_DIM ** -0.5
    pr = jax.nn.softmax(s, axis=-1).astype(q.dtype)
    o = jnp.einsum('bhtm,bmhd->bthd', pr, mv.astype(q.dtype)).reshape(b, t, XA_WIDTH)
    return o @ wo


def trunk(x, conv_bufs, qkv_bufs, dn_states, pool_bufs, mem_k, mem_v, prm, start_pos, recurrent):
    new_conv, new_qkv, new_dn, new_pool = [], [], [], []
    for l in range(DEPTH):
        h = rms_norm(x, prm['norm_mix'][l])
        if l % 2 == 0:
            e = l // 2
            y, cb, qb, st = even_mixer(h, conv_bufs[e], qkv_bufs[e], dn_states[e], prm['w_in_even'][e],
                                       prm['dw_w'][e], prm['dw_b'][e], prm['ln_a_g'][e], prm['ln_a_b'][e],
                                       prm['sc_w'][e], prm['a_log'][e], prm['dt_bias'][e],
                                       prm['dn_norm_g'][e], prm['w_out_even'][e], recurrent)
            new_conv.append(cb)
            new_qkv.append(qb)
            new_dn.append(st)
        else:
            o = l // 2
            y, pb = odd_mixer(h, pool_bufs[o], prm['w_in_odd'][o], prm['w_pool'][o], prm['b_pool'][o],
                              prm['pool_scale'][o], prm['w_out_odd'][o], start_pos)
            new_pool.append(pb)
        x = x + y
        h = rms_norm(x, prm['norm_xattn'][l])
        x = x + memory_cross_attention(h, mem_k[l], mem_v[l], prm['w_xq'][l], prm['w_xo'][l])
    y = rms_norm(x, prm['norm_final'])
    return y, jnp.stack(new_conv), jnp.stack(new_qkv), jnp.stack(new_dn), jnp.stack(new_pool)


def setup_inputs(seed: int = 0) -> dict:
    key = jax.random.key(seed)
    ks = jax.random.split(key, 40)

    def nrm(k, shape, scale):
        return jax.random.normal(k, shape, jnp.float32) * scale

    dt = jnp.exp(jax.random.uniform(ks[20], (N_EVEN, DN_HEADS), jnp.float32, np.log(1e-3), np.log(1e-1)))
    return {
        'x_prompt': nrm(ks[0], (BATCH, SEQ, D_MODEL), 1.0),
        'x_sample': nrm(ks[1], (DEC_BATCH, DEC_SEQ, D_MODEL), 1.0),
        'state_conv_a': nrm(ks[2], (N_EVEN, DEC_BATCH, CONV_K - 1, CONV_CH), 0.5),
        'state_qkv_conv': nrm(ks[3], (N_EVEN, DEC_BATCH, SC_K - 1, QKV_CH), 1.0),
        'state_delta': nrm(ks[4], (N_EVEN, DEC_BATCH, DN_HEADS, DN_HEAD_DIM, DN_HEAD_DIM), 0.1),
        'state_pool': nrm(ks[5], (N_ODD, DEC_BATCH, POOL_BUF, POOL_WIDTH), 1.0),
        'cache_mem_k': nrm(ks[6], (DEPTH, DEC_BATCH, N_MEM, XA_HEADS, XA_HEAD_DIM), 1.0),
        'cache_mem_v': nrm(ks[7], (DEPTH, DEC_BATCH, N_MEM, XA_HEADS, XA_HEAD_DIM), 1.0),
        'mem_prompt': nrm(ks[8], (BATCH, N_MEM, D_MODEL), 1.0),
        'norm_mix': 1.0 + nrm(ks[9], (DEPTH, D_MODEL), 0.02),
        'norm_xattn': 1.0 + nrm(ks[10], (DEPTH, D_MODEL), 0.02),
        'norm_final': 1.0 + nrm(ks[11], (D_MODEL,), 0.02),
        'w_in_even': nrm(ks[12], (N_EVEN, D_MODEL, EVEN_IN), D_MODEL ** -0.5),
        'w_out_even': nrm(ks[13], (N_EVEN, EVEN_MIX, D_MODEL), EVEN_MIX ** -0.5),
        'dw_w': nrm(ks[14], (N_EVEN, CONV_K, CONV_CH), CONV_K ** -0.5),
        'dw_b': nrm(ks[15], (N_EVEN, CONV_CH), 0.02),
        'ln_a_g': 1.0 + nrm(ks[16], (N_EVEN, CONV_CH), 0.02),
        'ln_a_b': nrm(ks[17], (N_EVEN, CONV_CH), 0.02),
        'sc_w': nrm(ks[18], (N_EVEN, SC_K, QKV_CH), SC_K ** -0.5),
        'a_log': jnp.log(jax.random.uniform(ks[19], (N_EVEN, DN_HEADS), jnp.float32, 1.0, 16.0)),
        'dt_bias': dt + jnp.log(-jnp.expm1(-dt)),
        'dn_norm_g': 1.0 + nrm(ks[21], (N_EVEN, DN_HEAD_DIM), 0.02),
        'w_in_odd': nrm(ks[22], (N_ODD, D_MODEL, 2 * POOL_WIDTH), D_MODEL ** -0.5),
        'w_pool': nrm(ks[23], (N_ODD, len(POOL_WINDOWS), POOL_GROUP, POOL_GROUP), POOL_GROUP ** -0.5),
        'b_pool': nrm(ks[24], (N_ODD, len(POOL_WINDOWS), POOL_GROUP), 0.02),
        'pool_scale': 1.0 + nrm(ks[25], (N_ODD, POOL_WIDTH), 0.02),
        'w_out_odd': nrm(ks[26], (N_ODD, POOL_WIDTH, D_MODEL), POOL_WIDTH ** -0.5),
        'w_xq': nrm(ks[27], (DEPTH, D_MODEL, XA_WIDTH), D_MODEL ** -0.5),
        'w_xk': nrm(ks[28], (DEPTH, D_MODEL, XA_WIDTH), D_MODEL ** -0.5),
        'w_xv': nrm(ks[29], (DEPTH, D_MODEL, XA_WIDTH), D_MODEL ** -0.5),
        'w_xo': nrm(ks[30], (DEPTH, XA_WIDTH, D_MODEL), XA_WIDTH ** -0.5),
    }


def reference(x_prompt, x_sample, state_conv_a, state_qkv_conv, state_delta, state_pool, cache_mem_k,
              cache_mem_v, mem_prompt, norm_mix, norm_xattn, norm_final, w_in_even, w_out_even, dw_w, dw_b,
              ln_a_g, ln_a_b, sc_w, a_log, dt_bias, dn_norm_g, w_in_odd, w_pool, b_pool, pool_scale,
              w_out_odd, w_xq, w_xk, w_xv, w_xo):
    prm = {'norm_mix': norm_mix, 'norm_xattn': norm_xattn, 'norm_final': norm_final,
           'w_in_even': w_in_even, 'w_out_even': w_out_even, 'dw_w': dw_w, 'dw_b': dw_b,
           'ln_a_g': ln_a_g, 'ln_a_b': ln_a_b, 'sc_w': sc_w, 'a_log': a_log, 'dt_bias': dt_bias,
           'dn_norm_g': dn_norm_g, 'w_in_odd': w_in_odd, 'w_pool': w_pool, 'b_pool': b_pool,
           'pool_scale': pool_scale, 'w_out_odd': w_out_odd, 'w_xq': w_xq, 'w_xo': w_xo}
    bp = x_prompt.shape[0]
    new_mem_k_p = jnp.einsum('bmd,lde->lbme', mem_prompt, w_xk).reshape(DEPTH, bp, N_MEM, XA_HEADS, XA_HEAD_DIM)
    new_mem_v_p = jnp.einsum('bmd,lde->lbme', mem_prompt, w_xv).reshape(DEPTH, bp, N_MEM, XA_HEADS, XA_HEAD_DIM)
    conv0 = jnp.zeros((N_EVEN, bp, CONV_K - 1, CONV_CH), x_prompt.dtype)
    qkv0 = jnp.zeros((N_EVEN, bp, SC_K - 1, QKV_CH), x_prompt.dtype)
    dn0 = jnp.zeros((N_EVEN, bp, DN_HEADS, DN_HEAD_DIM, DN_HEAD_DIM), state_delta.dtype)
    pool0 = jnp.zeros((N_ODD, bp, POOL_BUF, POOL_WIDTH), x_prompt.dtype)
    y_prompt, new_conv_a_p, new_qkv_conv_p, new_delta_p, new_pool_p = trunk(
        x_prompt, conv0, qkv0, dn0, pool0, new_mem_k_p, new_mem_v_p, prm, 0, False)
    y_sample, new_conv_a_s, new_qkv_conv_s, new_delta_s, new_pool_s = trunk(
        x_sample, state_conv_a, state_qkv_conv, state_delta, state_pool, cache_mem_k, cache_mem_v, prm,
        PAST_LEN, True)
    return (y_prompt, y_sample, new_conv_a_p, new_qkv_conv_p, new_delta_p, new_pool_p, new_mem_k_p,
            new_mem_v_p, new_conv_a_s, new_qkv_conv_s, new_delta_s, new_pool_s)
```

```python
import numpy as np
import concourse.bass as bass
import concourse.mybir as mybir
from concourse.bass_utils import run_bass_kernel_spmd

F32 = mybir.dt.float32
BF16 = mybir.dt.bfloat16
ALU = mybir.AluOpType
AF = mybir.ActivationFunctionType
AX = mybir.AxisListType

NCORES = 8
D = 1024
T = 2048
NMEM = 256
DEPTH = 2
NS = 16
EPS = 1e-6

COMPUTE = ("pe", "act", "dve", "pool")


class V:
    def __init__(self, ap, tname, rect, excl=False):
        self.ap = ap
        self.t = tname
        self.rect = rect
        self.excl = excl

    def rr(self, pat, **kw):
        return V(self.ap.rearrange(pat, **kw), self.t, self.rect, self.excl)

    def bc(self, shape):
        return V(self.ap.to_broadcast(list(shape)), self.t, self.rect, self.excl)

    def sub(self, *idx):
        return V(self.ap[idx], self.t, self.rect, self.excl)


_uid = [0]


class Tn:
    def __init__(self, nc, name, shape, dtype, psum=False, offset=None):
        _uid[0] += 1
        self.name = "%s_%d" % (name, _uid[0])
        self.shape = list(shape)
        self.dtype = dtype
        self.psum = psum
        self.esz = 4 if dtype == F32 else 2
        if psum:
            self.h = nc.alloc_psum_tensor(self.name, self.shape, dtype)
            self.off = 0
        elif offset is None:
            self.h = nc.alloc_sbuf_tensor(self.name, self.shape, dtype)
            self.off = None
        else:
            self.h = nc.alloc_sbuf_tensor_at(self.name, self.shape, dtype, offset=offset)
            self.off = offset
        st = [1] * len(shape)
        for i in range(len(shape) - 2, 0, -1):
            st[i] = st[i + 1] * shape[i + 1]
        self.st = st

    def __getitem__(self, idx):
        if not isinstance(idx, tuple):
            idx = (idx,)
        idx = tuple(idx) + (slice(None),) * (len(self.shape) - len(idx))
        lo = 0
        hi = 0
        p0, p1 = 0, self.shape[0]
        for d, (ix, n) in enumerate(zip(idx, self.shape)):
            if isinstance(ix, slice):
                a = 0 if ix.start is None else ix.start
                b = n if ix.stop is None else ix.stop
                assert ix.step in (None, 1)
            else:
                a, b = ix, ix + 1
            assert 0 <= a < b <= n, (self.name, idx)
            if d == 0:
                p0, p1 = a, b
            else:
                lo += a * self.st[d]
                hi += (b - 1) * self.st[d]
        hi += 1
        lo *= self.esz
        hi *= self.esz
        if self.psum:
            lo = (lo // 2048) * 2048
            hi = ((hi + 2047) // 2048) * 2048
            p0, p1 = 0, 128
            return V(self.h[idx], "ps", (p0, p1, lo, hi), True)
        if self.off is None:
            return V(self.h[idx], self.name, (p0, p1, lo, hi), False)
        return V(self.h[idx], "sb", (p0, p1, self.off + lo, self.off + hi), False)


def _overlap(a, b):
    return a[0] < b[1] and b[0] < a[1] and a[2] < b[3] and b[2] < a[3]


def _contains(outer, inner):
    return outer[0] <= inner[0] and inner[1] <= outer[1] and outer[2] <= inner[2] and inner[3] <= outer[3]


class Op:
    __slots__ = ("eng", "fn", "reads", "writes", "dma", "seq", "waits", "flag", "semkey", "semcnt")


class Prog:
    def __init__(self, nc):
        self.nc = nc
        self.ops = []
        self.acc = {}
        self.nseq = {e: 0 for e in ("pe", "act", "dve", "pool", "sp")}
        self.dmacnt = {}
        self.lastdma = {}
        self.by_eng = {e: [] for e in ("pe", "act", "dve", "pool", "sp")}
        self.flagged = {e: set() for e in COMPUTE}
        self.waited = {e: {} for e in ("pe", "act", "dve", "pool", "sp")}

    def _deps(self, op):
        deps = set()
        for v in op.reads:
            for (rect, kind, dep) in self.acc.get(v.t, ()):
                if (kind == "w" or v.excl) and _overlap(rect, v.rect):
                    deps.add(dep)
        for v in op.writes:
            for (rect, kind, dep) in self.acc.get(v.t, ()):
                if _overlap(rect, v.rect):
                    deps.add(dep)
        return deps

    def add(self, eng, fn, reads=(), writes=(), dma=None):
        op = Op()
        op.eng = eng
        op.fn = fn
        op.reads = [r for r in reads if r is not None]
        op.writes = [w for w in writes if w is not None]
        op.dma = dma
        op.flag = False
        deps = self._deps(op)
        self.nseq[eng] += 1
        op.seq = self.nseq[eng]
        if dma is not None:
            prev = self.lastdma.get(dma)
            if prev is not None:
                deps.add(prev)
            self.dmacnt[dma] = self.dmacnt.get(dma, 0) + 16
            op.semkey = dma
            op.semcnt = self.dmacnt[dma]
            me = ("dma", dma, op.semcnt)
            self.lastdma[dma] = me
        else:
            me = (eng, op.seq)
        need = {}
        for d in deps:
            if d[0] == "dma":
                key = ("dma", d[1])
                val = d[2]
            else:
                if d[0] == eng and dma is None:
                    pass
                key = d[0]
                val = d[1]
            if need.get(key, 0) < val:
                need[key] = val
        if dma is None and eng in need and eng == "pe":
            raw = 0
            for v in op.reads:
                for (rect, kind, dep) in self.acc.get(v.t, ()):
                    if kind == "w" and dep[0] == eng and _overlap(rect, v.rect):
                        raw = max(raw, dep[1])
            if raw:
                need[eng] = raw
            else:
                del need[eng]
        waits = []
        wd = self.waited[eng]
        for key, val in need.items():
            if wd.get(key, 0) >= val:
                continue
            wd[key] = val
            waits.append((key, val))
            if not (isinstance(key, tuple)):
                self.flagged[key].add(val)
        op.waits = waits
        for v in op.writes:
            lst = self.acc.setdefault(v.t, [])
            lst[:] = [a for a in lst if not _contains(v.rect, a[0])]
            lst.append((v.rect, "w", me))
        for v in op.reads:
            lst = self.acc.setdefault(v.t, [])
            if v.excl:
                lst[:] = [a for a in lst if not _contains(v.rect, a[0])]
                lst.append((v.rect, "x", me))
                continue
            if dma is None:
                lst[:] = [a for a in lst if not (a[1] == "r" and a[2][0] == eng and _contains(v.rect, a[0]))]
            lst.append((v.rect, "r", me))
        self.ops.append(op)
        self.by_eng[eng].append(op)
        return op

    def mm(self, out, lhsT, rhs, start=True, stop=True):
        self.add("pe", lambda e: e.matmul(out.ap, lhsT.ap, rhs.ap, start=start, stop=stop),
                 reads=[lhsT, rhs], writes=[out])

    def transpose(self, out, in_, ident):
        self.add("pe", lambda e: e.transpose(out.ap, in_.ap, ident.ap), reads=[in_, ident], writes=[out])

    def act(self, out, in_, func, bias=None, scale=1.0, accum=None, eng="act"):
        rd = [in_]
        kw = {}
        if isinstance(bias, V):
            rd.append(bias)
            kw["bias"] = bias.ap
        elif bias is not None:
            kw["bias"] = bias
        if isinstance(scale, V):
            rd.append(scale)
            kw["scale"] = scale.ap
        else:
            kw["scale"] = scale
        wr = [out]
        if accum is not None:
            wr.append(accum)
            kw["accum_out"] = accum.ap
        self.add("act", lambda e: e.activation(out.ap, in_.ap, func, **kw), reads=rd, writes=wr)

    def copy(self, eng, out, in_):
        if eng == "act":
            self.add("act", lambda e: e.copy(out.ap, in_.ap), reads=[in_], writes=[out])
        else:
            self.add(eng, lambda e: e.tensor_copy(out.ap, in_.ap), reads=[in_], writes=[out])

    def tt(self, eng, out, a, b, op):
        self.add(eng, lambda e: e.tensor_tensor(out.ap, a.ap, b.ap, op), reads=[a, b], writes=[out])

    def ts(self, eng, out, a, s1, s2, op0, op1=None, accum=None):
        rd = [a]
        s1a = s1.ap if isinstance(s1, V) else s1
        s2a = s2.ap if isinstance(s2, V) else s2
        if isinstance(s1, V):
            rd.append(s1)
        if isinstance(s2, V):
            rd.append(s2)
        wr = [out]
        kw = {}
        if accum is not None:
            wr.append(accum)
            kw["accum_out"] = accum.ap
        if op1 is None:
            self.add(eng, lambda e: e.tensor_scalar(out.ap, a.ap, s1a, None, op0, **kw), reads=rd, writes=wr)
        else:
            self.add(eng, lambda e: e.tensor_scalar(out.ap, a.ap, s1a, s2a, op0, op1, **kw), reads=rd, writes=wr)

    def stt(self, eng, out, a, s, b, op0, op1):
        rd = [a, b]
        sa = s.ap if isinstance(s, V) else s
        if isinstance(s, V):
            rd.append(s)
        self.add(eng, lambda e: e.scalar_tensor_tensor(out.ap, a.ap, sa, b.ap, op0, op1), reads=rd, writes=[out])

    def dma(self, q, out, in_, sem, reads=(), writes=()):
        o = out.ap if isinstance(out, V) else out
        i = in_.ap if isinstance(in_, V) else in_
        rd = list(reads) + ([in_] if isinstance(in_, V) else [])
        wr = list(writes) + ([out] if isinstance(out, V) else [])
        self.add(q, lambda e: e.dma_start(out=o, in_=i), reads=rd, writes=wr, dma=sem)

    def emit(self):
        nc = self.nc
        sems = {e: nc.alloc_semaphore("s_" + e) for e in COMPUTE}
        dsems = {k: nc.alloc_semaphore("d_" + str(k)) for k in self.dmacnt}
        rank = {}
        for e in COMPUTE:
            fl = sorted(self.flagged[e])
            rank[e] = {s: i + 1 for i, s in enumerate(fl)}
        engobj = {"pe": "tensor", "act": "scalar", "dve": "vector", "pool": "gpsimd", "sp": "sync"}

        def run(ename, eng):
            for op in self.by_eng[ename]:
                for key, val in op.waits:
                    if isinstance(key, tuple):
                        eng.wait_ge(dsems[key[1]], val)
                    else:
                        eng.wait_ge(sems[key], rank[key][val])
                ins = op.fn(eng)
                if op.dma is not None:
                    ins.then_inc(dsems[op.semkey], 16)
                elif op.seq in rank[ename]:
                    ins.then_inc(sems[ename], 1)
            if ename == "sp":
                for k, cnt in self.dmacnt.items():
                    eng.wait_ge(dsems[k], cnt)

        with nc.Block() as block:
            block.tensor(lambda e: run("pe", e))
            block.scalar(lambda e: run("act", e))
            block.vector(lambda e: run("dve", e))
            block.gpsimd(lambda e: run("pool", e))
            block.sync(lambda e: run("sp", e))


import os as _os


class Arena:
    def __init__(self, nc):
        self.nc = nc
        self.cur = 16512
        self.top = 229344
        self.limit = 229344
        self.peak = 0

    def alloc(self, name, shape, dtype):
        esz = 4 if dtype == F32 else 2
        n = esz
        for s in shape[1:]:
            n *= s
        off = self.cur
        self.cur = (off + n + 31) // 32 * 32
        self.peak = max(self.peak, self.cur)
        assert self.cur <= min(self.top, self.limit), ("SBUF overflow", name, self.cur, self.limit)
        return Tn(self.nc, name, shape, dtype, offset=off)

    def mark(self):
        return self.cur

    def release(self, m):
        self.cur = m


TT = int(_os.environ.get('K_TT', '1024'))
NMT = T // TT
NTT = TT // 512
NBLK = TT // 128
WG = 256
NEG = -32768.0
SAMPLE_BYTES = 66560
STAGE = int(_os.environ.get("K_STAGE", "99"))


def build_program():
    nc = bass.Bass("TRN2", target_bir_lowering=False)
    nc.allow_low_precision("bf16 matmul operands with fp32 accumulation (problem tolerance)")
    P = Prog(nc)
    A = Arena(nc)
    AS = Arena(nc)
    AS.cur = AS.top - SAMPLE_BYTES
    AS_BASE = AS.cur

    def din(name, shape):
        return nc.dram_tensor(name, list(shape), F32, kind="ExternalInput").ap()

    def dout(name, shape):
        return nc.dram_tensor(name, list(shape), F32, kind="ExternalOutput").ap()

    x_p = din("x_p", [T, D])
    mem_p = din("mem_p", [NMEM, D])
    w_xk = din("w_xk", [DEPTH, D, D])
    w_xv = din("w_xv", [DEPTH, D, D])
    w_xq = din("w_xq", [DEPTH, D, D])
    w_xo = din("w_xo", [DEPTH, D, D])
    w_in_even = din("w_in_even", [D, 3592])
    w_out_even = din("w_out_even", [D, D])
    w_in_odd = din("w_in_odd", [D, 2048])
    w_out_odd = din("w_out_odd", [D, D])
    w_pool = din("w_pool", [4, 256, 256])
    norm_mix = din("norm_mix", [2, D])
    norm_xattn = din("norm_xattn", [2, D])
    norm_final = din("norm_final", [1, D])
    dw_w = din("dw_w", [31, 512])
    dw_b = din("dw_b", [1, 512])
    ln_a_g = din("ln_a_g", [1, 512])
    ln_a_b = din("ln_a_b", [1, 512])
    sc_w = din("sc_w", [4, 1536])
    a_log = din("a_log", [1, 4])
    dt_bias = din("dt_bias", [1, 4])
    dn_norm_g = din("dn_norm_g", [1, 128])
    pool_scale = din("pool_scale", [1, D])
    b_pool = din("b_pool", [4, 256])

    o_y = dout("o_y", [T, D])
    o_conv = dout("o_conv", [30, 512])
    o_qkv = dout("o_qkv", [3, 1536])
    o_delta = dout("o_delta", [4, 128, 128])
    o_pool = dout("o_pool", [15, D])
    o_mem_k = dout("o_mem_k", [DEPTH, NMEM, D])
    o_mem_v = dout("o_mem_v", [DEPTH, NMEM, D])
    dbg = {}

    ps = Tn(nc, "ps", [128, 8, 512], F32, psum=True)
    bankc = [0]

    reserved = set()

    dom = ["p"]
    allowed = {"p": set(range(8)), "s": {4, 5, 6, 7}}
    bankcs = {"p": bankc, "s": [3]}

    def nb():
        bc_ = bankcs[dom[0]]
        while True:
            bc_[0] = (bc_[0] + 1) % 8
            if bc_[0] in allowed[dom[0]] and bc_[0] not in reserved:
                return bc_[0]

    evc = [0]

    def ev():
        evc[0] += 1
        return "dve" if evc[0] % 2 else "act"

    identF = A.alloc("identF", [128, 128], F32)
    identB = A.alloc("identB", [128, 128], BF16)
    onesF = A.alloc("onesF", [128, 128], F32)
    onesB = A.alloc("onesB", [128, 128], BF16)
    Mincl = A.alloc("Mincl", [128, 128], F32)
    Msame = A.alloc("Msame", [128, 128], F32)
    Mgt = A.alloc("Mgt", [128, 128], F32)
    NEGs = A.alloc("NEGs", [128, 128], F32)
    NEGiT = A.alloc("NEGiT", [128, 128], F32)
    sel = A.alloc("sel", [4, 4, 128], F32)
    NEGsB = A.alloc("NEGsB", [128, 128], BF16)
    NEGiTB = A.alloc("NEGiTB", [128, 128], BF16)
    pcol0 = A.alloc("pcol0", [128, 128], F32)
    pcol1 = A.alloc("pcol1", [128, 128], F32)
    pch = A.alloc("pch", [128, 160], F32)
    dtB = A.alloc("dtB", [128, NBLK, 4], F32)
    nAB = A.alloc("nAB", [128, NBLK, 4], F32)
    invc = A.alloc("invc", [128, 16], F32)

    def msel(t, cmp, fill, pattern, cm, base=0):
        P.add("pool", lambda e: e.affine_select(out=t.ap, in_=t.ap, compare_op=cmp, fill=fill, base=base,
                                                pattern=pattern, channel_multiplier=cm), reads=[t], writes=[t])

    def mset(t, val):
        P.add("pool", lambda e: e.memset(t.ap, val), writes=[t])

    mset(identF[:, :], 1.0)
    msel(identF[:, :], ALU.is_equal, 0.0, [[-1, 128]], 1)
    P.copy("pool", identB[:, :], identF[:, :])
    mset(onesF[:, :], 1.0)
    mset(onesB[:, :], 1.0)
    mset(Mincl[:, :], 1.0)
    msel(Mincl[:, :], ALU.is_ge, 0.0, [[1, 128]], -1)
    mset(Mincl[0:64, 64:128], 0.0)
    mset(Msame[:, :], 1.0)
    mset(Msame[0:64, 64:128], 0.0)
    mset(Msame[64:128, 0:64], 0.0)
    mset(Mgt[:, :], 1.0)
    msel(Mgt[:, :], ALU.is_gt, 0.0, [[-1, 128]], 1)
    mset(Mgt[64:128, 0:64], 0.0)
    mset(NEGs[:, :], 0.0)
    msel(NEGs[:, :], ALU.is_gt, NEG, [[-1, 128]], 1)
    mset(NEGs[64:128, 0:64], NEG)
    mset(NEGiT[:, :], 0.0)
    msel(NEGiT[:, :], ALU.is_ge, NEG, [[1, 128]], -1)
    mset(NEGiT[0:64, 64:128], NEG)
    P.copy("pool", NEGsB[:, :], NEGs[:, :])
    P.copy("pool", NEGiTB[:, :], NEGiT[:, :])
    mset(sel[:, :, :], 1.0)
    msel(sel[:, :, :], ALU.is_equal, 0.0, [[-1, 4], [0, 128]], 1)
    ind0 = Msame[:, 0:1]
    ind1 = Msame[:, 64:65]
    _bk = nb()
    P.mm(ps[:, _bk, 0:16], onesF[:, :], Mincl[:, 0:16], start=True, stop=True)
    P.add("dve", lambda e: e.reciprocal(invc[:, :].ap, ps[:, _bk, 0:16].ap), reads=[ps[:, _bk, 0:16]], writes=[invc[:, :]])

    m0 = A.mark()
    pst0 = A.alloc("pst0", [128, 128], F32)
    pst1 = A.alloc("pst1", [128, 128], F32)
    mset(pst0[:, :], 0.0)
    mset(pst1[:, :], 0.0)
    R_NM, R_NX, R_NF, R_DWB, R_LNG, R_LNB, R_SC, R_DNG, R_PSC, R_BP = 0, 16, 32, 40, 44, 48, 52, 100, 101, 109
    prm_loads = [
        (pst0, R_NM, 16, norm_mix.rearrange("l (c p) -> (l c) p", p=128)),
        (pst0, R_NX, 16, norm_xattn.rearrange("l (c p) -> (l c) p", p=128)),
        (pst0, R_NF, 8, norm_final.rearrange("l (c p) -> (l c) p", p=128)),
        (pst0, R_DWB, 4, dw_b.rearrange("l (c p) -> (l c) p", p=128)),
        (pst0, R_LNG, 4, ln_a_g.rearrange("l (c p) -> (l c) p", p=128)),
        (pst0, R_LNB, 4, ln_a_b.rearrange("l (c p) -> (l c) p", p=128)),
        (pst0, R_SC, 48, sc_w.rearrange("k (c p) -> (k c) p", p=128)),
        (pst0, R_DNG, 1, dn_norm_g),
        (pst0, R_PSC, 8, pool_scale.rearrange("l (c p) -> (l c) p", p=128)),
        (pst0, R_BP, 8, b_pool.rearrange("g (c p) -> (g c) p", p=128)),
        (pst1, 0, 124, dw_w.rearrange("k (c p) -> (k c) p", p=128)),
    ]
    for i, (dst, r0, n, src) in enumerate(prm_loads):
        P.dma("sp", dst[r0:r0 + n, :], src, sem="prm%d" % (i % 4))
    bk = nb()
    P.transpose(ps[:, bk, 0:128], pst0[:, :], identF[:, :])
    P.transpose(ps[:, bk, 128:256], pst1[:, :], identF[:, :])
    P.copy("dve", pcol0[:, :], ps[:, bk, 0:128])
    P.copy("dve", pcol1[:, :], ps[:, bk, 128:256])
    P.ts("dve", pch[:, 0:124], pcol1[:, 0:124], 0.5, None, ALU.mult)
    P.ts("dve", pch[:, 124:128], pcol0[:, R_LNG:R_LNG + 4], 0.5, None, ALU.mult)
    P.ts("dve", pch[:, 128:132], pcol0[:, R_LNB:R_LNB + 4], 0.5, None, ALU.mult)
    for b in range(NBLK):
        P.dma("sp", dtB[:, b, :], dt_bias[0].partition_broadcast(128), sem="prm%d" % (b % 4))
        P.dma("sp", nAB[:, b, :], a_log[0].partition_broadcast(128), sem="prm%d" % ((b + 1) % 4))
    P.act(nAB[:, :, :], nAB[:, :, :], AF.Exp)
    P.ts("dve", nAB[:, :, :], nAB[:, :, :], -1.0, None, ALU.mult)
    A.release(m0)

    def pc(r):
        return pcol0[:, r:r + 1]

    xsT = A.alloc("xsT", [128, 8, NS], F32)
    hsT = A.alloc("hsT", [128, 8, NS], BF16)
    xT = A.alloc("xT", [128, 8, T], F32)
    memT = A.alloc("memT", [128, 8, NMEM], BF16)
    KT = A.alloc("KT", [128, 8, NMEM], BF16)
    Vtok = A.alloc("Vtok", [128, 2, D], BF16)
    NW = int(_os.environ.get("K_NW", "3"))
    wbuf = [A.alloc("wbuf%d" % i, [128, 8, WG], BF16) for i in range(NW)]
    wctr = [0]
    Sst = A.alloc("Sst", [128, 4, 128], F32)
    Sbf = A.alloc("Sbf", [128, 4, 128], BF16)
    gluhalo = A.alloc("gluhalo", [128, 4, 32], BF16)
    qkvhalo = A.alloc("qkvhalo", [128, 12, 4], BF16)
    uhalo = A.alloc("uhalo", [128, 8, 16], F32)
    mset(Sst[:, :, :], 0.0)
    mset(Sbf[:, :, :], 0.0)
    mset(gluhalo[:, :, :], 0.0)
    mset(qkvhalo[:, :, :], 0.0)
    mset(uhalo[:, :, :], 0.0)

    def wload(src2d, col0, ncols):
        slot = wctr[0] % NW
        wctr[0] += 1
        wt = wbuf[slot]
        P.dma("pool", wt[:, :, 0:ncols], src2d[:, col0:col0 + ncols].rearrange("(c p) n -> p c n", p=128),
              sem="w%d" % slot)
        return wt

    swring = {"bufs": None, "ctr": 0}

    def wload_s(src2d, col0, ncols):
        slot = swring["ctr"] % 2
        swring["ctr"] += 1
        wt = swring["bufs"][slot]
        P.dma("pool", wt[:, :, 0:ncols], src2d[:, col0:col0 + ncols].rearrange("(c p) n -> p c n", p=128),
              sem="sw%d" % slot)
        return wt

    def mem_prep():
        m = A.mark()
        memtok = A.alloc("memtok", [128, 2, D], F32)
        P.dma("sp", memtok[:, :, :], mem_p.rearrange("(c p) d -> p c d", p=128), sem="ld0")
        for mc in range(2):
            for g in range(2):
                bank = nb()
                for j in range(4):
                    dc = g * 4 + j
                    P.transpose(ps[:, bank, j * 128:(j + 1) * 128], memtok[:, mc, dc * 128:(dc + 1) * 128], identF[:, :])
                P.copy(ev(), memT[:, g * 4:g * 4 + 4, mc * 128:(mc + 1) * 128],
                       ps[:, bank, :].rr("p (a b) -> p a b", a=4))
        A.release(m)

    def mem_kv(l):
        m = A.mark()
        kvst = [A.alloc("kvst%d" % i, [128, D], F32) for i in range(2)]
        for which, wsrc, odst in (("k", w_xk, o_mem_k), ("v", w_xv, o_mem_v)):
            for g in range(D // WG):
                wt = wload(wsrc[l], g * WG, WG)
                for mc in range(2):
                    bank = nb()
                    for dc in range(8):
                        P.mm(ps[:, bank, 0:WG], memT[:, dc, mc * 128:(mc + 1) * 128], wt[:, dc, 0:WG],
                             start=(dc == 0), stop=(dc == 7))
                    P.copy(ev(), kvst[mc][:, g * WG:(g + 1) * WG], ps[:, bank, 0:WG])
                if which == "k":
                    for j in range(WG // 128):
                        nch = g * (WG // 128) + j
                        bank2 = nb()
                        for dc in range(8):
                            P.mm(ps[:, bank2, 0:NMEM], wt[:, dc, j * 128:(j + 1) * 128], memT[:, dc, :],
                                 start=(dc == 0), stop=(dc == 7))
                        P.copy(ev(), KT[:, nch, :], ps[:, bank2, 0:NMEM])
            for mc in range(2):
                if which == "v":
                    P.copy("pool", Vtok[:, mc, :], kvst[mc][:, :])
                P.dma("sp", odst[l, mc * 128:(mc + 1) * 128, :], kvst[mc][:, :], sem="kvst%d" % mc)
        A.release(m)

    def load_x():
        m = A.mark()
        xs = [A.alloc("xs%d" % i, [128, D], F32) for i in range(2)]
        for b in range(T // 128):
            st = xs[b % 2]
            P.dma("sp", st[:, :], x_p[b * 128:(b + 1) * 128, :], sem="ld%d" % (b % 2))
            for g in range(2):
                bank = nb()
                for j in range(4):
                    kc = g * 4 + j
                    P.transpose(ps[:, bank, j * 128:(j + 1) * 128], st[:, kc * 128:(kc + 1) * 128], identF[:, :])
                P.copy(ev(), xT[:, g * 4:g * 4 + 4, b * 128:(b + 1) * 128], ps[:, bank, :].rr("p (a b) -> p a b", a=4))
        A.release(m)

    def rmsnorm(hT, t0, grow):
        m = A.mark()
        sq = [A.alloc("sq%d" % i, [128, 512], BF16) for i in range(2)]
        rs = A.alloc("rs", [128, 512], F32)
        for tt in range(NTT):
            c0 = t0 + tt * 512
            bank = nb()
            for kc in range(8):
                s = sq[kc % 2]
                P.act(s[:, :], xT[:, kc, c0:c0 + 512], AF.Square)
                P.mm(ps[:, bank, :], onesB[:, :], s[:, :], start=(kc == 0), stop=(kc == 7))
            P.act(rs[:, :], ps[:, bank, :], AF.Ln, bias=EPSC[:, 0:1], scale=1.0 / D)
            P.act(rs[:, :], rs[:, :], AF.Exp, scale=-0.5)
            for kc in range(8):
                P.stt("dve", hT[:, kc, tt * 512:(tt + 1) * 512], xT[:, kc, c0:c0 + 512], pc(grow + kc), rs[:, :],
                      ALU.mult, ALU.mult)
        A.release(m)

    EPSC = A.alloc("EPSC", [128, 4], F32)
    mset(EPSC[:, 0:1], EPS)
    mset(EPSC[:, 1:2], 4.0 * EPS)
    mset(EPSC[:, 2:3], 1.0)

    def proj(wt, wc, hT, banks):
        for kc in range(8):
            for tt in range(NTT):
                P.mm(ps[:, banks[tt], :], wt[:, kc, wc:wc + 128], hT[:, kc, tt * 512:(tt + 1) * 512],
                     start=(kc == 0), stop=(kc == 7))

    def psv(banks):
        assert banks[-1] == banks[0] + len(banks) - 1
        return ps[:, banks[0]:banks[0] + len(banks), :]

    def pair_banks():
        b = nb()
        while b % 2 or (b + 1) in reserved:
            b = nb()
        nb()
        return [b, b + 1]

    def two_banks():
        if NTT == 1:
            return [nb()]
        b = nb()
        while b % 2:
            b = nb()
        nb()
        return [b, b + 1]

    def out_proj_add(wsrc, inT_fn, t0, pumpn=0):
        for g in range(D // WG):
            wt = wload(wsrc, g * WG, WG)
            for j in range(WG // 128):
                n = g * (WG // 128) + j
                banks = two_banks()
                for kc in range(8):
                    for tt in range(NTT):
                        P.mm(ps[:, banks[tt], :], wt[:, kc, j * 128:(j + 1) * 128], inT_fn(kc, tt),
                             start=(kc == 0), stop=(kc == 7))
                P.tt("dve", xT[:, n, t0:t0 + TT].rr("p (a b) -> p a b", a=NTT), xT[:, n, t0:t0 + TT].rr("p (a b) -> p a b", a=NTT),
                     psv(banks), ALU.add)
                if pumpn:
                    pump(pumpn)

    def xattn(l, mt):
        t0 = mt * TT
        m = A.mark()
        win_begin()
        qx = A.alloc("qx", [128, 8, TT], BF16)
        mh_ = A.mark()
        hT = A.alloc("hT", [128, 8, TT], BF16)
        rmsnorm(hT, t0, R_NX + l * 8)
        for g in range(D // WG):
            wt = wload(w_xq[l], g * WG, WG)
            for j in range(WG // 128):
                n = g * (WG // 128) + j
                banks = two_banks()
                proj(wt, j * 128, hT, banks)
                P.act(qx[:, n, :].rr("p (a b) -> p a b", a=NTT), psv(banks), AF.Copy, scale=1.0 / 16.0)
                pump(PUMP)
        A.release(mh_)
        ox = A.alloc("ox", [128, 8, TT], BF16)
        Et = [A.alloc("Et%d" % i, [128, 512], BF16) for i in range(4)]
        rden = [A.alloc("rden%d" % i, [128, 512], F32) for i in range(2)]
        ei = 0
        for h in range(4):
            for tt in range(NTT):
                es = []
                for mc in range(2):
                    bank = nb()
                    for half in range(2):
                        P.mm(ps[:, bank, :], KT[:, h * 2 + half, mc * 128:(mc + 1) * 128],
                             qx[:, h * 2 + half, tt * 512:(tt + 1) * 512], start=(half == 0), stop=(half == 1))
                    E = Et[ei % 4]
                    ei += 1
                    P.act(E[:, :], ps[:, bank, :], AF.Exp)
                    es.append(E)
                pump(PUMP)
                bden = nb()
                for mc in range(2):
                    P.mm(ps[:, bden, :], onesB[:, :], es[mc][:, :], start=(mc == 0), stop=(mc == 1))
                rd = rden[(h * NTT + tt) % 2]
                P.act(rd[:, :], ps[:, bden, :], AF.Ln)
                P.act(rd[:, :], rd[:, :], AF.Exp, scale=-1.0)
                for dv in range(2):
                    bo = nb()
                    n = h * 2 + dv
                    for mc in range(2):
                        P.mm(ps[:, bo, :], Vtok[:, mc, n * 128:(n + 1) * 128], es[mc][:, :],
                             start=(mc == 0), stop=(mc == 1))
                    P.tt("dve", ox[:, n, tt * 512:(tt + 1) * 512], ps[:, bo, :], rd[:, :], ALU.mult)
                pump(PUMP)
        out_proj_add(w_xo[l], lambda kc, tt: ox[:, kc, tt * 512:(tt + 1) * 512], t0, pumpn=PUMP)
        win_end()
        A.release(m)

    def even_mixer(mt):
        t0 = mt * TT
        last = (mt == NMT - 1)
        m = A.mark()
        aoT = A.alloc("aoT", [128, 4, TT], BF16)
        G_BETA, G_G, G_GCS, G_GCL, G_EGC, G_EKD, G_NBEG, G_VBS, G_NBETA, G_TMP = range(10)
        mh = A.mark()
        hT = A.alloc("hT", [128, 8, TT], BF16)
        rmsnorm(hT, t0, R_NM + 0)

        ma = A.mark()
        gluT = A.alloc("gluT", [128, 4, 32 + TT], BF16)
        cbuf = A.alloc("cbuf", [128, 4, TT], F32)
        Dg = A.alloc("Dg", [128, 31, 128], BF16)
        tA = [A.alloc("tA%d" % i, [128, TT], F32) for i in range(2)]
        tB = [A.alloc("tB%d" % i, [128, 1024], F32) for i in range(2)]
        stat = [A.alloc("stat%d" % i, [128, 512], F32) for i in range(3)]
        sga = A.alloc("sga", [128, 4, TT], BF16)
        wv = {}
        for c in range(4):
            g = c // 2
            if c % 2 == 0:
                wv = {"val": wload(w_in_even, g * WG, WG), "gate": wload(w_in_even, 512 + g * WG, WG)}
            wc = (c % 2) * 128
            bv = two_banks()
            bg = two_banks()
            proj(wv["val"], wc, hT, bv)
            proj(wv["gate"], wc, hT, bg)
            t = tA[c % 2]
            P.act(t[:, :].rr("p (a b) -> p a b", a=NTT), psv(bg), AF.Tanh, scale=0.5)
            P.copy("pool", gluT[:, c, 0:32], gluhalo[:, c, :])
            P.stt("dve", gluT[:, c, 32:32 + TT].rr("p (a b) -> p a b", a=NTT), t[:, :].rr("p (a b) -> p a b", a=NTT),
                  1.0, psv(bv), ALU.add, ALU.mult)
            P.copy("pool", gluhalo[:, c, :], gluT[:, c, TT:TT + 32])
            for k in range(31):
                P.ts("dve", Dg[:, k, :], identB[:, :], pch[:, k * 4 + c:k * 4 + c + 1], None, ALU.mult)
            for tt in range(NTT):
                bank = nb()
                for k in range(31):
                    P.mm(ps[:, bank, :], Dg[:, k, :], gluT[:, c, 2 + tt * 512 + k:2 + tt * 512 + k + 512],
                         start=(k == 0), stop=(k == 30))
                P.act(cbuf[:, c, tt * 512:(tt + 1) * 512], ps[:, bank, :], AF.Identity, bias=pc(R_DWB + c))
            if c % 2 == 0:
                wv["ga"] = wload(w_in_even, 1024 + (c // 2) * WG, WG)
            bg2 = two_banks()
            proj(wv["ga"], (c % 2) * 128, hT, bg2)
            t = tA[c % 2]
            P.act(t[:, :].rr("p (a b) -> p a b", a=NTT), psv(bg2), AF.Tanh, scale=0.5)
            P.stt("dve", sga[:, c, :].rr("p (a b) -> p a b", a=NTT), t[:, :].rr("p (a b) -> p a b", a=NTT), 1.0, psv(bg2),
                  ALU.add, ALU.mult)
        if last:
            cst = A.alloc("cst", [128, 4, 32], F32)
            csto = A.alloc("csto", [32, 512], F32)
            P.act(cst[:, :, :], gluhalo[:, :, :], AF.Copy, scale=0.5)
            bank = nb()
            for c in range(4):
                P.transpose(ps[0:32, bank, c * 128:(c + 1) * 128], cst[:, c, :], identF[:, :])
            P.copy("dve", csto[:, :], ps[0:32, bank, :])
            P.dma("sp", o_conv[:, :], csto[2:32, :], sem="st0")
        wga = None
        for tt in range(NTT):
            b1, b2 = nb(), nb()
            for c in range(4):
                s = tB[c % 2]
                P.act(s[:, 0:512], cbuf[:, c, tt * 512:(tt + 1) * 512], AF.Square)
                P.mm(ps[:, b1, :], onesF[:, :], cbuf[:, c, tt * 512:(tt + 1) * 512], start=(c == 0), stop=(c == 3))
                P.mm(ps[:, b2, :], onesF[:, :], s[:, 0:512], start=(c == 0), stop=(c == 3))
            mean, msq, rstd = stat
            P.act(mean[:, :], ps[:, b1, :], AF.Copy, scale=1.0 / 512)
            P.tt("dve", msq[:, :], mean[:, :], mean[:, :], ALU.mult)
            P.stt("dve", rstd[:, :], ps[:, b2, :], 1.0 / 512, msq[:, :], ALU.mult, ALU.subtract)
            P.act(rstd[:, :], rstd[:, :], AF.Ln, bias=EPSC[:, 0:1])
            P.act(rstd[:, :], rstd[:, :], AF.Exp, scale=-0.5)
            for c in range(4):
                cv = cbuf[:, c, tt * 512:(tt + 1) * 512]
                P.tt("dve", cv, cv, mean[:, :], ALU.subtract)
                P.tt("dve", cv, cv, rstd[:, :], ALU.mult)
                P.act(cv, cv, AF.Identity, bias=pch[:, 128 + c:129 + c], scale=pch[:, 124 + c:125 + c])
                th = tB[c % 2]
                P.act(th[:, 512:1024], cv, AF.Tanh)
                P.stt("dve", cv, th[:, 512:1024], 1.0, cv, ALU.add, ALU.mult)
        for c in range(4):
            P.stt("dve", aoT[:, c, :], sga[:, c, :], 0.5, cbuf[:, c, :], ALU.mult, ALU.mult)
        A.release(mh)
        if STAGE <= 3:
            dbg["aoT"] = aoT
            A.release(m)
            return

        qT = A.alloc("qT", [128, 4, TT], BF16)
        qdT = A.alloc("qdT", [128, 4, TT], BF16)
        kT = A.alloc("kT", [128, 4, TT], BF16)
        vb = A.alloc("vb", [128, NBLK, 4, 128], BF16)
        kd = A.alloc("kd", [128, NBLK, 4, 128], BF16)
        zs2 = A.alloc("zs2", [128, 4, TT], BF16)
        gts = A.alloc("gts", [128, 12, NBLK, 4], F32)
        eglB = A.alloc("eglB", [128, NBLK, 2, 4], F32)
        egcF = A.alloc("egcF", [4, TT], F32)
        mh = A.mark()
        hT = A.alloc("hT", [128, 8, TT], BF16)
        rmsnorm(hT, t0, R_NM + 0)
        mb = A.mark()
        w8 = A.alloc("w8", [128, 8, 8], BF16)
        P.dma("pool", w8[:, :, :], w_in_even[:, 3584:3592].rearrange("(c p) n -> p c n", p=128), sem="w8")
        bank = nb()
        for b in range(NBLK):
            for part in range(2):
                for kc in range(8):
                    P.mm(ps[:, bank, part * NBLK * 4 + b * 4:part * NBLK * 4 + b * 4 + 4], hT[:, kc, b * 128:(b + 1) * 128],
                         w8[:, kc, part * 4:part * 4 + 4], start=(kc == 0), stop=(kc == 7))
        gl = A.alloc("gl", [128, 2, NBLK, 4], F32)
        P.copy("dve", gl[:, :, :, :], ps[:, bank, 0:NBLK * 8].rr("p (a b c) -> p a b c", a=2, b=NBLK))

        def G(i):
            return gts[:, i, :, :]
        P.act(G(G_TMP), gl[:, 0, :, :], AF.Tanh, scale=0.5)
        P.ts("dve", G(G_BETA), G(G_TMP), 0.5, 0.5, ALU.mult, ALU.add)
        P.tt("dve", G(G_TMP), gl[:, 1, :, :], dtB[:, :, :], ALU.add)
        P.act(G(G_TMP), G(G_TMP), AF.Exp)
        P.act(G(G_TMP), G(G_TMP), AF.Ln, bias=EPSC[:, 2:3])
        P.tt("dve", G(G_G), G(G_TMP), nAB[:, :, :], ALU.mult)
        bank = nb()
        gm = A.alloc("gm", [128, NBLK, 2, 4], F32)
        for b in range(NBLK):
            P.mm(ps[:, bank, b * 4:b * 4 + 4], Mincl[:, :], gts[:, G_G, b, :], start=True, stop=True)
            P.mm(ps[:, bank, NBLK * 4 + b * 4:NBLK * 4 + b * 4 + 4], Msame[:, :], gts[:, G_G, b, :], start=True, stop=True)
            P.ts("dve", gm[:, b, 0, :], gts[:, G_G, b, :], ind0, None, ALU.mult)
            P.ts("dve", gm[:, b, 1, :], gts[:, G_G, b, :], ind1, None, ALU.mult)
        P.copy("dve", gts[:, G_GCS:G_GCL + 1, :, :], ps[:, bank, 0:NBLK * 8].rr("p (a b c) -> p a b c", a=2, b=NBLK))
        bank = nb()
        P.mm(ps[:, bank, 0:NBLK * 8], onesF[:, :], gm[:, :, :, :].rr("p a b c -> p (a b c)"), start=True, stop=True)
        P.act(eglB[:, :, :, :].rr("p a b c -> p (a b c)"), ps[:, bank, 0:NBLK * 8], AF.Exp)
        P.act(G(G_EGC), G(G_GCS), AF.Exp)
        P.tt("dve", G(G_TMP), G(G_GCL), G(G_GCS), ALU.subtract)
        P.act(G(G_EKD), G(G_TMP), AF.Exp)
        P.stt("dve", G(G_NBEG), G(G_BETA), -1.0, G(G_EGC), ALU.mult, ALU.mult)
        P.ts("dve", G(G_VBS), G(G_BETA), 0.5, None, ALU.mult)
        P.ts("dve", G(G_NBETA), G(G_BETA), -1.0, None, ALU.mult)
        for g2 in range(TT // 512):
            bank = nb()
            for j in range(4):
                b = g2 * 4 + j
                P.transpose(ps[0:4, bank, j * 128:(j + 1) * 128], gts[:, G_EGC, b, :], identF[:, :])
            P.copy("dve", egcF[:, g2 * 512:(g2 + 1) * 512], ps[0:4, bank, :])

        pre = [A.alloc("pre%d" % i, [128, 4 + TT], BF16) for i in range(2)]
        Dq = [A.alloc("Dq%d" % i, [128, 4, 128], BF16) for i in range(2)]
        s2 = [A.alloc("s2%d" % i, [128, TT], F32) for i in range(2)]
        tq = [A.alloc("tq%d" % i, [128, TT], F32) for i in range(1)] * 2
        sqb = [A.alloc("sqb%d" % i, [128, 512], BF16) for i in range(2)]
        r1 = A.alloc("r1", [128, 512], F32)
        r2 = A.alloc("r2", [128, 512], F32)
        qst = A.alloc("qst", [128, 12, 4], F32)
        if last:
            mset(qst[:, :, :], 0.0)
        wqs = {}

        def st1(c12):
            if c12 % 2 == 0:
                wqs[c12 // 2] = wload(w_in_even, 1536 + (c12 // 2) * WG, WG)
            wq = wqs[c12 // 2]
            bq = two_banks()
            proj(wq, (c12 % 2) * 128, hT, bq)
            pr = pre[c12 % 2]
            P.copy("pool", pr[:, 0:4], qkvhalo[:, c12, :])
            P.copy("act", pr[:, 4:4 + TT].rr("p (a b) -> p a b", a=NTT), psv(bq))
            if last:
                P.copy("dve", qst[:, c12, 0:3], ps[:, bq[NTT - 1], 509:512])
            P.copy("pool", qkvhalo[:, c12, :], pr[:, TT:TT + 4])

        def st2(c12):
            pr = pre[c12 % 2]
            dq = Dq[c12 % 2]
            for k in range(4):
                P.ts("dve", dq[:, k, :], identB[:, :], pc(R_SC + k * 12 + c12), None, ALU.mult)
            bc_ = two_banks()
            for tt in range(NTT):
                for k in range(4):
                    P.mm(ps[:, bc_[tt], :], dq[:, k, :], pr[:, 1 + tt * 512 + k:1 + tt * 512 + k + 512],
                         start=(k == 0), stop=(k == 3))
            t = tq[c12 % 2]
            s = s2[c12 % 2]
            P.act(t[:, :].rr("p (a b) -> p a b", a=NTT), psv(bc_), AF.Tanh, scale=0.5)
            P.stt("dve", s[:, :].rr("p (a b) -> p a b", a=NTT), t[:, :].rr("p (a b) -> p a b", a=NTT), 1.0, psv(bc_),
                  ALU.add, ALU.mult)

        def st3(c12):
            kind, h = c12 // 4, c12 % 4
            s = s2[c12 % 2]
            for tt in range(NTT):
                sv = s[:, tt * 512:(tt + 1) * 512]
                if kind < 2:
                    sb_ = sqb[tt % 2]
                    P.act(sb_[:, :], sv, AF.Square)
                    bank = nb()
                    P.mm(ps[:, bank, :], onesB[:, :], sb_[:, :], start=True, stop=True)
                    P.act(r1[:, :], ps[:, bank, :], AF.Ln, bias=EPSC[:, 1:2])
                    P.act(r1[:, :], r1[:, :], AF.Exp, scale=-0.5)
                if kind == 0:
                    P.stt("dve", qT[:, h, tt * 512:(tt + 1) * 512], sv, 128.0 ** -0.5, r1[:, :], ALU.mult, ALU.mult)
                    bank = nb()
                    P.mm(ps[:, bank, :], sel[:, h, :], egcF[:, tt * 512:(tt + 1) * 512], start=True, stop=True)
                    P.tt("dve", r2[:, :], r1[:, :], ps[:, bank, :], ALU.mult)
                    P.stt("dve", qdT[:, h, tt * 512:(tt + 1) * 512], sv, 128.0 ** -0.5, r2[:, :], ALU.mult, ALU.mult)
                elif kind == 1:
                    P.tt("dve", sv, sv, r1[:, :], ALU.mult)
                    P.copy("act", kT[:, h, tt * 512:(tt + 1) * 512], sv)
                if kind >= 1:
                    bank = nb()
                    for j in range(4):
                        P.transpose(ps[:, bank, j * 128:(j + 1) * 128], s[:, tt * 512 + j * 128:tt * 512 + (j + 1) * 128],
                                    identF[:, :])
                    dst = kd if kind == 1 else vb
                    gsc4 = gts[:, G_EKD if kind == 1 else G_VBS, tt * 4:(tt + 1) * 4, h]
                    P.tt("dve", dst[:, tt * 4:(tt + 1) * 4, h, :], ps[:, bank, :].rr("p (a b) -> p a b", a=4),
                         V(gsc4.ap.unsqueeze(2).to_broadcast([128, 4, 128]), gsc4.t, gsc4.rect), ALU.mult)

        for step in range(12 + 2):
            if step < 12:
                st1(step)
            if 0 <= step - 1 < 12:
                st2(step - 1)
            if 0 <= step - 2 < 12:
                st3(step - 2)
        if last:
            qsto = A.alloc("qsto", [4, 512], F32)
            for g3 in range(3):
                bank = nb()
                for j in range(4):
                    P.transpose(ps[0:4, bank, j * 128:(j + 1) * 128], qst[:, g3 * 4 + j, :], identF[:, :])
                P.copy("dve", qsto[:, :], ps[0:4, bank, :])
                P.dma("sp", o_qkv[:, g3 * 512:(g3 + 1) * 512], qsto[0:3, :], sem="st1")
        wz = None
        for c in range(4):
            if c % 2 == 0:
                wz = wload(w_in_even, 3072 + (c // 2) * WG, WG)
            bz = two_banks()
            proj(wz, (c % 2) * 128, hT, bz)
            t = tq[c % 2]
            P.act(t[:, :].rr("p (a b) -> p a b", a=NTT), psv(bz), AF.Tanh, scale=0.5)
            P.stt("dve", zs2[:, c, :].rr("p (a b) -> p a b", a=NTT), t[:, :].rr("p (a b) -> p a b", a=NTT), 1.0, psv(bz),
                  ALU.add, ALU.mult)
        A.release(mh)

        TTm = A.alloc("TTm", [128, NBLK, 4, 128], BF16)
        QKm = A.alloc("QKm", [128, NBLK, 4, 128], BF16)
        mprep = A.mark()
        Gm = [A.alloc("Gm%d" % i, [128, 128], F32) for i in range(4)]
        Es = [A.alloc("Es%d" % i, [128, 128], F32) for i in range(4)]
        Ei = [A.alloc("Ei%d" % i, [128, 128], F32) for i in range(4)]
        Mb = [A.alloc("Mb%d" % i, [128, 4, 128], F32) for i in range(2)]
        MTb = [A.alloc("MTb%d" % i, [128, 4, 128], F32) for i in range(2)]
        Xb = [A.alloc("Xb%d" % i, [128, 4, 128], F32) for i in range(2)]
        for b in range(NBLK):
            cs = slice(b * 128, (b + 1) * 128)
            for h in range(4):
                P.ts("pool", Gm[h][:, :], Mgt[:, :], gts[:, G_G, b, h:h + 1], None, ALU.mult)
            b1s = [nb() for _ in range(4)]
            b2s = [nb() for _ in range(4)]
            for h in range(4):
                P.mm(ps[:, b2s[h], 0:128], kT[:, h, cs], kT[:, h, cs], start=True, stop=True)
                P.mm(ps[:, b2s[h], 128:256], kT[:, h, cs], qT[:, h, cs], start=True, stop=True)
            for h in range(4):
                P.mm(ps[:, b1s[h], 0:128], Mincl[:, :], Gm[h][:, :], start=True, stop=False)
                P.mm(ps[:, b1s[h], 0:128], identB[:, :], NEGsB[:, :], start=False, stop=True)
                P.mm(ps[:, b1s[h], 128:256], Gm[h][:, :], Mincl[:, :], start=True, stop=False)
                P.mm(ps[:, b1s[h], 128:256], identB[:, :], NEGiTB[:, :], start=False, stop=True)
            for h in range(4):
                P.act(Es[h][:, :], ps[:, b1s[h], 0:128], AF.Exp)
                P.act(Ei[h][:, :], ps[:, b1s[h], 128:256], AF.Exp)
            for h in range(4):
                P.stt("dve", MTb[0][:, h, :], ps[:, b2s[h], 0:128], gts[:, G_NBETA, b, h:h + 1], Es[h][:, :], ALU.mult, ALU.mult)
                P.tt("dve", QKm[:, b, h, :], ps[:, b2s[h], 128:256], Ei[h][:, :], ALU.mult)
            bank = nb()
            for h in range(4):
                P.transpose(ps[:, bank, h * 128:(h + 1) * 128], MTb[0][:, h, :], identF[:, :])
            P.copy("act", Mb[0][:, :, :], ps[:, bank, :].rr("p (a b) -> p a b", a=4))
            for h in range(4):
                P.tt("pool", Xb[0][:, h, :], Mb[0][:, h, :], identF[:, :], ALU.add)
            cur = 0
            for k in range(1, 6):
                nxt = 1 - cur
                bm, bmt, bx = nb(), nb(), nb()
                for h in range(4):
                    hs = slice(h * 128, (h + 1) * 128)
                    if k < 5:
                        P.mm(ps[:, bm, hs], MTb[cur][:, h, :], Mb[cur][:, h, :], start=True, stop=True)
                    P.mm(ps[:, bmt, hs], Mb[cur][:, h, :], MTb[cur][:, h, :], start=True, stop=True)
                if k < 5:
                    P.copy("act", Mb[nxt][:, :, :], ps[:, bm, :].rr("p (a b) -> p a b", a=4))
                P.copy("dve", MTb[nxt][:, :, :], ps[:, bmt, :].rr("p (a b) -> p a b", a=4))
                for h in range(4):
                    hs = slice(h * 128, (h + 1) * 128)
                    P.mm(ps[:, bx, hs], MTb[nxt][:, h, :], Xb[cur][:, h, :], start=True, stop=True)
                if k < 5:
                    P.tt("dve", Xb[nxt][:, :, :], Xb[cur][:, :, :], ps[:, bx, :].rr("p (a b) -> p a b", a=4), ALU.add)
                else:
                    P.tt("dve", TTm[:, b, :, :], Xb[cur][:, :, :], ps[:, bx, :].rr("p (a b) -> p a b", a=4), ALU.add)
                cur = nxt

        A.release(mprep)
        bT = qT
        R4 = A.alloc("R4", [128, 4, 128], BF16)
        vn4 = A.alloc("vn4", [128, 4, 128], BF16)
        of = [A.alloc("of%d" % i, [128, 512], F32) for i in range(2)]
        osq = [A.alloc("osq%d" % i, [128, 512], BF16) for i in range(2)]
        orr = [A.alloc("orr%d" % i, [128, 512], F32) for i in range(2)]
        obank = {}
        for tt in range(NTT):
            for h in range(4):
                obank[h] = h
                reserved.add(h)
            for cc in range(8):
                c = tt * 8 + cc
                b, par = c // 2, c % 2
                r0 = par * 64
                tok = slice(c * 64, (c + 1) * 64)
                bA, bB, bC = nb(), nb(), nb()
                for h in range(4):
                    P.mm(ps[r0:r0 + 64, bA, h * 128:(h + 1) * 128], kT[:, h, tok], Sbf[:, h, :], start=True, stop=True)
                for h in range(4):
                    P.mm(ps[:, obank[h], cc * 64:(cc + 1) * 64], Sbf[:, h, :], qdT[:, h, tok], start=True, stop=False)
                for h in range(4):
                    P.stt("dve", R4[r0:r0 + 64, h, :], ps[r0:r0 + 64, bA, h * 128:(h + 1) * 128],
                          gts[r0:r0 + 64, G_NBEG, b, h:h + 1], vb[r0:r0 + 64, b, h, :], ALU.mult, ALU.add)
                for h in range(4):
                    P.mm(ps[r0:r0 + 64, bB, h * 128:(h + 1) * 128], TTm[r0:r0 + 64, b, h, r0:r0 + 64], R4[r0:r0 + 64, h, :],
                         start=True, stop=True)
                P.copy("act", vn4[r0:r0 + 64, :, :], ps[r0:r0 + 64, bB, :].rr("p (a b) -> p a b", a=4))
                for h in range(4):
                    P.mm(ps[:, obank[h], cc * 64:(cc + 1) * 64], vn4[r0:r0 + 64, h, :], QKm[r0:r0 + 64, b, h, r0:r0 + 64],
                         start=False, stop=True)
                for h in range(4):
                    P.mm(ps[:, bC, h * 128:(h + 1) * 128], kd[r0:r0 + 64, b, h, :], vn4[r0:r0 + 64, h, :], start=True, stop=True)
                for h in range(4):
                    P.stt("dve", Sst[:, h, :], Sst[:, h, :], eglB[:, b, par, h:h + 1], ps[:, bC, h * 128:(h + 1) * 128],
                          ALU.mult, ALU.add)
                P.copy("act", Sbf[:, :, :], Sst[:, :, :])
            for h in range(4):
                o_ = of[h % 2]
                P.copy("act", o_[:, :], ps[:, obank[h], :])
                sq_ = osq[h % 2]
                P.act(sq_[:, :], o_[:, :], AF.Square)
                bank = nb()
                P.mm(ps[:, bank, :], onesB[:, :], sq_[:, :], start=True, stop=True)
                rr_ = orr[h % 2]
                P.act(rr_[:, :], ps[:, bank, :], AF.Ln, bias=EPSC[:, 0:1], scale=1.0 / 128)
                P.act(rr_[:, :], rr_[:, :], AF.Exp, scale=-0.5)
                P.stt("dve", o_[:, :], o_[:, :], pc(R_DNG), rr_[:, :], ALU.mult, ALU.mult)
                P.stt("dve", bT[:, h, tt * 512:(tt + 1) * 512], o_[:, :], 0.5, zs2[:, h, tt * 512:(tt + 1) * 512], ALU.mult, ALU.mult)
            reserved.clear()
        if last:
            P.dma("sp", o_delta.rearrange("h d e -> d h e"), Sst[:, :, :], sem="st2")
        if STAGE <= 4:
            A.release(m)
            return
        out_proj_add(w_out_even, lambda kc, tt: (aoT if kc < 4 else bT)[:, kc % 4, tt * 512:(tt + 1) * 512], t0)
        A.release(m)

    def odd_mixer(mt):
        t0 = mt * TT
        last = (mt == NMT - 1)
        m = A.mark()
        hT = A.alloc("hT", [128, 8, TT], BF16)
        rmsnorm(hT, t0, R_NM + 8)
        plT = A.alloc("plT", [128, 8, TT], BF16)
        ub = [A.alloc("ub%d" % i, [128, 16 + TT], F32) for i in range(2)]
        sa = A.alloc("sa", [128, 16 + TT], F32)
        sb2 = A.alloc("sb2", [128, 16 + TT], F32)
        wu = None
        for c in range(8):
            if c % 2 == 0:
                wu = wload(w_in_odd, (c // 2) * WG, WG)
            bu = two_banks()
            proj(wu, (c % 2) * 128, hT, bu)
            u = ub[c % 2]
            P.copy("pool", u[:, 0:16], uhalo[:, c, :])
            P.copy("act", u[:, 16:16 + TT].rr("p (a b) -> p a b", a=NTT), psv(bu))
            P.copy("pool", uhalo[:, c, :], u[:, TT:TT + 16])
            gi = c // 2
            win = 2 << gi
            src = u
            sh, lo, bi = 1, 1, 0
            bufs = [sa, sb2]
            while sh < win:
                dst = bufs[bi]
                bi = 1 - bi
                P.tt("dve", dst[:, lo:16 + TT], src[:, lo:16 + TT], src[:, lo - sh:16 + TT - sh], ALU.add)
                src = dst
                sh *= 2
                lo = 2 * lo + 1
            P.stt("dve", plT[:, c, :], src[:, 16:16 + TT], 1.0 / win, u[:, 16:16 + TT], ALU.mult, ALU.subtract)
            if mt == 0:
                fx = sa if src is sb2 else sb2
                P.tt("dve", fx[:, 0:win - 1], src[:, 16:16 + win - 1], invc[:, 0:win - 1], ALU.mult)
                P.tt("dve", plT[:, c, 0:win - 1], fx[:, 0:win - 1], u[:, 16:16 + win - 1], ALU.subtract)
        if last:
            pst = A.alloc("pst", [128, 8, 16], F32)
            psto = A.alloc("psto", [16, D], F32)
            P.copy("dve", pst[:, :, :], uhalo[:, :, :])
            for g2 in range(2):
                bank = nb()
                for j in range(4):
                    P.transpose(ps[0:16, bank, j * 128:(j + 1) * 128], pst[:, g2 * 4 + j, :], identF[:, :])
                P.copy("dve", psto[:, g2 * 512:(g2 + 1) * 512], ps[0:16, bank, :])
            P.dma("sp", o_pool[:, :], psto[1:16, :], sem="st3")
        zT = A.alloc("zT", [128, 8, TT], BF16)
        wp = A.alloc("wp", [128, 4, 2, 256], BF16)
        P.dma("pool", wp[:, :, :, :], w_pool.rearrange("g (c p) d -> p g c d", p=128), sem="wp")
        tz = [A.alloc("tz%d" % i, [128, TT], F32) for i in range(2)]
        tg = [A.alloc("tg%d" % i, [128, TT], F32) for i in range(2)]
        wgt = None
        for n in range(8):
            gi, half = n // 2, n % 2
            bz = two_banks()
            for cc in range(2):
                for tt in range(NTT):
                    P.mm(ps[:, bz[tt], :], wp[:, gi, cc, half * 128:(half + 1) * 128], plT[:, gi * 2 + cc, tt * 512:(tt + 1) * 512],
                         start=(cc == 0), stop=(cc == 1))
            z_ = tz[n % 2]
            P.ts("dve", z_[:, :].rr("p (a b) -> p a b", a=NTT), psv(bz), pc(R_BP + n), pc(R_PSC + n), ALU.add, ALU.mult)
            if n % 2 == 0:
                wgt = wload(w_in_odd, 1024 + (n // 2) * WG, WG)
            bg = two_banks()
            proj(wgt, (n % 2) * 128, hT, bg)
            t = tg[n % 2]
            P.act(t[:, :].rr("p (a b) -> p a b", a=NTT), psv(bg), AF.Tanh, scale=0.5)
            P.stt("dve", t[:, :].rr("p (a b) -> p a b", a=NTT), t[:, :].rr("p (a b) -> p a b", a=NTT), 1.0, psv(bg),
                  ALU.add, ALU.mult)
            P.stt("dve", zT[:, n, :], t[:, :], 0.5, z_[:, :], ALU.mult, ALU.mult)
        out_proj_add(w_out_odd, lambda kc, tt: zT[:, kc, tt * 512:(tt + 1) * 512], t0)
        A.release(m)

    def final_out():
        m = A.mark()
        sq = [A.alloc("fsq%d" % i, [128, 512], BF16) for i in range(2)]
        rs = A.alloc("frs", [128, 512], F32)
        yf = [A.alloc("yf%d" % i, [128, 512], F32) for i in range(2)]
        yst = A.alloc("yst", [128, 4, D], F32)
        for tt in range(T // 512):
            c0 = tt * 512
            bank = nb()
            for kc in range(8):
                s = sq[kc % 2]
                P.act(s[:, :], xT[:, kc, c0:c0 + 512], AF.Square)
                P.mm(ps[:, bank, :], onesB[:, :], s[:, :], start=(kc == 0), stop=(kc == 7))
            P.act(rs[:, :], ps[:, bank, :], AF.Ln, bias=EPSC[:, 0:1], scale=1.0 / D)
            P.act(rs[:, :], rs[:, :], AF.Exp, scale=-0.5)
            for kc in range(8):
                y_ = yf[kc % 2]
                P.stt("dve", y_[:, :], xT[:, kc, c0:c0 + 512], pc(R_NF + kc), rs[:, :], ALU.mult, ALU.mult)
                bank2 = nb()
                for j in range(4):
                    P.transpose(ps[:, bank2, j * 128:(j + 1) * 128], y_[:, j * 128:(j + 1) * 128], identF[:, :])
                P.copy(ev(), yst[:, :, kc * 128:(kc + 1) * 128], ps[:, bank2, :].rr("p (a b) -> p a b", a=4))
            for j in range(4):
                P.dma("sp", o_y[c0 + j * 128:c0 + (j + 1) * 128, :], yst[:, j, :], sem="sty%d" % j)
            pump(PUMP)
        A.release(m)

    x_s = din("x_s", [NS, D])
    st_conv = din("st_conv", [NS, 30, 512])
    st_qkv = din("st_qkv", [NS, 3, 1536])
    st_delta = din("st_delta", [NS, 4, 128, 128])
    st_pool = din("st_pool", [NS, 15, D])
    c_k = din("c_k", [DEPTH, NS, NMEM, D])
    c_v = din("c_v", [DEPTH, NS, NMEM, D])
    o_ys = dout("o_ys", [NS, D])
    o_conv_s = dout("o_conv_s", [NS, 30, 512])
    o_qkv_s = dout("o_qkv_s", [NS, 3, 1536])
    o_delta_s = dout("o_delta_s", [NS, 4, 128, 128])
    o_pool_s = dout("o_pool_s", [NS, 15, D])

    def bcv(v, axis, shape):
        return V(v.ap.unsqueeze(axis).to_broadcast(list(shape)), v.t, v.rect, v.excl)

    def recip(eng_view):
        P.add("dve", lambda e, t=eng_view: e.reciprocal(t.ap, t.ap), reads=[eng_view], writes=[eng_view])

    def reduce_x(out, in_):
        P.add("dve", lambda e: e.tensor_reduce(out.ap, in_.ap, AX.X, ALU.add), reads=[in_], writes=[out])

    def sample_path():
        eye16 = identF[0:NS, 0:NS]
        eye4 = identF[0:4, 0:4]

        def ring():
            swring["bufs"] = [AS.alloc("swb%d" % i, [128, 8, WG], BF16) for i in range(2)]

        def rmsnorm_s(grow, f32out=None):
            m = AS.mark()
            sq = AS.alloc("ssq", [128, 8, NS], BF16)
            rs = AS.alloc("srs", [128, NS], F32)
            P.act(sq[:, :, :], xsT[:, :, :], AF.Square)
            bank = nb()
            for kc in range(8):
                P.mm(ps[:, bank, 0:NS], onesB[:, :], sq[:, kc, :], start=(kc == 0), stop=(kc == 7))
            P.act(rs[:, :], ps[:, bank, 0:NS], AF.Sqrt, bias=EPSC[:, 0:1], scale=1.0 / D)
            recip(rs[:, :])
            dst = hsT if f32out is None else f32out
            for kc in range(8):
                P.stt("dve", dst[:, kc, :], xsT[:, kc, :], pc(grow + kc), rs[:, :], ALU.mult, ALU.mult)
            AS.release(m)

        def s_outproj(wsrc, in_fn):
            bank = nb()
            reserved.add(bank)
            for g in range(D // WG):
                wt = wload_s(wsrc, g * WG, WG)
                for j in range(WG // 128):
                    n = g * (WG // 128) + j
                    for kc in range(8):
                        P.mm(ps[:, bank, n * NS:(n + 1) * NS], wt[:, kc, j * 128:(j + 1) * 128], in_fn(kc),
                             start=(kc == 0), stop=(kc == 7))
                yield
            reserved.discard(bank)
            P.tt("dve", xsT[:, :, :], xsT[:, :, :], ps[:, bank, 0:8 * NS].rr("p (a b) -> p a b", a=8), ALU.add)

        m_seg = AS.mark()
        ring()
        mx_ = AS.mark()
        xst = AS.alloc("xst", [NS, D], F32)
        P.dma("sp", xst[:, :], x_s, sem="sld0")
        bank = nb()
        for kc in range(8):
            P.transpose(ps[:, bank, kc * NS:(kc + 1) * NS], xst[:, kc * 128:(kc + 1) * 128], eye16)
        P.copy("dve", xsT[:, :, :], ps[:, bank, 0:8 * NS].rr("p (a b) -> p a b", a=8))
        AS.release(mx_)
        rmsnorm_s(R_NM + 0)
        yield
        pF = AS.alloc("pF", [128, 16, NS], F32)
        qkvS = AS.alloc("qkvS", [NS, 1536], F32)
        gls = AS.alloc("gls", [NS, 8], F32)
        aoS = AS.alloc("aoS", [128, 4, NS], BF16)
        bS = AS.alloc("bS", [128, 4, NS], BF16)
        w8s = AS.alloc("w8s", [128, 8, 8], BF16)
        BF_, BQ_ = 4, 5
        reserved.update([BF_, BQ_])
        P.dma("pool", w8s[:, :, :], w_in_even[:, 3584:3592].rearrange("(c p) n -> p c n", p=128), sem="sw8")
        for g in range(14):
            col0 = g * WG
            wt = wload_s(w_in_even, col0, WG)
            if col0 < 1536 or col0 >= 3072:
                for j in range(2):
                    ci = (col0 // 128 + j) if col0 < 1536 else (12 + (col0 - 3072) // 128 + j)
                    for kc in range(8):
                        P.mm(ps[:, BF_, ci * NS:(ci + 1) * NS], wt[:, kc, j * 128:(j + 1) * 128], hsT[:, kc, :],
                             start=(kc == 0), stop=(kc == 7))
            else:
                gq = (col0 - 1536) // WG
                for kc in range(8):
                    P.mm(ps[0:NS, BQ_, (gq % 2) * WG:(gq % 2 + 1) * WG], hsT[:, kc, :], wt[:, kc, 0:WG],
                         start=(kc == 0), stop=(kc == 7))
                if gq % 2 == 1:
                    P.copy("act", qkvS[:, (gq // 2) * 512:(gq // 2 + 1) * 512], ps[0:NS, BQ_, :])
            yield
        for kc in range(8):
            P.mm(ps[0:NS, BQ_, 0:8], hsT[:, kc, :], w8s[:, kc, :], start=(kc == 0), stop=(kc == 7))
        P.copy("dve", pF[:, :, :], ps[:, BF_, 0:16 * NS].rr("p (a b) -> p a b", a=16))
        P.copy("dve", gls[:, :], ps[0:NS, BQ_, 0:8])
        reserved.discard(BF_)
        reserved.discard(BQ_)
        P.dma("sp", o_conv_s[:, 0:29, :], st_conv[:, 1:30, :], sem="sst0")
        P.dma("sp", o_qkv_s[:, 0:2, :], st_qkv[:, 1:3, :], sem="sst1")
        P.dma("sp", o_qkv_s[:, 2, :], qkvS[:, :], sem="sst2")
        yield

        m = AS.mark()
        gluS = AS.alloc("gluS", [128, 4, NS], F32)
        tS = AS.alloc("tS", [128, 4, NS], F32)
        P.act(tS[:, :, :], pF[:, 4:8, :], AF.Tanh, scale=0.5)
        P.ts("dve", tS[:, :, :], tS[:, :, :], 0.5, 0.5, ALU.mult, ALU.add)
        P.tt("dve", gluS[:, :, :], tS[:, :, :], pF[:, 0:4, :], ALU.mult)
        gst = AS.alloc("gst", [NS, 512], F32)
        bank = nb()
        for c in range(4):
            P.transpose(ps[0:NS, bank, c * 128:(c + 1) * 128], gluS[:, c, :], identF[:, :])
        P.copy("act", gst[:, :], ps[0:NS, bank, :])
        P.dma("sp", o_conv_s[:, 29, :], gst[:, :], sem="sst3")
        HS = NS // 2
        stc = AS.alloc("stc", [30, HS, 512], F32)
        cst = AS.alloc("cst", [128, NS, 4, 32], F32)
        prod = AS.alloc("prod", [128, NS, 4, 31], F32)
        convS = AS.alloc("convS", [128, NS, 4], F32)
        cS = AS.alloc("cS", [128, 4, NS], F32)
        for hf in range(2):
            P.dma("sp", stc[:, :, :], st_conv[hf * HS:(hf + 1) * HS].rearrange("s k c -> k s c"), sem="sld1")
            for s4 in range(HS // 4):
                bank = nb()
                for si in range(4):
                    for c in range(4):
                        slot = si * 4 + c
                        P.transpose(ps[:, bank, slot * 32:slot * 32 + 30], stc[:, s4 * 4 + si, c * 128:(c + 1) * 128],
                                    identF[0:30, 0:30])
                s0 = hf * HS + s4 * 4
                P.copy(ev(), cst[:, s0:s0 + 4, :, 0:30],
                       ps[:, bank, :].rr("p (a b c) -> p a b c", a=4, b=4).sub(slice(None), slice(None), slice(None), slice(0, 30)))
                yield
        P.copy("dve", cst[:, :, :, 30], gluS[:, :, :].rr("p c s -> p s c"))
        wv = bcv(pcol1[:, 0:124].rr("p (k c) -> p c k", c=4), 1, [128, NS, 4, 31])
        P.tt("dve", prod[:, :, :, :], cst[:, :, :, 0:31], wv, ALU.mult)
        reduce_x(convS[:, :, :], prod[:, :, :, :])
        for c in range(4):
            P.ts("dve", cS[:, c, :], convS[:, :, c], pc(R_DWB + c), None, ALU.add)
        yield
        sqS = AS.alloc("sqS", [128, 4, NS], F32)
        P.act(sqS[:, :, :], cS[:, :, :], AF.Square)
        bank = nb()
        for c in range(4):
            P.mm(ps[:, bank, 0:NS], onesF[:, :], cS[:, c, :], start=(c == 0), stop=(c == 3))
        for c in range(4):
            P.mm(ps[:, bank, NS:2 * NS], onesF[:, :], sqS[:, c, :], start=(c == 0), stop=(c == 3))
        mean = AS.alloc("smean", [128, NS], F32)
        msq = AS.alloc("smsq", [128, NS], F32)
        rstd = AS.alloc("srstd", [128, NS], F32)
        P.act(mean[:, :], ps[:, bank, 0:NS], AF.Copy, scale=1.0 / 512)
        P.tt("dve", msq[:, :], mean[:, :], mean[:, :], ALU.mult)
        P.stt("dve", rstd[:, :], ps[:, bank, NS:2 * NS], 1.0 / 512, msq[:, :], ALU.mult, ALU.subtract)
        P.act(rstd[:, :], rstd[:, :], AF.Sqrt, bias=EPSC[:, 0:1])
        recip(rstd[:, :])
        thS = AS.alloc("thS", [128, 4, NS], F32)
        for c in range(4):
            P.tt("dve", cS[:, c, :], cS[:, c, :], mean[:, :], ALU.subtract)
            P.tt("dve", cS[:, c, :], cS[:, c, :], rstd[:, :], ALU.mult)
            P.act(cS[:, c, :], cS[:, c, :], AF.Identity, bias=pch[:, 128 + c:129 + c], scale=pch[:, 124 + c:125 + c])
        P.act(thS[:, :, :], cS[:, :, :], AF.Tanh)
        P.stt("dve", cS[:, :, :], thS[:, :, :], 1.0, cS[:, :, :], ALU.add, ALU.mult)
        P.act(thS[:, :, :], pF[:, 8:12, :], AF.Tanh, scale=0.5)
        P.stt("dve", thS[:, :, :], thS[:, :, :], 1.0, pF[:, 8:12, :], ALU.add, ALU.mult)
        P.stt("dve", aoS[:, :, :], thS[:, :, :], 0.5, cS[:, :, :], ALU.mult, ALU.mult)
        AS.release(m)
        yield

        m = AS.mark()
        gS = AS.alloc("gS", [NS, 6, 4], F32)
        P.act(gS[:, 1, :], gls[:, 0:4], AF.Tanh, scale=0.5)
        P.ts("dve", gS[:, 0, :], gS[:, 1, :], 0.5, 0.5, ALU.mult, ALU.add)
        P.tt("dve", gS[:, 1, :], gls[:, 4:8], dtB[0:NS, 0, :], ALU.add)
        P.act(gS[:, 1, :], gS[:, 1, :], AF.Exp)
        P.act(gS[:, 1, :], gS[:, 1, :], AF.Ln, bias=EPSC[0:NS, 2:3])
        P.tt("dve", gS[:, 1, :], gS[:, 1, :], nAB[0:NS, 0, :], ALU.mult)
        P.act(gS[:, 2, :], gS[:, 1, :], AF.Exp)
        rhsAB = AS.alloc("rhsAB", [NS, 2, 4, NS], F32)
        P.tt("dve", rhsAB[:, 0, :, :], bcv(gS[:, 2, :], 2, [NS, 4, NS]), bcv(eye16, 1, [NS, 4, NS]), ALU.mult)
        P.tt("dve", rhsAB[:, 1, :, :], bcv(gS[:, 0, :], 2, [NS, 4, NS]), bcv(eye16, 1, [NS, 4, NS]), ALU.mult)
        bank = nb()
        P.mm(ps[:, bank, 0:2 * 4 * NS], onesF[0:NS, :], rhsAB[:, :, :, :].rr("p a b c -> p (a b c)"), start=True, stop=True)
        abB = AS.alloc("abB", [128, 2, 4, NS], F32)
        P.copy("dve", abB[:, :, :, :], ps[:, bank, 0:2 * 4 * NS].rr("p (a b c) -> p a b c", a=2, b=4))
        aB = abB[:, 0, :, :]
        betaB = abB[:, 1, :, :]
        yield
        qkvn = AS.alloc("qkvn", [NS, 12, 128], F32)
        ss = AS.alloc("ss", [NS, 8], F32)
        qkvT = AS.alloc("qkvT", [128, 12, NS], F32)
        macc = AS.mark()
        accf = AS.alloc("accf", [NS, 1536], F32)
        m2 = AS.mark()
        stq = AS.alloc("stq", [NS, 3, 512], F32)
        wB = AS.alloc("wB", [NS, 4, 512], F32)
        tmp = AS.alloc("tmpq", [NS, 512], F32)
        for q3 in range(3):
            cs3 = slice(q3 * 512, (q3 + 1) * 512)
            acc = accf[:, cs3]
            P.dma("sp", stq[:, :, :], st_qkv[:, :, cs3], sem="sld0")
            for k in range(4):
                P.dma("sp", wB[:, k, :], sc_w[k, cs3].partition_broadcast(NS), sem="sld%d" % (1 + k % 2))
            P.tt("dve", acc, qkvS[:, cs3], wB[:, 3, :], ALU.mult)
            for k in range(3):
                P.tt("dve", tmp[:, :], stq[:, k, :], wB[:, k, :], ALU.mult)
                P.tt("dve", acc, acc, tmp[:, :], ALU.add)
            P.act(tmp[:, :], acc, AF.Tanh, scale=0.5)
            P.stt("dve", acc, tmp[:, :], 1.0, acc, ALU.add, ALU.mult)
            yield
        AS.release(m2)
        m2 = AS.mark()
        tmp2 = AS.alloc("tmp2", [NS, 1024], F32)
        P.tt("dve", tmp2[:, :], accf[:, 0:1024], accf[:, 0:1024], ALU.mult)
        reduce_x(ss[:, :], tmp2[:, :].rr("p (a b) -> p a b", a=8))
        P.act(ss[:, :], ss[:, :], AF.Sqrt, bias=EPSC[0:NS, 1:2])
        recip(ss[:, :])
        P.ts("dve", ss[:, 0:4], ss[:, 0:4], 128.0 ** -0.5, None, ALU.mult)
        P.tt("dve", qkvn[:, 0:8, :], accf[:, 0:1024].rr("p (a b) -> p a b", a=8), bcv(ss[:, :], 2, [NS, 8, 128]), ALU.mult)
        P.ts("dve", qkvn[:, 8:12, :], accf[:, 1024:1536].rr("p (a b) -> p a b", a=4), 0.5, None, ALU.mult)
        bank = nb()
        for j in range(12):
            P.transpose(ps[:, bank, j * NS:(j + 1) * NS], qkvn[:, j, :], eye16)
        P.copy("dve", qkvT[:, :, :], ps[:, bank, 0:12 * NS].rr("p (a b) -> p a b", a=12))
        AS.release(macc)
        yield
        oS = AS.alloc("oS", [128, 4, NS], F32)
        Ssm = AS.alloc("Ssm", [128, 2, NS, 128], F32)
        vnT = AS.alloc("vnT", [128, 2, NS], F32)
        vnt = AS.alloc("vnt", [NS, 2, 128], F32)
        vbd = AS.alloc("vbd", [NS, NS, 128], F32)
        stmp = [AS.alloc("stmp%d" % i, [128, 4, 128], F32) for i in range(2)]
        for hp in range(2):
            for hh in range(2):
                h = hp * 2 + hh
                P.dma("sp", Ssm[:, hh, :, :], st_delta[:, h].rearrange("s d e -> d s e"), sem="slds%d" % hh)
            bank = nb()
            for hh in range(2):
                h = hp * 2 + hh
                for s in range(NS):
                    P.mm(ps[:, bank, hh * NS + s:hh * NS + s + 1], Ssm[:, hh, s, :], qkvT[:, 4 + h, s:s + 1], start=True, stop=True)
            P.tt("dve", vnT[:, :, :], ps[:, bank, 0:2 * NS].rr("p (a b) -> p a b", a=2), abB[:, 0, hp * 2:hp * 2 + 2, :], ALU.mult)
            P.tt("dve", vnT[:, :, :], qkvT[:, 8 + hp * 2:10 + hp * 2, :], vnT[:, :, :], ALU.subtract)
            P.tt("dve", vnT[:, :, :], vnT[:, :, :], abB[:, 1, hp * 2:hp * 2 + 2, :], ALU.mult)
            bank = nb()
            for hh in range(2):
                P.transpose(ps[0:NS, bank, hh * 128:(hh + 1) * 128], vnT[:, hh, :], identF[:, :])
            P.copy("act", vnt[:, :, :], ps[0:NS, bank, 0:256].rr("p (a b) -> p a b", a=2))
            yield
            for hh in range(2):
                h = hp * 2 + hh
                P.tt("dve", vbd[:, :, :], bcv(vnt[:, hh, :], 1, [NS, NS, 128]), bcv(eye16, 2, [NS, NS, 128]), ALU.mult)
                for q4 in range(4):
                    bk = nb()
                    P.mm(ps[:, bk, :], qkvn[:, 4 + h, :], vbd[:, q4 * 4:(q4 + 1) * 4, :].rr("p a b -> p (a b)"), start=True, stop=True)
                    st_ = stmp[q4 % 2]
                    sv = Ssm[:, hh, q4 * 4:(q4 + 1) * 4, :]
                    P.tt("dve", st_[:, :, :], sv, bcv(abB[:, 0, h, q4 * 4:(q4 + 1) * 4], 2, [128, 4, 128]), ALU.mult)
                    P.tt("dve", sv, st_[:, :, :], ps[:, bk, :].rr("p (a b) -> p a b", a=4), ALU.add)
                yield
            for hh in range(2):
                h = hp * 2 + hh
                P.dma("sp", o_delta_s[:, h].rearrange("s d e -> d s e"), Ssm[:, hh, :, :], sem="slds%d" % hh)
            bank = nb()
            for hh in range(2):
                h = hp * 2 + hh
                for s in range(NS):
                    P.mm(ps[:, bank, hh * NS + s:hh * NS + s + 1], Ssm[:, hh, s, :], qkvT[:, h, s:s + 1], start=True, stop=True)
            P.copy("act", oS[:, hp * 2:hp * 2 + 2, :], ps[:, bank, 0:2 * NS].rr("p (a b) -> p a b", a=2))
            yield
        osq_ = AS.alloc("osqS", [128, 4, NS], F32)
        P.act(osq_[:, :, :], oS[:, :, :], AF.Square)
        bank = nb()
        P.mm(ps[:, bank, 0:4 * NS], onesF[:, :], osq_[:, :, :].rr("p a b -> p (a b)"), start=True, stop=True)
        orr_ = AS.alloc("orrS", [128, 4, NS], F32)
        P.act(orr_[:, :, :], ps[:, bank, 0:4 * NS].rr("p (a b) -> p a b", a=4), AF.Sqrt, bias=EPSC[:, 0:1], scale=1.0 / 128)
        recip(orr_[:, :, :])
        P.stt("dve", oS[:, :, :], oS[:, :, :], pc(R_DNG), orr_[:, :, :], ALU.mult, ALU.mult)
        P.act(osq_[:, :, :], pF[:, 12:16, :], AF.Tanh, scale=0.5)
        P.stt("dve", osq_[:, :, :], osq_[:, :, :], 1.0, pF[:, 12:16, :], ALU.add, ALU.mult)
        P.stt("dve", bS[:, :, :], oS[:, :, :], 0.5, osq_[:, :, :], ALU.mult, ALU.mult)
        AS.release(m)
        yield
        yield from s_outproj(w_out_even, lambda kc: (aoS if kc < 4 else bS)[:, kc % 4, :])
        AS.release(m_seg)
        yield "B"

        def s_xattn(l):
            m = AS.mark()
            ring()
            selF = AS.alloc("selF", [NS, NS, 128], F32)
            selS = AS.alloc("selS", [NS, NS, 128], BF16)
            mset(selF[:, :, :], 1.0)
            msel(selF[:, :, :], ALU.is_equal, 0.0, [[-1, NS], [0, 128]], 1)
            P.copy("pool", selS[:, :, :], selF[:, :, :])
            rmsnorm_s(R_NX + l * 8)
            qs = AS.alloc("qs", [NS, D], BF16)
            bq2 = pair_banks()
            reserved.update(bq2)
            for g in range(D // WG):
                wt = wload_s(w_xq[l], g * WG, WG)
                for kc in range(8):
                    P.mm(ps[0:NS, bq2[g // 2], (g % 2) * WG:(g % 2 + 1) * WG], hsT[:, kc, :], wt[:, kc, 0:WG],
                         start=(kc == 0), stop=(kc == 7))
                yield
            for hf in range(2):
                P.act(qs[:, hf * 512:(hf + 1) * 512], ps[0:NS, bq2[hf], :], AF.Copy, scale=1.0 / 16.0)
            reserved.difference_update(bq2)
            oxS = AS.alloc("oxS", [128, 8, NS], BF16)
            NKB = 3
            kbuf = [AS.alloc("kbuf%d" % i, [128, 2, D], BF16) for i in range(NKB)]
            vbuf = [AS.alloc("vbuf%d" % i, [128, 2, D], BF16) for i in range(NKB)]
            qBs = [AS.alloc("qBs%d" % i, [128, D], BF16) for i in range(2)]
            prd = AS.alloc("prd", [128, 2, D], F32)
            scb = [AS.alloc("sc%d" % i, [128, 2, 4], F32) for i in range(3)]
            scbb = [AS.alloc("scb%d" % i, [128, 2, 4], BF16) for i in range(3)]
            o4m = AS.alloc("o4m", [4, 4, 256], F32)
            o4 = AS.alloc("o4", [4, 256], F32)
            rden = AS.alloc("rdenS", [4, 1], F32)

            def stage_load(s):
                P.dma("pool", kbuf[s % NKB][:, :, :], c_k[l, s].rearrange("(c p) n -> p c n", p=128), sem="ck%d" % (s % NKB))
                P.dma("pool", vbuf[s % NKB][:, :, :], c_v[l, s].rearrange("(c p) n -> p c n", p=128), sem="cv%d" % (s % NKB))

            def stage_a(s):
                Kc = kbuf[s % NKB]
                sc = scb[s % 3]
                qb = qBs[s % 2]
                bq = pair_banks()
                for hf in range(2):
                    P.mm(ps[:, bq[hf], :], selS[:, s, :], qs[:, hf * 512:(hf + 1) * 512], start=True, stop=True)
                P.copy("act", qb[:, :].rr("p (a b) -> p a b", a=2), ps[:, bq[0]:bq[0] + 2, :])
                P.tt("dve", prd[:, :, :], Kc[:, :, :], bcv(qb[:, :], 1, [128, 2, D]), ALU.mult)
                reduce_x(sc[:, :, :], prd[:, :, :].rr("p a (h d) -> p a h d", h=4))
                P.act(scbb[s % 3][:, :, :], sc[:, :, :], AF.Exp)

            def stage_b(s):
                Vc = vbuf[s % NKB]
                sc = scbb[s % 3]
                bo = pair_banks()
                for hf in range(2):
                    for mc in range(2):
                        P.mm(ps[0:4, bo[hf], :], sc[:, mc, :], Vc[:, mc, hf * 512:(hf + 1) * 512], start=(mc == 0), stop=(mc == 1))
                P.tt("dve", o4m[:, :, :], ps[0:4, bo[0]:bo[0] + 2, :].rr("p a (h d) -> p (a h) d", h=2), bcv(eye4, 2, [4, 4, 256]), ALU.mult)
                bd = nb()
                for mc in range(2):
                    P.mm(ps[0:4, bd, 0:1], sc[:, mc, :], onesB[:, 0:1], start=(mc == 0), stop=(mc == 1))
                P.add("dve", lambda e, bd=bd: e.reciprocal(rden[:, :].ap, ps[0:4, bd, 0:1].ap), reads=[ps[0:4, bd, 0:1]], writes=[rden[:, :]])
                reduce_x(o4[:, :], o4m[:, :, :].rr("p h d -> p d h"))
                P.ts("dve", o4[:, :], o4[:, :], rden[:, 0:1], None, ALU.mult)
                bt = nb()
                for hf in range(2):
                    P.transpose(ps[:, bt, hf * 4:(hf + 1) * 4], o4[:, hf * 128:(hf + 1) * 128], eye4)
                P.copy("act", oxS[:, :, s].rr("p (h f) -> p f h", f=2), ps[:, bt, 0:8].rr("p (f h) -> p f h", f=2))

            stage_load(0)
            stage_load(1)
            stage_a(0)
            yield
            for s in range(NS):
                if s + 1 < NS:
                    stage_a(s + 1)
                stage_b(s)
                if s + 2 < NS:
                    stage_load(s + 2)
                yield
            yield from s_outproj(w_xo[l], lambda kc: oxS[:, kc, :])
            AS.release(m)

        yield from s_xattn(0)
        yield "B"

        m_odd = AS.mark()
        ring()
        rmsnorm_s(R_NM + 8)
        uS = AS.alloc("uS", [NS, D], F32)
        gF = AS.alloc("gF", [128, 8, NS], F32)
        bu2 = pair_banks()
        bgf = nb()
        reserved.update(bu2 + [bgf])
        for g in range(8):
            wt = wload_s(w_in_odd, g * WG, WG)
            if g < 4:
                for kc in range(8):
                    P.mm(ps[0:NS, bu2[g // 2], (g % 2) * WG:(g % 2 + 1) * WG], hsT[:, kc, :], wt[:, kc, 0:WG],
                         start=(kc == 0), stop=(kc == 7))
            else:
                for j in range(2):
                    n = (g - 4) * 2 + j
                    for kc in range(8):
                        P.mm(ps[:, bgf, n * NS:(n + 1) * NS], wt[:, kc, j * 128:(j + 1) * 128], hsT[:, kc, :],
                             start=(kc == 0), stop=(kc == 7))
            yield
        for hf in range(2):
            P.copy("act", uS[:, hf * 512:(hf + 1) * 512], ps[0:NS, bu2[hf], :])
        P.copy("dve", gF[:, :, :], ps[:, bgf, 0:8 * NS].rr("p (a b) -> p a b", a=8))
        reserved.difference_update(bu2 + [bgf])
        P.dma("sp", o_pool_s[:, 0:14, :], st_pool[:, 1:15, :], sem="sst0")
        P.dma("sp", o_pool_s[:, 14, :], uS[:, :], sem="sst1")
        plt = AS.alloc("plt", [NS, D], F32)
        bsum = AS.alloc("bsum", [NS, 256], F32)
        stp = AS.alloc("stp", [NS, 15, 256], F32)
        for gi in range(4):
            win = 2 << gi
            cs = slice(gi * 256, (gi + 1) * 256)
            P.dma("sp", stp[:, :, :], st_pool[:, :, cs], sem="sld0")
            reduce_x(bsum[:, :], stp[:, 16 - win:15, :].rr("p k c -> p c k"))
            P.tt("dve", bsum[:, :], bsum[:, :], uS[:, cs], ALU.add)
            P.stt("dve", plt[:, cs], bsum[:, :], 1.0 / win, uS[:, cs], ALU.mult, ALU.subtract)
            yield
        plS = AS.alloc("plS", [128, 8, NS], BF16)
        bank = nb()
        for c in range(8):
            P.transpose(ps[:, bank, c * NS:(c + 1) * NS], plt[:, c * 128:(c + 1) * 128], eye16)
        P.copy("dve", plS[:, :, :], ps[:, bank, 0:8 * NS].rr("p (a b) -> p a b", a=8))
        wps = AS.alloc("wps", [128, 4, 2, 256], BF16)
        P.dma("pool", wps[:, :, :, :], w_pool.rearrange("g (c p) d -> p g c d", p=128), sem="swp")
        zS = AS.alloc("zS", [128, 8, NS], BF16)
        zf = AS.alloc("zf", [128, 8, NS], F32)
        tgs = AS.alloc("tgs", [128, 8, NS], F32)
        bank = nb()
        for n in range(8):
            gi, half = n // 2, n % 2
            for cc in range(2):
                P.mm(ps[:, bank, n * NS:(n + 1) * NS], wps[:, gi, cc, half * 128:(half + 1) * 128], plS[:, gi * 2 + cc, :],
                     start=(cc == 0), stop=(cc == 1))
        for n in range(8):
            P.ts("dve", zf[:, n, :], ps[:, bank, n * NS:(n + 1) * NS], pc(R_BP + n), pc(R_PSC + n), ALU.add, ALU.mult)
        P.act(tgs[:, :, :], gF[:, :, :], AF.Tanh, scale=0.5)
        P.stt("dve", tgs[:, :, :], tgs[:, :, :], 1.0, gF[:, :, :], ALU.add, ALU.mult)
        P.stt("dve", zS[:, :, :], tgs[:, :, :], 0.5, zf[:, :, :], ALU.mult, ALU.mult)
        yield
        yield from s_outproj(w_out_odd, lambda kc: zS[:, kc, :])
        AS.release(m_odd)
        yield "B"
        yield from s_xattn(1)

        m = AS.mark()
        yT = AS.alloc("yTs", [128, 8, NS], F32)
        rmsnorm_s(R_NF, f32out=yT)
        yst_ = AS.alloc("ysts", [NS, D], F32)
        for g in range(2):
            bank = nb()
            for j in range(4):
                P.transpose(ps[0:NS, bank, j * 128:(j + 1) * 128], yT[:, g * 4 + j, :], identF[:, :])
            P.copy("act", yst_[:, g * 512:(g + 1) * 512], ps[0:NS, bank, :])
        P.dma("sp", o_ys, yst_[:, :], sem="sst2")
        AS.release(m)
        yield "B"

    sgen = sample_path()
    sdone = [False]

    def pump(n=1, to_boundary=False):
        if sdone[0] or (win_done[0] and not to_boundary):
            return
        dom[0] = "s"
        try:
            k = 0
            while True:
                r = next(sgen)
                k += 1
                if r == "B":
                    win_done[0] = True
                    break
                if not to_boundary and k >= n:
                    break
        except StopIteration:
            sdone[0] = True
        dom[0] = "p"

    win_done = [False]
    PUMP = int(_os.environ.get("K_PUMP", "1"))

    def win_begin():
        win_done[0] = False
        A.limit = AS_BASE
        allowed["p"] = {0, 1, 2, 3}

    def win_end():
        if not win_done[0]:
            pump(to_boundary=True)
        A.limit = A.top
        allowed["p"] = set(range(8))

    mem_prep()
    load_x()
    for l in range(DEPTH):
        mem_kv(l)
        for mt in range(NMT):
            if l % 2 == 0:
                even_mixer(mt)
            else:
                odd_mixer(mt)
        if STAGE <= 4:
            break
        for mt in range(NMT):
            xattn(l, mt)
        if STAGE <= 5:
            break
    if STAGE >= 7:
        win_begin()
        final_out()
        win_end()
    while not sdone[0]:
        win_done[0] = False
        pump(to_boundary=True)
    for name, t in dbg.items():
        od = nc.dram_tensor("dbg_" + name, t.shape, F32, kind="ExternalOutput").ap()
        m = A.mark()
        st = A.alloc("dbgst", t.shape, F32)
        P.copy("dve", st[tuple(slice(None) for _ in t.shape)], t[tuple(slice(None) for _ in t.shape)])
        P.dma("sp", od, st[tuple(slice(None) for _ in t.shape)], sem="dbg")
        A.release(m)
    print("SBUF peak bytes/partition:", A.peak, "ops:", len(P.ops))
    P.emit()
    return nc


_REPL = ["w_xk", "w_xv", "w_xq", "w_xo", "norm_mix", "norm_xattn"]
_SQ0 = ["w_in_even", "w_out_even", "w_in_odd", "w_out_odd", "w_pool", "dw_w", "sc_w", "b_pool"]
_ROW = ["norm_final", "dw_b", "ln_a_g", "ln_a_b", "a_log", "dt_bias", "dn_norm_g", "pool_scale"]


def kernel(**inputs):
    nc = build_program()
    f = lambda a: np.ascontiguousarray(a, dtype=np.float32)
    shared = {}
    for k in _REPL:
        shared[k] = f(inputs[k])
    for k in _SQ0:
        shared[k] = f(inputs[k][0])
    for k in _ROW:
        shared[k] = f(np.asarray(inputs[k]).reshape(1, -1))
    in_maps = []
    for c in range(NCORES):
        m = dict(shared)
        m["x_p"] = f(inputs["x_prompt"][c])
        m["mem_p"] = f(inputs["mem_prompt"][c])
        sl = slice(c * NS, (c + 1) * NS)
        m["x_s"] = f(inputs["x_sample"][sl, 0])
        m["st_conv"] = f(inputs["state_conv_a"][0, sl])
        m["st_qkv"] = f(inputs["state_qkv_conv"][0, sl])
        m["st_delta"] = f(inputs["state_delta"][0, sl])
        m["st_pool"] = f(inputs["state_pool"][0, sl])
        m["c_k"] = f(inputs["cache_mem_k"][:, sl]).reshape(DEPTH, NS, NMEM, D)
        m["c_v"] = f(inputs["cache_mem_v"][:, sl]).reshape(DEPTH, NS, NMEM, D)
        in_maps.append(m)
    res = run_bass_kernel_spmd(nc, in_maps, core_ids=list(range(NCORES)))
    R = res.results
    B = NCORES
    out = {}
    out["y_prompt"] = np.stack([R[c]["o_y"] for c in range(B)], axis=0)
    out["new_conv_a_p"] = np.stack([R[c]["o_conv"] for c in range(B)], axis=0)[None]
    out["new_qkv_conv_p"] = np.stack([R[c]["o_qkv"] for c in range(B)], axis=0)[None]
    out["new_delta_p"] = np.stack([R[c]["o_delta"] for c in range(B)], axis=0)[None]
    out["new_pool_p"] = np.stack([R[c]["o_pool"] for c in range(B)], axis=0)[None]
    out["new_mem_k_p"] = np.stack([R[c]["o_mem_k"] for c in range(B)], axis=1).reshape(DEPTH, B, NMEM, 4, 256)
    out["new_mem_v_p"] = np.stack([R[c]["o_mem_v"] for c in range(B)], axis=1).reshape(DEPTH, B, NMEM, 4, 256)
    if STAGE >= 8:
        out["y_sample"] = np.concatenate([R[c]["o_ys"] for c in range(B)], axis=0)[:, None, :]
        out["new_conv_a_s"] = np.concatenate([R[c]["o_conv_s"] for c in range(B)], axis=0)[None]
        out["new_qkv_conv_s"] = np.concatenate([R[c]["o_qkv_s"] for c in range(B)], axis=0)[None]
        out["new_delta_s"] = np.concatenate([R[c]["o_delta_s"] for c in range(B)], axis=0)[None]
        out["new_pool_s"] = np.concatenate([R[c]["o_pool_s"] for c in range(B)], axis=0)[None]
    if _os.environ.get("K_DBG"):
        for k in R[0]:
            if k.startswith("dbg_"):
                out[k] = R[0][k]
        return out
    NSM = NS * NCORES
    for k, shp in (("y_sample", (NSM, 1, D)), ("new_conv_a_s", (1, NSM, 30, 512)), ("new_qkv_conv_s", (1, NSM, 3, 1536)),
                   ("new_delta_s", (1, NSM, 4, 128, 128)), ("new_pool_s", (1, NSM, 15, D))):
        if k not in out:
            out[k] = np.zeros(shp, np.float32)
    order = ["y_prompt", "y_sample", "new_conv_a_p", "new_qkv_conv_p", "new_delta_p", "new_pool_p", "new_mem_k_p",
             "new_mem_v_p", "new_conv_a_s", "new_qkv_conv_s", "new_delta_s", "new_pool_s"]
    return tuple(out[k] for k in order)
```

```python
import numpy as np
import concourse.bass as bass
import concourse.mybir as mybir
from concourse.bass_utils import run_bass_kernel_spmd

F32 = mybir.dt.float32
BF16 = mybir.dt.bfloat16
ALU = mybir.AluOpType
AF = mybir.ActivationFunctionType
AX = mybir.AxisListType

NCORES = 8
D = 1024
T = 2048
NMEM = 256
DEPTH = 2
NS = 16
EPS = 1e-6

COMPUTE = ("pe", "act", "dve", "pool")


class V:
    def __init__(self, ap, tname, rect, excl=False):
        self.ap = ap
        self.t = tname
        self.rect = rect
        self.excl = excl

    def rr(self, pat, **kw):
        return V(self.ap.rearrange(pat, **kw), self.t, self.rect, self.excl)

    def bc(self, shape):
        return V(self.ap.to_broadcast(list(shape)), self.t, self.rect, self.excl)

    def sub(self, *idx):
        return V(self.ap[idx], self.t, self.rect, self.excl)


_uid = [0]


class Tn:
    def __init__(self, nc, name, shape, dtype, psum=False, offset=None):
        _uid[0] += 1
        self.name = "%s_%d" % (name, _uid[0])
        self.shape = list(shape)
        self.dtype = dtype
        self.psum = psum
        self.esz = 4 if dtype == F32 else 2
        if psum:
            self.h = nc.alloc_psum_tensor(self.name, self.shape, dtype)
            self.off = 0
        elif offset is None:
            self.h = nc.alloc_sbuf_tensor(self.name, self.shape, dtype)
            self.off = None
        else:
            self.h = nc.alloc_sbuf_tensor_at(self.name, self.shape, dtype, offset=offset)
            self.off = offset
        st = [1] * len(shape)
        for i in range(len(shape) - 2, 0, -1):
            st[i] = st[i + 1] * shape[i + 1]
        self.st = st

    def __getitem__(self, idx):
        if not isinstance(idx, tuple):
            idx = (idx,)
        idx = tuple(idx) + (slice(None),) * (len(self.shape) - len(idx))
        lo = 0
        hi = 0
        p0, p1 = 0, self.shape[0]
        for d, (ix, n) in enumerate(zip(idx, self.shape)):
            if isinstance(ix, slice):
                a = 0 if ix.start is None else ix.start
                b = n if ix.stop is None else ix.stop
                assert ix.step in (None, 1)
            else:
                a, b = ix, ix + 1
            assert 0 <= a < b <= n, (self.name, idx)
            if d == 0:
                p0, p1 = a, b
            else:
                lo += a * self.st[d]
                hi += (b - 1) * self.st[d]
        hi += 1
        lo *= self.esz
        hi *= self.esz
        if self.psum:
            lo = (lo // 2048) * 2048
            hi = ((hi + 2047) // 2048) * 2048
            p0, p1 = 0, 128
            return V(self.h[idx], "ps", (p0, p1, lo, hi), True)
        if self.off is None:
            return V(self.h[idx], self.name, (p0, p1, lo, hi), False)
        return V(self.h[idx], "sb", (p0, p1, self.off + lo, self.off + hi), False)


def _overlap(a, b):
    return a[0] < b[1] and b[0] < a[1] and a[2] < b[3] and b[2] < a[3]


def _contains(outer, inner):
    return outer[0] <= inner[0] and inner[1] <= outer[1] and outer[2] <= inner[2] and inner[3] <= outer[3]


class Op:
    __slots__ = ("eng", "fn", "reads", "writes", "dma", "seq", "waits", "flag", "semkey", "semcnt")


class Prog:
    def __init__(self, nc):
        self.nc = nc
        self.ops = []
        self.acc = {}
        self.nseq = {e: 0 for e in ("pe", "act", "dve", "pool", "sp")}
        self.dmacnt = {}
        self.lastdma = {}
        self.by_eng = {e: [] for e in ("pe", "act", "dve", "pool", "sp")}
        self.flagged = {e: set() for e in COMPUTE}
        self.waited = {e: {} for e in ("pe", "act", "dve", "pool", "sp")}

    def _deps(self, op):
        deps = set()
        for v in op.reads:
            for (rect, kind, dep) in self.acc.get(v.t, ()):
                if (kind == "w" or v.excl) and _overlap(rect, v.rect):
                    deps.add(dep)
        for v in op.writes:
            for (rect, kind, dep) in self.acc.get(v.t, ()):
                if _overlap(rect, v.rect):
                    deps.add(dep)
        return deps

    def add(self, eng, fn, reads=(), writes=(), dma=None):
        op = Op()
        op.eng = eng
        op.fn = fn
        op.reads = [r for r in reads if r is not None]
        op.writes = [w for w in writes if w is not None]
        op.dma = dma
        op.flag = False
        deps = self._deps(op)
        self.nseq[eng] += 1
        op.seq = self.nseq[eng]
        if dma is not None:
            prev = self.lastdma.get(dma)
            if prev is not None:
                deps.add(prev)
            self.dmacnt[dma] = self.dmacnt.get(dma, 0) + 16
            op.semkey = dma
            op.semcnt = self.dmacnt[dma]
            me = ("dma", dma, op.semcnt)
            self.lastdma[dma] = me
        else:
            me = (eng, op.seq)
        need = {}
        for d in deps:
            if d[0] == "dma":
                key = ("dma", d[1])
                val = d[2]
            else:
                if d[0] == eng and dma is None:
                    pass
                key = d[0]
                val = d[1]
            if need.get(key, 0) < val:
                need[key] = val
        if dma is None and eng in need and eng == "pe":
            raw = 0
            for v in op.reads:
                for (rect, kind, dep) in self.acc.get(v.t, ()):
                    if kind == "w" and dep[0] == eng and _overlap(rect, v.rect):
                        raw = max(raw, dep[1])
            if raw:
                need[eng] = raw
            else:
                del need[eng]
        waits = []
        wd = self.waited[eng]
        for key, val in need.items():
            if wd.get(key, 0) >= val:
                continue
            wd[key] = val
            waits.append((key, val))
            if not (isinstance(key, tuple)):
                self.flagged[key].add(val)
        op.waits = waits
        for v in op.writes:
            lst = self.acc.setdefault(v.t, [])
            lst[:] = [a for a in lst if not _contains(v.rect, a[0])]
            lst.append((v.rect, "w", me))
        for v in op.reads:
            lst = self.acc.setdefault(v.t, [])
            if v.excl:
                lst[:] = [a for a in lst if not _contains(v.rect, a[0])]
                lst.append((v.rect, "x", me))
                continue
            if dma is None:
                lst[:] = [a for a in lst if not (a[1] == "r" and a[2][0] == eng and _contains(v.rect, a[0]))]
            lst.append((v.rect, "r", me))
        self.ops.append(op)
        self.by_eng[eng].append(op)
        return op

    def mm(self, out, lhsT, rhs, start=True, stop=True):
        self.add("pe", lambda e: e.matmul(out.ap, lhsT.ap, rhs.ap, start=start, stop=stop),
                 reads=[lhsT, rhs], writes=[out])

    def transpose(self, out, in_, ident):
        self.add("pe", lambda e: e.transpose(out.ap, in_.ap, ident.ap), reads=[in_, ident], writes=[out])

    def act(self, out, in_, func, bias=None, scale=1.0, accum=None, eng="act"):
        rd = [in_]
        kw = {}
        if isinstance(bias, V):
            rd.append(bias)
            kw["bias"] = bias.ap
        elif bias is not None:
            kw["bias"] = bias
        if isinstance(scale, V):
            rd.append(scale)
            kw["scale"] = scale.ap
        else:
            kw["scale"] = scale
        wr = [out]
        if accum is not None:
            wr.append(accum)
            kw["accum_out"] = accum.ap
        self.add("act", lambda e: e.activation(out.ap, in_.ap, func, **kw), reads=rd, writes=wr)

    def copy(self, eng, out, in_):
        if eng == "act":
            self.add("act", lambda e: e.copy(out.ap, in_.ap), reads=[in_], writes=[out])
        else:
            self.add(eng, lambda e: e.tensor_copy(out.ap, in_.ap), reads=[in_], writes=[out])

    def tt(self, eng, out, a, b, op):
        self.add(eng, lambda e: e.tensor_tensor(out.ap, a.ap, b.ap, op), reads=[a, b], writes=[out])

    def ts(self, eng, out, a, s1, s2, op0, op1=None, accum=None):
        rd = [a]
        s1a = s1.ap if isinstance(s1, V) else s1
        s2a = s2.ap if isinstance(s2, V) else s2
        if isinstance(s1, V):
            rd.append(s1)
        if isinstance(s2, V):
            rd.append(s2)
        wr = [out]
        kw = {}
        if accum is not None:
            wr.append(accum)
            kw["accum_out"] = accum.ap
        if op1 is None:
            self.add(eng, lambda e: e.tensor_scalar(out.ap, a.ap, s1a, None, op0, **kw), reads=rd, writes=wr)
        else:
            self.add(eng, lambda e: e.tensor_scalar(out.ap, a.ap, s1a, s2a, op0, op1, **kw), reads=rd, writes=wr)

    def stt(self, eng, out, a, s, b, op0, op1):
        rd = [a, b]
        sa = s.ap if isinstance(s, V) else s
        if isinstance(s, V):
            rd.append(s)
        self.add(eng, lambda e: e.scalar_tensor_tensor(out.ap, a.ap, sa, b.ap, op0, op1), reads=rd, writes=[out])

    def dma(self, q, out, in_, sem, reads=(), writes=()):
        o = out.ap if isinstance(out, V) else out
        i = in_.ap if isinstance(in_, V) else in_
        rd = list(reads) + ([in_] if isinstance(in_, V) else [])
        wr = list(writes) + ([out] if isinstance(out, V) else [])
        self.add(q, lambda e: e.dma_start(out=o, in_=i), reads=rd, writes=wr, dma=sem)

    def emit(self):
        nc = self.nc
        sems = {e: nc.alloc_semaphore("s_" + e) for e in COMPUTE}
        dsems = {k: nc.alloc_semaphore("d_" + str(k)) for k in self.dmacnt}
        rank = {}
        for e in COMPUTE:
            fl = sorted(self.flagged[e])
            rank[e] = {s: i + 1 for i, s in enumerate(fl)}
        engobj = {"pe": "tensor", "act": "scalar", "dve": "vector", "pool": "gpsimd", "sp": "sync"}

        def run(ename, eng):
            for op in self.by_eng[ename]:
                for key, val in op.waits:
                    if isinstance(key, tuple):
                        eng.wait_ge(dsems[key[1]], val)
                    else:
                        eng.wait_ge(sems[key], rank[key][val])
                ins = op.fn(eng)
                if op.dma is not None:
                    ins.then_inc(dsems[op.semkey], 16)
                elif op.seq in rank[ename]:
                    ins.then_inc(sems[ename], 1)
            if ename == "sp":
                for k, cnt in self.dmacnt.items():
                    eng.wait_ge(dsems[k], cnt)

        with nc.Block() as block:
            block.tensor(lambda e: run("pe", e))
            block.scalar(lambda e: run("act", e))
            block.vector(lambda e: run("dve", e))
            block.gpsimd(lambda e: run("pool", e))
            block.sync(lambda e: run("sp", e))


import os as _os


class Arena:
    def __init__(self, nc):
        self.nc = nc
        self.cur = 16512
        self.top = 229344
        self.limit = 229344
        self.peak = 0

    def alloc(self, name, shape, dtype):
        esz = 4 if dtype == F32 else 2
        n = esz
        for s in shape[1:]:
            n *= s
        off = self.cur
        self.cur = (off + n + 31) // 32 * 32
        self.peak = max(self.peak, self.cur)
        assert self.cur <= min(self.top, self.limit), ("SBUF overflow", name, self.cur, self.limit)
        return Tn(self.nc, name, shape, dtype, offset=off)

    def mark(self):
        return self.cur

    def release(self, m):
        self.cur = m


TT = int(_os.environ.get('K_TT', '1024'))
NMT = T // TT
NTT = TT // 512
NBLK = TT // 128
WG = 256
NEG = -32768.0
SAMPLE_BYTES = 66560
STAGE = int(_os.environ.get("K_STAGE", "99"))


def build_program():
    nc = bass.Bass("TRN2", target_bir_lowering=False)
    nc.allow_low_precision("bf16 matmul operands with fp32 accumulation (problem tolerance)")
    P = Prog(nc)
    A = Arena(nc)
    AS = Arena(nc)
    AS.cur = AS.top - SAMPLE_BYTES
    AS_BASE = AS.cur

    def din(name, shape):
        return nc.dram_tensor(name, list(shape), F32, kind="ExternalInput").ap()

    def dout(name, shape):
        return nc.dram_tensor(name, list(shape), F32, kind="ExternalOutput").ap()

    x_p = din("x_p", [T, D])
    mem_p = din("mem_p", [NMEM, D])
    w_xk = din("w_xk", [DEPTH, D, D])
    w_xv = din("w_xv", [DEPTH, D, D])
    w_xq = din("w_xq", [DEPTH, D, D])
    w_xo = din("w_xo", [DEPTH, D, D])
    w_in_even = din("w_in_even", [D, 3592])
    w_out_even = din("w_out_even", [D, D])
    w_in_odd = din("w_in_odd", [D, 2048])
    w_out_odd = din("w_out_odd", [D, D])
    w_pool = din("w_pool", [4, 256, 256])
    norm_mix = din("norm_mix", [2, D])
    norm_xattn = din("norm_xattn", [2, D])
    norm_final = din("norm_final", [1, D])
    dw_w = din("dw_w", [31, 512])
    dw_b = din("dw_b", [1, 512])
    ln_a_g = din("ln_a_g", [1, 512])
    ln_a_b = din("ln_a_b", [1, 512])
    sc_w = din("sc_w", [4, 1536])
    a_log = din("a_log", [1, 4])
    dt_bias = din("dt_bias", [1, 4])
    dn_norm_g = din("dn_norm_g", [1, 128])
    pool_scale = din("pool_scale", [1, D])
    b_pool = din("b_pool", [4, 256])

    o_y = dout("o_y", [T, D])
    o_conv = dout("o_conv", [30, 512])
    o_qkv = dout("o_qkv", [3, 1536])
    o_delta = dout("o_delta", [4, 128, 128])
    o_pool = dout("o_pool", [15, D])
    o_mem_k = dout("o_mem_k", [DEPTH, NMEM, D])
    o_mem_v = dout("o_mem_v", [DEPTH, NMEM, D])
    dbg = {}

    ps = Tn(nc, "ps", [128, 8, 512], F32, psum=True)
    bankc = [0]

    reserved = set()

    dom = ["p"]
    allowed = {"p": set(range(8)), "s": {4, 5, 6, 7}}
    bankcs = {"p": bankc, "s": [3]}

    def nb():
        bc_ = bankcs[dom[0]]
        while True:
            bc_[0] = (bc_[0] + 1) % 8
            if bc_[0] in allowed[dom[0]] and bc_[0] not in reserved:
                return bc_[0]

    evc = [0]

    def ev():
        evc[0] += 1
        return "dve" if evc[0] % 2 else "act"

    identF = A.alloc("identF", [128, 128], F32)
    identB = A.alloc("identB", [128, 128], BF16)
    onesF = A.alloc("onesF", [128, 128], F32)
    onesB = A.alloc("onesB", [128, 128], BF16)
    Mincl = A.alloc("Mincl", [128, 128], F32)
    Msame = A.alloc("Msame", [128, 128], F32)
    Mgt = A.alloc("Mgt", [128, 128], F32)
    NEGs = A.alloc("NEGs", [128, 128], F32)
    NEGiT = A.alloc("NEGiT", [128, 128], F32)
    sel = A.alloc("sel", [4, 4, 128], F32)
    NEGsB = A.alloc("NEGsB", [128, 128], BF16)
    NEGiTB = A.alloc("NEGiTB", [128, 128], BF16)
    pcol0 = A.alloc("pcol0", [128, 128], F32)
    pcol1 = A.alloc("pcol1", [128, 128], F32)
    pch = A.alloc("pch", [128, 160], F32)
    dtB = A.alloc("dtB", [128, NBLK, 4], F32)
    nAB = A.alloc("nAB", [128, NBLK, 4], F32)
    invc = A.alloc("invc", [128, 16], F32)

    def msel(t, cmp, fill, pattern, cm, base=0):
        P.add("pool", lambda e: e.affine_select(out=t.ap, in_=t.ap, compare_op=cmp, fill=fill, base=base,
                                                pattern=pattern, channel_multiplier=cm), reads=[t], writes=[t])

    def mset(t, val):
        P.add("pool", lambda e: e.memset(t.ap, val), writes=[t])

    mset(identF[:, :], 1.0)
    msel(identF[:, :], ALU.is_equal, 0.0, [[-1, 128]], 1)
    P.copy("pool", identB[:, :], identF[:, :])
    mset(onesF[:, :], 1.0)
    mset(onesB[:, :], 1.0)
    mset(Mincl[:, :], 1.0)
    msel(Mincl[:, :], ALU.is_ge, 0.0, [[1, 128]], -1)
    mset(Mincl[0:64, 64:128], 0.0)
    mset(Msame[:, :], 1.0)
    mset(Msame[0:64, 64:128], 0.0)
    mset(Msame[64:128, 0:64], 0.0)
    mset(Mgt[:, :], 1.0)
    msel(Mgt[:, :], ALU.is_gt, 0.0, [[-1, 128]], 1)
    mset(Mgt[64:128, 0:64], 0.0)
    mset(NEGs[:, :], 0.0)
    msel(NEGs[:, :], ALU.is_gt, NEG, [[-1, 128]], 1)
    mset(NEGs[64:128, 0:64], NEG)
    mset(NEGiT[:, :], 0.0)
    msel(NEGiT[:, :], ALU.is_ge, NEG, [[1, 128]], -1)
    mset(NEGiT[0:64, 64:128], NEG)
    P.copy("pool", NEGsB[:, :], NEGs[:, :])
    P.copy("pool", NEGiTB[:, :], NEGiT[:, :])
    mset(sel[:, :, :], 1.0)
    msel(sel[:, :, :], ALU.is_equal, 0.0, [[-1, 4], [0, 128]], 1)
    ind0 = Msame[:, 0:1]
    ind1 = Msame[:, 64:65]
    _bk = nb()
    P.mm(ps[:, _bk, 0:16], onesF[:, :], Mincl[:, 0:16], start=True, stop=True)
    P.add("dve", lambda e: e.reciprocal(invc[:, :].ap, ps[:, _bk, 0:16].ap), reads=[ps[:, _bk, 0:16]], writes=[invc[:, :]])

    m0 = A.mark()
    pst0 = A.alloc("pst0", [128, 128], F32)
    pst1 = A.alloc("pst1", [128, 128], F32)
    mset(pst0[:, :], 0.0)
    mset(pst1[:, :], 0.0)
    R_NM, R_NX, R_NF, R_DWB, R_LNG, R_LNB, R_SC, R_DNG, R_PSC, R_BP = 0, 16, 32, 40, 44, 48, 52, 100, 101, 109
    prm_loads = [
        (pst0, R_NM, 16, norm_mix.rearrange("l (c p) -> (l c) p", p=128)),
        (pst0, R_NX, 16, norm_xattn.rearrange("l (c p) -> (l c) p", p=128)),
        (pst0, R_NF, 8, norm_final.rearrange("l (c p) -> (l c) p", p=128)),
        (pst0, R_DWB, 4, dw_b.rearrange("l (c p) -> (l c) p", p=128)),
        (pst0, R_LNG, 4, ln_a_g.rearrange("l (c p) -> (l c) p", p=128)),
        (pst0, R_LNB, 4, ln_a_b.rearrange("l (c p) -> (l c) p", p=128)),
        (pst0, R_SC, 48, sc_w.rearrange("k (c p) -> (k c) p", p=128)),
        (pst0, R_DNG, 1, dn_norm_g),
        (pst0, R_PSC, 8, pool_scale.rearrange("l (c p) -> (l c) p", p=128)),
        (pst0, R_BP, 8, b_pool.rearrange("g (c p) -> (g c) p", p=128)),
        (pst1, 0, 124, dw_w.rearrange("k (c p) -> (k c) p", p=128)),
    ]
    for i, (dst, r0, n, src) in enumerate(prm_loads):
        P.dma("sp", dst[r0:r0 + n, :], src, sem="prm%d" % (i % 4))
    bk = nb()
    P.transpose(ps[:, bk, 0:128], pst0[:, :], identF[:, :])
    P.transpose(ps[:, bk, 128:256], pst1[:, :], identF[:, :])
    P.copy("dve", pcol0[:, :], ps[:, bk, 0:128])
    P.copy("dve", pcol1[:, :], ps[:, bk, 128:256])
    P.ts("dve", pch[:, 0:124], pcol1[:, 0:124], 0.5, None, ALU.mult)
    P.ts("dve", pch[:, 124:128], pcol0[:, R_LNG:R_LNG + 4], 0.5, None, ALU.mult)
    P.ts("dve", pch[:, 128:132], pcol0[:, R_LNB:R_LNB + 4], 0.5, None, ALU.mult)
    for b in range(NBLK):
        P.dma("sp", dtB[:, b, :], dt_bias[0].partition_broadcast(128), sem="prm%d" % (b % 4))
        P.dma("sp", nAB[:, b, :], a_log[0].partition_broadcast(128), sem="prm%d" % ((b + 1) % 4))
    P.act(nAB[:, :, :], nAB[:, :, :], AF.Exp)
    P.ts("dve", nAB[:, :, :], nAB[:, :, :], -1.0, None, ALU.mult)
    A.release(m0)

    def pc(r):
        return pcol0[:, r:r + 1]

    xsT = A.alloc("xsT", [128, 8, NS], F32)
    hsT = A.alloc("hsT", [128, 8, NS], BF16)
    xT = A.alloc("xT", [128, 8, T], F32)
    memT = A.alloc("memT", [128, 8, NMEM], BF16)
    KT = A.alloc("KT", [128, 8, NMEM], BF16)
    Vtok = A.alloc("Vtok", [128, 2, D], BF16)
    NW = int(_os.environ.get("K_NW", "3"))
    wbuf = [A.alloc("wbuf%d" % i, [128, 8, WG], BF16) for i in range(NW)]
    wctr = [0]
    Sst = A.alloc("Sst", [128, 4, 128], F32)
    Sbf = A.alloc("Sbf", [128, 4, 128], BF16)
    gluhalo = A.alloc("gluhalo", [128, 4, 32], BF16)
    qkvhalo = A.alloc("qkvhalo", [128, 12, 4], BF16)
    uhalo = A.alloc("uhalo", [128, 8, 16], F32)
    mset(Sst[:, :, :], 0.0)
    mset(Sbf[:, :, :], 0.0)
    mset(gluhalo[:, :, :], 0.0)
    mset(qkvhalo[:, :, :], 0.0)
    mset(uhalo[:, :, :], 0.0)

    def wload(src2d, col0, ncols):
        slot = wctr[0] % NW
        wctr[0] += 1
        wt = wbuf[slot]
        P.dma("pool", wt[:, :, 0:ncols], src2d[:, col0:col0 + ncols].rearrange("(c p) n -> p c n", p=128),
              sem="w%d" % slot)
        return wt

    swring = {"bufs": None, "ctr": 0}

    def wload_s(src2d, col0, ncols):
        slot = swring["ctr"] % 2
        swring["ctr"] += 1
        wt = swring["bufs"][slot]
        P.dma("pool", wt[:, :, 0:ncols], src2d[:, col0:col0 + ncols].rearrange("(c p) n -> p c n", p=128),
              sem="sw%d" % slot)
        return wt

    def mem_prep():
        m = A.mark()
        memtok = A.alloc("memtok", [128, 2, D], F32)
        P.dma("sp", memtok[:, :, :], mem_p.rearrange("(c p) d -> p c d", p=128), sem="ld0")
        for mc in range(2):
            for g in range(2):
                bank = nb()
                for j in range(4):
                    dc = g * 4 + j
                    P.transpose(ps[:, bank, j * 128:(j + 1) * 128], memtok[:, mc, dc * 128:(dc + 1) * 128], identF[:, :])
                P.copy(ev(), memT[:, g * 4:g * 4 + 4, mc * 128:(mc + 1) * 128],
                       ps[:, bank, :].rr("p (a b) -> p a b", a=4))
        A.release(m)

    def mem_kv(l):
        m = A.mark()
        kvst = [A.alloc("kvst%d" % i, [128, D], F32) for i in range(2)]
        for which, wsrc, odst in (("k", w_xk, o_mem_k), ("v", w_xv, o_mem_v)):
            for g in range(D // WG):
                wt = wload(wsrc[l], g * WG, WG)
                for mc in range(2):
                    bank = nb()
                    for dc in range(8):
                        P.mm(ps[:, bank, 0:WG], memT[:, dc, mc * 128:(mc + 1) * 128], wt[:, dc, 0:WG],
                             start=(dc == 0), stop=(dc == 7))
                    P.copy(ev(), kvst[mc][:, g * WG:(g + 1) * WG], ps[:, bank, 0:WG])
                if which == "k":
                    for j in range(WG // 128):
                        nch = g * (WG // 128) + j
                        bank2 = nb()
                        for dc in range(8):
                            P.mm(ps[:, bank2, 0:NMEM], wt[:, dc, j * 128:(j + 1) * 128], memT[:, dc, :],
                                 start=(dc == 0), stop=(dc == 7))
                        P.copy(ev(), KT[:, nch, :], ps[:, bank2, 0:NMEM])
            for mc in range(2):
                if which == "v":
                    P.copy("pool", Vtok[:, mc, :], kvst[mc][:, :])
                P.dma("sp", odst[l, mc * 128:(mc + 1) * 128, :], kvst[mc][:, :], sem="kvst%d" % mc)
        A.release(m)

    def load_x():
        m = A.mark()
        xs = [A.alloc("xs%d" % i, [128, D], F32) for i in range(2)]
        for b in range(T // 128):
            st = xs[b % 2]
            P.dma("sp", st[:, :], x_p[b * 128:(b + 1) * 128, :], sem="ld%d" % (b % 2))
            for g in range(2):
                bank = nb()
                for j in range(4):
                    kc = g * 4 + j
                    P.transpose(ps[:, bank, j * 128:(j + 1) * 128], st[:, kc * 128:(kc + 1) * 128], identF[:, :])
                P.copy(ev(), xT[:, g * 4:g * 4 + 4, b * 128:(b + 1) * 128], ps[:, bank, :].rr("p (a b) -> p a b", a=4))
        A.release(m)

    def rmsnorm(hT, t0, grow):
        m = A.mark()
        sq = [A.alloc("sq%d" % i, [128, 512], BF16) for i in range(2)]
        rs = A.alloc("rs", [128, 512], F32)
        for tt in range(NTT):
            c0 = t0 + tt * 512
            bank = nb()
            for kc in range(8):
                s = sq[kc % 2]
                P.act(s[:, :], xT[:, kc, c0:c0 + 512], AF.Square)
                P.mm(ps[:, bank, :], onesB[:, :], s[:, :], start=(kc == 0), stop=(kc == 7))
            P.act(rs[:, :], ps[:, bank, :], AF.Ln, bias=EPSC[:, 0:1], scale=1.0 / D)
            P.act(rs[:, :], rs[:, :], AF.Exp, scale=-0.5)
            for kc in range(8):
                P.stt("dve", hT[:, kc, tt * 512:(tt + 1) * 512], xT[:, kc, c0:c0 + 512], pc(grow + kc), rs[:, :],
                      ALU.mult, ALU.mult)
        A.release(m)

    EPSC = A.alloc("EPSC", [128, 4], F32)
    mset(EPSC[:, 0:1], EPS)
    mset(EPSC[:, 1:2], 4.0 * EPS)
    mset(EPSC[:, 2:3], 1.0)

    def proj(wt, wc, hT, banks):
        for kc in range(8):
            for tt in range(NTT):
                P.mm(ps[:, banks[tt], :], wt[:, kc, wc:wc + 128], hT[:, kc, tt * 512:(tt + 1) * 512],
                     start=(kc == 0), stop=(kc == 7))

    def psv(banks):
        assert banks[-1] == banks[0] + len(banks) - 1
        return ps[:, banks[0]:banks[0] + len(banks), :]

    def pair_banks():
        b = nb()
        while b % 2 or (b + 1) in reserved:
            b = nb()
        nb()
        return [b, b + 1]

    def two_banks():
        if NTT == 1:
            return [nb()]
        b = nb()
        while b % 2:
            b = nb()
        nb()
        return [b, b + 1]

    def out_proj_add(wsrc, inT_fn, t0, pumpn=0):
        for g in range(D // WG):
            wt = wload(wsrc, g * WG, WG)
            for j in range(WG // 128):
                n = g * (WG // 128) + j
                banks = two_banks()
                for kc in range(8):
                    for tt in range(NTT):
                        P.mm(ps[:, banks[tt], :], wt[:, kc, j * 128:(j + 1) * 128], inT_fn(kc, tt),
                             start=(kc == 0), stop=(kc == 7))
                P.tt("dve", xT[:, n, t0:t0 + TT].rr("p (a b) -> p a b", a=NTT), xT[:, n, t0:t0 + TT].rr("p (a b) -> p a b", a=NTT),
                     psv(banks), ALU.add)
                if pumpn:
                    pump(pumpn)

    def xattn(l, mt):
        t0 = mt * TT
        m = A.mark()
        win_begin()
        qx = A.alloc("qx", [128, 8, TT], BF16)
        mh_ = A.mark()
        hT = A.alloc("hT", [128, 8, TT], BF16)
        rmsnorm(hT, t0, R_NX + l * 8)
        for g in range(D // WG):
            wt = wload(w_xq[l], g * WG, WG)
            for j in range(WG // 128):
                n = g * (WG // 128) + j
                banks = two_banks()
                proj(wt, j * 128, hT, banks)
                P.act(qx[:, n, :].rr("p (a b) -> p a b", a=NTT), psv(banks), AF.Copy, scale=1.0 / 16.0)
                pump(PUMP)
        A.release(mh_)
        ox = A.alloc("ox", [128, 8, TT], BF16)
        Et = [A.alloc("Et%d" % i, [128, 512], BF16) for i in range(4)]
        rden = [A.alloc("rden%d" % i, [128, 512], F32) for i in range(2)]
        ei = 0
        for h in range(4):
            for tt in range(NTT):
                es = []
                for mc in range(2):
                    bank = nb()
                    for half in range(2):
                        P.mm(ps[:, bank, :], KT[:, h * 2 + half, mc * 128:(mc + 1) * 128],
                             qx[:, h * 2 + half, tt * 512:(tt + 1) * 512], start=(half == 0), stop=(half == 1))
                    E = Et[ei % 4]
                    ei += 1
                    P.act(E[:, :], ps[:, bank, :], AF.Exp)
                    es.append(E)
                pump(PUMP)
                bden = nb()
                for mc in range(2):
                    P.mm(ps[:, bden, :], onesB[:, :], es[mc][:, :], start=(mc == 0), stop=(mc == 1))
                rd = rden[(h * NTT + tt) % 2]
                P.act(rd[:, :], ps[:, bden, :], AF.Ln)
                P.act(rd[:, :], rd[:, :], AF.Exp, scale=-1.0)
                for dv in range(2):
                    bo = nb()
                    n = h * 2 + dv
                    for mc in range(2):
                        P.mm(ps[:, bo, :], Vtok[:, mc, n * 128:(n + 1) * 128], es[mc][:, :],
                             start=(mc == 0), stop=(mc == 1))
                    P.tt("dve", ox[:, n, tt * 512:(tt + 1) * 512], ps[:, bo, :], rd[:, :], ALU.mult)
                pump(PUMP)
        out_proj_add(w_xo[l], lambda kc, tt: ox[:, kc, tt * 512:(tt + 1) * 512], t0, pumpn=PUMP)
        win_end()
        A.release(m)

    def even_mixer(mt):
        t0 = mt * TT
        last = (mt == NMT - 1)
        m = A.mark()
        aoT = A.alloc("aoT", [128, 4, TT], BF16)
        G_BETA, G_G, G_GCS, G_GCL, G_EGC, G_EKD, G_NBEG, G_VBS, G_NBETA, G_TMP = range(10)
        mh = A.mark()
        hT = A.alloc("hT", [128, 8, TT], BF16)
        rmsnorm(hT, t0, R_NM + 0)

        ma = A.mark()
        gluT = A.alloc("gluT", [128, 4, 32 + TT], BF16)
        cbuf = A.alloc("cbuf", [128, 4, TT], F32)
        Dg = A.alloc("Dg", [128, 31, 128], BF16)
        tA = [A.alloc("tA%d" % i, [128, TT], F32) for i in range(2)]
        tB = [A.alloc("tB%d" % i, [128, 1024], F32) for i in range(2)]
        stat = [A.alloc("stat%d" % i, [128, 512], F32) for i in range(3)]
        sga = A.alloc("sga", [128, 4, TT], BF16)
        wv = {}
        for c in range(4):
            g = c // 2
            if c % 2 == 0:
                wv = {"val": wload(w_in_even, g * WG, WG), "gate": wload(w_in_even, 512 + g * WG, WG)}
            wc = (c % 2) * 128
            bv = two_banks()
            bg = two_banks()
            proj(wv["val"], wc, hT, bv)
            proj(wv["gate"], wc, hT, bg)
            t = tA[c % 2]
            P.act(t[:, :].rr("p (a b) -> p a b", a=NTT), psv(bg), AF.Tanh, scale=0.5)
            P.copy("pool", gluT[:, c, 0:32], gluhalo[:, c, :])
            P.stt("dve", gluT[:, c, 32:32 + TT].rr("p (a b) -> p a b", a=NTT), t[:, :].rr("p (a b) -> p a b", a=NTT),
                  1.0, psv(bv), ALU.add, ALU.mult)
            P.copy("pool", gluhalo[:, c, :], gluT[:, c, TT:TT + 32])
            for k in range(31):
                P.ts("dve", Dg[:, k, :], identB[:, :], pch[:, k * 4 + c:k * 4 + c + 1], None, ALU.mult)
            for tt in range(NTT):
                bank = nb()
                for k in range(31):
                    P.mm(ps[:, bank, :], Dg[:, k, :], gluT[:, c, 2 + tt * 512 + k:2 + tt * 512 + k + 512],
                         start=(k == 0), stop=(k == 30))
                P.act(cbuf[:, c, tt * 512:(tt + 1) * 512], ps[:, bank, :], AF.Identity, bias=pc(R_DWB + c))
            if c % 2 == 0:
                wv["ga"] = wload(w_in_even, 1024 + (c // 2) * WG, WG)
            bg2 = two_banks()
            proj(wv["ga"], (c % 2) * 128, hT, bg2)
            t = tA[c % 2]
            P.act(t[:, :].rr("p (a b) -> p a b", a=NTT), psv(bg2), AF.Tanh, scale=0.5)
            P.stt("dve", sga[:, c, :].rr("p (a b) -> p a b", a=NTT), t[:, :].rr("p (a b) -> p a b", a=NTT), 1.0, psv(bg2),
                  ALU.add, ALU.mult)
        if last:
            cst = A.alloc("cst", [128, 4, 32], F32)
            csto = A.alloc("csto", [32, 512], F32)
            P.act(cst[:, :, :], gluhalo[:, :, :], AF.Copy, scale=0.5)
            bank = nb()
            for c in range(4):
                P.transpose(ps[0:32, bank, c * 128:(c + 1) * 128], cst[:, c, :], identF[:, :])
            P.copy("dve", csto[:, :], ps[0:32, bank, :])
            P.dma("sp", o_conv[:, :], csto[2:32, :], sem="st0")
        wga = None
        for tt in range(NTT):
            b1, b2 = nb(), nb()
            for c in range(4):
                s = tB[c % 2]
                P.act(s[:, 0:512], cbuf[:, c, tt * 512:(tt + 1) * 512], AF.Square)
                P.mm(ps[:, b1, :], onesF[:, :], cbuf[:, c, tt * 512:(tt + 1) * 512], start=(c == 0), stop=(c == 3))
                P.mm(ps[:, b2, :], onesF[:, :], s[:, 0:512], start=(c == 0), stop=(c == 3))
            mean, msq, rstd = stat
            P.act(mean[:, :], ps[:, b1, :], AF.Copy, scale=1.0 / 512)
            P.tt("dve", msq[:, :], mean[:, :], mean[:, :], ALU.mult)
            P.stt("dve", rstd[:, :], ps[:, b2, :], 1.0 / 512, msq[:, :], ALU.mult, ALU.subtract)
            P.act(rstd[:, :], rstd[:, :], AF.Ln, bias=EPSC[:, 0:1])
            P.act(rstd[:, :], rstd[:, :], AF.Exp, scale=-0.5)
            for c in range(4):
                cv = cbuf[:, c, tt * 512:(tt + 1) * 512]
                P.tt("dve", cv, cv, mean[:, :], ALU.subtract)
                P.tt("dve", cv, cv, rstd[:, :], ALU.mult)
                P.act(cv, cv, AF.Identity, bias=pch[:, 128 + c:129 + c], scale=pch[:, 124 + c:125 + c])
                th = tB[c % 2]
                P.act(th[:, 512:1024], cv, AF.Tanh)
                P.stt("dve", cv, th[:, 512:1024], 1.0, cv, ALU.add, ALU.mult)
        for c in range(4):
            P.stt("dve", aoT[:, c, :], sga[:, c, :], 0.5, cbuf[:, c, :], ALU.mult, ALU.mult)
        A.release(mh)
        if STAGE <= 3:
            dbg["aoT"] = aoT
            A.release(m)
            return

        qT = A.alloc("qT", [128, 4, TT], BF16)
        qdT = A.alloc("qdT", [128, 4, TT], BF16)
        kT = A.alloc("kT", [128, 4, TT], BF16)
        vb = A.alloc("vb", [128, NBLK, 4, 128], BF16)
        kd = A.alloc("kd", [128, NBLK, 4, 128], BF16)
        zs2 = A.alloc("zs2", [128, 4, TT], BF16)
        gts = A.alloc("gts", [128, 12, NBLK, 4], F32)
        eglB = A.alloc("eglB", [128, NBLK, 2, 4], F32)
        egcF = A.alloc("egcF", [4, TT], F32)
        mh = A.mark()
        hT = A.alloc("hT", [128, 8, TT], BF16)
        rmsnorm(hT, t0, R_NM + 0)
        mb = A.mark()
        w8 = A.alloc("w8", [128, 8, 8], BF16)
        P.dma("pool", w8[:, :, :], w_in_even[:, 3584:3592].rearrange("(c p) n -> p c n", p=128), sem="w8")
        bank = nb()
        for b in range(NBLK):
            for part in range(2):
                for kc in range(8):
                    P.mm(ps[:, bank, part * NBLK * 4 + b * 4:part * NBLK * 4 + b * 4 + 4], hT[:, kc, b * 128:(b + 1) * 128],
                         w8[:, kc, part * 4:part * 4 + 4], start=(kc == 0), stop=(kc == 7))
        gl = A.alloc("gl", [128, 2, NBLK, 4], F32)
        P.copy("dve", gl[:, :, :, :], ps[:, bank, 0:NBLK * 8].rr("p (a b c) -> p a b c", a=2, b=NBLK))

        def G(i):
            return gts[:, i, :, :]
        P.act(G(G_TMP), gl[:, 0, :, :], AF.Tanh, scale=0.5)
        P.ts("dve", G(G_BETA), G(G_TMP), 0.5, 0.5, ALU.mult, ALU.add)
        P.tt("dve", G(G_TMP), gl[:, 1, :, :], dtB[:, :, :], ALU.add)
        P.act(G(G_TMP), G(G_TMP), AF.Exp)
        P.act(G(G_TMP), G(G_TMP), AF.Ln, bias=EPSC[:, 2:3])
        P.tt("dve", G(G_G), G(G_TMP), nAB[:, :, :], ALU.mult)
        bank = nb()
        gm = A.alloc("gm", [128, NBLK, 2, 4], F32)
        for b in range(NBLK):
            P.mm(ps[:, bank, b * 4:b * 4 + 4], Mincl[:, :], gts[:, G_G, b, :], start=True, stop=True)
            P.mm(ps[:, bank, NBLK * 4 + b * 4:NBLK * 4 + b * 4 + 4], Msame[:, :], gts[:, G_G, b, :], start=True, stop=True)
            P.ts("dve", gm[:, b, 0, :], gts[:, G_G, b, :], ind0, None, ALU.mult)
            P.ts("dve", gm[:, b, 1, :], gts[:, G_G, b, :], ind1, None, ALU.mult)
        P.copy("dve", gts[:, G_GCS:G_GCL + 1, :, :], ps[:, bank, 0:NBLK * 8].rr("p (a b c) -> p a b c", a=2, b=NBLK))
        bank = nb()
        P.mm(ps[:, bank, 0:NBLK * 8], onesF[:, :], gm[:, :, :, :].rr("p a b c -> p (a b c)"), start=True, stop=True)
        P.act(eglB[:, :, :, :].rr("p a b c -> p (a b c)"), ps[:, bank, 0:NBLK * 8], AF.Exp)
        P.act(G(G_EGC), G(G_GCS), AF.Exp)
        P.tt("dve", G(G_TMP), G(G_GCL), G(G_GCS), ALU.subtract)
        P.act(G(G_EKD), G(G_TMP), AF.Exp)
        P.stt("dve", G(G_NBEG), G(G_BETA), -1.0, G(G_EGC), ALU.mult, ALU.mult)
        P.ts("dve", G(G_VBS), G(G_BETA), 0.5, None, ALU.mult)
        P.ts("dve", G(G_NBETA), G(G_BETA), -1.0, None, ALU.mult)
        for g2 in range(TT // 512):
            bank = nb()
            for j in range(4):
                b = g2 * 4 + j
                P.transpose(ps[0:4, bank, j * 128:(j + 1) * 128], gts[:, G_EGC, b, :], identF[:, :])
            P.copy("dve", egcF[:, g2 * 512:(g2 + 1) * 512], ps[0:4, bank, :])

        pre = [A.alloc("pre%d" % i, [128, 4 + TT], BF16) for i in range(2)]
        Dq = [A.alloc("Dq%d" % i, [128, 4, 128], BF16) for i in range(2)]
        s2 = [A.alloc("s2%d" % i, [128, TT], F32) for i in range(2)]
        tq = [A.alloc("tq%d" % i, [128, TT], F32) for i in range(1)] * 2
        sqb = [A.alloc("sqb%d" % i, [128, 512], BF16) for i in range(2)]
        r1 = A.alloc("r1", [128, 512], F32)
        r2 = A.alloc("r2", [128, 512], F32)
        qst = A.alloc("qst", [128, 12, 4], F32)
        if last:
            mset(qst[:, :, :], 0.0)
        wqs = {}

        def st1(c12):
            if c12 % 2 == 0:
                wqs[c12 // 2] = wload(w_in_even, 1536 + (c12 // 2) * WG, WG)
            wq = wqs[c12 // 2]
            bq = two_banks()
            proj(wq, (c12 % 2) * 128, hT, bq)
            pr = pre[c12 % 2]
            P.copy("pool", pr[:, 0:4], qkvhalo[:, c12, :])
            P.copy("act", pr[:, 4:4 + TT].rr("p (a b) -> p a b", a=NTT), psv(bq))
            if last:
                P.copy("dve", qst[:, c12, 0:3], ps[:, bq[NTT - 1], 509:512])
            P.copy("pool", qkvhalo[:, c12, :], pr[:, TT:TT + 4])

        def st2(c12):
            pr = pre[c12 % 2]
            dq = Dq[c12 % 2]
            for k in range(4):
                P.ts("dve", dq[:, k, :], identB[:, :], pc(R_SC + k * 12 + c12), None, ALU.mult)
            bc_ = two_banks()
            for tt in range(NTT):
                for k in range(4):
                    P.mm(ps[:, bc_[tt], :], dq[:, k, :], pr[:, 1 + tt * 512 + k:1 + tt * 512 + k + 512],
                         start=(k == 0), stop=(k == 3))
            t = tq[c12 % 2]
            s = s2[c12 % 2]
            P.act(t[:, :].rr("p (a b) -> p a b", a=NTT), psv(bc_), AF.Tanh, scale=0.5)
            P.stt("dve", s[:, :].rr("p (a b) -> p a b", a=NTT), t[:, :].rr("p (a b) -> p a b", a=NTT), 1.0, psv(bc_),
                  ALU.add, ALU.mult)

        def st3(c12):
            kind, h = c12 // 4, c12 % 4
            s = s2[c12 % 2]
            for tt in range(NTT):
                sv = s[:, tt * 512:(tt + 1) * 512]
                if kind < 2:
                    sb_ = sqb[tt % 2]
                    P.act(sb_[:, :], sv, AF.Square)
                    bank = nb()
                    P.mm(ps[:, bank, :], onesB[:, :], sb_[:, :], start=True, stop=True)
                    P.act(r1[:, :], ps[:, bank, :], AF.Ln, bias=EPSC[:, 1:2])
                    P.act(r1[:, :], r1[:, :], AF.Exp, scale=-0.5)
                if kind == 0:
                    P.stt("dve", qT[:, h, tt * 512:(tt + 1) * 512], sv, 128.0 ** -0.5, r1[:, :], ALU.mult, ALU.mult)
                    bank = nb()
                    P.mm(ps[:, bank, :], sel[:, h, :], egcF[:, tt * 512:(tt + 1) * 512], start=True, stop=True)
                    P.tt("dve", r2[:, :], r1[:, :], ps[:, bank, :], ALU.mult)
                    P.stt("dve", qdT[:, h, tt * 512:(tt + 1) * 512], sv, 128.0 ** -0.5, r2[:, :], ALU.mult, ALU.mult)
                elif kind == 1:
                    P.tt("dve", sv, sv, r1[:, :], ALU.mult)
                    P.copy("act", kT[:, h, tt * 512:(tt + 1) * 512], sv)
                if kind >= 1:
                    bank = nb()
                    for j in range(4):
                        P.transpose(ps[:, bank, j * 128:(j + 1) * 128], s[:, tt * 512 + j * 128:tt * 512 + (j + 1) * 128],
                                    identF[:, :])
                    dst = kd if kind == 1 else vb
                    gsc4 = gts[:, G_EKD if kind == 1 else G_VBS, tt * 4:(tt + 1) * 4, h]
                    P.tt("dve", dst[:, tt * 4:(tt + 1) * 4, h, :], ps[:, bank, :].rr("p (a b) -> p a b", a=4),
                         V(gsc4.ap.unsqueeze(2).to_broadcast([128, 4, 128]), gsc4.t, gsc4.rect), ALU.mult)

        for step in range(12 + 2):
            if step < 12:
                st1(step)
            if 0 <= step - 1 < 12:
                st2(step - 1)
            if 0 <= step - 2 < 12:
                st3(step - 2)
        if last:
            qsto = A.alloc("qsto", [4, 512], F32)
            for g3 in range(3):
                bank = nb()
                for j in range(4):
                    P.transpose(ps[0:4, bank, j * 128:(j + 1) * 128], qst[:, g3 * 4 + j, :], identF[:, :])
                P.copy("dve", qsto[:, :], ps[0:4, bank, :])
                P.dma("sp", o_qkv[:, g3 * 512:(g3 + 1) * 512], qsto[0:3, :], sem="st1")
        wz = None
        for c in range(4):
            if c % 2 == 0:
                wz = wload(w_in_even, 3072 + (c // 2) * WG, WG)
            bz = two_banks()
            proj(wz, (c % 2) * 128, hT, bz)
            t = tq[c % 2]
            P.act(t[:, :].rr("p (a b) -> p a b", a=NTT), psv(bz), AF.Tanh, scale=0.5)
            P.stt("dve", zs2[:, c, :].rr("p (a b) -> p a b", a=NTT), t[:, :].rr("p (a b) -> p a b", a=NTT), 1.0, psv(bz),
                  ALU.add, ALU.mult)
        A.release(mh)

        TTm = A.alloc("TTm", [128, NBLK, 4, 128], BF16)
        QKm = A.alloc("QKm", [128, NBLK, 4, 128], BF16)
        mprep = A.mark()
        Gm = [A.alloc("Gm%d" % i, [128, 128], F32) for i in range(4)]
        Es4 = A.alloc("Es4", [128, 4, 128], F32)
        Ei4 = A.alloc("Ei4", [128, 4, 128], F32)
        Mb = [A.alloc("Mb%d" % i, [128, 4, 128], F32) for i in range(2)]
        MTb = [A.alloc("MTb%d" % i, [128, 4, 128], F32) for i in range(2)]
        Xb = [A.alloc("Xb%d" % i, [128, 4, 128], F32) for i in range(2)]
        for b in range(NBLK):
            cs = slice(b * 128, (b + 1) * 128)
            for h in range(4):
                P.ts("pool", Gm[h][:, :], Mgt[:, :], gts[:, G_G, b, h:h + 1], None, ALU.mult)
            bS_, bI_, bK_, bQ_ = nb(), nb(), nb(), nb()
            for h in range(4):
                hs = slice(h * 128, (h + 1) * 128)
                P.mm(ps[:, bK_, hs], kT[:, h, cs], kT[:, h, cs], start=True, stop=True)
                P.mm(ps[:, bQ_, hs], kT[:, h, cs], qT[:, h, cs], start=True, stop=True)
            for h in range(4):
                hs = slice(h * 128, (h + 1) * 128)
                P.mm(ps[:, bS_, hs], Mincl[:, :], Gm[h][:, :], start=True, stop=False)
                P.mm(ps[:, bS_, hs], identB[:, :], NEGsB[:, :], start=False, stop=True)
                P.mm(ps[:, bI_, hs], Gm[h][:, :], Mincl[:, :], start=True, stop=False)
                P.mm(ps[:, bI_, hs], identB[:, :], NEGiTB[:, :], start=False, stop=True)
            P.act(Es4[:, :, :], ps[:, bS_, :].rr("p (a b) -> p a b", a=4), AF.Exp)
            P.act(Ei4[:, :, :], ps[:, bI_, :].rr("p (a b) -> p a b", a=4), AF.Exp)
            for h in range(4):
                P.stt("dve", MTb[0][:, h, :], ps[:, bK_, h * 128:(h + 1) * 128], gts[:, G_NBETA, b, h:h + 1], Es4[:, h, :],
                      ALU.mult, ALU.mult)
            P.tt("dve", QKm[:, b, :, :], ps[:, bQ_, :].rr("p (a b) -> p a b", a=4), Ei4[:, :, :], ALU.mult)
            bank = nb()
            for h in range(4):
                P.transpose(ps[:, bank, h * 128:(h + 1) * 128], MTb[0][:, h, :], identF[:, :])
            P.copy("act", Mb[0][:, :, :], ps[:, bank, :].rr("p (a b) -> p a b", a=4))
            for h in range(4):
                P.tt("pool", Xb[0][:, h, :], Mb[0][:, h, :], identF[:, :], ALU.add)
            cur = 0
            for k in range(1, 6):
                nxt = 1 - cur
                bm, bmt, bx = nb(), nb(), nb()
                for h in range(4):
                    hs = slice(h * 128, (h + 1) * 128)
                    if k < 5:
                        P.mm(ps[:, bm, hs], MTb[cur][:, h, :], Mb[cur][:, h, :], start=True, stop=True)
                    P.mm(ps[:, bmt, hs], Mb[cur][:, h, :], MTb[cur][:, h, :], start=True, stop=True)
                if k < 5:
                    P.copy("act", Mb[nxt][:, :, :], ps[:, bm, :].rr("p (a b) -> p a b", a=4))
                P.copy("dve", MTb[nxt][:, :, :], ps[:, bmt, :].rr("p (a b) -> p a b", a=4))
                for h in range(4):
                    hs = slice(h * 128, (h + 1) * 128)
                    P.mm(ps[:, bx, hs], MTb[nxt][:, h, :], Xb[cur][:, h, :], start=True, stop=True)
                if k < 5:
                    P.tt("dve", Xb[nxt][:, :, :], Xb[cur][:, :, :], ps[:, bx, :].rr("p (a b) -> p a b", a=4), ALU.add)
                else:
                    P.tt("dve", TTm[:, b, :, :], Xb[cur][:, :, :], ps[:, bx, :].rr("p (a b) -> p a b", a=4), ALU.add)
                cur = nxt

        A.release(mprep)
        bT = qT
        R4 = A.alloc("R4", [128, 4, 128], BF16)
        vn4 = A.alloc("vn4", [128, 4, 128], BF16)
        of = [A.alloc("of%d" % i, [128, 512], F32) for i in range(2)]
        osq = [A.alloc("osq%d" % i, [128, 512], BF16) for i in range(2)]
        orr = [A.alloc("orr%d" % i, [128, 512], F32) for i in range(2)]
        obank = {}
        for tt in range(NTT):
            for h in range(4):
                obank[h] = h
                reserved.add(h)
            for cc in range(8):
                c = tt * 8 + cc
                b, par = c // 2, c % 2
                r0 = par * 64
                tok = slice(c * 64, (c + 1) * 64)
                bA, bB, bC = nb(), nb(), nb()
                for h in range(4):
                    P.mm(ps[r0:r0 + 64, bA, h * 128:(h + 1) * 128], kT[:, h, tok], Sbf[:, h, :], start=True, stop=True)
                for h in range(4):
                    P.mm(ps[:, obank[h], cc * 64:(cc + 1) * 64], Sbf[:, h, :], qdT[:, h, tok], start=True, stop=False)
                for h in range(4):
                    P.stt("dve", R4[r0:r0 + 64, h, :], ps[r0:r0 + 64, bA, h * 128:(h + 1) * 128],
                          gts[r0:r0 + 64, G_NBEG, b, h:h + 1], vb[r0:r0 + 64, b, h, :], ALU.mult, ALU.add)
                for h in range(4):
                    P.mm(ps[r0:r0 + 64, bB, h * 128:(h + 1) * 128], TTm[r0:r0 + 64, b, h, r0:r0 + 64], R4[r0:r0 + 64, h, :],
                         start=True, stop=True)
                P.copy("act", vn4[r0:r0 + 64, :, :], ps[r0:r0 + 64, bB, :].rr("p (a b) -> p a b", a=4))
                for h in range(4):
                    P.mm(ps[:, obank[h], cc * 64:(cc + 1) * 64], vn4[r0:r0 + 64, h, :], QKm[r0:r0 + 64, b, h, r0:r0 + 64],
                         start=False, stop=True)
                for h in range(4):
                    P.mm(ps[:, bC, h * 128:(h + 1) * 128], kd[r0:r0 + 64, b, h, :], vn4[r0:r0 + 64, h, :], start=True, stop=True)
                for h in range(4):
                    P.stt("dve", Sst[:, h, :], Sst[:, h, :], eglB[:, b, par, h:h + 1], ps[:, bC, h * 128:(h + 1) * 128],
                          ALU.mult, ALU.add)
                P.copy("act", Sbf[:, :, :], Sst[:, :, :])
            for h in range(4):
                o_ = of[h % 2]
                P.copy("act", o_[:, :], ps[:, obank[h], :])
                sq_ = osq[h % 2]
                P.act(sq_[:, :], o_[:, :], AF.Square)
                bank = nb()
                P.mm(ps[:, bank, :], onesB[:, :], sq_[:, :], start=True, stop=True)
                rr_ = orr[h % 2]
                P.act(rr_[:, :], ps[:, bank, :], AF.Ln, bias=EPSC[:, 0:1], scale=1.0 / 128)
                P.act(rr_[:, :], rr_[:, :], AF.Exp, scale=-0.5)
                P.stt("dve", o_[:, :], o_[:, :], pc(R_DNG), rr_[:, :], ALU.mult, ALU.mult)
                P.stt("dve", bT[:, h, tt * 512:(tt + 1) * 512], o_[:, :], 0.5, zs2[:, h, tt * 512:(tt + 1) * 512], ALU.mult, ALU.mult)
            reserved.clear()
        if last:
            P.dma("sp", o_delta.rearrange("h d e -> d h e"), Sst[:, :, :], sem="st2")
        if STAGE <= 4:
            A.release(m)
            return
        out_proj_add(w_out_even, lambda kc, tt: (aoT if kc < 4 else bT)[:, kc % 4, tt * 512:(tt + 1) * 512], t0)
        A.release(m)

    def odd_mixer(mt):
        t0 = mt * TT
        last = (mt == NMT - 1)
        m = A.mark()
        hT = A.alloc("hT", [128, 8, TT], BF16)
        rmsnorm(hT, t0, R_NM + 8)
        plT = A.alloc("plT", [128, 8, TT], BF16)
        ub = [A.alloc("ub%d" % i, [128, 16 + TT], F32) for i in range(2)]
        sa = A.alloc("sa", [128, 16 + TT], F32)
        sb2 = A.alloc("sb2", [128, 16 + TT], F32)
        wu = None
        for c in range(8):
            if c % 2 == 0:
                wu = wload(w_in_odd, (c // 2) * WG, WG)
            bu = two_banks()
            proj(wu, (c % 2) * 128, hT, bu)
            u = ub[c % 2]
            P.copy("pool", u[:, 0:16], uhalo[:, c, :])
            P.copy("act", u[:, 16:16 + TT].rr("p (a b) -> p a b", a=NTT), psv(bu))
            P.copy("pool", uhalo[:, c, :], u[:, TT:TT + 16])
            gi = c // 2
            win = 2 << gi
            src = u
            sh, lo, bi = 1, 1, 0
            bufs = [sa, sb2]
            while sh < win:
                dst = bufs[bi]
                bi = 1 - bi
                P.tt("dve", dst[:, lo:16 + TT], src[:, lo:16 + TT], src[:, lo - sh:16 + TT - sh], ALU.add)
                src = dst
                sh *= 2
                lo = 2 * lo + 1
            P.stt("dve", plT[:, c, :], src[:, 16:16 + TT], 1.0 / win, u[:, 16:16 + TT], ALU.mult, ALU.subtract)
            if mt == 0:
                fx = sa if src is sb2 else sb2
                P.tt("dve", fx[:, 0:win - 1], src[:, 16:16 + win - 1], invc[:, 0:win - 1], ALU.mult)
                P.tt("dve", plT[:, c, 0:win - 1], fx[:, 0:win - 1], u[:, 16:16 + win - 1], ALU.subtract)
        if last:
            pst = A.alloc("pst", [128, 8, 16], F32)
            psto = A.alloc("psto", [16, D], F32)
            P.copy("dve", pst[:, :, :], uhalo[:, :, :])
            for g2 in range(2):
                bank = nb()
                for j in range(4):
                    P.transpose(ps[0:16, bank, j * 128:(j + 1) * 128], pst[:, g2 * 4 + j, :], identF[:, :])
                P.copy("dve", psto[:, g2 * 512:(g2 + 1) * 512], ps[0:16, bank, :])
            P.dma("sp", o_pool[:, :], psto[1:16, :], sem="st3")
        zT = A.alloc("zT", [128, 8, TT], BF16)
        wp = A.alloc("wp", [128, 4, 2, 256], BF16)
        P.dma("pool", wp[:, :, :, :], w_pool.rearrange("g (c p) d -> p g c d", p=128), sem="wp")
        tz = [A.alloc("tz%d" % i, [128, TT], F32) for i in range(2)]
        tg = [A.alloc("tg%d" % i, [128, TT], F32) for i in range(2)]
        wgt = None
        for n in range(8):
            gi, half = n // 2, n % 2
            bz = two_banks()
            for cc in range(2):
                for tt in range(NTT):
                    P.mm(ps[:, bz[tt], :], wp[:, gi, cc, half * 128:(half + 1) * 128], plT[:, gi * 2 + cc, tt * 512:(tt + 1) * 512],
                         start=(cc == 0), stop=(cc == 1))
            z_ = tz[n % 2]
            P.ts("dve", z_[:, :].rr("p (a b) -> p a b", a=NTT), psv(bz), pc(R_BP + n), pc(R_PSC + n), ALU.add, ALU.mult)
            if n % 2 == 0:
                wgt = wload(w_in_odd, 1024 + (n // 2) * WG, WG)
            bg = two_banks()
            proj(wgt, (n % 2) * 128, hT, bg)
            t = tg[n % 2]
            P.act(t[:, :].rr("p (a b) -> p a b", a=NTT), psv(bg), AF.Tanh, scale=0.5)
            P.stt("dve", t[:, :].rr("p (a b) -> p a b", a=NTT), t[:, :].rr("p (a b) -> p a b", a=NTT), 1.0, psv(bg),
                  ALU.add, ALU.mult)
            P.stt("dve", zT[:, n, :], t[:, :], 0.5, z_[:, :], ALU.mult, ALU.mult)
        out_proj_add(w_out_odd, lambda kc, tt: zT[:, kc, tt * 512:(tt + 1) * 512], t0)
        A.release(m)

    def final_out():
        m = A.mark()
        sq = [A.alloc("fsq%d" % i, [128, 512], BF16) for i in range(2)]
        rs = A.alloc("frs", [128, 512], F32)
        yf = [A.alloc("yf%d" % i, [128, 512], F32) for i in range(2)]
        yst = A.alloc("yst", [128, 4, D], F32)
        for tt in range(T // 512):
            c0 = tt * 512
            bank = nb()
            for kc in range(8):
                s = sq[kc % 2]
                P.act(s[:, :], xT[:, kc, c0:c0 + 512], AF.Square)
                P.mm(ps[:, bank, :], onesB[:, :], s[:, :], start=(kc == 0), stop=(kc == 7))
            P.act(rs[:, :], ps[:, bank, :], AF.Ln, bias=EPSC[:, 0:1], scale=1.0 / D)
            P.act(rs[:, :], rs[:, :], AF.Exp, scale=-0.5)
            for kc in range(8):
                y_ = yf[kc % 2]
                P.stt("dve", y_[:, :], xT[:, kc, c0:c0 + 512], pc(R_NF + kc), rs[:, :], ALU.mult, ALU.mult)
                bank2 = nb()
                for j in range(4):
                    P.transpose(ps[:, bank2, j * 128:(j + 1) * 128], y_[:, j * 128:(j + 1) * 128], identF[:, :])
                P.copy(ev(), yst[:, :, kc * 128:(kc + 1) * 128], ps[:, bank2, :].rr("p (a b) -> p a b", a=4))
            for j in range(4):
                P.dma("sp", o_y[c0 + j * 128:c0 + (j + 1) * 128, :], yst[:, j, :], sem="sty%d" % j)
            pump(PUMP)
        A.release(m)

    x_s = din("x_s", [NS, D])
    st_conv = din("st_conv", [NS, 30, 512])
    st_qkv = din("st_qkv", [NS, 3, 1536])
    st_delta = din("st_delta", [NS, 4, 128, 128])
    st_pool = din("st_pool", [NS, 15, D])
    c_k = din("c_k", [DEPTH, NS, NMEM, D])
    c_v = din("c_v", [DEPTH, NS, NMEM, D])
    o_ys = dout("o_ys", [NS, D])
    o_conv_s = dout("o_conv_s", [NS, 30, 512])
    o_qkv_s = dout("o_qkv_s", [NS, 3, 1536])
    o_delta_s = dout("o_delta_s", [NS, 4, 128, 128])
    o_pool_s = dout("o_pool_s", [NS, 15, D])

    def bcv(v, axis, shape):
        return V(v.ap.unsqueeze(axis).to_broadcast(list(shape)), v.t, v.rect, v.excl)

    def recip(eng_view):
        P.add("dve", lambda e, t=eng_view: e.reciprocal(t.ap, t.ap), reads=[eng_view], writes=[eng_view])

    def reduce_x(out, in_):
        P.add("dve", lambda e: e.tensor_reduce(out.ap, in_.ap, AX.X, ALU.add), reads=[in_], writes=[out])

    def sample_path():
        eye16 = identF[0:NS, 0:NS]
        eye4 = identF[0:4, 0:4]

        def ring():
            swring["bufs"] = [AS.alloc("swb%d" % i, [128, 8, WG], BF16) for i in range(2)]

        def rmsnorm_s(grow, f32out=None):
            m = AS.mark()
            sq = AS.alloc("ssq", [128, 8, NS], BF16)
            rs = AS.alloc("srs", [128, NS], F32)
            P.act(sq[:, :, :], xsT[:, :, :], AF.Square)
            bank = nb()
            for kc in range(8):
                P.mm(ps[:, bank, 0:NS], onesB[:, :], sq[:, kc, :], start=(kc == 0), stop=(kc == 7))
            P.act(rs[:, :], ps[:, bank, 0:NS], AF.Sqrt, bias=EPSC[:, 0:1], scale=1.0 / D)
            recip(rs[:, :])
            dst = hsT if f32out is None else f32out
            for kc in range(8):
                P.stt("dve", dst[:, kc, :], xsT[:, kc, :], pc(grow + kc), rs[:, :], ALU.mult, ALU.mult)
            AS.release(m)

        def s_outproj(wsrc, in_fn):
            bank = nb()
            reserved.add(bank)
            for g in range(D // WG):
                wt = wload_s(wsrc, g * WG, WG)
                for j in range(WG // 128):
                    n = g * (WG // 128) + j
                    for kc in range(8):
                        P.mm(ps[:, bank, n * NS:(n + 1) * NS], wt[:, kc, j * 128:(j + 1) * 128], in_fn(kc),
                             start=(kc == 0), stop=(kc == 7))
                yield
            reserved.discard(bank)
            P.tt("dve", xsT[:, :, :], xsT[:, :, :], ps[:, bank, 0:8 * NS].rr("p (a b) -> p a b", a=8), ALU.add)

        m_seg = AS.mark()
        ring()
        mx_ = AS.mark()
        xst = AS.alloc("xst", [NS, D], F32)
        P.dma("sp", xst[:, :], x_s, sem="sld0")
        bank = nb()
        for kc in range(8):
            P.transpose(ps[:, bank, kc * NS:(kc + 1) * NS], xst[:, kc * 128:(kc + 1) * 128], eye16)
        P.copy("dve", xsT[:, :, :], ps[:, bank, 0:8 * NS].rr("p (a b) -> p a b", a=8))
        AS.release(mx_)
        rmsnorm_s(R_NM + 0)
        yield
        pF = AS.alloc("pF", [128, 16, NS], F32)
        qkvS = AS.alloc("qkvS", [NS, 1536], F32)
        gls = AS.alloc("gls", [NS, 8], F32)
        aoS = AS.alloc("aoS", [128, 4, NS], BF16)
        bS = AS.alloc("bS", [128, 4, NS], BF16)
        w8s = AS.alloc("w8s", [128, 8, 8], BF16)
        BF_, BQ_ = 4, 5
        reserved.update([BF_, BQ_])
        P.dma("pool", w8s[:, :, :], w_in_even[:, 3584:3592].rearrange("(c p) n -> p c n", p=128), sem="sw8")
        for g in range(14):
            col0 = g * WG
            wt = wload_s(w_in_even, col0, WG)
            if col0 < 1536 or col0 >= 3072:
                for j in range(2):
                    ci = (col0 // 128 + j) if col0 < 1536 else (12 + (col0 - 3072) // 128 + j)
                    for kc in range(8):
                        P.mm(ps[:, BF_, ci * NS:(ci + 1) * NS], wt[:, kc, j * 128:(j + 1) * 128], hsT[:, kc, :],
                             start=(kc == 0), stop=(kc == 7))
            else:
                gq = (col0 - 1536) // WG
                for kc in range(8):
                    P.mm(ps[0:NS, BQ_, (gq % 2) * WG:(gq % 2 + 1) * WG], hsT[:, kc, :], wt[:, kc, 0:WG],
                         start=(kc == 0), stop=(kc == 7))
                if gq % 2 == 1:
                    P.copy("act", qkvS[:, (gq // 2) * 512:(gq // 2 + 1) * 512], ps[0:NS, BQ_, :])
            yield
        for kc in range(8):
            P.mm(ps[0:NS, BQ_, 0:8], hsT[:, kc, :], w8s[:, kc, :], start=(kc == 0), stop=(kc == 7))
        P.copy("dve", pF[:, :, :], ps[:, BF_, 0:16 * NS].rr("p (a b) -> p a b", a=16))
        P.copy("dve", gls[:, :], ps[0:NS, BQ_, 0:8])
        reserved.discard(BF_)
        reserved.discard(BQ_)
        P.dma("sp", o_conv_s[:, 0:29, :], st_conv[:, 1:30, :], sem="sst0")
        P.dma("sp", o_qkv_s[:, 0:2, :], st_qkv[:, 1:3, :], sem="sst1")
        P.dma("sp", o_qkv_s[:, 2, :], qkvS[:, :], sem="sst2")
        yield

        m = AS.mark()
        gluS = AS.alloc("gluS", [128, 4, NS], F32)
        tS = AS.alloc("tS", [128, 4, NS], F32)
        P.act(tS[:, :, :], pF[:, 4:8, :], AF.Tanh, scale=0.5)
        P.ts("dve", tS[:, :, :], tS[:, :, :], 0.5, 0.5, ALU.mult, ALU.add)
        P.tt("dve", gluS[:, :, :], tS[:, :, :], pF[:, 0:4, :], ALU.mult)
        gst = AS.alloc("gst", [NS, 512], F32)
        bank = nb()
        for c in range(4):
            P.transpose(ps[0:NS, bank, c * 128:(c + 1) * 128], gluS[:, c, :], identF[:, :])
        P.copy("act", gst[:, :], ps[0:NS, bank, :])
        P.dma("sp", o_conv_s[:, 29, :], gst[:, :], sem="sst3")
        HS = NS // 2
        stc = AS.alloc("stc", [30, HS, 512], F32)
        cst = AS.alloc("cst", [128, NS, 4, 32], F32)
        prod = AS.alloc("prod", [128, NS, 4, 31], F32)
        convS = AS.alloc("convS", [128, NS, 4], F32)
        cS = AS.alloc("cS", [128, 4, NS], F32)
        for hf in range(2):
            P.dma("sp", stc[:, :, :], st_conv[hf * HS:(hf + 1) * HS].rearrange("s k c -> k s c"), sem="sld1")
            for s4 in range(HS // 4):
                bank = nb()
                for si in range(4):
                    for c in range(4):
                        slot = si * 4 + c
                        P.transpose(ps[:, bank, slot * 32:slot * 32 + 30], stc[:, s4 * 4 + si, c * 128:(c + 1) * 128],
                                    identF[0:30, 0:30])
                s0 = hf * HS + s4 * 4
                P.copy(ev(), cst[:, s0:s0 + 4, :, 0:30],
                       ps[:, bank, :].rr("p (a b c) -> p a b c", a=4, b=4).sub(slice(None), slice(None), slice(None), slice(0, 30)))
                yield
        P.copy("dve", cst[:, :, :, 30], gluS[:, :, :].rr("p c s -> p s c"))
        wv = bcv(pcol1[:, 0:124].rr("p (k c) -> p c k", c=4), 1, [128, NS, 4, 31])
        P.tt("dve", prod[:, :, :, :], cst[:, :, :, 0:31], wv, ALU.mult)
        reduce_x(convS[:, :, :], prod[:, :, :, :])
        for c in range(4):
            P.ts("dve", cS[:, c, :], convS[:, :, c], pc(R_DWB + c), None, ALU.add)
        yield
        sqS = AS.alloc("sqS", [128, 4, NS], F32)
        P.act(sqS[:, :, :], cS[:, :, :], AF.Square)
        bank = nb()
        for c in range(4):
            P.mm(ps[:, bank, 0:NS], onesF[:, :], cS[:, c, :], start=(c == 0), stop=(c == 3))
        for c in range(4):
            P.mm(ps[:, bank, NS:2 * NS], onesF[:, :], sqS[:, c, :], start=(c == 0), stop=(c == 3))
        mean = AS.alloc("smean", [128, NS], F32)
        msq = AS.alloc("smsq", [128, NS], F32)
        rstd = AS.alloc("srstd", [128, NS], F32)
        P.act(mean[:, :], ps[:, bank, 0:NS], AF.Copy, scale=1.0 / 512)
        P.tt("dve", msq[:, :], mean[:, :], mean[:, :], ALU.mult)
        P.stt("dve", rstd[:, :], ps[:, bank, NS:2 * NS], 1.0 / 512, msq[:, :], ALU.mult, ALU.subtract)
        P.act(rstd[:, :], rstd[:, :], AF.Sqrt, bias=EPSC[:, 0:1])
        recip(rstd[:, :])
        thS = AS.alloc("thS", [128, 4, NS], F32)
        for c in range(4):
            P.tt("dve", cS[:, c, :], cS[:, c, :], mean[:, :], ALU.subtract)
            P.tt("dve", cS[:, c, :], cS[:, c, :], rstd[:, :], ALU.mult)
            P.act(cS[:, c, :], cS[:, c, :], AF.Identity, bias=pch[:, 128 + c:129 + c], scale=pch[:, 124 + c:125 + c])
        P.act(thS[:, :, :], cS[:, :, :], AF.Tanh)
        P.stt("dve", cS[:, :, :], thS[:, :, :], 1.0, cS[:, :, :], ALU.add, ALU.mult)
        P.act(thS[:, :, :], pF[:, 8:12, :], AF.Tanh, scale=0.5)
        P.stt("dve", thS[:, :, :], thS[:, :, :], 1.0, pF[:, 8:12, :], ALU.add, ALU.mult)
        P.stt("dve", aoS[:, :, :], thS[:, :, :], 0.5, cS[:, :, :], ALU.mult, ALU.mult)
        AS.release(m)
        yield

        m = AS.mark()
        gS = AS.alloc("gS", [NS, 6, 4], F32)
        P.act(gS[:, 1, :], gls[:, 0:4], AF.Tanh, scale=0.5)
        P.ts("dve", gS[:, 0, :], gS[:, 1, :], 0.5, 0.5, ALU.mult, ALU.add)
        P.tt("dve", gS[:, 1, :], gls[:, 4:8], dtB[0:NS, 0, :], ALU.add)
        P.act(gS[:, 1, :], gS[:, 1, :], AF.Exp)
        P.act(gS[:, 1, :], gS[:, 1, :], AF.Ln, bias=EPSC[0:NS, 2:3])
        P.tt("dve", gS[:, 1, :], gS[:, 1, :], nAB[0:NS, 0, :], ALU.mult)
        P.act(gS[:, 2, :], gS[:, 1, :], AF.Exp)
        rhsAB = AS.alloc("rhsAB", [NS, 2, 4, NS], F32)
        P.tt("dve", rhsAB[:, 0, :, :], bcv(gS[:, 2, :], 2, [NS, 4, NS]), bcv(eye16, 1, [NS, 4, NS]), ALU.mult)
        P.tt("dve", rhsAB[:, 1, :, :], bcv(gS[:, 0, :], 2, [NS, 4, NS]), bcv(eye16, 1, [NS, 4, NS]), ALU.mult)
        bank = nb()
        P.mm(ps[:, bank, 0:2 * 4 * NS], onesF[0:NS, :], rhsAB[:, :, :, :].rr("p a b c -> p (a b c)"), start=True, stop=True)
        abB = AS.alloc("abB", [128, 2, 4, NS], F32)
        P.copy("dve", abB[:, :, :, :], ps[:, bank, 0:2 * 4 * NS].rr("p (a b c) -> p a b c", a=2, b=4))
        aB = abB[:, 0, :, :]
        betaB = abB[:, 1, :, :]
        yield
        qkvn = AS.alloc("qkvn", [NS, 12, 128], F32)
        ss = AS.alloc("ss", [NS, 8], F32)
        qkvT = AS.alloc("qkvT", [128, 12, NS], F32)
        macc = AS.mark()
        accf = AS.alloc("accf", [NS, 1536], F32)
        m2 = AS.mark()
        stq = AS.alloc("stq", [NS, 3, 512], F32)
        wB = AS.alloc("wB", [NS, 4, 512], F32)
        tmp = AS.alloc("tmpq", [NS, 512], F32)
        for q3 in range(3):
            cs3 = slice(q3 * 512, (q3 + 1) * 512)
            acc = accf[:, cs3]
            P.dma("sp", stq[:, :, :], st_qkv[:, :, cs3], sem="sld0")
            for k in range(4):
                P.dma("sp", wB[:, k, :], sc_w[k, cs3].partition_broadcast(NS), sem="sld%d" % (1 + k % 2))
            P.tt("dve", acc, qkvS[:, cs3], wB[:, 3, :], ALU.mult)
            for k in range(3):
                P.tt("dve", tmp[:, :], stq[:, k, :], wB[:, k, :], ALU.mult)
                P.tt("dve", acc, acc, tmp[:, :], ALU.add)
            P.act(tmp[:, :], acc, AF.Tanh, scale=0.5)
            P.stt("dve", acc, tmp[:, :], 1.0, acc, ALU.add, ALU.mult)
            yield
        AS.release(m2)
        m2 = AS.mark()
        tmp2 = AS.alloc("tmp2", [NS, 1024], F32)
        P.tt("dve", tmp2[:, :], accf[:, 0:1024], accf[:, 0:1024], ALU.mult)
        reduce_x(ss[:, :], tmp2[:, :].rr("p (a b) -> p a b", a=8))
        P.act(ss[:, :], ss[:, :], AF.Sqrt, bias=EPSC[0:NS, 1:2])
        recip(ss[:, :])
        P.ts("dve", ss[:, 0:4], ss[:, 0:4], 128.0 ** -0.5, None, ALU.mult)
        P.tt("dve", qkvn[:, 0:8, :], accf[:, 0:1024].rr("p (a b) -> p a b", a=8), bcv(ss[:, :], 2, [NS, 8, 128]), ALU.mult)
        P.ts("dve", qkvn[:, 8:12, :], accf[:, 1024:1536].rr("p (a b) -> p a b", a=4), 0.5, None, ALU.mult)
        bank = nb()
        for j in range(12):
            P.transpose(ps[:, bank, j * NS:(j + 1) * NS], qkvn[:, j, :], eye16)
        P.copy("dve", qkvT[:, :, :], ps[:, bank, 0:12 * NS].rr("p (a b) -> p a b", a=12))
        AS.release(macc)
        yield
        oS = AS.alloc("oS", [128, 4, NS], F32)
        Ssm = AS.alloc("Ssm", [128, 2, NS, 128], F32)
        vnT = AS.alloc("vnT", [128, 2, NS], F32)
        vnt = AS.alloc("vnt", [NS, 2, 128], F32)
        vbd = AS.alloc("vbd", [NS, NS, 128], F32)
        stmp = [AS.alloc("stmp%d" % i, [128, 4, 128], F32) for i in range(2)]
        for hp in range(2):
            for hh in range(2):
                h = hp * 2 + hh
                P.dma("sp", Ssm[:, hh, :, :], st_delta[:, h].rearrange("s d e -> d s e"), sem="slds%d" % hh)
            bank = nb()
            for hh in range(2):
                h = hp * 2 + hh
                for s in range(NS):
                    P.mm(ps[:, bank, hh * NS + s:hh * NS + s + 1], Ssm[:, hh, s, :], qkvT[:, 4 + h, s:s + 1], start=True, stop=True)
            P.tt("dve", vnT[:, :, :], ps[:, bank, 0:2 * NS].rr("p (a b) -> p a b", a=2), abB[:, 0, hp * 2:hp * 2 + 2, :], ALU.mult)
            P.tt("dve", vnT[:, :, :], qkvT[:, 8 + hp * 2:10 + hp * 2, :], vnT[:, :, :], ALU.subtract)
            P.tt("dve", vnT[:, :, :], vnT[:, :, :], abB[:, 1, hp * 2:hp * 2 + 2, :], ALU.mult)
            bank = nb()
            for hh in range(2):
                P.transpose(ps[0:NS, bank, hh * 128:(hh + 1) * 128], vnT[:, hh, :], identF[:, :])
            P.copy("act", vnt[:, :, :], ps[0:NS, bank, 0:256].rr("p (a b) -> p a b", a=2))
            yield
            for hh in range(2):
                h = hp * 2 + hh
                P.tt("dve", vbd[:, :, :], bcv(vnt[:, hh, :], 1, [NS, NS, 128]), bcv(eye16, 2, [NS, NS, 128]), ALU.mult)
                for q4 in range(4):
                    bk = nb()
                    P.mm(ps[:, bk, :], qkvn[:, 4 + h, :], vbd[:, q4 * 4:(q4 + 1) * 4, :].rr("p a b -> p (a b)"), start=True, stop=True)
                    st_ = stmp[q4 % 2]
                    sv = Ssm[:, hh, q4 * 4:(q4 + 1) * 4, :]
                    P.tt("dve", st_[:, :, :], sv, bcv(abB[:, 0, h, q4 * 4:(q4 + 1) * 4], 2, [128, 4, 128]), ALU.mult)
                    P.tt("dve", sv, st_[:, :, :], ps[:, bk, :].rr("p (a b) -> p a b", a=4), ALU.add)
                yield
            for hh in range(2):
                h = hp * 2 + hh
                P.dma("sp", o_delta_s[:, h].rearrange("s d e -> d s e"), Ssm[:, hh, :, :], sem="slds%d" % hh)
            bank = nb()
            for hh in range(2):
                h = hp * 2 + hh
                for s in range(NS):
                    P.mm(ps[:, bank, hh * NS + s:hh * NS + s + 1], Ssm[:, hh, s, :], qkvT[:, h, s:s + 1], start=True, stop=True)
            P.copy("act", oS[:, hp * 2:hp * 2 + 2, :], ps[:, bank, 0:2 * NS].rr("p (a b) -> p a b", a=2))
            yield
        osq_ = AS.alloc("osqS", [128, 4, NS], F32)
        P.act(osq_[:, :, :], oS[:, :, :], AF.Square)
        bank = nb()
        P.mm(ps[:, bank, 0:4 * NS], onesF[:, :], osq_[:, :, :].rr("p a b -> p (a b)"), start=True, stop=True)
        orr_ = AS.alloc("orrS", [128, 4, NS], F32)
        P.act(orr_[:, :, :], ps[:, bank, 0:4 * NS].rr("p (a b) -> p a b", a=4), AF.Sqrt, bias=EPSC[:, 0:1], scale=1.0 / 128)
        recip(orr_[:, :, :])
        P.stt("dve", oS[:, :, :], oS[:, :, :], pc(R_DNG), orr_[:, :, :], ALU.mult, ALU.mult)
        P.act(osq_[:, :, :], pF[:, 12:16, :], AF.Tanh, scale=0.5)
        P.stt("dve", osq_[:, :, :], osq_[:, :, :], 1.0, pF[:, 12:16, :], ALU.add, ALU.mult)
        P.stt("dve", bS[:, :, :], oS[:, :, :], 0.5, osq_[:, :, :], ALU.mult, ALU.mult)
        AS.release(m)
        yield
        yield from s_outproj(w_out_even, lambda kc: (aoS if kc < 4 else bS)[:, kc % 4, :])
        AS.release(m_seg)
        yield "B"

        def s_xattn(l):
            m = AS.mark()
            ring()
            selF = AS.alloc("selF", [NS, NS, 128], F32)
            selS = AS.alloc("selS", [NS, NS, 128], BF16)
            mset(selF[:, :, :], 1.0)
            msel(selF[:, :, :], ALU.is_equal, 0.0, [[-1, NS], [0, 128]], 1)
            P.copy("pool", selS[:, :, :], selF[:, :, :])
            rmsnorm_s(R_NX + l * 8)
            qs = AS.alloc("qs", [NS, D], BF16)
            bq2 = pair_banks()
            reserved.update(bq2)
            for g in range(D // WG):
                wt = wload_s(w_xq[l], g * WG, WG)
                for kc in range(8):
                    P.mm(ps[0:NS, bq2[g // 2], (g % 2) * WG:(g % 2 + 1) * WG], hsT[:, kc, :], wt[:, kc, 0:WG],
                         start=(kc == 0), stop=(kc == 7))
                yield
            for hf in range(2):
                P.act(qs[:, hf * 512:(hf + 1) * 512], ps[0:NS, bq2[hf], :], AF.Copy, scale=1.0 / 16.0)
            reserved.difference_update(bq2)
            oxS = AS.alloc("oxS", [128, 8, NS], BF16)
            NKB = 3
            kbuf = [AS.alloc("kbuf%d" % i, [128, 2, D], BF16) for i in range(NKB)]
            vbuf = [AS.alloc("vbuf%d" % i, [128, 2, D], BF16) for i in range(NKB)]
            qBs = [AS.alloc("qBs%d" % i, [128, D], BF16) for i in range(2)]
            prd = AS.alloc("prd", [128, 2, D], F32)
            scb = [AS.alloc("sc%d" % i, [128, 2, 4], F32) for i in range(3)]
            scbb = [AS.alloc("scb%d" % i, [128, 2, 4], BF16) for i in range(3)]
            o4m = AS.alloc("o4m", [4, 4, 256], F32)
            o4 = AS.alloc("o4", [4, 256], F32)
            rden = AS.alloc("rdenS", [4, 1], F32)

            def stage_load(s):
                P.dma("pool", kbuf[s % NKB][:, :, :], c_k[l, s].rearrange("(c p) n -> p c n", p=128), sem="ck%d" % (s % NKB))
                P.dma("pool", vbuf[s % NKB][:, :, :], c_v[l, s].rearrange("(c p) n -> p c n", p=128), sem="cv%d" % (s % NKB))

            def stage_a(s):
                Kc = kbuf[s % NKB]
                sc = scb[s % 3]
                qb = qBs[s % 2]
                bq = pair_banks()
                for hf in range(2):
                    P.mm(ps[:, bq[hf], :], selS[:, s, :], qs[:, hf * 512:(hf + 1) * 512], start=True, stop=True)
                P.copy("act", qb[:, :].rr("p (a b) -> p a b", a=2), ps[:, bq[0]:bq[0] + 2, :])
                P.tt("dve", prd[:, :, :], Kc[:, :, :], bcv(qb[:, :], 1, [128, 2, D]), ALU.mult)
                reduce_x(sc[:, :, :], prd[:, :, :].rr("p a (h d) -> p a h d", h=4))
                P.act(scbb[s % 3][:, :, :], sc[:, :, :], AF.Exp)

            def stage_b(s):
                Vc = vbuf[s % NKB]
                sc = scbb[s % 3]
                bo = pair_banks()
                for hf in range(2):
                    for mc in range(2):
                        P.mm(ps[0:4, bo[hf], :], sc[:, mc, :], Vc[:, mc, hf * 512:(hf + 1) * 512], start=(mc == 0), stop=(mc == 1))
                P.tt("dve", o4m[:, :, :], ps[0:4, bo[0]:bo[0] + 2, :].rr("p a (h d) -> p (a h) d", h=2), bcv(eye4, 2, [4, 4, 256]), ALU.mult)
                bd = nb()
                for mc in range(2):
                    P.mm(ps[0:4, bd, 0:1], sc[:, mc, :], onesB[:, 0:1], start=(mc == 0), stop=(mc == 1))
                P.add("dve", lambda e, bd=bd: e.reciprocal(rden[:, :].ap, ps[0:4, bd, 0:1].ap), reads=[ps[0:4, bd, 0:1]], writes=[rden[:, :]])
                reduce_x(o4[:, :], o4m[:, :, :].rr("p h d -> p d h"))
                P.ts("dve", o4[:, :], o4[:, :], rden[:, 0:1], None, ALU.mult)
                bt = nb()
                for hf in range(2):
                    P.transpose(ps[:, bt, hf * 4:(hf + 1) * 4], o4[:, hf * 128:(hf + 1) * 128], eye4)
                P.copy("act", oxS[:, :, s].rr("p (h f) -> p f h", f=2), ps[:, bt, 0:8].rr("p (f h) -> p f h", f=2))

            stage_load(0)
            stage_load(1)
            stage_a(0)
            yield
            for s in range(NS):
                if s + 1 < NS:
                    stage_a(s + 1)
                stage_b(s)
                if s + 2 < NS:
                    stage_load(s + 2)
                yield
            yield from s_outproj(w_xo[l], lambda kc: oxS[:, kc, :])
            AS.release(m)

        yield from s_xattn(0)
        yield "B"

        m_odd = AS.mark()
        ring()
        rmsnorm_s(R_NM + 8)
        uS = AS.alloc("uS", [NS, D], F32)
        gF = AS.alloc("gF", [128, 8, NS], F32)
        bu2 = pair_banks()
        bgf = nb()
        reserved.update(bu2 + [bgf])
        for g in range(8):
            wt = wload_s(w_in_odd, g * WG, WG)
            if g < 4:
                for kc in range(8):
                    P.mm(ps[0:NS, bu2[g // 2], (g % 2) * WG:(g % 2 + 1) * WG], hsT[:, kc, :], wt[:, kc, 0:WG],
                         start=(kc == 0), stop=(kc == 7))
            else:
                for j in range(2):
                    n = (g - 4) * 2 + j
                    for kc in range(8):
                        P.mm(ps[:, bgf, n * NS:(n + 1) * NS], wt[:, kc, j * 128:(j + 1) * 128], hsT[:, kc, :],
                             start=(kc == 0), stop=(kc == 7))
            yield
        for hf in range(2):
            P.copy("act", uS[:, hf * 512:(hf + 1) * 512], ps[0:NS, bu2[hf], :])
        P.copy("dve", gF[:, :, :], ps[:, bgf, 0:8 * NS].rr("p (a b) -> p a b", a=8))
        reserved.difference_update(bu2 + [bgf])
        P.dma("sp", o_pool_s[:, 0:14, :], st_pool[:, 1:15, :], sem="sst0")
        P.dma("sp", o_pool_s[:, 14, :], uS[:, :], sem="sst1")
        plt = AS.alloc("plt", [NS, D], F32)
        bsum = AS.alloc("bsum", [NS, 256], F32)
        stp = AS.alloc("stp", [NS, 15, 256], F32)
        for gi in range(4):
            win = 2 << gi
            cs = slice(gi * 256, (gi + 1) * 256)
            P.dma("sp", stp[:, :, :], st_pool[:, :, cs], sem="sld0")
            reduce_x(bsum[:, :], stp[:, 16 - win:15, :].rr("p k c -> p c k"))
            P.tt("dve", bsum[:, :], bsum[:, :], uS[:, cs], ALU.add)
            P.stt("dve", plt[:, cs], bsum[:, :], 1.0 / win, uS[:, cs], ALU.mult, ALU.subtract)
            yield
        plS = AS.alloc("plS", [128, 8, NS], BF16)
        bank = nb()
        for c in range(8):
            P.transpose(ps[:, bank, c * NS:(c + 1) * NS], plt[:, c * 128:(c + 1) * 128], eye16)
        P.copy("dve", plS[:, :, :], ps[:, bank, 0:8 * NS].rr("p (a b) -> p a b", a=8))
        wps = AS.alloc("wps", [128, 4, 2, 256], BF16)
        P.dma("pool", wps[:, :, :, :], w_pool.rearrange("g (c p) d -> p g c d", p=128), sem="swp")
        zS = AS.alloc("zS", [128, 8, NS], BF16)
        zf = AS.alloc("zf", [128, 8, NS], F32)
        tgs = AS.alloc("tgs", [128, 8, NS], F32)
        bank = nb()
        for n in range(8):
            gi, half = n // 2, n % 2
            for cc in range(2):
                P.mm(ps[:, bank, n * NS:(n + 1) * NS], wps[:, gi, cc, half * 128:(half + 1) * 128], plS[:, gi * 2 + cc, :],
                     start=(cc == 0), stop=(cc == 1))
        for n in range(8):
            P.ts("dve", zf[:, n, :], ps[:, bank, n * NS:(n + 1) * NS], pc(R_BP + n), pc(R_PSC + n), ALU.add, ALU.mult)
        P.act(tgs[:, :, :], gF[:, :, :], AF.Tanh, scale=0.5)
        P.stt("dve", tgs[:, :, :], tgs[:, :, :], 1.0, gF[:, :, :], ALU.add, ALU.mult)
        P.stt("dve", zS[:, :, :], tgs[:, :, :], 0.5, zf[:, :, :], ALU.mult, ALU.mult)
        yield
        yield from s_outproj(w_out_odd, lambda kc: zS[:, kc, :])
        AS.release(m_odd)
        yield "B"
        yield from s_xattn(1)

        m = AS.mark()
        yT = AS.alloc("yTs", [128, 8, NS], F32)
        rmsnorm_s(R_NF, f32out=yT)
        yst_ = AS.alloc("ysts", [NS, D], F32)
        for g in range(2):
            bank = nb()
            for j in range(4):
                P.transpose(ps[0:NS, bank, j * 128:(j + 1) * 128], yT[:, g * 4 + j, :], identF[:, :])
            P.copy("act", yst_[:, g * 512:(g + 1) * 512], ps[0:NS, bank, :])
        P.dma("sp", o_ys, yst_[:, :], sem="sst2")
        AS.release(m)
        yield "B"

    sgen = sample_path()
    sdone = [False]

    def pump(n=1, to_boundary=False):
        if sdone[0] or (win_done[0] and not to_boundary):
            return
        dom[0] = "s"
        try:
            k = 0
            while True:
                r = next(sgen)
                k += 1
                if r == "B":
                    win_done[0] = True
                    break
                if not to_boundary and k >= n:
                    break
        except StopIteration:
            sdone[0] = True
        dom[0] = "p"

    win_done = [False]
    PUMP = int(_os.environ.get("K_PUMP", "1"))

    def win_begin():
        win_done[0] = False
        A.limit = AS_BASE
        allowed["p"] = {0, 1, 2, 3}

    def win_end():
        if not win_done[0]:
            pump(to_boundary=True)
        A.limit = A.top
        allowed["p"] = set(range(8))

    mem_prep()
    load_x()
    for l in range(DEPTH):
        mem_kv(l)
        for mt in range(NMT):
            if l % 2 == 0:
                even_mixer(mt)
            else:
                odd_mixer(mt)
        if STAGE <= 4:
            break
        for mt in range(NMT):
            xattn(l, mt)
        if STAGE <= 5:
            break
    if STAGE >= 7:
        win_begin()
        final_out()
        win_end()
    while not sdone[0]:
        win_done[0] = False
        pump(to_boundary=True)
    for name, t in dbg.items():
        od = nc.dram_tensor("dbg_" + name, t.shape, F32, kind="ExternalOutput").ap()
        m = A.mark()
        st = A.alloc("dbgst", t.shape, F32)
        P.copy("dve", st[tuple(slice(None) for _ in t.shape)], t[tuple(slice(None) for _ in t.shape)])
        P.dma("sp", od, st[tuple(slice(None) for _ in t.shape)], sem="dbg")
        A.release(m)
    print("SBUF peak bytes/partition:", A.peak, "ops:", len(P.ops))
    P.emit()
    return nc


_REPL = ["w_xk", "w_xv", "w_xq", "w_xo", "norm_mix", "norm_xattn"]
_SQ0 = ["w_in_even", "w_out_even", "w_in_odd", "w_out_odd", "w_pool", "dw_w", "sc_w", "b_pool"]
_ROW = ["norm_final", "dw_b", "ln_a_g", "ln_a_b", "a_log", "dt_bias", "dn_norm_g", "pool_scale"]


def kernel(**inputs):
    nc = build_program()
    f = lambda a: np.ascontiguousarray(a, dtype=np.float32)
    shared = {}
    for k in _REPL:
        shared[k] = f(inputs[k])
    for k in _SQ0:
        shared[k] = f(inputs[k][0])
    for k in _ROW:
        shared[k] = f(np.asarray(inputs[k]).reshape(1, -1))
    in_maps = []
    for c in range(NCORES):
        m = dict(shared)
        m["x_p"] = f(inputs["x_prompt"][c])
        m["mem_p"] = f(inputs["mem_prompt"][c])
        sl = slice(c * NS, (c + 1) * NS)
        m["x_s"] = f(inputs["x_sample"][sl, 0])
        m["st_conv"] = f(inputs["state_conv_a"][0, sl])
        m["st_qkv"] = f(inputs["state_qkv_conv"][0, sl])
        m["st_delta"] = f(inputs["state_delta"][0, sl])
        m["st_pool"] = f(inputs["state_pool"][0, sl])
        m["c_k"] = f(inputs["cache_mem_k"][:, sl]).reshape(DEPTH, NS, NMEM, D)
        m["c_v"] = f(inputs["cache_mem_v"][:, sl]).reshape(DEPTH, NS, NMEM, D)
        in_maps.append(m)
    res = run_bass_kernel_spmd(nc, in_maps, core_ids=list(range(NCORES)))
    R = res.results
    B = NCORES
    out = {}
    out["y_prompt"] = np.stack([R[c]["o_y"] for c in range(B)], axis=0)
    out["new_conv_a_p"] = np.stack([R[c]["o_conv"] for c in range(B)], axis=0)[None]
    out["new_qkv_conv_p"] = np.stack([R[c]["o_qkv"] for c in range(B)], axis=0)[None]
    out["new_delta_p"] = np.stack([R[c]["o_delta"] for c in range(B)], axis=0)[None]
    out["new_pool_p"] = np.stack([R[c]["o_pool"] for c in range(B)], axis=0)[None]
    out["new_mem_k_p"] = np.stack([R[c]["o_mem_k"] for c in range(B)], axis=1).reshape(DEPTH, B, NMEM, 4, 256)
    out["new_mem_v_p"] = np.stack([R[c]["o_mem_v"] for c in range(B)], axis=1).reshape(DEPTH, B, NMEM, 4, 256)
    if STAGE >= 8:
        out["y_sample"] = np.concatenate([R[c]["o_ys"] for c in range(B)], axis=0)[:, None, :]
        out["new_conv_a_s"] = np.concatenate([R[c]["o_conv_s"] for c in range(B)], axis=0)[None]
        out["new_qkv_conv_s"] = np.concatenate([R[c]["o_qkv_s"] for c in range(B)], axis=0)[None]
        out["new_delta_s"] = np.concatenate([R[c]["o_delta_s"] for c in range(B)], axis=0)[None]
        out["new_pool_s"] = np.concatenate([R[c]["o_pool_s"] for c in range(B)], axis=0)[None]
    if _os.environ.get("K_DBG"):
        for k in R[0]:
            if k.startswith("dbg_"):
                out[k] = R[0][k]
        return out
    NSM = NS * NCORES
    for k, shp in (("y_sample", (NSM, 1, D)), ("new_conv_a_s", (1, NSM, 30, 512)), ("new_qkv_conv_s", (1, NSM, 3, 1536)),
                   ("new_delta_s", (1, NSM, 4, 128, 128)), ("new_pool_s", (1, NSM, 15, D))):
        if k not in out:
            out[k] = np.zeros(shp, np.float32)
    order = ["y_prompt", "y_sample", "new_conv_a_p", "new_qkv_conv_p", "new_delta_p", "new_pool_p", "new_mem_k_p",
             "new_mem_v_p", "new_conv_a_s", "new_qkv_conv_s", "new_delta_s", "new_pool_s"]
    return tuple(out[k] for k in order)
```

```python
import numpy as np
import concourse.bass as bass
import concourse.mybir as mybir
from concourse.bass_utils import run_bass_kernel_spmd

F32 = mybir.dt.float32
BF16 = mybir.dt.bfloat16
ALU = mybir.AluOpType
AF = mybir.ActivationFunctionType
AX = mybir.AxisListType

NCORES = 8
D = 1024
T = 2048
NMEM = 256
DEPTH = 2
NS = 16
EPS = 1e-6

COMPUTE = ("pe", "act", "dve", "pool")


class V:
    def __init__(self, ap, tname, rect, excl=False):
        self.ap = ap
        self.t = tname
        self.rect = rect
        self.excl = excl

    def rr(self, pat, **kw):
        return V(self.ap.rearrange(pat, **kw), self.t, self.rect, self.excl)

    def bc(self, shape):
        return V(self.ap.to_broadcast(list(shape)), self.t, self.rect, self.excl)

    def sub(self, *idx):
        return V(self.ap[idx], self.t, self.rect, self.excl)


_uid = [0]


class Tn:
    def __init__(self, nc, name, shape, dtype, psum=False, offset=None):
        _uid[0] += 1
        self.name = "%s_%d" % (name, _uid[0])
        self.shape = list(shape)
        self.dtype = dtype
        self.psum = psum
        self.esz = 4 if dtype == F32 else 2
        if psum:
            self.h = nc.alloc_psum_tensor(self.name, self.shape, dtype)
            self.off = 0
        elif offset is None:
            self.h = nc.alloc_sbuf_tensor(self.name, self.shape, dtype)
            self.off = None
        else:
            self.h = nc.alloc_sbuf_tensor_at(self.name, self.shape, dtype, offset=offset)
            self.off = offset
        st = [1] * len(shape)
        for i in range(len(shape) - 2, 0, -1):
            st[i] = st[i + 1] * shape[i + 1]
        self.st = st

    def __getitem__(self, idx):
        if not isinstance(idx, tuple):
            idx = (idx,)
        idx = tuple(idx) + (slice(None),) * (len(self.shape) - len(idx))
        lo = 0
        hi = 0
        p0, p1 = 0, self.shape[0]
        for d, (ix, n) in enumerate(zip(idx, self.shape)):
            if isinstance(ix, slice):
                a = 0 if ix.start is None else ix.start
                b = n if ix.stop is None else ix.stop
                assert ix.step in (None, 1)
            else:
                a, b = ix, ix + 1
            assert 0 <= a < b <= n, (self.name, idx)
            if d == 0:
                p0, p1 = a, b
            else:
                lo += a * self.st[d]
                hi += (b - 1) * self.st[d]
        hi += 1
        lo *= self.esz
        hi *= self.esz
        if self.psum:
            lo = (lo // 2048) * 2048
            hi = ((hi + 2047) // 2048) * 2048
            p0, p1 = 0, 128
            return V(self.h[idx], "ps", (p0, p1, lo, hi), True)
        if self.off is None:
            return V(self.h[idx], self.name, (p0, p1, lo, hi), False)
        return V(self.h[idx], "sb", (p0, p1, self.off + lo, self.off + hi), False)


def _overlap(a, b):
    return a[0] < b[1] and b[0] < a[1] and a[2] < b[3] and b[2] < a[3]


def _contains(outer, inner):
    return outer[0] <= inner[0] and inner[1] <= outer[1] and outer[2] <= inner[2] and inner[3] <= outer[3]


class Op:
    __slots__ = ("eng", "fn", "reads", "writes", "dma", "seq", "waits", "flag", "semkey", "semcnt")


class Prog:
    def __init__(self, nc):
        self.nc = nc
        self.ops = []
        self.acc = {}
        self.nseq = {e: 0 for e in ("pe", "act", "dve", "pool", "sp")}
        self.dmacnt = {}
        self.lastdma = {}
        self.by_eng = {e: [] for e in ("pe", "act", "dve", "pool", "sp")}
        self.flagged = {e: set() for e in COMPUTE}
        self.waited = {e: {} for e in ("pe", "act", "dve", "pool", "sp")}

    def _deps(self, op):
        deps = set()
        for v in op.reads:
            for (rect, kind, dep) in self.acc.get(v.t, ()):
                if (kind == "w" or v.excl) and _overlap(rect, v.rect):
                    deps.add(dep)
        for v in op.writes:
            for (rect, kind, dep) in self.acc.get(v.t, ()):
                if _overlap(rect, v.rect):
                    deps.add(dep)
        return deps

    def add(self, eng, fn, reads=(), writes=(), dma=None):
        op = Op()
        op.eng = eng
        op.fn = fn
        op.reads = [r for r in reads if r is not None]
        op.writes = [w for w in writes if w is not None]
        op.dma = dma
        op.flag = False
        deps = self._deps(op)
        self.nseq[eng] += 1
        op.seq = self.nseq[eng]
        if dma is not None:
            prev = self.lastdma.get(dma)
            if prev is not None:
                deps.add(prev)
            self.dmacnt[dma] = self.dmacnt.get(dma, 0) + 16
            op.semkey = dma
            op.semcnt = self.dmacnt[dma]
            me = ("dma", dma, op.semcnt)
            self.lastdma[dma] = me
        else:
            me = (eng, op.seq)
        need = {}
        for d in deps:
            if d[0] == "dma":
                key = ("dma", d[1])
                val = d[2]
            else:
                if d[0] == eng and dma is None:
                    pass
                key = d[0]
                val = d[1]
            if need.get(key, 0) < val:
                need[key] = val
        if dma is None and eng in need and eng == "pe":
            raw = 0
            for v in op.reads:
                for (rect, kind, dep) in self.acc.get(v.t, ()):
                    if kind == "w" and dep[0] == eng and _overlap(rect, v.rect):
                        raw = max(raw, dep[1])
            if raw:
                need[eng] = raw
            else:
                del need[eng]
        waits = []
        wd = self.waited[eng]
        for key, val in need.items():
            if wd.get(key, 0) >= val:
                continue
            wd[key] = val
            waits.append((key, val))
            if not (isinstance(key, tuple)):
                self.flagged[key].add(val)
        op.waits = waits
        for v in op.writes:
            lst = self.acc.setdefault(v.t, [])
            lst[:] = [a for a in lst if not _contains(v.rect, a[0])]
            lst.append((v.rect, "w", me))
        for v in op.reads:
            lst = self.acc.setdefault(v.t, [])
            if v.excl:
                lst[:] = [a for a in lst if not _contains(v.rect, a[0])]
                lst.append((v.rect, "x", me))
                continue
            if dma is None:
                lst[:] = [a for a in lst if not (a[1] == "r" and a[2][0] == eng and _contains(v.rect, a[0]))]
            lst.append((v.rect, "r", me))
        self.ops.append(op)
        self.by_eng[eng].append(op)
        return op

    def mm(self, out, lhsT, rhs, start=True, stop=True):
        self.add("pe", lambda e: e.matmul(out.ap, lhsT.ap, rhs.ap, start=start, stop=stop),
                 reads=[lhsT, rhs], writes=[out])

    def transpose(self, out, in_, ident):
        self.add("pe", lambda e: e.transpose(out.ap, in_.ap, ident.ap), reads=[in_, ident], writes=[out])

    def act(self, out, in_, func, bias=None, scale=1.0, accum=None, eng="act"):
        rd = [in_]
        kw = {}
        if isinstance(bias, V):
            rd.append(bias)
            kw["bias"] = bias.ap
        elif bias is not None:
            kw["bias"] = bias
        if isinstance(scale, V):
            rd.append(scale)
            kw["scale"] = scale.ap
        else:
            kw["scale"] = scale
        wr = [out]
        if accum is not None:
            wr.append(accum)
            kw["accum_out"] = accum.ap
        self.add("act", lambda e: e.activation(out.ap, in_.ap, func, **kw), reads=rd, writes=wr)

    def copy(self, eng, out, in_):
        if eng == "act":
            self.add("act", lambda e: e.copy(out.ap, in_.ap), reads=[in_], writes=[out])
        else:
            self.add(eng, lambda e: e.tensor_copy(out.ap, in_.ap), reads=[in_], writes=[out])

    def tt(self, eng, out, a, b, op):
        self.add(eng, lambda e: e.tensor_tensor(out.ap, a.ap, b.ap, op), reads=[a, b], writes=[out])

    def ts(self, eng, out, a, s1, s2, op0, op1=None, accum=None):
        rd = [a]
        s1a = s1.ap if isinstance(s1, V) else s1
        s2a = s2.ap if isinstance(s2, V) else s2
        if isinstance(s1, V):
            rd.append(s1)
        if isinstance(s2, V):
            rd.append(s2)
        wr = [out]
        kw = {}
        if accum is not None:
            wr.append(accum)
            kw["accum_out"] = accum.ap
        if op1 is None:
            self.add(eng, lambda e: e.tensor_scalar(out.ap, a.ap, s1a, None, op0, **kw), reads=rd, writes=wr)
        else:
            self.add(eng, lambda e: e.tensor_scalar(out.ap, a.ap, s1a, s2a, op0, op1, **kw), reads=rd, writes=wr)

    def stt(self, eng, out, a, s, b, op0, op1):
        rd = [a, b]
        sa = s.ap if isinstance(s, V) else s
        if isinstance(s, V):
            rd.append(s)
        self.add(eng, lambda e: e.scalar_tensor_tensor(out.ap, a.ap, sa, b.ap, op0, op1), reads=rd, writes=[out])

    def dma(self, q, out, in_, sem, reads=(), writes=()):
        o = out.ap if isinstance(out, V) else out
        i = in_.ap if isinstance(in_, V) else in_
        rd = list(reads) + ([in_] if isinstance(in_, V) else [])
        wr = list(writes) + ([out] if isinstance(out, V) else [])
        self.add(q, lambda e: e.dma_start(out=o, in_=i), reads=rd, writes=wr, dma=sem)

    def emit(self):
        nc = self.nc
        sems = {e: nc.alloc_semaphore("s_" + e) for e in COMPUTE}
        dsems = {k: nc.alloc_semaphore("d_" + str(k)) for k in self.dmacnt}
        rank = {}
        for e in COMPUTE:
            fl = sorted(self.flagged[e])
            rank[e] = {s: i + 1 for i, s in enumerate(fl)}
        engobj = {"pe": "tensor", "act": "scalar", "dve": "vector", "pool": "gpsimd", "sp": "sync"}

        def run(ename, eng):
            for op in self.by_eng[ename]:
                for key, val in op.waits:
                    if isinstance(key, tuple):
                        eng.wait_ge(dsems[key[1]], val)
                    else:
                        eng.wait_ge(sems[key], rank[key][val])
                ins = op.fn(eng)
                if op.dma is not None:
                    ins.then_inc(dsems[op.semkey], 16)
                elif op.seq in rank[ename]:
                    ins.then_inc(sems[ename], 1)
            if ename == "sp":
                for k, cnt in self.dmacnt.items():
                    eng.wait_ge(dsems[k], cnt)

        with nc.Block() as block:
            block.tensor(lambda e: run("pe", e))
            block.scalar(lambda e: run("act", e))
            block.vector(lambda e: run("dve", e))
            block.gpsimd(lambda e: run("pool", e))
            block.sync(lambda e: run("sp", e))


import os as _os


class Arena:
    def __init__(self, nc):
        self.nc = nc
        self.cur = 16512
        self.top = 229344
        self.limit = 229344
        self.peak = 0

    def alloc(self, name, shape, dtype):
        esz = 4 if dtype == F32 else 2
        n = esz
        for s in shape[1:]:
            n *= s
        off = self.cur
        self.cur = (off + n + 31) // 32 * 32
        self.peak = max(self.peak, self.cur)
        assert self.cur <= min(self.top, self.limit), ("SBUF overflow", name, self.cur, self.limit)
        return Tn(self.nc, name, shape, dtype, offset=off)

    def mark(self):
        return self.cur

    def release(self, m):
        self.cur = m


TT = int(_os.environ.get('K_TT', '1024'))
NMT = T // TT
NTT = TT // 512
NBLK = TT // 128
WG = 256
NEG = -32768.0
SAMPLE_BYTES = 66560
STAGE = int(_os.environ.get("K_STAGE", "99"))


def build_program():
    nc = bass.Bass("TRN2", target_bir_lowering=False)
    nc.allow_low_precision("bf16 matmul operands with fp32 accumulation (problem tolerance)")
    P = Prog(nc)
    A = Arena(nc)
    AS = Arena(nc)
    AS.cur = AS.top - SAMPLE_BYTES
    AS_BASE = AS.cur

    def din(name, shape):
        return nc.dram_tensor(name, list(shape), F32, kind="ExternalInput").ap()

    def dout(name, shape):
        return nc.dram_tensor(name, list(shape), F32, kind="ExternalOutput").ap()

    x_p = din("x_p", [T, D])
    mem_p = din("mem_p", [NMEM, D])
    w_xk = din("w_xk", [DEPTH, D, D])
    w_xv = din("w_xv", [DEPTH, D, D])
    w_xq = din("w_xq", [DEPTH, D, D])
    w_xo = din("w_xo", [DEPTH, D, D])
    w_in_even = din("w_in_even", [D, 3592])
    w_out_even = din("w_out_even", [D, D])
    w_in_odd = din("w_in_odd", [D, 2048])
    w_out_odd = din("w_out_odd", [D, D])
    w_pool = din("w_pool", [4, 256, 256])
    norm_mix = din("norm_mix", [2, D])
    norm_xattn = din("norm_xattn", [2, D])
    norm_final = din("norm_final", [1, D])
    dw_w = din("dw_w", [31, 512])
    dw_b = din("dw_b", [1, 512])
    ln_a_g = din("ln_a_g", [1, 512])
    ln_a_b = din("ln_a_b", [1, 512])
    sc_w = din("sc_w", [4, 1536])
    a_log = din("a_log", [1, 4])
    dt_bias = din("dt_bias", [1, 4])
    dn_norm_g = din("dn_norm_g", [1, 128])
    pool_scale = din("pool_scale", [1, D])
    b_pool = din("b_pool", [4, 256])

    o_y = dout("o_y", [T, D])
    o_conv = dout("o_conv", [30, 512])
    o_qkv = dout("o_qkv", [3, 1536])
    o_delta = dout("o_delta", [4, 128, 128])
    o_pool = dout("o_pool", [15, D])
    o_mem_k = dout("o_mem_k", [DEPTH, NMEM, D])
    o_mem_v = dout("o_mem_v", [DEPTH, NMEM, D])
    dbg = {}

    ps = Tn(nc, "ps", [128, 8, 512], F32, psum=True)
    bankc = [0]

    reserved = set()

    dom = ["p"]
    allowed = {"p": set(range(8)), "s": {4, 5, 6, 7}}
    bankcs = {"p": bankc, "s": [3]}

    def nb():
        bc_ = bankcs[dom[0]]
        while True:
            bc_[0] = (bc_[0] + 1) % 8
            if bc_[0] in allowed[dom[0]] and bc_[0] not in reserved:
                return bc_[0]

    evc = [0]

    def ev():
        evc[0] += 1
        return "dve" if evc[0] % 2 else "act"

    identF = A.alloc("identF", [128, 128], F32)
    identB = A.alloc("identB", [128, 128], BF16)
    onesF = A.alloc("onesF", [128, 128], F32)
    onesB = A.alloc("onesB", [128, 128], BF16)
    Mincl = A.alloc("Mincl", [128, 128], F32)
    Msame = A.alloc("Msame", [128, 128], F32)
    Mgt = A.alloc("Mgt", [128, 128], F32)
    NEGs = A.alloc("NEGs", [128, 128], F32)
    NEGiT = A.alloc("NEGiT", [128, 128], F32)
    sel = A.alloc("sel", [4, 4, 128], F32)
    NEGsB = A.alloc("NEGsB", [128, 128], BF16)
    NEGiTB = A.alloc("NEGiTB", [128, 128], BF16)
    pcol0 = A.alloc("pcol0", [128, 128], F32)
    pcol1 = A.alloc("pcol1", [128, 128], F32)
    pch = A.alloc("pch", [128, 160], F32)
    dtB = A.alloc("dtB", [128, NBLK, 4], F32)
    nAB = A.alloc("nAB", [128, NBLK, 4], F32)
    invc = A.alloc("invc", [128, 16], F32)

    def msel(t, cmp, fill, pattern, cm, base=0):
        P.add("pool", lambda e: e.affine_select(out=t.ap, in_=t.ap, compare_op=cmp, fill=fill, base=base,
                                                pattern=pattern, channel_multiplier=cm), reads=[t], writes=[t])

    def mset(t, val):
        P.add("pool", lambda e: e.memset(t.ap, val), writes=[t])

    mset(identF[:, :], 1.0)
    msel(identF[:, :], ALU.is_equal, 0.0, [[-1, 128]], 1)
    P.copy("pool", identB[:, :], identF[:, :])
    mset(onesF[:, :], 1.0)
    mset(onesB[:, :], 1.0)
    mset(Mincl[:, :], 1.0)
    msel(Mincl[:, :], ALU.is_ge, 0.0, [[1, 128]], -1)
    mset(Mincl[0:64, 64:128], 0.0)
    mset(Msame[:, :], 1.0)
    mset(Msame[0:64, 64:128], 0.0)
    mset(Msame[64:128, 0:64], 0.0)
    mset(Mgt[:, :], 1.0)
    msel(Mgt[:, :], ALU.is_gt, 0.0, [[-1, 128]], 1)
    mset(Mgt[64:128, 0:64], 0.0)
    mset(NEGs[:, :], 0.0)
    msel(NEGs[:, :], ALU.is_gt, NEG, [[-1, 128]], 1)
    mset(NEGs[64:128, 0:64], NEG)
    mset(NEGiT[:, :], 0.0)
    msel(NEGiT[:, :], ALU.is_ge, NEG, [[1, 128]], -1)
    mset(NEGiT[0:64, 64:128], NEG)
    P.copy("pool", NEGsB[:, :], NEGs[:, :])
    P.copy("pool", NEGiTB[:, :], NEGiT[:, :])
    mset(sel[:, :, :], 1.0)
    msel(sel[:, :, :], ALU.is_equal, 0.0, [[-1, 4], [0, 128]], 1)
    ind0 = Msame[:, 0:1]
    ind1 = Msame[:, 64:65]
    _bk = nb()
    P.mm(ps[:, _bk, 0:16], onesF[:, :], Mincl[:, 0:16], start=True, stop=True)
    P.add("dve", lambda e: e.reciprocal(invc[:, :].ap, ps[:, _bk, 0:16].ap), reads=[ps[:, _bk, 0:16]], writes=[invc[:, :]])

    m0 = A.mark()
    pst0 = A.alloc("pst0", [128, 128], F32)
    pst1 = A.alloc("pst1", [128, 128], F32)
    mset(pst0[:, :], 0.0)
    mset(pst1[:, :], 0.0)
    R_NM, R_NX, R_NF, R_DWB, R_LNG, R_LNB, R_SC, R_DNG, R_PSC, R_BP = 0, 16, 32, 40, 44, 48, 52, 100, 101, 109
    prm_loads = [
        (pst0, R_NM, 16, norm_mix.rearrange("l (c p) -> (l c) p", p=128)),
        (pst0, R_NX, 16, norm_xattn.rearrange("l (c p) -> (l c) p", p=128)),
        (pst0, R_NF, 8, norm_final.rearrange("l (c p) -> (l c) p", p=128)),
        (pst0, R_DWB, 4, dw_b.rearrange("l (c p) -> (l c) p", p=128)),
        (pst0, R_LNG, 4, ln_a_g.rearrange("l (c p) -> (l c) p", p=128)),
        (pst0, R_LNB, 4, ln_a_b.rearrange("l (c p) -> (l c) p", p=128)),
        (pst0, R_SC, 48, sc_w.rearrange("k (c p) -> (k c) p", p=128)),
        (pst0, R_DNG, 1, dn_norm_g),
        (pst0, R_PSC, 8, pool_scale.rearrange("l (c p) -> (l c) p", p=128)),
        (pst0, R_BP, 8, b_pool.rearrange("g (c p) -> (g c) p", p=128)),
        (pst1, 0, 124, dw_w.rearrange("k (c p) -> (k c) p", p=128)),
    ]
    for i, (dst, r0, n, src) in enumerate(prm_loads):
        P.dma("sp", dst[r0:r0 + n, :], src, sem="prm%d" % (i % 4))
    bk = nb()
    P.transpose(ps[:, bk, 0:128], pst0[:, :], identF[:, :])
    P.transpose(ps[:, bk, 128:256], pst1[:, :], identF[:, :])
    P.copy("dve", pcol0[:, :], ps[:, bk, 0:128])
    P.copy("dve", pcol1[:, :], ps[:, bk, 128:256])
    P.ts("dve", pch[:, 0:124], pcol1[:, 0:124], 0.5, None, ALU.mult)
    P.ts("dve", pch[:, 124:128], pcol0[:, R_LNG:R_LNG + 4], 0.5, None, ALU.mult)
    P.ts("dve", pch[:, 128:132], pcol0[:, R_LNB:R_LNB + 4], 0.5, None, ALU.mult)
    for b in range(NBLK):
        P.dma("sp", dtB[:, b, :], dt_bias[0].partition_broadcast(128), sem="prm%d" % (b % 4))
        P.dma("sp", nAB[:, b, :], a_log[0].partition_broadcast(128), sem="prm%d" % ((b + 1) % 4))
    P.act(nAB[:, :, :], nAB[:, :, :], AF.Exp)
    P.ts("dve", nAB[:, :, :], nAB[:, :, :], -1.0, None, ALU.mult)
    A.release(m0)

    def pc(r):
        return pcol0[:, r:r + 1]

    xsT = A.alloc("xsT", [128, 8, NS], F32)
    hsT = A.alloc("hsT", [128, 8, NS], BF16)
    xT = A.alloc("xT", [128, 8, T], F32)
    memT = A.alloc("memT", [128, 8, NMEM], BF16)
    KT = A.alloc("KT", [128, 8, NMEM], BF16)
    Vtok = A.alloc("Vtok", [128, 2, D], BF16)
    NW = int(_os.environ.get("K_NW", "3"))
    wbuf = [A.alloc("wbuf%d" % i, [128, 8, WG], BF16) for i in range(NW)]
    wctr = [0]
    Sst = A.alloc("Sst", [128, 4, 128], F32)
    Sbf = A.alloc("Sbf", [128, 4, 128], BF16)
    gluhalo = A.alloc("gluhalo", [128, 4, 32], BF16)
    qkvhalo = A.alloc("qkvhalo", [128, 12, 4], BF16)
    uhalo = A.alloc("uhalo", [128, 8, 16], F32)
    mset(Sst[:, :, :], 0.0)
    mset(Sbf[:, :, :], 0.0)
    mset(gluhalo[:, :, :], 0.0)
    mset(qkvhalo[:, :, :], 0.0)
    mset(uhalo[:, :, :], 0.0)

    def wload(src2d, col0, ncols):
        slot = wctr[0] % NW
        wctr[0] += 1
        wt = wbuf[slot]
        P.dma("pool", wt[:, :, 0:ncols], src2d[:, col0:col0 + ncols].rearrange("(c p) n -> p c n", p=128),
              sem="w%d" % slot)
        return wt

    swring = {"bufs": None, "ctr": 0}

    def wload_s(src2d, col0, ncols):
        slot = swring["ctr"] % 2
        swring["ctr"] += 1
        wt = swring["bufs"][slot]
        P.dma("pool", wt[:, :, 0:ncols], src2d[:, col0:col0 + ncols].rearrange("(c p) n -> p c n", p=128),
              sem="sw%d" % slot)
        return wt

    def mem_prep():
        m = A.mark()
        memtok = A.alloc("memtok", [128, 2, D], F32)
        P.dma("sp", memtok[:, :, :], mem_p.rearrange("(c p) d -> p c d", p=128), sem="ld0")
        for mc in range(2):
            for g in range(2):
                bank = nb()
                for j in range(4):
                    dc = g * 4 + j
                    P.transpose(ps[:, bank, j * 128:(j + 1) * 128], memtok[:, mc, dc * 128:(dc + 1) * 128], identF[:, :])
                P.copy(ev(), memT[:, g * 4:g * 4 + 4, mc * 128:(mc + 1) * 128],
                       ps[:, bank, :].rr("p (a b) -> p a b", a=4))
        A.release(m)

    def mem_kv(l):
        m = A.mark()
        kvst = [A.alloc("kvst%d" % i, [128, D], F32) for i in range(2)]
        for which, wsrc, odst in (("k", w_xk, o_mem_k), ("v", w_xv, o_mem_v)):
            for g in range(D // WG):
                wt = wload(wsrc[l], g * WG, WG)
                for mc in range(2):
                    bank = nb()
                    for dc in range(8):
                        P.mm(ps[:, bank, 0:WG], memT[:, dc, mc * 128:(mc + 1) * 128], wt[:, dc, 0:WG],
                             start=(dc == 0), stop=(dc == 7))
                    P.copy(ev(), kvst[mc][:, g * WG:(g + 1) * WG], ps[:, bank, 0:WG])
                if which == "k":
                    for j in range(WG // 128):
                        nch = g * (WG // 128) + j
                        bank2 = nb()
                        for dc in range(8):
                            P.mm(ps[:, bank2, 0:NMEM], wt[:, dc, j * 128:(j + 1) * 128], memT[:, dc, :],
                                 start=(dc == 0), stop=(dc == 7))
                        P.copy(ev(), KT[:, nch, :], ps[:, bank2, 0:NMEM])
            for mc in range(2):
                if which == "v":
                    P.copy("pool", Vtok[:, mc, :], kvst[mc][:, :])
                P.dma("sp", odst[l, mc * 128:(mc + 1) * 128, :], kvst[mc][:, :], sem="kvst%d" % mc)
        A.release(m)

    def load_x():
        m = A.mark()
        NXB = 6
        xs = [A.alloc("xs%d" % i, [128, D], F32) for i in range(NXB)]
        for b in range(T // 128):
            st = xs[b % NXB]
            P.dma("sp", st[:, :], x_p[b * 128:(b + 1) * 128, :], sem="ldx%d" % (b % NXB))
            for g in range(2):
                bank = nb()
                for j in range(4):
                    kc = g * 4 + j
                    P.transpose(ps[:, bank, j * 128:(j + 1) * 128], st[:, kc * 128:(kc + 1) * 128], identF[:, :])
                P.copy(ev(), xT[:, g * 4:g * 4 + 4, b * 128:(b + 1) * 128], ps[:, bank, :].rr("p (a b) -> p a b", a=4))
        A.release(m)

    def rmsnorm(hT, t0, grow):
        m = A.mark()
        sq = [A.alloc("sq%d" % i, [128, 512], BF16) for i in range(2)]
        rs = A.alloc("rs", [128, 512], F32)
        for tt in range(NTT):
            c0 = t0 + tt * 512
            bank = nb()
            for kc in range(8):
                s = sq[kc % 2]
                P.act(s[:, :], xT[:, kc, c0:c0 + 512], AF.Square)
                P.mm(ps[:, bank, :], onesB[:, :], s[:, :], start=(kc == 0), stop=(kc == 7))
            P.act(rs[:, :], ps[:, bank, :], AF.Ln, bias=EPSC[:, 0:1], scale=1.0 / D)
            P.act(rs[:, :], rs[:, :], AF.Exp, scale=-0.5)
            for kc in range(8):
                P.stt("dve", hT[:, kc, tt * 512:(tt + 1) * 512], xT[:, kc, c0:c0 + 512], pc(grow + kc), rs[:, :],
                      ALU.mult, ALU.mult)
        A.release(m)

    EPSC = A.alloc("EPSC", [128, 4], F32)
    mset(EPSC[:, 0:1], EPS)
    mset(EPSC[:, 1:2], 4.0 * EPS)
    mset(EPSC[:, 2:3], 1.0)

    def proj(wt, wc, hT, banks):
        for kc in range(8):
            for tt in range(NTT):
                P.mm(ps[:, banks[tt], :], wt[:, kc, wc:wc + 128], hT[:, kc, tt * 512:(tt + 1) * 512],
                     start=(kc == 0), stop=(kc == 7))

    def psv(banks):
        assert banks[-1] == banks[0] + len(banks) - 1
        return ps[:, banks[0]:banks[0] + len(banks), :]

    def pair_banks():
        b = nb()
        while b % 2 or (b + 1) in reserved:
            b = nb()
        nb()
        return [b, b + 1]

    def two_banks():
        if NTT == 1:
            return [nb()]
        b = nb()
        while b % 2:
            b = nb()
        nb()
        return [b, b + 1]

    def out_proj_add(wsrc, inT_fn, t0, pumpn=0):
        for g in range(D // WG):
            wt = wload(wsrc, g * WG, WG)
            for j in range(WG // 128):
                n = g * (WG // 128) + j
                banks = two_banks()
                for kc in range(8):
                    for tt in range(NTT):
                        P.mm(ps[:, banks[tt], :], wt[:, kc, j * 128:(j + 1) * 128], inT_fn(kc, tt),
                             start=(kc == 0), stop=(kc == 7))
                P.tt("dve", xT[:, n, t0:t0 + TT].rr("p (a b) -> p a b", a=NTT), xT[:, n, t0:t0 + TT].rr("p (a b) -> p a b", a=NTT),
                     psv(banks), ALU.add)
                if pumpn:
                    pump(pumpn)

    def xattn(l, mt):
        t0 = mt * TT
        m = A.mark()
        win_begin()
        qx = A.alloc("qx", [128, 8, TT], BF16)
        mh_ = A.mark()
        hT = A.alloc("hT", [128, 8, TT], BF16)
        rmsnorm(hT, t0, R_NX + l * 8)
        for g in range(D // WG):
            wt = wload(w_xq[l], g * WG, WG)
            for j in range(WG // 128):
                n = g * (WG // 128) + j
                banks = two_banks()
                proj(wt, j * 128, hT, banks)
                P.act(qx[:, n, :].rr("p (a b) -> p a b", a=NTT), psv(banks), AF.Copy, scale=1.0 / 16.0)
                pump(PUMP)
        A.release(mh_)
        ox = A.alloc("ox", [128, 8, TT], BF16)
        Et = [A.alloc("Et%d" % i, [128, 512], BF16) for i in range(4)]
        rden = [A.alloc("rden%d" % i, [128, 512], F32) for i in range(2)]
        ei = 0
        for h in range(4):
            for tt in range(NTT):
                es = []
                for mc in range(2):
                    bank = nb()
                    for half in range(2):
                        P.mm(ps[:, bank, :], KT[:, h * 2 + half, mc * 128:(mc + 1) * 128],
                             qx[:, h * 2 + half, tt * 512:(tt + 1) * 512], start=(half == 0), stop=(half == 1))
                    E = Et[ei % 4]
                    ei += 1
                    P.act(E[:, :], ps[:, bank, :], AF.Exp)
                    es.append(E)
                pump(PUMP)
                bden = nb()
                for mc in range(2):
                    P.mm(ps[:, bden, :], onesB[:, :], es[mc][:, :], start=(mc == 0), stop=(mc == 1))
                rd = rden[(h * NTT + tt) % 2]
                P.act(rd[:, :], ps[:, bden, :], AF.Ln)
                P.act(rd[:, :], rd[:, :], AF.Exp, scale=-1.0)
                for dv in range(2):
                    bo = nb()
                    n = h * 2 + dv
                    for mc in range(2):
                        P.mm(ps[:, bo, :], Vtok[:, mc, n * 128:(n + 1) * 128], es[mc][:, :],
                             start=(mc == 0), stop=(mc == 1))
                    P.tt("dve", ox[:, n, tt * 512:(tt + 1) * 512], ps[:, bo, :], rd[:, :], ALU.mult)
                pump(PUMP)
        out_proj_add(w_xo[l], lambda kc, tt: ox[:, kc, tt * 512:(tt + 1) * 512], t0, pumpn=PUMP)
        win_end()
        A.release(m)

    def even_mixer(mt):
        t0 = mt * TT
        last = (mt == NMT - 1)
        m = A.mark()
        aoT = A.alloc("aoT", [128, 4, TT], BF16)
        G_BETA, G_G, G_GCS, G_GCL, G_EGC, G_EKD, G_NBEG, G_VBS, G_NBETA, G_TMP = range(10)
        mh = A.mark()
        hT = A.alloc("hT", [128, 8, TT], BF16)
        rmsnorm(hT, t0, R_NM + 0)

        ma = A.mark()
        gluT = A.alloc("gluT", [128, 4, 32 + TT], BF16)
        cbuf = A.alloc("cbuf", [128, 4, TT], F32)
        Dg = A.alloc("Dg", [128, 31, 128], BF16)
        tA = [A.alloc("tA%d" % i, [128, TT], F32) for i in range(2)]
        tB = [A.alloc("tB%d" % i, [128, 1024], F32) for i in range(2)]
        stat = [A.alloc("stat%d" % i, [128, 512], F32) for i in range(3)]
        sga = A.alloc("sga", [128, 4, TT], BF16)
        wv = {}
        for c in range(4):
            g = c // 2
            if c % 2 == 0:
                wv = {"val": wload(w_in_even, g * WG, WG), "gate": wload(w_in_even, 512 + g * WG, WG)}
            wc = (c % 2) * 128
            bv = two_banks()
            bg = two_banks()
            proj(wv["val"], wc, hT, bv)
            proj(wv["gate"], wc, hT, bg)
            t = tA[c % 2]
            P.act(t[:, :].rr("p (a b) -> p a b", a=NTT), psv(bg), AF.Tanh, scale=0.5)
            P.copy("pool", gluT[:, c, 0:32], gluhalo[:, c, :])
            P.stt("dve", gluT[:, c, 32:32 + TT].rr("p (a b) -> p a b", a=NTT), t[:, :].rr("p (a b) -> p a b", a=NTT),
                  1.0, psv(bv), ALU.add, ALU.mult)
            P.copy("pool", gluhalo[:, c, :], gluT[:, c, TT:TT + 32])
            for k in range(31):
                P.ts("dve", Dg[:, k, :], identB[:, :], pch[:, k * 4 + c:k * 4 + c + 1], None, ALU.mult)
            for tt in range(NTT):
                bank = nb()
                for k in range(31):
                    P.mm(ps[:, bank, :], Dg[:, k, :], gluT[:, c, 2 + tt * 512 + k:2 + tt * 512 + k + 512],
                         start=(k == 0), stop=(k == 30))
                P.act(cbuf[:, c, tt * 512:(tt + 1) * 512], ps[:, bank, :], AF.Identity, bias=pc(R_DWB + c))
            if c % 2 == 0:
                wv["ga"] = wload(w_in_even, 1024 + (c // 2) * WG, WG)
            bg2 = two_banks()
            proj(wv["ga"], (c % 2) * 128, hT, bg2)
            t = tA[c % 2]
            P.act(t[:, :].rr("p (a b) -> p a b", a=NTT), psv(bg2), AF.Tanh, scale=0.5)
            P.stt("dve", sga[:, c, :].rr("p (a b) -> p a b", a=NTT), t[:, :].rr("p (a b) -> p a b", a=NTT), 1.0, psv(bg2),
                  ALU.add, ALU.mult)
        if last:
            cst = A.alloc("cst", [128, 4, 32], F32)
            csto = A.alloc("csto", [32, 512], F32)
            P.act(cst[:, :, :], gluhalo[:, :, :], AF.Copy, scale=0.5)
            bank = nb()
            for c in range(4):
                P.transpose(ps[0:32, bank, c * 128:(c + 1) * 128], cst[:, c, :], identF[:, :])
            P.copy("dve", csto[:, :], ps[0:32, bank, :])
            P.dma("sp", o_conv[:, :], csto[2:32, :], sem="st0")
        wga = None
        for tt in range(NTT):
            b1, b2 = nb(), nb()
            for c in range(4):
                s = tB[c % 2]
                P.act(s[:, 0:512], cbuf[:, c, tt * 512:(tt + 1) * 512], AF.Square)
                P.mm(ps[:, b1, :], onesF[:, :], cbuf[:, c, tt * 512:(tt + 1) * 512], start=(c == 0), stop=(c == 3))
                P.mm(ps[:, b2, :], onesF[:, :], s[:, 0:512], start=(c == 0), stop=(c == 3))
            mean, msq, rstd = stat
            P.act(mean[:, :], ps[:, b1, :], AF.Copy, scale=1.0 / 512)
            P.tt("dve", msq[:, :], mean[:, :], mean[:, :], ALU.mult)
            P.stt("dve", rstd[:, :], ps[:, b2, :], 1.0 / 512, msq[:, :], ALU.mult, ALU.subtract)
            P.act(rstd[:, :], rstd[:, :], AF.Ln, bias=EPSC[:, 0:1])
            P.act(rstd[:, :], rstd[:, :], AF.Exp, scale=-0.5)
            for c in range(4):
                cv = cbuf[:, c, tt * 512:(tt + 1) * 512]
                P.tt("dve", cv, cv, mean[:, :], ALU.subtract)
                P.tt("dve", cv, cv, rstd[:, :], ALU.mult)
                P.act(cv, cv, AF.Identity, bias=pch[:, 128 + c:129 + c], scale=pch[:, 124 + c:125 + c])
                th = tB[c % 2]
                P.act(th[:, 512:1024], cv, AF.Tanh)
                P.stt("dve", cv, th[:, 512:1024], 1.0, cv, ALU.add, ALU.mult)
        for c in range(4):
            P.stt("dve", aoT[:, c, :], sga[:, c, :], 0.5, cbuf[:, c, :], ALU.mult, ALU.mult)
        A.release(mh)
        if STAGE <= 3:
            dbg["aoT"] = aoT
            A.release(m)
            return

        qT = A.alloc("qT", [128, 4, TT], BF16)
        qdT = A.alloc("qdT", [128, 4, TT], BF16)
        kT = A.alloc("kT", [128, 4, TT], BF16)
        vb = A.alloc("vb", [128, NBLK, 4, 128], BF16)
        kd = A.alloc("kd", [128, NBLK, 4, 128], BF16)
        zs2 = A.alloc("zs2", [128, 4, TT], BF16)
        gts = A.alloc("gts", [128, 12, NBLK, 4], F32)
        eglB = A.alloc("eglB", [128, NBLK, 2, 4], F32)
        egcF = A.alloc("egcF", [4, TT], F32)
        mh = A.mark()
        hT = A.alloc("hT", [128, 8, TT], BF16)
        rmsnorm(hT, t0, R_NM + 0)
        mb = A.mark()
        w8 = A.alloc("w8", [128, 8, 8], BF16)
        P.dma("pool", w8[:, :, :], w_in_even[:, 3584:3592].rearrange("(c p) n -> p c n", p=128), sem="w8")
        bank = nb()
        for b in range(NBLK):
            for part in range(2):
                for kc in range(8):
                    P.mm(ps[:, bank, part * NBLK * 4 + b * 4:part * NBLK * 4 + b * 4 + 4], hT[:, kc, b * 128:(b + 1) * 128],
                         w8[:, kc, part * 4:part * 4 + 4], start=(kc == 0), stop=(kc == 7))
        gl = A.alloc("gl", [128, 2, NBLK, 4], F32)
        P.copy("dve", gl[:, :, :, :], ps[:, bank, 0:NBLK * 8].rr("p (a b c) -> p a b c", a=2, b=NBLK))

        def G(i):
            return gts[:, i, :, :]
        P.act(G(G_TMP), gl[:, 0, :, :], AF.Tanh, scale=0.5)
        P.ts("dve", G(G_BETA), G(G_TMP), 0.5, 0.5, ALU.mult, ALU.add)
        P.tt("dve", G(G_TMP), gl[:, 1, :, :], dtB[:, :, :], ALU.add)
        P.act(G(G_TMP), G(G_TMP), AF.Exp)
        P.act(G(G_TMP), G(G_TMP), AF.Ln, bias=EPSC[:, 2:3])
        P.tt("dve", G(G_G), G(G_TMP), nAB[:, :, :], ALU.mult)
        bank = nb()
        gm = A.alloc("gm", [128, NBLK, 2, 4], F32)
        for b in range(NBLK):
            P.mm(ps[:, bank, b * 4:b * 4 + 4], Mincl[:, :], gts[:, G_G, b, :], start=True, stop=True)
            P.mm(ps[:, bank, NBLK * 4 + b * 4:NBLK * 4 + b * 4 + 4], Msame[:, :], gts[:, G_G, b, :], start=True, stop=True)
            P.ts("dve", gm[:, b, 0, :], gts[:, G_G, b, :], ind0, None, ALU.mult)
            P.ts("dve", gm[:, b, 1, :], gts[:, G_G, b, :], ind1, None, ALU.mult)
        P.copy("dve", gts[:, G_GCS:G_GCL + 1, :, :], ps[:, bank, 0:NBLK * 8].rr("p (a b c) -> p a b c", a=2, b=NBLK))
        bank = nb()
        P.mm(ps[:, bank, 0:NBLK * 8], onesF[:, :], gm[:, :, :, :].rr("p a b c -> p (a b c)"), start=True, stop=True)
        P.act(eglB[:, :, :, :].rr("p a b c -> p (a b c)"), ps[:, bank, 0:NBLK * 8], AF.Exp)
        P.act(G(G_EGC), G(G_GCS), AF.Exp)
        P.tt("dve", G(G_TMP), G(G_GCL), G(G_GCS), ALU.subtract)
        P.act(G(G_EKD), G(G_TMP), AF.Exp)
        P.stt("dve", G(G_NBEG), G(G_BETA), -1.0, G(G_EGC), ALU.mult, ALU.mult)
        P.ts("dve", G(G_VBS), G(G_BETA), 0.5, None, ALU.mult)
        P.ts("dve", G(G_NBETA), G(G_BETA), -1.0, None, ALU.mult)
        for g2 in range(TT // 512):
            bank = nb()
            for j in range(4):
                b = g2 * 4 + j
                P.transpose(ps[0:4, bank, j * 128:(j + 1) * 128], gts[:, G_EGC, b, :], identF[:, :])
            P.copy("dve", egcF[:, g2 * 512:(g2 + 1) * 512], ps[0:4, bank, :])

        pre = [A.alloc("pre%d" % i, [128, 4 + TT], BF16) for i in range(2)]
        Dq = [A.alloc("Dq%d" % i, [128, 4, 128], BF16) for i in range(2)]
        s2 = [A.alloc("s2%d" % i, [128, TT], F32) for i in range(2)]
        tq = [A.alloc("tq%d" % i, [128, TT], F32) for i in range(1)] * 2
        sqb = [A.alloc("sqb%d" % i, [128, 512], BF16) for i in range(2)]
        r1 = A.alloc("r1", [128, 512], F32)
        r2 = A.alloc("r2", [128, 512], F32)
        qst = A.alloc("qst", [128, 12, 4], F32)
        if last:
            mset(qst[:, :, :], 0.0)
        wqs = {}

        def st1(c12):
            if c12 % 2 == 0:
                wqs[c12 // 2] = wload(w_in_even, 1536 + (c12 // 2) * WG, WG)
            wq = wqs[c12 // 2]
            bq = two_banks()
            proj(wq, (c12 % 2) * 128, hT, bq)
            pr = pre[c12 % 2]
            P.copy("pool", pr[:, 0:4], qkvhalo[:, c12, :])
            P.copy("act", pr[:, 4:4 + TT].rr("p (a b) -> p a b", a=NTT), psv(bq))
            if last:
                P.copy("dve", qst[:, c12, 0:3], ps[:, bq[NTT - 1], 509:512])
            P.copy("pool", qkvhalo[:, c12, :], pr[:, TT:TT + 4])

        def st2(c12):
            pr = pre[c12 % 2]
            dq = Dq[c12 % 2]
            for k in range(4):
                P.ts("dve", dq[:, k, :], identB[:, :], pc(R_SC + k * 12 + c12), None, ALU.mult)
            bc_ = two_banks()
            for tt in range(NTT):
                for k in range(4):
                    P.mm(ps[:, bc_[tt], :], dq[:, k, :], pr[:, 1 + tt * 512 + k:1 + tt * 512 + k + 512],
                         start=(k == 0), stop=(k == 3))
            t = tq[c12 % 2]
            s = s2[c12 % 2]
            P.act(t[:, :].rr("p (a b) -> p a b", a=NTT), psv(bc_), AF.Tanh, scale=0.5)
            P.stt("dve", s[:, :].rr("p (a b) -> p a b", a=NTT), t[:, :].rr("p (a b) -> p a b", a=NTT), 1.0, psv(bc_),
                  ALU.add, ALU.mult)

        def st3(c12):
            kind, h = c12 // 4, c12 % 4
            s = s2[c12 % 2]
            for tt in range(NTT):
                sv = s[:, tt * 512:(tt + 1) * 512]
                if kind < 2:
                    sb_ = sqb[tt % 2]
                    P.act(sb_[:, :], sv, AF.Square)
                    bank = nb()
                    P.mm(ps[:, bank, :], onesB[:, :], sb_[:, :], start=True, stop=True)
                    P.act(r1[:, :], ps[:, bank, :], AF.Ln, bias=EPSC[:, 1:2])
                    P.act(r1[:, :], r1[:, :], AF.Exp, scale=-0.5)
                if kind == 0:
                    P.stt("dve", qT[:, h, tt * 512:(tt + 1) * 512], sv, 128.0 ** -0.5, r1[:, :], ALU.mult, ALU.mult)
                    bank = nb()
                    P.mm(ps[:, bank, :], sel[:, h, :], egcF[:, tt * 512:(tt + 1) * 512], start=True, stop=True)
                    P.tt("dve", r2[:, :], r1[:, :], ps[:, bank, :], ALU.mult)
                    P.stt("dve", qdT[:, h, tt * 512:(tt + 1) * 512], sv, 128.0 ** -0.5, r2[:, :], ALU.mult, ALU.mult)
                elif kind == 1:
                    P.tt("dve", sv, sv, r1[:, :], ALU.mult)
                    P.copy("act", kT[:, h, tt * 512:(tt + 1) * 512], sv)
                if kind >= 1:
                    bank = nb()
                    for j in range(4):
                        P.transpose(ps[:, bank, j * 128:(j + 1) * 128], s[:, tt * 512 + j * 128:tt * 512 + (j + 1) * 128],
                                    identF[:, :])
                    dst = kd if kind == 1 else vb
                    gsc4 = gts[:, G_EKD if kind == 1 else G_VBS, tt * 4:(tt + 1) * 4, h]
                    P.tt("dve", dst[:, tt * 4:(tt + 1) * 4, h, :], ps[:, bank, :].rr("p (a b) -> p a b", a=4),
                         V(gsc4.ap.unsqueeze(2).to_broadcast([128, 4, 128]), gsc4.t, gsc4.rect), ALU.mult)

        for step in range(12 + 2):
            if step < 12:
                st1(step)
            if 0 <= step - 1 < 12:
                st2(step - 1)
            if 0 <= step - 2 < 12:
                st3(step - 2)
        if last:
            qsto = A.alloc("qsto", [4, 512], F32)
            for g3 in range(3):
                bank = nb()
                for j in range(4):
                    P.transpose(ps[0:4, bank, j * 128:(j + 1) * 128], qst[:, g3 * 4 + j, :], identF[:, :])
                P.copy("dve", qsto[:, :], ps[0:4, bank, :])
                P.dma("sp", o_qkv[:, g3 * 512:(g3 + 1) * 512], qsto[0:3, :], sem="st1")
        wz = None
        for c in range(4):
            if c % 2 == 0:
                wz = wload(w_in_even, 3072 + (c // 2) * WG, WG)
            bz = two_banks()
            proj(wz, (c % 2) * 128, hT, bz)
            t = tq[c % 2]
            P.act(t[:, :].rr("p (a b) -> p a b", a=NTT), psv(bz), AF.Tanh, scale=0.5)
            P.stt("dve", zs2[:, c, :].rr("p (a b) -> p a b", a=NTT), t[:, :].rr("p (a b) -> p a b", a=NTT), 1.0, psv(bz),
                  ALU.add, ALU.mult)
        A.release(mh)

        TTm = A.alloc("TTm", [128, NBLK, 4, 128], BF16)
        QKm = A.alloc("QKm", [128, NBLK, 4, 128], BF16)
        mprep = A.mark()
        Gm = [A.alloc("Gm%d" % i, [128, 128], F32) for i in range(4)]
        Es = [A.alloc("Es%d" % i, [128, 128], F32) for i in range(4)]
        Ei = [A.alloc("Ei%d" % i, [128, 128], F32) for i in range(4)]
        Mb = [A.alloc("Mb%d" % i, [128, 4, 128], F32) for i in range(2)]
        MTb = [A.alloc("MTb%d" % i, [128, 4, 128], F32) for i in range(2)]
        Xb = [A.alloc("Xb%d" % i, [128, 4, 128], F32) for i in range(2)]
        for b in range(NBLK):
            cs = slice(b * 128, (b + 1) * 128)
            for h in range(4):
                P.ts("pool", Gm[h][:, :], Mgt[:, :], gts[:, G_G, b, h:h + 1], None, ALU.mult)
            b1s = [nb() for _ in range(4)]
            b2s = [nb() for _ in range(4)]
            for h in range(4):
                P.mm(ps[:, b2s[h], 0:128], kT[:, h, cs], kT[:, h, cs], start=True, stop=True)
                P.mm(ps[:, b2s[h], 128:256], kT[:, h, cs], qT[:, h, cs], start=True, stop=True)
            for h in range(4):
                P.mm(ps[:, b1s[h], 0:128], Mincl[:, :], Gm[h][:, :], start=True, stop=False)
                P.mm(ps[:, b1s[h], 0:128], identB[:, :], NEGsB[:, :], start=False, stop=True)
                P.mm(ps[:, b1s[h], 128:256], Gm[h][:, :], Mincl[:, :], start=True, stop=False)
                P.mm(ps[:, b1s[h], 128:256], identB[:, :], NEGiTB[:, :], start=False, stop=True)
            for h in range(4):
                P.act(Es[h][:, :], ps[:, b1s[h], 0:128], AF.Exp)
                P.act(Ei[h][:, :], ps[:, b1s[h], 128:256], AF.Exp)
            for h in range(4):
                P.stt("dve", MTb[0][:, h, :], ps[:, b2s[h], 0:128], gts[:, G_NBETA, b, h:h + 1], Es[h][:, :], ALU.mult, ALU.mult)
                P.tt("dve", QKm[:, b, h, :], ps[:, b2s[h], 128:256], Ei[h][:, :], ALU.mult)
            bank = nb()
            for h in range(4):
                P.transpose(ps[:, bank, h * 128:(h + 1) * 128], MTb[0][:, h, :], identF[:, :])
            P.copy("act", Mb[0][:, :, :], ps[:, bank, :].rr("p (a b) -> p a b", a=4))
            for h in range(4):
                P.tt("pool", Xb[0][:, h, :], Mb[0][:, h, :], identF[:, :], ALU.add)
            cur = 0
            for k in range(1, 6):
                nxt = 1 - cur
                bm, bmt, bx = nb(), nb(), nb()
                for h in range(4):
                    hs = slice(h * 128, (h + 1) * 128)
                    if k < 5:
                        P.mm(ps[:, bm, hs], MTb[cur][:, h, :], Mb[cur][:, h, :], start=True, stop=True)
                    P.mm(ps[:, bmt, hs], Mb[cur][:, h, :], MTb[cur][:, h, :], start=True, stop=True)
                if k < 5:
                    P.copy("act", Mb[nxt][:, :, :], ps[:, bm, :].rr("p (a b) -> p a b", a=4))
                P.copy("dve", MTb[nxt][:, :, :], ps[:, bmt, :].rr("p (a b) -> p a b", a=4))
                for h in range(4):
                    hs = slice(h * 128, (h + 1) * 128)
                    P.mm(ps[:, bx, hs], MTb[nxt][:, h, :], Xb[cur][:, h, :], start=True, stop=True)
                if k < 5:
                    P.tt("dve", Xb[nxt][:, :, :], Xb[cur][:, :, :], ps[:, bx, :].rr("p (a b) -> p a b", a=4), ALU.add)
                else:
                    P.tt("dve", TTm[:, b, :, :], Xb[cur][:, :, :], ps[:, bx, :].rr("p (a b) -> p a b", a=4), ALU.add)
                cur = nxt

        A.release(mprep)
        bT = qT
        R4 = A.alloc("R4", [128, 4, 128], BF16)
        vn4 = A.alloc("vn4", [128, 4, 128], BF16)
        of = [A.alloc("of%d" % i, [128, 512], F32) for i in range(2)]
        osq = [A.alloc("osq%d" % i, [128, 512], BF16) for i in range(2)]
        orr = [A.alloc("orr%d" % i, [128, 512], F32) for i in range(2)]
        obank = {}
        for tt in range(NTT):
            for h in range(4):
                obank[h] = h
                reserved.add(h)
            for cc in range(8):
                c = tt * 8 + cc
                b, par = c // 2, c % 2
                r0 = par * 64
                tok = slice(c * 64, (c + 1) * 64)
                bA, bB, bC = nb(), nb(), nb()
                for h in range(4):
                    P.mm(ps[r0:r0 + 64, bA, h * 128:(h + 1) * 128], kT[:, h, tok], Sbf[:, h, :], start=True, stop=True)
                for h in range(4):
                    P.mm(ps[:, obank[h], cc * 64:(cc + 1) * 64], Sbf[:, h, :], qdT[:, h, tok], start=True, stop=False)
                for h in range(4):
                    P.stt("dve", R4[r0:r0 + 64, h, :], ps[r0:r0 + 64, bA, h * 128:(h + 1) * 128],
                          gts[r0:r0 + 64, G_NBEG, b, h:h + 1], vb[r0:r0 + 64, b, h, :], ALU.mult, ALU.add)
                for h in range(4):
                    P.mm(ps[r0:r0 + 64, bB, h * 128:(h + 1) * 128], TTm[r0:r0 + 64, b, h, r0:r0 + 64], R4[r0:r0 + 64, h, :],
                         start=True, stop=True)
                P.copy("act", vn4[r0:r0 + 64, :, :], ps[r0:r0 + 64, bB, :].rr("p (a b) -> p a b", a=4))
                for h in range(4):
                    P.mm(ps[:, obank[h], cc * 64:(cc + 1) * 64], vn4[r0:r0 + 64, h, :], QKm[r0:r0 + 64, b, h, r0:r0 + 64],
                         start=False, stop=True)
                for h in range(4):
                    P.mm(ps[:, bC, h * 128:(h + 1) * 128], kd[r0:r0 + 64, b, h, :], vn4[r0:r0 + 64, h, :], start=True, stop=True)
                for h in range(4):
                    P.stt("dve", Sst[:, h, :], Sst[:, h, :], eglB[:, b, par, h:h + 1], ps[:, bC, h * 128:(h + 1) * 128],
                          ALU.mult, ALU.add)
                P.copy("act", Sbf[:, :, :], Sst[:, :, :])
            for h in range(4):
                o_ = of[h % 2]
                P.copy("act", o_[:, :], ps[:, obank[h], :])
                sq_ = osq[h % 2]
                P.act(sq_[:, :], o_[:, :], AF.Square)
                bank = nb()
                P.mm(ps[:, bank, :], onesB[:, :], sq_[:, :], start=True, stop=True)
                rr_ = orr[h % 2]
                P.act(rr_[:, :], ps[:, bank, :], AF.Ln, bias=EPSC[:, 0:1], scale=1.0 / 128)
                P.act(rr_[:, :], rr_[:, :], AF.Exp, scale=-0.5)
                P.stt("dve", o_[:, :], o_[:, :], pc(R_DNG), rr_[:, :], ALU.mult, ALU.mult)
                P.stt("dve", bT[:, h, tt * 512:(tt + 1) * 512], o_[:, :], 0.5, zs2[:, h, tt * 512:(tt + 1) * 512], ALU.mult, ALU.mult)
            reserved.clear()
        if last:
            P.dma("sp", o_delta.rearrange("h d e -> d h e"), Sst[:, :, :], sem="st2")
        if STAGE <= 4:
            A.release(m)
            return
        out_proj_add(w_out_even, lambda kc, tt: (aoT if kc < 4 else bT)[:, kc % 4, tt * 512:(tt + 1) * 512], t0)
        A.release(m)

    def odd_mixer(mt):
        t0 = mt * TT
        last = (mt == NMT - 1)
        m = A.mark()
        hT = A.alloc("hT", [128, 8, TT], BF16)
        rmsnorm(hT, t0, R_NM + 8)
        plT = A.alloc("plT", [128, 8, TT], BF16)
        ub = [A.alloc("ub%d" % i, [128, 16 + TT], F32) for i in range(2)]
        sa = A.alloc("sa", [128, 16 + TT], F32)
        sb2 = A.alloc("sb2", [128, 16 + TT], F32)
        wu = None
        for c in range(8):
            if c % 2 == 0:
                wu = wload(w_in_odd, (c // 2) * WG, WG)
            bu = two_banks()
            proj(wu, (c % 2) * 128, hT, bu)
            u = ub[c % 2]
            P.copy("pool", u[:, 0:16], uhalo[:, c, :])
            P.copy("act", u[:, 16:16 + TT].rr("p (a b) -> p a b", a=NTT), psv(bu))
            P.copy("pool", uhalo[:, c, :], u[:, TT:TT + 16])
            gi = c // 2
            win = 2 << gi
            src = u
            sh, lo, bi = 1, 1, 0
            bufs = [sa, sb2]
            while sh < win:
                dst = bufs[bi]
                bi = 1 - bi
                P.tt("dve", dst[:, lo:16 + TT], src[:, lo:16 + TT], src[:, lo - sh:16 + TT - sh], ALU.add)
                src = dst
                sh *= 2
                lo = 2 * lo + 1
            P.stt("dve", plT[:, c, :], src[:, 16:16 + TT], 1.0 / win, u[:, 16:16 + TT], ALU.mult, ALU.subtract)
            if mt == 0:
                fx = sa if src is sb2 else sb2
                P.tt("dve", fx[:, 0:win - 1], src[:, 16:16 + win - 1], invc[:, 0:win - 1], ALU.mult)
                P.tt("dve", plT[:, c, 0:win - 1], fx[:, 0:win - 1], u[:, 16:16 + win - 1], ALU.subtract)
        if last:
            pst = A.alloc("pst", [128, 8, 16], F32)
            psto = A.alloc("psto", [16, D], F32)
            P.copy("dve", pst[:, :, :], uhalo[:, :, :])
            for g2 in range(2):
                bank = nb()
                for j in range(4):
                    P.transpose(ps[0:16, bank, j * 128:(j + 1) * 128], pst[:, g2 * 4 + j, :], identF[:, :])
                P.copy("dve", psto[:, g2 * 512:(g2 + 1) * 512], ps[0:16, bank, :])
            P.dma("sp", o_pool[:, :], psto[1:16, :], sem="st3")
        zT = A.alloc("zT", [128, 8, TT], BF16)
        wp = A.alloc("wp", [128, 4, 2, 256], BF16)
        P.dma("pool", wp[:, :, :, :], w_pool.rearrange("g (c p) d -> p g c d", p=128), sem="wp")
        tz = [A.alloc("tz%d" % i, [128, TT], F32) for i in range(2)]
        tg = [A.alloc("tg%d" % i, [128, TT], F32) for i in range(2)]
        wgt = None
        for n in range(8):
            gi, half = n // 2, n % 2
            bz = two_banks()
            for cc in range(2):
                for tt in range(NTT):
                    P.mm(ps[:, bz[tt], :], wp[:, gi, cc, half * 128:(half + 1) * 128], plT[:, gi * 2 + cc, tt * 512:(tt + 1) * 512],
                         start=(cc == 0), stop=(cc == 1))
            z_ = tz[n % 2]
            P.ts("dve", z_[:, :].rr("p (a b) -> p a b", a=NTT), psv(bz), pc(R_BP + n), pc(R_PSC + n), ALU.add, ALU.mult)
            if n % 2 == 0:
                wgt = wload(w_in_odd, 1024 + (n // 2) * WG, WG)
            bg = two_banks()
            proj(wgt, (n % 2) * 128, hT, bg)
            t = tg[n % 2]
            P.act(t[:, :].rr("p (a b) -> p a b", a=NTT), psv(bg), AF.Tanh, scale=0.5)
            P.stt("dve", t[:, :].rr("p (a b) -> p a b", a=NTT), t[:, :].rr("p (a b) -> p a b", a=NTT), 1.0, psv(bg),
                  ALU.add, ALU.mult)
            P.stt("dve", zT[:, n, :], t[:, :], 0.5, z_[:, :], ALU.mult, ALU.mult)
        out_proj_add(w_out_odd, lambda kc, tt: zT[:, kc, tt * 512:(tt + 1) * 512], t0)
        A.release(m)

    def final_out():
        m = A.mark()
        sq = [A.alloc("fsq%d" % i, [128, 512], BF16) for i in range(2)]
        rs = A.alloc("frs", [128, 512], F32)
        yf = [A.alloc("yf%d" % i, [128, 512], F32) for i in range(2)]
        yst = A.alloc("yst", [128, 4, D], F32)
        for tt in range(T // 512):
            c0 = tt * 512
            bank = nb()
            for kc in range(8):
                s = sq[kc % 2]
                P.act(s[:, :], xT[:, kc, c0:c0 + 512], AF.Square)
                P.mm(ps[:, bank, :], onesB[:, :], s[:, :], start=(kc == 0), stop=(kc == 7))
            P.act(rs[:, :], ps[:, bank, :], AF.Ln, bias=EPSC[:, 0:1], scale=1.0 / D)
            P.act(rs[:, :], rs[:, :], AF.Exp, scale=-0.5)
            for kc in range(8):
                y_ = yf[kc % 2]
                P.stt("dve", y_[:, :], xT[:, kc, c0:c0 + 512], pc(R_NF + kc), rs[:, :], ALU.mult, ALU.mult)
                bank2 = nb()
                for j in range(4):
                    P.transpose(ps[:, bank2, j * 128:(j + 1) * 128], y_[:, j * 128:(j + 1) * 128], identF[:, :])
                P.copy(ev(), yst[:, :, kc * 128:(kc + 1) * 128], ps[:, bank2, :].rr("p (a b) -> p a b", a=4))
            for j in range(4):
                P.dma("sp", o_y[c0 + j * 128:c0 + (j + 1) * 128, :], yst[:, j, :], sem="sty%d" % j)
            pump(PUMP)
        A.release(m)

    x_s = din("x_s", [NS, D])
    st_conv = din("st_conv", [NS, 30, 512])
    st_qkv = din("st_qkv", [NS, 3, 1536])
    st_delta = din("st_delta", [NS, 4, 128, 128])
    st_pool = din("st_pool", [NS, 15, D])
    c_k = din("c_k", [DEPTH, NS, NMEM, D])
    c_v = din("c_v", [DEPTH, NS, NMEM, D])
    o_ys = dout("o_ys", [NS, D])
    o_conv_s = dout("o_conv_s", [NS, 30, 512])
    o_qkv_s = dout("o_qkv_s", [NS, 3, 1536])
    o_delta_s = dout("o_delta_s", [NS, 4, 128, 128])
    o_pool_s = dout("o_pool_s", [NS, 15, D])

    def bcv(v, axis, shape):
        return V(v.ap.unsqueeze(axis).to_broadcast(list(shape)), v.t, v.rect, v.excl)

    def recip(eng_view):
        P.add("dve", lambda e, t=eng_view: e.reciprocal(t.ap, t.ap), reads=[eng_view], writes=[eng_view])

    def reduce_x(out, in_):
        P.add("dve", lambda e: e.tensor_reduce(out.ap, in_.ap, AX.X, ALU.add), reads=[in_], writes=[out])

    def sample_path():
        eye16 = identF[0:NS, 0:NS]
        eye4 = identF[0:4, 0:4]

        def ring():
            swring["bufs"] = [AS.alloc("swb%d" % i, [128, 8, WG], BF16) for i in range(2)]

        def rmsnorm_s(grow, f32out=None):
            m = AS.mark()
            sq = AS.alloc("ssq", [128, 8, NS], BF16)
            rs = AS.alloc("srs", [128, NS], F32)
            P.act(sq[:, :, :], xsT[:, :, :], AF.Square)
            bank = nb()
            for kc in range(8):
                P.mm(ps[:, bank, 0:NS], onesB[:, :], sq[:, kc, :], start=(kc == 0), stop=(kc == 7))
            P.act(rs[:, :], ps[:, bank, 0:NS], AF.Sqrt, bias=EPSC[:, 0:1], scale=1.0 / D)
            recip(rs[:, :])
            dst = hsT if f32out is None else f32out
            for kc in range(8):
                P.stt("dve", dst[:, kc, :], xsT[:, kc, :], pc(grow + kc), rs[:, :], ALU.mult, ALU.mult)
            AS.release(m)

        def s_outproj(wsrc, in_fn):
            bank = nb()
            reserved.add(bank)
            for g in range(D // WG):
                wt = wload_s(wsrc, g * WG, WG)
                for j in range(WG // 128):
                    n = g * (WG // 128) + j
                    for kc in range(8):
                        P.mm(ps[:, bank, n * NS:(n + 1) * NS], wt[:, kc, j * 128:(j + 1) * 128], in_fn(kc),
                             start=(kc == 0), stop=(kc == 7))
                yield
            reserved.discard(bank)
            P.tt("dve", xsT[:, :, :], xsT[:, :, :], ps[:, bank, 0:8 * NS].rr("p (a b) -> p a b", a=8), ALU.add)

        m_seg = AS.mark()
        ring()
        mx_ = AS.mark()
        xst = AS.alloc("xst", [NS, D], F32)
        P.dma("sp", xst[:, :], x_s, sem="sld0")
        bank = nb()
        for kc in range(8):
            P.transpose(ps[:, bank, kc * NS:(kc + 1) * NS], xst[:, kc * 128:(kc + 1) * 128], eye16)
        P.copy("dve", xsT[:, :, :], ps[:, bank, 0:8 * NS].rr("p (a b) -> p a b", a=8))
        AS.release(mx_)
        rmsnorm_s(R_NM + 0)
        yield
        pF = AS.alloc("pF", [128, 16, NS], F32)
        qkvS = AS.alloc("qkvS", [NS, 1536], F32)
        gls = AS.alloc("gls", [NS, 8], F32)
        aoS = AS.alloc("aoS", [128, 4, NS], BF16)
        bS = AS.alloc("bS", [128, 4, NS], BF16)
        w8s = AS.alloc("w8s", [128, 8, 8], BF16)
        BF_, BQ_ = 4, 5
        reserved.update([BF_, BQ_])
        P.dma("pool", w8s[:, :, :], w_in_even[:, 3584:3592].rearrange("(c p) n -> p c n", p=128), sem="sw8")
        for g in range(14):
            col0 = g * WG
            wt = wload_s(w_in_even, col0, WG)
            if col0 < 1536 or col0 >= 3072:
                for j in range(2):
                    ci = (col0 // 128 + j) if col0 < 1536 else (12 + (col0 - 3072) // 128 + j)
                    for kc in range(8):
                        P.mm(ps[:, BF_, ci * NS:(ci + 1) * NS], wt[:, kc, j * 128:(j + 1) * 128], hsT[:, kc, :],
                             start=(kc == 0), stop=(kc == 7))
            else:
                gq = (col0 - 1536) // WG
                for kc in range(8):
                    P.mm(ps[0:NS, BQ_, (gq % 2) * WG:(gq % 2 + 1) * WG], hsT[:, kc, :], wt[:, kc, 0:WG],
                         start=(kc == 0), stop=(kc == 7))
                if gq % 2 == 1:
                    P.copy("act", qkvS[:, (gq // 2) * 512:(gq // 2 + 1) * 512], ps[0:NS, BQ_, :])
            yield
        for kc in range(8):
            P.mm(ps[0:NS, BQ_, 0:8], hsT[:, kc, :], w8s[:, kc, :], start=(kc == 0), stop=(kc == 7))
        P.copy("dve", pF[:, :, :], ps[:, BF_, 0:16 * NS].rr("p (a b) -> p a b", a=16))
        P.copy("dve", gls[:, :], ps[0:NS, BQ_, 0:8])
        reserved.discard(BF_)
        reserved.discard(BQ_)
        P.dma("sp", o_conv_s[:, 0:29, :], st_conv[:, 1:30, :], sem="sst0")
        P.dma("sp", o_qkv_s[:, 0:2, :], st_qkv[:, 1:3, :], sem="sst1")
        P.dma("sp", o_qkv_s[:, 2, :], qkvS[:, :], sem="sst2")
        yield

        m = AS.mark()
        gluS = AS.alloc("gluS", [128, 4, NS], F32)
        tS = AS.alloc("tS", [128, 4, NS], F32)
        P.act(tS[:, :, :], pF[:, 4:8, :], AF.Tanh, scale=0.5)
        P.ts("dve", tS[:, :, :], tS[:, :, :], 0.5, 0.5, ALU.mult, ALU.add)
        P.tt("dve", gluS[:, :, :], tS[:, :, :], pF[:, 0:4, :], ALU.mult)
        gst = AS.alloc("gst", [NS, 512], F32)
        bank = nb()
        for c in range(4):
            P.transpose(ps[0:NS, bank, c * 128:(c + 1) * 128], gluS[:, c, :], identF[:, :])
        P.copy("act", gst[:, :], ps[0:NS, bank, :])
        P.dma("sp", o_conv_s[:, 29, :], gst[:, :], sem="sst3")
        HS = NS // 2
        stc = AS.alloc("stc", [30, HS, 512], F32)
        cst = AS.alloc("cst", [128, NS, 4, 32], F32)
        prod = AS.alloc("prod", [128, NS, 4, 31], F32)
        convS = AS.alloc("convS", [128, NS, 4], F32)
        cS = AS.alloc("cS", [128, 4, NS], F32)
        for hf in range(2):
            P.dma("sp", stc[:, :, :], st_conv[hf * HS:(hf + 1) * HS].rearrange("s k c -> k s c"), sem="sld1")
            for s4 in range(HS // 4):
                bank = nb()
                for si in range(4):
                    for c in range(4):
                        slot = si * 4 + c
                        P.transpose(ps[:, bank, slot * 32:slot * 32 + 30], stc[:, s4 * 4 + si, c * 128:(c + 1) * 128],
                                    identF[0:30, 0:30])
                s0 = hf * HS + s4 * 4
                P.copy(ev(), cst[:, s0:s0 + 4, :, 0:30],
                       ps[:, bank, :].rr("p (a b c) -> p a b c", a=4, b=4).sub(slice(None), slice(None), slice(None), slice(0, 30)))
                yield
        P.copy("dve", cst[:, :, :, 30], gluS[:, :, :].rr("p c s -> p s c"))
        wv = bcv(pcol1[:, 0:124].rr("p (k c) -> p c k", c=4), 1, [128, NS, 4, 31])
        P.tt("dve", prod[:, :, :, :], cst[:, :, :, 0:31], wv, ALU.mult)
        reduce_x(convS[:, :, :], prod[:, :, :, :])
        for c in range(4):
            P.ts("dve", cS[:, c, :], convS[:, :, c], pc(R_DWB + c), None, ALU.add)
        yield
        sqS = AS.alloc("sqS", [128, 4, NS], F32)
        P.act(sqS[:, :, :], cS[:, :, :], AF.Square)
        bank = nb()
        for c in range(4):
            P.mm(ps[:, bank, 0:NS], onesF[:, :], cS[:, c, :], start=(c == 0), stop=(c == 3))
        for c in range(4):
            P.mm(ps[:, bank, NS:2 * NS], onesF[:, :], sqS[:, c, :], start=(c == 0), stop=(c == 3))
        mean = AS.alloc("smean", [128, NS], F32)
        msq = AS.alloc("smsq", [128, NS], F32)
        rstd = AS.alloc("srstd", [128, NS], F32)
        P.act(mean[:, :], ps[:, bank, 0:NS], AF.Copy, scale=1.0 / 512)
        P.tt("dve", msq[:, :], mean[:, :], mean[:, :], ALU.mult)
        P.stt("dve", rstd[:, :], ps[:, bank, NS:2 * NS], 1.0 / 512, msq[:, :], ALU.mult, ALU.subtract)
        P.act(rstd[:, :], rstd[:, :], AF.Sqrt, bias=EPSC[:, 0:1])
        recip(rstd[:, :])
        thS = AS.alloc("thS", [128, 4, NS], F32)
        for c in range(4):
            P.tt("dve", cS[:, c, :], cS[:, c, :], mean[:, :], ALU.subtract)
            P.tt("dve", cS[:, c, :], cS[:, c, :], rstd[:, :], ALU.mult)
            P.act(cS[:, c, :], cS[:, c, :], AF.Identity, bias=pch[:, 128 + c:129 + c], scale=pch[:, 124 + c:125 + c])
        P.act(thS[:, :, :], cS[:, :, :], AF.Tanh)
        P.stt("dve", cS[:, :, :], thS[:, :, :], 1.0, cS[:, :, :], ALU.add, ALU.mult)
        P.act(thS[:, :, :], pF[:, 8:12, :], AF.Tanh, scale=0.5)
        P.stt("dve", thS[:, :, :], thS[:, :, :], 1.0, pF[:, 8:12, :], ALU.add, ALU.mult)
        P.stt("dve", aoS[:, :, :], thS[:, :, :], 0.5, cS[:, :, :], ALU.mult, ALU.mult)
        AS.release(m)
        yield

        m = AS.mark()
        gS = AS.alloc("gS", [NS, 6, 4], F32)
        P.act(gS[:, 1, :], gls[:, 0:4], AF.Tanh, scale=0.5)
        P.ts("dve", gS[:, 0, :], gS[:, 1, :], 0.5, 0.5, ALU.mult, ALU.add)
        P.tt("dve", gS[:, 1, :], gls[:, 4:8], dtB[0:NS, 0, :], ALU.add)
        P.act(gS[:, 1, :], gS[:, 1, :], AF.Exp)
        P.act(gS[:, 1, :], gS[:, 1, :], AF.Ln, bias=EPSC[0:NS, 2:3])
        P.tt("dve", gS[:, 1, :], gS[:, 1, :], nAB[0:NS, 0, :], ALU.mult)
        P.act(gS[:, 2, :], gS[:, 1, :], AF.Exp)
        rhsAB = AS.alloc("rhsAB", [NS, 2, 4, NS], F32)
        P.tt("dve", rhsAB[:, 0, :, :], bcv(gS[:, 2, :], 2, [NS, 4, NS]), bcv(eye16, 1, [NS, 4, NS]), ALU.mult)
        P.tt("dve", rhsAB[:, 1, :, :], bcv(gS[:, 0, :], 2, [NS, 4, NS]), bcv(eye16, 1, [NS, 4, NS]), ALU.mult)
        bank = nb()
        P.mm(ps[:, bank, 0:2 * 4 * NS], onesF[0:NS, :], rhsAB[:, :, :, :].rr("p a b c -> p (a b c)"), start=True, stop=True)
        abB = AS.alloc("abB", [128, 2, 4, NS], F32)
        P.copy("dve", abB[:, :, :, :], ps[:, bank, 0:2 * 4 * NS].rr("p (a b c) -> p a b c", a=2, b=4))
        aB = abB[:, 0, :, :]
        betaB = abB[:, 1, :, :]
        yield
        qkvn = AS.alloc("qkvn", [NS, 12, 128], F32)
        ss = AS.alloc("ss", [NS, 8], F32)
        qkvT = AS.alloc("qkvT", [128, 12, NS], F32)
        macc = AS.mark()
        accf = AS.alloc("accf", [NS, 1536], F32)
        m2 = AS.mark()
        stq = AS.alloc("stq", [NS, 3, 512], F32)
        wB = AS.alloc("wB", [NS, 4, 512], F32)
        tmp = AS.alloc("tmpq", [NS, 512], F32)
        for q3 in range(3):
            cs3 = slice(q3 * 512, (q3 + 1) * 512)
            acc = accf[:, cs3]
            P.dma("sp", stq[:, :, :], st_qkv[:, :, cs3], sem="sld0")
            for k in range(4):
                P.dma("sp", wB[:, k, :], sc_w[k, cs3].partition_broadcast(NS), sem="sld%d" % (1 + k % 2))
            P.tt("dve", acc, qkvS[:, cs3], wB[:, 3, :], ALU.mult)
            for k in range(3):
                P.tt("dve", tmp[:, :], stq[:, k, :], wB[:, k, :], ALU.mult)
                P.tt("dve", acc, acc, tmp[:, :], ALU.add)
            P.act(tmp[:, :], acc, AF.Tanh, scale=0.5)
            P.stt("dve", acc, tmp[:, :], 1.0, acc, ALU.add, ALU.mult)
            yield
        AS.release(m2)
        m2 = AS.mark()
        tmp2 = AS.alloc("tmp2", [NS, 1024], F32)
        P.tt("dve", tmp2[:, :], accf[:, 0:1024], accf[:, 0:1024], ALU.mult)
        reduce_x(ss[:, :], tmp2[:, :].rr("p (a b) -> p a b", a=8))
        P.act(ss[:, :], ss[:, :], AF.Sqrt, bias=EPSC[0:NS, 1:2])
        recip(ss[:, :])
        P.ts("dve", ss[:, 0:4], ss[:, 0:4], 128.0 ** -0.5, None, ALU.mult)
        P.tt("dve", qkvn[:, 0:8, :], accf[:, 0:1024].rr("p (a b) -> p a b", a=8), bcv(ss[:, :], 2, [NS, 8, 128]), ALU.mult)
        P.ts("dve", qkvn[:, 8:12, :], accf[:, 1024:1536].rr("p (a b) -> p a b", a=4), 0.5, None, ALU.mult)
        bank = nb()
        for j in range(12):
            P.transpose(ps[:, bank, j * NS:(j + 1) * NS], qkvn[:, j, :], eye16)
        P.copy("dve", qkvT[:, :, :], ps[:, bank, 0:12 * NS].rr("p (a b) -> p a b", a=12))
        AS.release(macc)
        yield
        oS = AS.alloc("oS", [128, 4, NS], F32)
        Ssm = AS.alloc("Ssm", [128, 2, NS, 128], F32)
        vnT = AS.alloc("vnT", [128, 2, NS], F32)
        vnt = AS.alloc("vnt", [NS, 2, 128], F32)
        vbd = AS.alloc("vbd", [NS, NS, 128], F32)
        stmp = [AS.alloc("stmp%d" % i, [128, 4, 128], F32) for i in range(2)]
        for hp in range(2):
            for hh in range(2):
                h = hp * 2 + hh
                P.dma("sp", Ssm[:, hh, :, :], st_delta[:, h].rearrange("s d e -> d s e"), sem="slds%d" % hh)
            bank = nb()
            for hh in range(2):
                h = hp * 2 + hh
                for s in range(NS):
                    P.mm(ps[:, bank, hh * NS + s:hh * NS + s + 1], Ssm[:, hh, s, :], qkvT[:, 4 + h, s:s + 1], start=True, stop=True)
            P.tt("dve", vnT[:, :, :], ps[:, bank, 0:2 * NS].rr("p (a b) -> p a b", a=2), abB[:, 0, hp * 2:hp * 2 + 2, :], ALU.mult)
            P.tt("dve", vnT[:, :, :], qkvT[:, 8 + hp * 2:10 + hp * 2, :], vnT[:, :, :], ALU.subtract)
            P.tt("dve", vnT[:, :, :], vnT[:, :, :], abB[:, 1, hp * 2:hp * 2 + 2, :], ALU.mult)
            bank = nb()
            for hh in range(2):
                P.transpose(ps[0:NS, bank, hh * 128:(hh + 1) * 128], vnT[:, hh, :], identF[:, :])
            P.copy("act", vnt[:, :, :], ps[0:NS, bank, 0:256].rr("p (a b) -> p a b", a=2))
            yield
            for hh in range(2):
                h = hp * 2 + hh
                P.tt("dve", vbd[:, :, :], bcv(vnt[:, hh, :], 1, [NS, NS, 128]), bcv(eye16, 2, [NS, NS, 128]), ALU.mult)
                for q4 in range(4):
                    bk = nb()
                    P.mm(ps[:, bk, :], qkvn[:, 4 + h, :], vbd[:, q4 * 4:(q4 + 1) * 4, :].rr("p a b -> p (a b)"), start=True, stop=True)
                    st_ = stmp[q4 % 2]
                    sv = Ssm[:, hh, q4 * 4:(q4 + 1) * 4, :]
                    P.tt("dve", st_[:, :, :], sv, bcv(abB[:, 0, h, q4 * 4:(q4 + 1) * 4], 2, [128, 4, 128]), ALU.mult)
                    P.tt("dve", sv, st_[:, :, :], ps[:, bk, :].rr("p (a b) -> p a b", a=4), ALU.add)
                yield
            for hh in range(2):
                h = hp * 2 + hh
                P.dma("sp", o_delta_s[:, h].rearrange("s d e -> d s e"), Ssm[:, hh, :, :], sem="slds%d" % hh)
            bank = nb()
            for hh in range(2):
                h = hp * 2 + hh
                for s in range(NS):
                    P.mm(ps[:, bank, hh * NS + s:hh * NS + s + 1], Ssm[:, hh, s, :], qkvT[:, h, s:s + 1], start=True, stop=True)
            P.copy("act", oS[:, hp * 2:hp * 2 + 2, :], ps[:, bank, 0:2 * NS].rr("p (a b) -> p a b", a=2))
            yield
        osq_ = AS.alloc("osqS", [128, 4, NS], F32)
        P.act(osq_[:, :, :], oS[:, :, :], AF.Square)
        bank = nb()
        P.mm(ps[:, bank, 0:4 * NS], onesF[:, :], osq_[:, :, :].rr("p a b -> p (a b)"), start=True, stop=True)
        orr_ = AS.alloc("orrS", [128, 4, NS], F32)
        P.act(orr_[:, :, :], ps[:, bank, 0:4 * NS].rr("p (a b) -> p a b", a=4), AF.Sqrt, bias=EPSC[:, 0:1], scale=1.0 / 128)
        recip(orr_[:, :, :])
        P.stt("dve", oS[:, :, :], oS[:, :, :], pc(R_DNG), orr_[:, :, :], ALU.mult, ALU.mult)
        P.act(osq_[:, :, :], pF[:, 12:16, :], AF.Tanh, scale=0.5)
        P.stt("dve", osq_[:, :, :], osq_[:, :, :], 1.0, pF[:, 12:16, :], ALU.add, ALU.mult)
        P.stt("dve", bS[:, :, :], oS[:, :, :], 0.5, osq_[:, :, :], ALU.mult, ALU.mult)
        AS.release(m)
        yield
        yield from s_outproj(w_out_even, lambda kc: (aoS if kc < 4 else bS)[:, kc % 4, :])
        AS.release(m_seg)
        yield "B"

        def s_xattn(l):
            m = AS.mark()
            ring()
            selF = AS.alloc("selF", [NS, NS, 128], F32)
            selS = AS.alloc("selS", [NS, NS, 128], BF16)
            mset(selF[:, :, :], 1.0)
            msel(selF[:, :, :], ALU.is_equal, 0.0, [[-1, NS], [0, 128]], 1)
            P.copy("pool", selS[:, :, :], selF[:, :, :])
            rmsnorm_s(R_NX + l * 8)
            qs = AS.alloc("qs", [NS, D], BF16)
            bq2 = pair_banks()
            reserved.update(bq2)
            for g in range(D // WG):
                wt = wload_s(w_xq[l], g * WG, WG)
                for kc in range(8):
                    P.mm(ps[0:NS, bq2[g // 2], (g % 2) * WG:(g % 2 + 1) * WG], hsT[:, kc, :], wt[:, kc, 0:WG],
                         start=(kc == 0), stop=(kc == 7))
                yield
            for hf in range(2):
                P.act(qs[:, hf * 512:(hf + 1) * 512], ps[0:NS, bq2[hf], :], AF.Copy, scale=1.0 / 16.0)
            reserved.difference_update(bq2)
            oxS = AS.alloc("oxS", [128, 8, NS], BF16)
            NKB = 3
            kbuf = [AS.alloc("kbuf%d" % i, [128, 2, D], BF16) for i in range(NKB)]
            vbuf = [AS.alloc("vbuf%d" % i, [128, 2, D], BF16) for i in range(NKB)]
            qBs = [AS.alloc("qBs%d" % i, [128, D], BF16) for i in range(2)]
            prd = AS.alloc("prd", [128, 2, D], F32)
            scb = [AS.alloc("sc%d" % i, [128, 2, 4], F32) for i in range(3)]
            scbb = [AS.alloc("scb%d" % i, [128, 2, 4], BF16) for i in range(3)]
            o4m = AS.alloc("o4m", [4, 4, 256], F32)
            o4 = AS.alloc("o4", [4, 256], F32)
            rden = AS.alloc("rdenS", [4, 1], F32)

            def stage_load(s):
                P.dma("pool", kbuf[s % NKB][:, :, :], c_k[l, s].rearrange("(c p) n -> p c n", p=128), sem="ck%d" % (s % NKB))
                P.dma("pool", vbuf[s % NKB][:, :, :], c_v[l, s].rearrange("(c p) n -> p c n", p=128), sem="cv%d" % (s % NKB))

            def stage_a(s):
                Kc = kbuf[s % NKB]
                sc = scb[s % 3]
                qb = qBs[s % 2]
                bq = pair_banks()
                for hf in range(2):
                    P.mm(ps[:, bq[hf], :], selS[:, s, :], qs[:, hf * 512:(hf + 1) * 512], start=True, stop=True)
                P.copy("act", qb[:, :].rr("p (a b) -> p a b", a=2), ps[:, bq[0]:bq[0] + 2, :])
                P.tt("dve", prd[:, :, :], Kc[:, :, :], bcv(qb[:, :], 1, [128, 2, D]), ALU.mult)
                reduce_x(sc[:, :, :], prd[:, :, :].rr("p a (h d) -> p a h d", h=4))
                P.act(scbb[s % 3][:, :, :], sc[:, :, :], AF.Exp)

            def stage_b(s):
                Vc = vbuf[s % NKB]
                sc = scbb[s % 3]
                bo = pair_banks()
                for hf in range(2):
                    for mc in range(2):
                        P.mm(ps[0:4, bo[hf], :], sc[:, mc, :], Vc[:, mc, hf * 512:(hf + 1) * 512], start=(mc == 0), stop=(mc == 1))
                P.tt("dve", o4m[:, :, :], ps[0:4, bo[0]:bo[0] + 2, :].rr("p a (h d) -> p (a h) d", h=2), bcv(eye4, 2, [4, 4, 256]), ALU.mult)
                bd = nb()
                for mc in range(2):
                    P.mm(ps[0:4, bd, 0:1], sc[:, mc, :], onesB[:, 0:1], start=(mc == 0), stop=(mc == 1))
                P.add("dve", lambda e, bd=bd: e.reciprocal(rden[:, :].ap, ps[0:4, bd, 0:1].ap), reads=[ps[0:4, bd, 0:1]], writes=[rden[:, :]])
                reduce_x(o4[:, :], o4m[:, :, :].rr("p h d -> p d h"))
                P.ts("dve", o4[:, :], o4[:, :], rden[:, 0:1], None, ALU.mult)
                bt = nb()
                for hf in range(2):
                    P.transpose(ps[:, bt, hf * 4:(hf + 1) * 4], o4[:, hf * 128:(hf + 1) * 128], eye4)
                P.copy("act", oxS[:, :, s].rr("p (h f) -> p f h", f=2), ps[:, bt, 0:8].rr("p (f h) -> p f h", f=2))

            stage_load(0)
            stage_load(1)
            stage_a(0)
            yield
            for s in range(NS):
                if s + 1 < NS:
                    stage_a(s + 1)
                stage_b(s)
                if s + 2 < NS:
                    stage_load(s + 2)
                yield
            yield from s_outproj(w_xo[l], lambda kc: oxS[:, kc, :])
            AS.release(m)

        yield from s_xattn(0)
        yield "B"

        m_odd = AS.mark()
        ring()
        rmsnorm_s(R_NM + 8)
        uS = AS.alloc("uS", [NS, D], F32)
        gF = AS.alloc("gF", [128, 8, NS], F32)
        bu2 = pair_banks()
        bgf = nb()
        reserved.update(bu2 + [bgf])
        for g in range(8):
            wt = wload_s(w_in_odd, g * WG, WG)
            if g < 4:
                for kc in range(8):
                    P.mm(ps[0:NS, bu2[g // 2], (g % 2) * WG:(g % 2 + 1) * WG], hsT[:, kc, :], wt[:, kc, 0:WG],
                         start=(kc == 0), stop=(kc == 7))
            else:
                for j in range(2):
                    n = (g - 4) * 2 + j
                    for kc in range(8):
                        P.mm(ps[:, bgf, n * NS:(n + 1) * NS], wt[:, kc, j * 128:(j + 1) * 128], hsT[:, kc, :],
                             start=(kc == 0), stop=(kc == 7))
            yield
        for hf in range(2):
            P.copy("act", uS[:, hf * 512:(hf + 1) * 512], ps[0:NS, bu2[hf], :])
        P.copy("dve", gF[:, :, :], ps[:, bgf, 0:8 * NS].rr("p (a b) -> p a b", a=8))
        reserved.difference_update(bu2 + [bgf])
        P.dma("sp", o_pool_s[:, 0:14, :], st_pool[:, 1:15, :], sem="sst0")
        P.dma("sp", o_pool_s[:, 14, :], uS[:, :], sem="sst1")
        plt = AS.alloc("plt", [NS, D], F32)
        bsum = AS.alloc("bsum", [NS, 256], F32)
        stp = AS.alloc("stp", [NS, 15, 256], F32)
        for gi in range(4):
            win = 2 << gi
            cs = slice(gi * 256, (gi + 1) * 256)
            P.dma("sp", stp[:, :, :], st_pool[:, :, cs], sem="sld0")
            reduce_x(bsum[:, :], stp[:, 16 - win:15, :].rr("p k c -> p c k"))
            P.tt("dve", bsum[:, :], bsum[:, :], uS[:, cs], ALU.add)
            P.stt("dve", plt[:, cs], bsum[:, :], 1.0 / win, uS[:, cs], ALU.mult, ALU.subtract)
            yield
        plS = AS.alloc("plS", [128, 8, NS], BF16)
        bank = nb()
        for c in range(8):
            P.transpose(ps[:, bank, c * NS:(c + 1) * NS], plt[:, c * 128:(c + 1) * 128], eye16)
        P.copy("dve", plS[:, :, :], ps[:, bank, 0:8 * NS].rr("p (a b) -> p a b", a=8))
        wps = AS.alloc("wps", [128, 4, 2, 256], BF16)
        P.dma("pool", wps[:, :, :, :], w_pool.rearrange("g (c p) d -> p g c d", p=128), sem="swp")
        zS = AS.alloc("zS", [128, 8, NS], BF16)
        zf = AS.alloc("zf", [128, 8, NS], F32)
        tgs = AS.alloc("tgs", [128, 8, NS], F32)
        bank = nb()
        for n in range(8):
            gi, half = n // 2, n % 2
            for cc in range(2):
                P.mm(ps[:, bank, n * NS:(n + 1) * NS], wps[:, gi, cc, half * 128:(half + 1) * 128], plS[:, gi * 2 + cc, :],
                     start=(cc == 0), stop=(cc == 1))
        for n in range(8):
            P.ts("dve", zf[:, n, :], ps[:, bank, n * NS:(n + 1) * NS], pc(R_BP + n), pc(R_PSC + n), ALU.add, ALU.mult)
        P.act(tgs[:, :, :], gF[:, :, :], AF.Tanh, scale=0.5)
        P.stt("dve", tgs[:, :, :], tgs[:, :, :], 1.0, gF[:, :, :], ALU.add, ALU.mult)
        P.stt("dve", zS[:, :, :], tgs[:, :, :], 0.5, zf[:, :, :], ALU.mult, ALU.mult)
        yield
        yield from s_outproj(w_out_odd, lambda kc: zS[:, kc, :])
        AS.release(m_odd)
        yield "B"
        yield from s_xattn(1)

        m = AS.mark()
        yT = AS.alloc("yTs", [128, 8, NS], F32)
        rmsnorm_s(R_NF, f32out=yT)
        yst_ = AS.alloc("ysts", [NS, D], F32)
        for g in range(2):
            bank = nb()
            for j in range(4):
                P.transpose(ps[0:NS, bank, j * 128:(j + 1) * 128], yT[:, g * 4 + j, :], identF[:, :])
            P.copy("act", yst_[:, g * 512:(g + 1) * 512], ps[0:NS, bank, :])
        P.dma("sp", o_ys, yst_[:, :], sem="sst2")
        AS.release(m)
        yield "B"

    sgen = sample_path()
    sdone = [False]

    def pump(n=1, to_boundary=False):
        if sdone[0] or (win_done[0] and not to_boundary):
            return
        dom[0] = "s"
        try:
            k = 0
            while True:
                r = next(sgen)
                k += 1
                if r == "B":
                    win_done[0] = True
                    break
                if not to_boundary and k >= n:
                    break
        except StopIteration:
            sdone[0] = True
        dom[0] = "p"

    win_done = [False]
    PUMP = int(_os.environ.get("K_PUMP", "1"))

    def win_begin():
        win_done[0] = False
        A.limit = AS_BASE
        allowed["p"] = {0, 1, 2, 3}

    def win_end():
        if not win_done[0]:
            pump(to_boundary=True)
        A.limit = A.top
        allowed["p"] = set(range(8))

    mem_prep()
    load_x()
    for l in range(DEPTH):
        mem_kv(l)
        for mt in range(NMT):
            if l % 2 == 0:
                even_mixer(mt)
            else:
                odd_mixer(mt)
        if STAGE <= 4:
            break
        for mt in range(NMT):
            xattn(l, mt)
        if STAGE <= 5:
            break
    if STAGE >= 7:
        win_begin()
        final_out()
        win_end()
    while not sdone[0]:
        win_done[0] = False
        pump(to_boundary=True)
    for name, t in dbg.items():
        od = nc.dram_tensor("dbg_" + name, t.shape, F32, kind="ExternalOutput").ap()
        m = A.mark()
        st = A.alloc("dbgst", t.shape, F32)
        P.copy("dve", st[tuple(slice(None) for _ in t.shape)], t[tuple(slice(None) for _ in t.shape)])
        P.dma("sp", od, st[tuple(slice(None) for _ in t.shape)], sem="dbg")
        A.release(m)
    print("SBUF peak bytes/partition:", A.peak, "ops:", len(P.ops))
    P.emit()
    return nc


_REPL = ["w_xk", "w_xv", "w_xq", "w_xo", "norm_mix", "norm_xattn"]
_SQ0 = ["w_in_even", "w_out_even", "w_in_odd", "w_out_odd", "w_pool", "dw_w", "sc_w", "b_pool"]
_ROW = ["norm_final", "dw_b", "ln_a_g", "ln_a_b", "a_log", "dt_bias", "dn_norm_g", "pool_scale"]


def kernel(**inputs):
    nc = build_program()
    f = lambda a: np.ascontiguousarray(a, dtype=np.float32)
    shared = {}
    for k in _REPL:
        shared[k] = f(inputs[k])
    for k in _SQ0:
        shared[k] = f(inputs[k][0])
    for k in _ROW:
        shared[k] = f(np.asarray(inputs[k]).reshape(1, -1))
    in_maps = []
    for c in range(NCORES):
        m = dict(shared)
        m["x_p"] = f(inputs["x_prompt"][c])
        m["mem_p"] = f(inputs["mem_prompt"][c])
        sl = slice(c * NS, (c + 1) * NS)
        m["x_s"] = f(inputs["x_sample"][sl, 0])
        m["st_conv"] = f(inputs["state_conv_a"][0, sl])
        m["st_qkv"] = f(inputs["state_qkv_conv"][0, sl])
        m["st_delta"] = f(inputs["state_delta"][0, sl])
        m["st_pool"] = f(inputs["state_pool"][0, sl])
        m["c_k"] = f(inputs["cache_mem_k"][:, sl]).reshape(DEPTH, NS, NMEM, D)
        m["c_v"] = f(inputs["cache_mem_v"][:, sl]).reshape(DEPTH, NS, NMEM, D)
        in_maps.append(m)
    res = run_bass_kernel_spmd(nc, in_maps, core_ids=list(range(NCORES)))
    R = res.results
    B = NCORES
    out = {}
    out["y_prompt"] = np.stack([R[c]["o_y"] for c in range(B)], axis=0)
    out["new_conv_a_p"] = np.stack([R[c]["o_conv"] for c in range(B)], axis=0)[None]
    out["new_qkv_conv_p"] = np.stack([R[c]["o_qkv"] for c in range(B)], axis=0)[None]
    out["new_delta_p"] = np.stack([R[c]["o_delta"] for c in range(B)], axis=0)[None]
    out["new_pool_p"] = np.stack([R[c]["o_pool"] for c in range(B)], axis=0)[None]
    out["new_mem_k_p"] = np.stack([R[c]["o_mem_k"] for c in range(B)], axis=1).reshape(DEPTH, B, NMEM, 4, 256)
    out["new_mem_v_p"] = np.stack([R[c]["o_mem_v"] for c in range(B)], axis=1).reshape(DEPTH, B, NMEM, 4, 256)
    if STAGE >= 8:
        out["y_sample"] = np.concatenate([R[c]["o_ys"] for c in range(B)], axis=0)[:, None, :]
        out["new_conv_a_s"] = np.concatenate([R[c]["o_conv_s"] for c in range(B)], axis=0)[None]
        out["new_qkv_conv_s"] = np.concatenate([R[c]["o_qkv_s"] for c in range(B)], axis=0)[None]
        out["new_delta_s"] = np.concatenate([R[c]["o_delta_s"] for c in range(B)], axis=0)[None]
        out["new_pool_s"] = np.concatenate([R[c]["o_pool_s"] for c in range(B)], axis=0)[None]
    if _os.environ.get("K_DBG"):
        for k in R[0]:
            if k.startswith("dbg_"):
                out[k] = R[0][k]
        return out
    NSM = NS * NCORES
    for k, shp in (("y_sample", (NSM, 1, D)), ("new_conv_a_s", (1, NSM, 30, 512)), ("new_qkv_conv_s", (1, NSM, 3, 1536)),
                   ("new_delta_s", (1, NSM, 4, 128, 128)), ("new_pool_s", (1, NSM, 15, D))):
        if k not in out:
            out[k] = np.zeros(shp, np.float32)
    order = ["y_prompt", "y_sample", "new_conv_a_p", "new_qkv_conv_p", "new_delta_p", "new_pool_p", "new_mem_k_p",
             "new_mem_v_p", "new_conv_a_s", "new_qkv_conv_s", "new_delta_s", "new_pool_s"]
    return tuple(out[k] for k in order)
```

```python
import numpy as np
import concourse.bass as bass
import concourse.mybir as mybir
from concourse.bass_utils import run_bass_kernel_spmd

F32 = mybir.dt.float32
BF16 = mybir.dt.bfloat16
ALU = mybir.AluOpType
AF = mybir.ActivationFunctionType
AX = mybir.AxisListType

NCORES = 8
D = 1024
T = 2048
NMEM = 256
DEPTH = 2
NS = 16
EPS = 1e-6

COMPUTE = ("pe", "act", "dve", "pool")


class V:
    def __init__(self, ap, tname, rect, excl=False):
        self.ap = ap
        self.t = tname
        self.rect = rect
        self.excl = excl

    def rr(self, pat, **kw):
        return V(self.ap.rearrange(pat, **kw), self.t, self.rect, self.excl)

    def bc(self, shape):
        return V(self.ap.to_broadcast(list(shape)), self.t, self.rect, self.excl)

    def sub(self, *idx):
        return V(self.ap[idx], self.t, self.rect, self.excl)


_uid = [0]


class Tn:
    def __init__(self, nc, name, shape, dtype, psum=False, offset=None):
        _uid[0] += 1
        self.name = "%s_%d" % (name, _uid[0])
        self.shape = list(shape)
        self.dtype = dtype
        self.psum = psum
        self.esz = 4 if dtype == F32 else 2
        if psum:
            self.h = nc.alloc_psum_tensor(self.name, self.shape, dtype)
            self.off = 0
        elif offset is None:
            self.h = nc.alloc_sbuf_tensor(self.name, self.shape, dtype)
            self.off = None
        else:
            self.h = nc.alloc_sbuf_tensor_at(self.name, self.shape, dtype, offset=offset)
            self.off = offset
        st = [1] * len(shape)
        for i in range(len(shape) - 2, 0, -1):
            st[i] = st[i + 1] * shape[i + 1]
        self.st = st

    def __getitem__(self, idx):
        if not isinstance(idx, tuple):
            idx = (idx,)
        idx = tuple(idx) + (slice(None),) * (len(self.shape) - len(idx))
        lo = 0
        hi = 0
        p0, p1 = 0, self.shape[0]
        for d, (ix, n) in enumerate(zip(idx, self.shape)):
            if isinstance(ix, slice):
                a = 0 if ix.start is None else ix.start
                b = n if ix.stop is None else ix.stop
                assert ix.step in (None, 1)
            else:
                a, b = ix, ix + 1
            assert 0 <= a < b <= n, (self.name, idx)
            if d == 0:
                p0, p1 = a, b
            else:
                lo += a * self.st[d]
                hi += (b - 1) * self.st[d]
        hi += 1
        lo *= self.esz
        hi *= self.esz
        if self.psum:
            lo = (lo // 2048) * 2048
            hi = ((hi + 2047) // 2048) * 2048
            p0, p1 = 0, 128
            return V(self.h[idx], "ps", (p0, p1, lo, hi), True)
        if self.off is None:
            return V(self.h[idx], self.name, (p0, p1, lo, hi), False)
        return V(self.h[idx], "sb", (p0, p1, self.off + lo, self.off + hi), False)


def _overlap(a, b):
    return a[0] < b[1] and b[0] < a[1] and a[2] < b[3] and b[2] < a[3]


def _contains(outer, inner):
    return outer[0] <= inner[0] and inner[1] <= outer[1] and outer[2] <= inner[2] and inner[3] <= outer[3]


class Op:
    __slots__ = ("eng", "fn", "reads", "writes", "dma", "seq", "waits", "flag", "semkey", "semcnt")


class Prog:
    def __init__(self, nc):
        self.nc = nc
        self.ops = []
        self.acc = {}
        self.nseq = {e: 0 for e in ("pe", "act", "dve", "pool", "sp")}
        self.dmacnt = {}
        self.lastdma = {}
        self.by_eng = {e: [] for e in ("pe", "act", "dve", "pool", "sp")}
        self.flagged = {e: set() for e in COMPUTE}
        self.waited = {e: {} for e in ("pe", "act", "dve", "pool", "sp")}

    def _deps(self, op):
        deps = set()
        for v in op.reads:
            for (rect, kind, dep) in self.acc.get(v.t, ()):
                if (kind == "w" or v.excl) and _overlap(rect, v.rect):
                    deps.add(dep)
        for v in op.writes:
            for (rect, kind, dep) in self.acc.get(v.t, ()):
                if _overlap(rect, v.rect):
                    deps.add(dep)
        return deps

    def add(self, eng, fn, reads=(), writes=(), dma=None):
        op = Op()
        op.eng = eng
        op.fn = fn
        op.reads = [r for r in reads if r is not None]
        op.writes = [w for w in writes if w is not None]
        op.dma = dma
        op.flag = False
        deps = self._deps(op)
        self.nseq[eng] += 1
        op.seq = self.nseq[eng]
        if dma is not None:
            prev = self.lastdma.get(dma)
            if prev is not None:
                deps.add(prev)
            self.dmacnt[dma] = self.dmacnt.get(dma, 0) + 16
            op.semkey = dma
            op.semcnt = self.dmacnt[dma]
            me = ("dma", dma, op.semcnt)
            self.lastdma[dma] = me
        else:
            me = (eng, op.seq)
        need = {}
        for d in deps:
            if d[0] == "dma":
                key = ("dma", d[1])
                val = d[2]
            else:
                if d[0] == eng and dma is None:
                    pass
                key = d[0]
                val = d[1]
            if need.get(key, 0) < val:
                need[key] = val
        if dma is None and eng in need and eng == "pe":
            raw = 0
            for v in op.reads:
                for (rect, kind, dep) in self.acc.get(v.t, ()):
                    if kind == "w" and dep[0] == eng and _overlap(rect, v.rect):
                        raw = max(raw, dep[1])
            if raw:
                need[eng] = raw
            else:
                del need[eng]
        waits = []
        wd = self.waited[eng]
        for key, val in need.items():
            if wd.get(key, 0) >= val:
                continue
            wd[key] = val
            waits.append((key, val))
            if not (isinstance(key, tuple)):
                self.flagged[key].add(val)
        op.waits = waits
        for v in op.writes:
            lst = self.acc.setdefault(v.t, [])
            lst[:] = [a for a in lst if not _contains(v.rect, a[0])]
            lst.append((v.rect, "w", me))
        for v in op.reads:
            lst = self.acc.setdefault(v.t, [])
            if v.excl:
                lst[:] = [a for a in lst if not _contains(v.rect, a[0])]
                lst.append((v.rect, "x", me))
                continue
            if dma is None:
                lst[:] = [a for a in lst if not (a[1] == "r" and a[2][0] == eng and _contains(v.rect, a[0]))]
            lst.append((v.rect, "r", me))
        self.ops.append(op)
        self.by_eng[eng].append(op)
        return op

    def mm(self, out, lhsT, rhs, start=True, stop=True):
        self.add("pe", lambda e: e.matmul(out.ap, lhsT.ap, rhs.ap, start=start, stop=stop),
                 reads=[lhsT, rhs], writes=[out])

    def transpose(self, out, in_, ident):
        self.add("pe", lambda e: e.transpose(out.ap, in_.ap, ident.ap), reads=[in_, ident], writes=[out])

    def act(self, out, in_, func, bias=None, scale=1.0, accum=None, eng="act"):
        rd = [in_]
        kw = {}
        if isinstance(bias, V):
            rd.append(bias)
            kw["bias"] = bias.ap
        elif bias is not None:
            kw["bias"] = bias
        if isinstance(scale, V):
            rd.append(scale)
            kw["scale"] = scale.ap
        else:
            kw["scale"] = scale
        wr = [out]
        if accum is not None:
            wr.append(accum)
            kw["accum_out"] = accum.ap
        self.add("act", lambda e: e.activation(out.ap, in_.ap, func, **kw), reads=rd, writes=wr)

    def copy(self, eng, out, in_):
        if eng == "act":
            self.add("act", lambda e: e.copy(out.ap, in_.ap), reads=[in_], writes=[out])
        else:
            self.add(eng, lambda e: e.tensor_copy(out.ap, in_.ap), reads=[in_], writes=[out])

    def tt(self, eng, out, a, b, op):
        self.add(eng, lambda e: e.tensor_tensor(out.ap, a.ap, b.ap, op), reads=[a, b], writes=[out])

    def ts(self, eng, out, a, s1, s2, op0, op1=None, accum=None):
        rd = [a]
        s1a = s1.ap if isinstance(s1, V) else s1
        s2a = s2.ap if isinstance(s2, V) else s2
        if isinstance(s1, V):
            rd.append(s1)
        if isinstance(s2, V):
            rd.append(s2)
        wr = [out]
        kw = {}
        if accum is not None:
            wr.append(accum)
            kw["accum_out"] = accum.ap
        if op1 is None:
            self.add(eng, lambda e: e.tensor_scalar(out.ap, a.ap, s1a, None, op0, **kw), reads=rd, writes=wr)
        else:
            self.add(eng, lambda e: e.tensor_scalar(out.ap, a.ap, s1a, s2a, op0, op1, **kw), reads=rd, writes=wr)

    def stt(self, eng, out, a, s, b, op0, op1):
        rd = [a, b]
        sa = s.ap if isinstance(s, V) else s
        if isinstance(s, V):
            rd.append(s)
        self.add(eng, lambda e: e.scalar_tensor_tensor(out.ap, a.ap, sa, b.ap, op0, op1), reads=rd, writes=[out])

    def dma(self, q, out, in_, sem, reads=(), writes=()):
        o = out.ap if isinstance(out, V) else out
        i = in_.ap if isinstance(in_, V) else in_
        rd = list(reads) + ([in_] if isinstance(in_, V) else [])
        wr = list(writes) + ([out] if isinstance(out, V) else [])
        self.add(q, lambda e: e.dma_start(out=o, in_=i), reads=rd, writes=wr, dma=sem)

    def emit(self):
        nc = self.nc
        sems = {e: nc.alloc_semaphore("s_" + e) for e in COMPUTE}
        dsems = {k: nc.alloc_semaphore("d_" + str(k)) for k in self.dmacnt}
        rank = {}
        for e in COMPUTE:
            fl = sorted(self.flagged[e])
            rank[e] = {s: i + 1 for i, s in enumerate(fl)}
        engobj = {"pe": "tensor", "act": "scalar", "dve": "vector", "pool": "gpsimd", "sp": "sync"}

        def run(ename, eng):
            for op in self.by_eng[ename]:
                for key, val in op.waits:
                    if isinstance(key, tuple):
                        eng.wait_ge(dsems[key[1]], val)
                    else:
                        eng.wait_ge(sems[key], rank[key][val])
                ins = op.fn(eng)
                if op.dma is not None:
                    ins.then_inc(dsems[op.semkey], 16)
                elif op.seq in rank[ename]:
                    ins.then_inc(sems[ename], 1)
            if ename == "sp":
                for k, cnt in self.dmacnt.items():
                    eng.wait_ge(dsems[k], cnt)

        with nc.Block() as block:
            block.tensor(lambda e: run("pe", e))
            block.scalar(lambda e: run("act", e))
            block.vector(lambda e: run("dve", e))
            block.gpsimd(lambda e: run("pool", e))
            block.sync(lambda e: run("sp", e))


import os as _os


class Arena:
    def __init__(self, nc):
        self.nc = nc
        self.cur = 16512
        self.top = 229344
        self.limit = 229344
        self.peak = 0

    def alloc(self, name, shape, dtype):
        esz = 4 if dtype == F32 else 2
        n = esz
        for s in shape[1:]:
            n *= s
        off = self.cur
        self.cur = (off + n + 31) // 32 * 32
        self.peak = max(self.peak, self.cur)
        assert self.cur <= min(self.top, self.limit), ("SBUF overflow", name, self.cur, self.limit)
        return Tn(self.nc, name, shape, dtype, offset=off)

    def mark(self):
        return self.cur

    def release(self, m):
        self.cur = m


TT = int(_os.environ.get('K_TT', '1024'))
NMT = T // TT
NTT = TT // 512
NBLK = TT // 128
WG = 256
NEG = -32768.0
SAMPLE_BYTES = 66560
STAGE = int(_os.environ.get("K_STAGE", "99"))


def build_program():
    nc = bass.Bass("TRN2", target_bir_lowering=False)
    nc.allow_low_precision("bf16 matmul operands with fp32 accumulation (problem tolerance)")
    P = Prog(nc)
    A = Arena(nc)
    AS = Arena(nc)
    AS.cur = AS.top - SAMPLE_BYTES
    AS_BASE = AS.cur

    def din(name, shape):
        return nc.dram_tensor(name, list(shape), F32, kind="ExternalInput").ap()

    def dout(name, shape):
        return nc.dram_tensor(name, list(shape), F32, kind="ExternalOutput").ap()

    x_p = din("x_p", [T, D])
    mem_p = din("mem_p", [NMEM, D])
    w_xk = din("w_xk", [DEPTH, D, D])
    w_xv = din("w_xv", [DEPTH, D, D])
    w_xq = din("w_xq", [DEPTH, D, D])
    w_xo = din("w_xo", [DEPTH, D, D])
    w_in_even = din("w_in_even", [D, 3592])
    w_out_even = din("w_out_even", [D, D])
    w_in_odd = din("w_in_odd", [D, 2048])
    w_out_odd = din("w_out_odd", [D, D])
    w_pool = din("w_pool", [4, 256, 256])
    norm_mix = din("norm_mix", [2, D])
    norm_xattn = din("norm_xattn", [2, D])
    norm_final = din("norm_final", [1, D])
    dw_w = din("dw_w", [31, 512])
    dw_b = din("dw_b", [1, 512])
    ln_a_g = din("ln_a_g", [1, 512])
    ln_a_b = din("ln_a_b", [1, 512])
    sc_w = din("sc_w", [4, 1536])
    a_log = din("a_log", [1, 4])
    dt_bias = din("dt_bias", [1, 4])
    dn_norm_g = din("dn_norm_g", [1, 128])
    pool_scale = din("pool_scale", [1, D])
    b_pool = din("b_pool", [4, 256])

    o_y = dout("o_y", [T, D])
    o_conv = dout("o_conv", [30, 512])
    o_qkv = dout("o_qkv", [3, 1536])
    o_delta = dout("o_delta", [4, 128, 128])
    o_pool = dout("o_pool", [15, D])
    o_mem_k = dout("o_mem_k", [DEPTH, NMEM, D])
    o_mem_v = dout("o_mem_v", [DEPTH, NMEM, D])
    dbg = {}

    ps = Tn(nc, "ps", [128, 8, 512], F32, psum=True)
    bankc = [0]

    reserved = set()

    dom = ["p"]
    allowed = {"p": set(range(8)), "s": {4, 5, 6, 7}}
    bankcs = {"p": bankc, "s": [3]}

    def nb():
        bc_ = bankcs[dom[0]]
        while True:
            bc_[0] = (bc_[0] + 1) % 8
            if bc_[0] in allowed[dom[0]] and bc_[0] not in reserved:
                return bc_[0]

    evc = [0]

    def ev():
        evc[0] += 1
        return "dve" if evc[0] % 2 else "act"

    identF = A.alloc("identF", [128, 128], F32)
    identB = A.alloc("identB", [128, 128], BF16)
    onesF = A.alloc("onesF", [128, 128], F32)
    onesB = A.alloc("onesB", [128, 128], BF16)
    Mincl = A.alloc("Mincl", [128, 128], F32)
    Msame = A.alloc("Msame", [128, 128], F32)
    Mgt = A.alloc("Mgt", [128, 128], F32)
    NEGs = A.alloc("NEGs", [128, 128], F32)
    NEGiT = A.alloc("NEGiT", [128, 128], F32)
    sel = A.alloc("sel", [4, 4, 128], F32)
    NEGsB = A.alloc("NEGsB", [128, 128], BF16)
    NEGiTB = A.alloc("NEGiTB", [128, 128], BF16)
    pcol0 = A.alloc("pcol0", [128, 128], F32)
    pcol1 = A.alloc("pcol1", [128, 128], F32)
    pch = A.alloc("pch", [128, 160], F32)
    dtB = A.alloc("dtB", [128, NBLK, 4], F32)
    nAB = A.alloc("nAB", [128, NBLK, 4], F32)
    invc = A.alloc("invc", [128, 16], F32)

    def msel(t, cmp, fill, pattern, cm, base=0):
        P.add("pool", lambda e: e.affine_select(out=t.ap, in_=t.ap, compare_op=cmp, fill=fill, base=base,
                                                pattern=pattern, channel_multiplier=cm), reads=[t], writes=[t])

    def mset(t, val):
        P.add("pool", lambda e: e.memset(t.ap, val), writes=[t])

    mset(identF[:, :], 1.0)
    msel(identF[:, :], ALU.is_equal, 0.0, [[-1, 128]], 1)
    P.copy("pool", identB[:, :], identF[:, :])
    mset(onesF[:, :], 1.0)
    mset(onesB[:, :], 1.0)
    mset(Mincl[:, :], 1.0)
    msel(Mincl[:, :], ALU.is_ge, 0.0, [[1, 128]], -1)
    mset(Mincl[0:64, 64:128], 0.0)
    mset(Msame[:, :], 1.0)
    mset(Msame[0:64, 64:128], 0.0)
    mset(Msame[64:128, 0:64], 0.0)
    mset(Mgt[:, :], 1.0)
    msel(Mgt[:, :], ALU.is_gt, 0.0, [[-1, 128]], 1)
    mset(Mgt[64:128, 0:64], 0.0)
    mset(NEGs[:, :], 0.0)
    msel(NEGs[:, :], ALU.is_gt, NEG, [[-1, 128]], 1)
    mset(NEGs[64:128, 0:64], NEG)
    mset(NEGiT[:, :], 0.0)
    msel(NEGiT[:, :], ALU.is_ge, NEG, [[1, 128]], -1)
    mset(NEGiT[0:64, 64:128], NEG)
    P.copy("pool", NEGsB[:, :], NEGs[:, :])
    P.copy("pool", NEGiTB[:, :], NEGiT[:, :])
    mset(sel[:, :, :], 1.0)
    msel(sel[:, :, :], ALU.is_equal, 0.0, [[-1, 4], [0, 128]], 1)
    ind0 = Msame[:, 0:1]
    ind1 = Msame[:, 64:65]
    _bk = nb()
    P.mm(ps[:, _bk, 0:16], onesF[:, :], Mincl[:, 0:16], start=True, stop=True)
    P.add("dve", lambda e: e.reciprocal(invc[:, :].ap, ps[:, _bk, 0:16].ap), reads=[ps[:, _bk, 0:16]], writes=[invc[:, :]])

    m0 = A.mark()
    pst0 = A.alloc("pst0", [128, 128], F32)
    pst1 = A.alloc("pst1", [128, 128], F32)
    mset(pst0[:, :], 0.0)
    mset(pst1[:, :], 0.0)
    R_NM, R_NX, R_NF, R_DWB, R_LNG, R_LNB, R_SC, R_DNG, R_PSC, R_BP = 0, 16, 32, 40, 44, 48, 52, 100, 101, 109
    prm_loads = [
        (pst0, R_NM, 16, norm_mix.rearrange("l (c p) -> (l c) p", p=128)),
        (pst0, R_NX, 16, norm_xattn.rearrange("l (c p) -> (l c) p", p=128)),
        (pst0, R_NF, 8, norm_final.rearrange("l (c p) -> (l c) p", p=128)),
        (pst0, R_DWB, 4, dw_b.rearrange("l (c p) -> (l c) p", p=128)),
        (pst0, R_LNG, 4, ln_a_g.rearrange("l (c p) -> (l c) p", p=128)),
        (pst0, R_LNB, 4, ln_a_b.rearrange("l (c p) -> (l c) p", p=128)),
        (pst0, R_SC, 48, sc_w.rearrange("k (c p) -> (k c) p", p=128)),
        (pst0, R_DNG, 1, dn_norm_g),
        (pst0, R_PSC, 8, pool_scale.rearrange("l (c p) -> (l c) p", p=128)),
        (pst0, R_BP, 8, b_pool.rearrange("g (c p) -> (g c) p", p=128)),
        (pst1, 0, 124, dw_w.rearrange("k (c p) -> (k c) p", p=128)),
    ]
    for i, (dst, r0, n, src) in enumerate(prm_loads):
        P.dma("sp", dst[r0:r0 + n, :], src, sem="prm%d" % (i % 4))
    bk = nb()
    P.transpose(ps[:, bk, 0:128], pst0[:, :], identF[:, :])
    P.transpose(ps[:, bk, 128:256], pst1[:, :], identF[:, :])
    P.copy("dve", pcol0[:, :], ps[:, bk, 0:128])
    P.copy("dve", pcol1[:, :], ps[:, bk, 128:256])
    P.ts("dve", pch[:, 0:124], pcol1[:, 0:124], 0.5, None, ALU.mult)
    P.ts("dve", pch[:, 124:128], pcol0[:, R_LNG:R_LNG + 4], 0.5, None, ALU.mult)
    P.ts("dve", pch[:, 128:132], pcol0[:, R_LNB:R_LNB + 4], 0.5, None, ALU.mult)
    for b in range(NBLK):
        P.dma("sp", dtB[:, b, :], dt_bias[0].partition_broadcast(128), sem="prm%d" % (b % 4))
        P.dma("sp", nAB[:, b, :], a_log[0].partition_broadcast(128), sem="prm%d" % ((b + 1) % 4))
    P.act(nAB[:, :, :], nAB[:, :, :], AF.Exp)
    P.ts("dve", nAB[:, :, :], nAB[:, :, :], -1.0, None, ALU.mult)
    A.release(m0)

    def pc(r):
        return pcol0[:, r:r + 1]

    xsT = A.alloc("xsT", [128, 8, NS], F32)
    hsT = A.alloc("hsT", [128, 8, NS], BF16)
    xT = A.alloc("xT", [128, 8, T], F32)
    memT = A.alloc("memT", [128, 8, NMEM], BF16)
    KT = A.alloc("KT", [128, 8, NMEM], BF16)
    Vtok = A.alloc("Vtok", [128, 2, D], BF16)
    NW = int(_os.environ.get("K_NW", "3"))
    wbuf = [A.alloc("wbuf%d" % i, [128, 8, WG], BF16) for i in range(NW)]
    wctr = [0]
    Sst = A.alloc("Sst", [128, 4, 128], F32)
    Sbf = A.alloc("Sbf", [128, 4, 128], BF16)
    gluhalo = A.alloc("gluhalo", [128, 4, 32], BF16)
    qkvhalo = A.alloc("qkvhalo", [128, 12, 4], BF16)
    uhalo = A.alloc("uhalo", [128, 8, 16], F32)
    mset(Sst[:, :, :], 0.0)
    mset(Sbf[:, :, :], 0.0)
    mset(gluhalo[:, :, :], 0.0)
    mset(qkvhalo[:, :, :], 0.0)
    mset(uhalo[:, :, :], 0.0)

    def wload(src2d, col0, ncols):
        slot = wctr[0] % NW
        wctr[0] += 1
        wt = wbuf[slot]
        P.dma("pool", wt[:, :, 0:ncols], src2d[:, col0:col0 + ncols].rearrange("(c p) n -> p c n", p=128),
              sem="w%d" % slot)
        return wt

    swring = {"bufs": None, "ctr": 0}

    def wload_s(src2d, col0, ncols):
        slot = swring["ctr"] % 2
        swring["ctr"] += 1
        wt = swring["bufs"][slot]
        P.dma("pool", wt[:, :, 0:ncols], src2d[:, col0:col0 + ncols].rearrange("(c p) n -> p c n", p=128),
              sem="sw%d" % slot)
        return wt

    def mem_prep():
        m = A.mark()
        memtok = A.alloc("memtok", [128, 2, D], F32)
        P.dma("sp", memtok[:, :, :], mem_p.rearrange("(c p) d -> p c d", p=128), sem="ld0")
        for mc in range(2):
            for g in range(2):
                bank = nb()
                for j in range(4):
                    dc = g * 4 + j
                    P.transpose(ps[:, bank, j * 128:(j + 1) * 128], memtok[:, mc, dc * 128:(dc + 1) * 128], identF[:, :])
                P.copy(ev(), memT[:, g * 4:g * 4 + 4, mc * 128:(mc + 1) * 128],
                       ps[:, bank, :].rr("p (a b) -> p a b", a=4))
        A.release(m)

    def mem_kv(l):
        m = A.mark()
        kvst = [A.alloc("kvst%d" % i, [128, D], F32) for i in range(2)]
        for which, wsrc, odst in (("k", w_xk, o_mem_k), ("v", w_xv, o_mem_v)):
            for g in range(D // WG):
                wt = wload(wsrc[l], g * WG, WG)
                for mc in range(2):
                    bank = nb()
                    for dc in range(8):
                        P.mm(ps[:, bank, 0:WG], memT[:, dc, mc * 128:(mc + 1) * 128], wt[:, dc, 0:WG],
                             start=(dc == 0), stop=(dc == 7))
                    P.copy(ev(), kvst[mc][:, g * WG:(g + 1) * WG], ps[:, bank, 0:WG])
                if which == "k":
                    for j in range(WG // 128):
                        nch = g * (WG // 128) + j
                        bank2 = nb()
                        for dc in range(8):
                            P.mm(ps[:, bank2, 0:NMEM], wt[:, dc, j * 128:(j + 1) * 128], memT[:, dc, :],
                                 start=(dc == 0), stop=(dc == 7))
                        P.copy(ev(), KT[:, nch, :], ps[:, bank2, 0:NMEM])
            for mc in range(2):
                if which == "v":
                    P.copy("pool", Vtok[:, mc, :], kvst[mc][:, :])
                P.dma("sp", odst[l, mc * 128:(mc + 1) * 128, :], kvst[mc][:, :], sem="kvst%d" % mc)
        A.release(m)

    def load_x_issue():
        m = A.mark()
        xs = [A.alloc("xs%d" % i, [128, D], F32) for i in range(T // 128)]
        for b in range(T // 128):
            P.dma("sp", xs[b][:, :], x_p[b * 128:(b + 1) * 128, :], sem="ldx%d" % (b % 8))
        return m, xs

    def load_x_finish(state):
        m, xs = state
        for b in range(T // 128):
            st = xs[b]
            for g in range(2):
                bank = nb()
                for j in range(4):
                    kc = g * 4 + j
                    P.transpose(ps[:, bank, j * 128:(j + 1) * 128], st[:, kc * 128:(kc + 1) * 128], identF[:, :])
                P.copy(ev(), xT[:, g * 4:g * 4 + 4, b * 128:(b + 1) * 128], ps[:, bank, :].rr("p (a b) -> p a b", a=4))
        A.release(m)

    def rmsnorm(hT, t0, grow):
        m = A.mark()
        sq = [A.alloc("sq%d" % i, [128, 512], BF16) for i in range(2)]
        rs = A.alloc("rs", [128, 512], F32)
        for tt in range(NTT):
            c0 = t0 + tt * 512
            bank = nb()
            for kc in range(8):
                s = sq[kc % 2]
                P.act(s[:, :], xT[:, kc, c0:c0 + 512], AF.Square)
                P.mm(ps[:, bank, :], onesB[:, :], s[:, :], start=(kc == 0), stop=(kc == 7))
            P.act(rs[:, :], ps[:, bank, :], AF.Ln, bias=EPSC[:, 0:1], scale=1.0 / D)
            P.act(rs[:, :], rs[:, :], AF.Exp, scale=-0.5)
            for kc in range(8):
                P.stt("dve", hT[:, kc, tt * 512:(tt + 1) * 512], xT[:, kc, c0:c0 + 512], pc(grow + kc), rs[:, :],
                      ALU.mult, ALU.mult)
        A.release(m)

    EPSC = A.alloc("EPSC", [128, 4], F32)
    mset(EPSC[:, 0:1], EPS)
    mset(EPSC[:, 1:2], 4.0 * EPS)
    mset(EPSC[:, 2:3], 1.0)

    def proj(wt, wc, hT, banks):
        for kc in range(8):
            for tt in range(NTT):
                P.mm(ps[:, banks[tt], :], wt[:, kc, wc:wc + 128], hT[:, kc, tt * 512:(tt + 1) * 512],
                     start=(kc == 0), stop=(kc == 7))

    def psv(banks):
        assert banks[-1] == banks[0] + len(banks) - 1
        return ps[:, banks[0]:banks[0] + len(banks), :]

    def pair_banks():
        b = nb()
        while b % 2 or (b + 1) in reserved:
            b = nb()
        nb()
        return [b, b + 1]

    def two_banks():
        if NTT == 1:
            return [nb()]
        b = nb()
        while b % 2:
            b = nb()
        nb()
        return [b, b + 1]

    def out_proj_add(wsrc, inT_fn, t0, pumpn=0):
        for g in range(D // WG):
            wt = wload(wsrc, g * WG, WG)
            for j in range(WG // 128):
                n = g * (WG // 128) + j
                banks = two_banks()
                for kc in range(8):
                    for tt in range(NTT):
                        P.mm(ps[:, banks[tt], :], wt[:, kc, j * 128:(j + 1) * 128], inT_fn(kc, tt),
                             start=(kc == 0), stop=(kc == 7))
                P.tt("dve", xT[:, n, t0:t0 + TT].rr("p (a b) -> p a b", a=NTT), xT[:, n, t0:t0 + TT].rr("p (a b) -> p a b", a=NTT),
                     psv(banks), ALU.add)
                if pumpn:
                    pump(pumpn)

    def xattn(l, mt):
        t0 = mt * TT
        m = A.mark()
        win_begin()
        qx = A.alloc("qx", [128, 8, TT], BF16)
        mh_ = A.mark()
        hT = A.alloc("hT", [128, 8, TT], BF16)
        rmsnorm(hT, t0, R_NX + l * 8)
        for g in range(D // WG):
            wt = wload(w_xq[l], g * WG, WG)
            for j in range(WG // 128):
                n = g * (WG // 128) + j
                banks = two_banks()
                proj(wt, j * 128, hT, banks)
                P.act(qx[:, n, :].rr("p (a b) -> p a b", a=NTT), psv(banks), AF.Copy, scale=1.0 / 16.0)
                pump(PUMP)
        A.release(mh_)
        ox = A.alloc("ox", [128, 8, TT], BF16)
        Et = [A.alloc("Et%d" % i, [128, 512], BF16) for i in range(4)]
        rden = [A.alloc("rden%d" % i, [128, 512], F32) for i in range(2)]
        ei = 0
        for h in range(4):
            for tt in range(NTT):
                es = []
                for mc in range(2):
                    bank = nb()
                    for half in range(2):
                        P.mm(ps[:, bank, :], KT[:, h * 2 + half, mc * 128:(mc + 1) * 128],
                             qx[:, h * 2 + half, tt * 512:(tt + 1) * 512], start=(half == 0), stop=(half == 1))
                    E = Et[ei % 4]
                    ei += 1
                    P.act(E[:, :], ps[:, bank, :], AF.Exp)
                    es.append(E)
                pump(PUMP)
                bden = nb()
                for mc in range(2):
                    P.mm(ps[:, bden, :], onesB[:, :], es[mc][:, :], start=(mc == 0), stop=(mc == 1))
                rd = rden[(h * NTT + tt) % 2]
                P.act(rd[:, :], ps[:, bden, :], AF.Ln)
                P.act(rd[:, :], rd[:, :], AF.Exp, scale=-1.0)
                for dv in range(2):
                    bo = nb()
                    n = h * 2 + dv
                    for mc in range(2):
                        P.mm(ps[:, bo, :], Vtok[:, mc, n * 128:(n + 1) * 128], es[mc][:, :],
                             start=(mc == 0), stop=(mc == 1))
                    P.tt("dve", ox[:, n, tt * 512:(tt + 1) * 512], ps[:, bo, :], rd[:, :], ALU.mult)
                pump(PUMP)
        out_proj_add(w_xo[l], lambda kc, tt: ox[:, kc, tt * 512:(tt + 1) * 512], t0, pumpn=PUMP)
        win_end()
        A.release(m)

    def even_mixer(mt):
        t0 = mt * TT
        last = (mt == NMT - 1)
        m = A.mark()
        aoT = A.alloc("aoT", [128, 4, TT], BF16)
        G_BETA, G_G, G_GCS, G_GCL, G_EGC, G_EKD, G_NBEG, G_VBS, G_NBETA, G_TMP = range(10)
        mh = A.mark()
        hT = A.alloc("hT", [128, 8, TT], BF16)
        rmsnorm(hT, t0, R_NM + 0)

        ma = A.mark()
        gluT = A.alloc("gluT", [128, 4, 32 + TT], BF16)
        cbuf = A.alloc("cbuf", [128, 4, TT], F32)
        Dg = A.alloc("Dg", [128, 31, 128], BF16)
        tA = [A.alloc("tA%d" % i, [128, TT], F32) for i in range(2)]
        tB = [A.alloc("tB%d" % i, [128, 1024], F32) for i in range(2)]
        stat = [A.alloc("stat%d" % i, [128, 512], F32) for i in range(3)]
        sga = A.alloc("sga", [128, 4, TT], BF16)
        wv = {}
        for c in range(4):
            g = c // 2
            if c % 2 == 0:
                wv = {"val": wload(w_in_even, g * WG, WG), "gate": wload(w_in_even, 512 + g * WG, WG)}
            wc = (c % 2) * 128
            bv = two_banks()
            bg = two_banks()
            proj(wv["val"], wc, hT, bv)
            proj(wv["gate"], wc, hT, bg)
            t = tA[c % 2]
            P.act(t[:, :].rr("p (a b) -> p a b", a=NTT), psv(bg), AF.Tanh, scale=0.5)
            P.copy("pool", gluT[:, c, 0:32], gluhalo[:, c, :])
            P.stt("dve", gluT[:, c, 32:32 + TT].rr("p (a b) -> p a b", a=NTT), t[:, :].rr("p (a b) -> p a b", a=NTT),
                  1.0, psv(bv), ALU.add, ALU.mult)
            P.copy("pool", gluhalo[:, c, :], gluT[:, c, TT:TT + 32])
            for k in range(31):
                P.ts("dve", Dg[:, k, :], identB[:, :], pch[:, k * 4 + c:k * 4 + c + 1], None, ALU.mult)
            for tt in range(NTT):
                bank = nb()
                for k in range(31):
                    P.mm(ps[:, bank, :], Dg[:, k, :], gluT[:, c, 2 + tt * 512 + k:2 + tt * 512 + k + 512],
                         start=(k == 0), stop=(k == 30))
                P.act(cbuf[:, c, tt * 512:(tt + 1) * 512], ps[:, bank, :], AF.Identity, bias=pc(R_DWB + c))
            if c % 2 == 0:
                wv["ga"] = wload(w_in_even, 1024 + (c // 2) * WG, WG)
            bg2 = two_banks()
            proj(wv["ga"], (c % 2) * 128, hT, bg2)
            t = tA[c % 2]
            P.act(t[:, :].rr("p (a b) -> p a b", a=NTT), psv(bg2), AF.Tanh, scale=0.5)
            P.stt("dve", sga[:, c, :].rr("p (a b) -> p a b", a=NTT), t[:, :].rr("p (a b) -> p a b", a=NTT), 1.0, psv(bg2),
                  ALU.add, ALU.mult)
        if last:
            cst = A.alloc("cst", [128, 4, 32], F32)
            csto = A.alloc("csto", [32, 512], F32)
            P.act(cst[:, :, :], gluhalo[:, :, :], AF.Copy, scale=0.5)
            bank = nb()
            for c in range(4):
                P.transpose(ps[0:32, bank, c * 128:(c + 1) * 128], cst[:, c, :], identF[:, :])
            P.copy("dve", csto[:, :], ps[0:32, bank, :])
            P.dma("sp", o_conv[:, :], csto[2:32, :], sem="st0")
        wga = None
        for tt in range(NTT):
            b1, b2 = nb(), nb()
            for c in range(4):
                s = tB[c % 2]
                P.act(s[:, 0:512], cbuf[:, c, tt * 512:(tt + 1) * 512], AF.Square)
                P.mm(ps[:, b1, :], onesF[:, :], cbuf[:, c, tt * 512:(tt + 1) * 512], start=(c == 0), stop=(c == 3))
                P.mm(ps[:, b2, :], onesF[:, :], s[:, 0:512], start=(c == 0), stop=(c == 3))
            mean, msq, rstd = stat
            P.act(mean[:, :], ps[:, b1, :], AF.Copy, scale=1.0 / 512)
            P.tt("dve", msq[:, :], mean[:, :], mean[:, :], ALU.mult)
            P.stt("dve", rstd[:, :], ps[:, b2, :], 1.0 / 512, msq[:, :], ALU.mult, ALU.subtract)
            P.act(rstd[:, :], rstd[:, :], AF.Ln, bias=EPSC[:, 0:1])
            P.act(rstd[:, :], rstd[:, :], AF.Exp, scale=-0.5)
            for c in range(4):
                cv = cbuf[:, c, tt * 512:(tt + 1) * 512]
                P.tt("dve", cv, cv, mean[:, :], ALU.subtract)
                P.tt("dve", cv, cv, rstd[:, :], ALU.mult)
                P.act(cv, cv, AF.Identity, bias=pch[:, 128 + c:129 + c], scale=pch[:, 124 + c:125 + c])
                th = tB[c % 2]
                P.act(th[:, 512:1024], cv, AF.Tanh)
                P.stt("dve", cv, th[:, 512:1024], 1.0, cv, ALU.add, ALU.mult)
        for c in range(4):
            P.stt("dve", aoT[:, c, :], sga[:, c, :], 0.5, cbuf[:, c, :], ALU.mult, ALU.mult)
        A.release(mh)
        if STAGE <= 3:
            dbg["aoT"] = aoT
            A.release(m)
            return

        qT = A.alloc("qT", [128, 4, TT], BF16)
        qdT = A.alloc("qdT", [128, 4, TT], BF16)
        kT = A.alloc("kT", [128, 4, TT], BF16)
        vb = A.alloc("vb", [128, NBLK, 4, 128], BF16)
        kd = A.alloc("kd", [128, NBLK, 4, 128], BF16)
        zs2 = A.alloc("zs2", [128, 4, TT], BF16)
        gts = A.alloc("gts", [128, 12, NBLK, 4], F32)
        eglB = A.alloc("eglB", [128, NBLK, 2, 4], F32)
        egcF = A.alloc("egcF", [4, TT], F32)
        mh = A.mark()
        hT = A.alloc("hT", [128, 8, TT], BF16)
        rmsnorm(hT, t0, R_NM + 0)
        mb = A.mark()
        w8 = A.alloc("w8", [128, 8, 8], BF16)
        P.dma("pool", w8[:, :, :], w_in_even[:, 3584:3592].rearrange("(c p) n -> p c n", p=128), sem="w8")
        bank = nb()
        for b in range(NBLK):
            for part in range(2):
                for kc in range(8):
                    P.mm(ps[:, bank, part * NBLK * 4 + b * 4:part * NBLK * 4 + b * 4 + 4], hT[:, kc, b * 128:(b + 1) * 128],
                         w8[:, kc, part * 4:part * 4 + 4], start=(kc == 0), stop=(kc == 7))
        gl = A.alloc("gl", [128, 2, NBLK, 4], F32)
        P.copy("dve", gl[:, :, :, :], ps[:, bank, 0:NBLK * 8].rr("p (a b c) -> p a b c", a=2, b=NBLK))

        def G(i):
            return gts[:, i, :, :]
        P.act(G(G_TMP), gl[:, 0, :, :], AF.Tanh, scale=0.5)
        P.ts("dve", G(G_BETA), G(G_TMP), 0.5, 0.5, ALU.mult, ALU.add)
        P.tt("dve", G(G_TMP), gl[:, 1, :, :], dtB[:, :, :], ALU.add)
        P.act(G(G_TMP), G(G_TMP), AF.Exp)
        P.act(G(G_TMP), G(G_TMP), AF.Ln, bias=EPSC[:, 2:3])
        P.tt("dve", G(G_G), G(G_TMP), nAB[:, :, :], ALU.mult)
        bank = nb()
        gm = A.alloc("gm", [128, NBLK, 2, 4], F32)
        for b in range(NBLK):
            P.mm(ps[:, bank, b * 4:b * 4 + 4], Mincl[:, :], gts[:, G_G, b, :], start=True, stop=True)
            P.mm(ps[:, bank, NBLK * 4 + b * 4:NBLK * 4 + b * 4 + 4], Msame[:, :], gts[:, G_G, b, :], start=True, stop=True)
            P.ts("dve", gm[:, b, 0, :], gts[:, G_G, b, :], ind0, None, ALU.mult)
            P.ts("dve", gm[:, b, 1, :], gts[:, G_G, b, :], ind1, None, ALU.mult)
        P.copy("dve", gts[:, G_GCS:G_GCL + 1, :, :], ps[:, bank, 0:NBLK * 8].rr("p (a b c) -> p a b c", a=2, b=NBLK))
        bank = nb()
        P.mm(ps[:, bank, 0:NBLK * 8], onesF[:, :], gm[:, :, :, :].rr("p a b c -> p (a b c)"), start=True, stop=True)
        P.act(eglB[:, :, :, :].rr("p a b c -> p (a b c)"), ps[:, bank, 0:NBLK * 8], AF.Exp)
        P.act(G(G_EGC), G(G_GCS), AF.Exp)
        P.tt("dve", G(G_TMP), G(G_GCL), G(G_GCS), ALU.subtract)
        P.act(G(G_EKD), G(G_TMP), AF.Exp)
        P.stt("dve", G(G_NBEG), G(G_BETA), -1.0, G(G_EGC), ALU.mult, ALU.mult)
        P.ts("dve", G(G_VBS), G(G_BETA), 0.5, None, ALU.mult)
        P.ts("dve", G(G_NBETA), G(G_BETA), -1.0, None, ALU.mult)
        for g2 in range(TT // 512):
            bank = nb()
            for j in range(4):
                b = g2 * 4 + j
                P.transpose(ps[0:4, bank, j * 128:(j + 1) * 128], gts[:, G_EGC, b, :], identF[:, :])
            P.copy("dve", egcF[:, g2 * 512:(g2 + 1) * 512], ps[0:4, bank, :])

        pre = [A.alloc("pre%d" % i, [128, 4 + TT], BF16) for i in range(2)]
        Dq = [A.alloc("Dq%d" % i, [128, 4, 128], BF16) for i in range(2)]
        s2 = [A.alloc("s2%d" % i, [128, TT], F32) for i in range(2)]
        tq = [A.alloc("tq%d" % i, [128, TT], F32) for i in range(1)] * 2
        sqb = [A.alloc("sqb%d" % i, [128, 512], BF16) for i in range(2)]
        r1 = A.alloc("r1", [128, 512], F32)
        r2 = A.alloc("r2", [128, 512], F32)
        qst = A.alloc("qst", [128, 12, 4], F32)
        if last:
            mset(qst[:, :, :], 0.0)
        wqs = {}

        def st1(c12):
            if c12 % 2 == 0:
                wqs[c12 // 2] = wload(w_in_even, 1536 + (c12 // 2) * WG, WG)
            wq = wqs[c12 // 2]
            bq = two_banks()
            proj(wq, (c12 % 2) * 128, hT, bq)
            pr = pre[c12 % 2]
            P.copy("pool", pr[:, 0:4], qkvhalo[:, c12, :])
            P.copy("act", pr[:, 4:4 + TT].rr("p (a b) -> p a b", a=NTT), psv(bq))
            if last:
                P.copy("dve", qst[:, c12, 0:3], ps[:, bq[NTT - 1], 509:512])
            P.copy("pool", qkvhalo[:, c12, :], pr[:, TT:TT + 4])

        def st2(c12):
            pr = pre[c12 % 2]
            dq = Dq[c12 % 2]
            for k in range(4):
                P.ts("dve", dq[:, k, :], identB[:, :], pc(R_SC + k * 12 + c12), None, ALU.mult)
            bc_ = two_banks()
            for tt in range(NTT):
                for k in range(4):
                    P.mm(ps[:, bc_[tt], :], dq[:, k, :], pr[:, 1 + tt * 512 + k:1 + tt * 512 + k + 512],
                         start=(k == 0), stop=(k == 3))
            t = tq[c12 % 2]
            s = s2[c12 % 2]
            P.act(t[:, :].rr("p (a b) -> p a b", a=NTT), psv(bc_), AF.Tanh, scale=0.5)
            P.stt("dve", s[:, :].rr("p (a b) -> p a b", a=NTT), t[:, :].rr("p (a b) -> p a b", a=NTT), 1.0, psv(bc_),
                  ALU.add, ALU.mult)

        def st3(c12):
            kind, h = c12 // 4, c12 % 4
            s = s2[c12 % 2]
            for tt in range(NTT):
                sv = s[:, tt * 512:(tt + 1) * 512]
                if kind < 2:
                    sb_ = sqb[tt % 2]
                    P.act(sb_[:, :], sv, AF.Square)
                    bank = nb()
                    P.mm(ps[:, bank, :], onesB[:, :], sb_[:, :], start=True, stop=True)
                    P.act(r1[:, :], ps[:, bank, :], AF.Ln, bias=EPSC[:, 1:2])
                    P.act(r1[:, :], r1[:, :], AF.Exp, scale=-0.5)
                if kind == 0:
                    P.stt("dve", qT[:, h, tt * 512:(tt + 1) * 512], sv, 128.0 ** -0.5, r1[:, :], ALU.mult, ALU.mult)
                    bank = nb()
                    P.mm(ps[:, bank, :], sel[:, h, :], egcF[:, tt * 512:(tt + 1) * 512], start=True, stop=True)
                    P.tt("dve", r2[:, :], r1[:, :], ps[:, bank, :], ALU.mult)
                    P.stt("dve", qdT[:, h, tt * 512:(tt + 1) * 512], sv, 128.0 ** -0.5, r2[:, :], ALU.mult, ALU.mult)
                elif kind == 1:
                    P.tt("dve", sv, sv, r1[:, :], ALU.mult)
                    P.copy("act", kT[:, h, tt * 512:(tt + 1) * 512], sv)
                if kind >= 1:
                    bank = nb()
                    for j in range(4):
                        P.transpose(ps[:, bank, j * 128:(j + 1) * 128], s[:, tt * 512 + j * 128:tt * 512 + (j + 1) * 128],
                                    identF[:, :])
                    dst = kd if kind == 1 else vb
                    gsc4 = gts[:, G_EKD if kind == 1 else G_VBS, tt * 4:(tt + 1) * 4, h]
                    P.tt("dve", dst[:, tt * 4:(tt + 1) * 4, h, :], ps[:, bank, :].rr("p (a b) -> p a b", a=4),
                         V(gsc4.ap.unsqueeze(2).to_broadcast([128, 4, 128]), gsc4.t, gsc4.rect), ALU.mult)

        for step in range(12 + 2):
            if step < 12:
                st1(step)
            if 0 <= step - 1 < 12:
                st2(step - 1)
            if 0 <= step - 2 < 12:
                st3(step - 2)
        if last:
            qsto = A.alloc("qsto", [4, 512], F32)
            for g3 in range(3):
                bank = nb()
                for j in range(4):
                    P.transpose(ps[0:4, bank, j * 128:(j + 1) * 128], qst[:, g3 * 4 + j, :], identF[:, :])
                P.copy("dve", qsto[:, :], ps[0:4, bank, :])
                P.dma("sp", o_qkv[:, g3 * 512:(g3 + 1) * 512], qsto[0:3, :], sem="st1")
        wz = None
        for c in range(4):
            if c % 2 == 0:
                wz = wload(w_in_even, 3072 + (c // 2) * WG, WG)
            bz = two_banks()
            proj(wz, (c % 2) * 128, hT, bz)
            t = tq[c % 2]
            P.act(t[:, :].rr("p (a b) -> p a b", a=NTT), psv(bz), AF.Tanh, scale=0.5)
            P.stt("dve", zs2[:, c, :].rr("p (a b) -> p a b", a=NTT), t[:, :].rr("p (a b) -> p a b", a=NTT), 1.0, psv(bz),
                  ALU.add, ALU.mult)
        A.release(mh)

        TTm = A.alloc("TTm", [128, NBLK, 4, 128], BF16)
        QKm = A.alloc("QKm", [128, NBLK, 4, 128], BF16)
        mprep = A.mark()
        Gm = [A.alloc("Gm%d" % i, [128, 128], F32) for i in range(4)]
        Es = [A.alloc("Es%d" % i, [128, 128], F32) for i in range(4)]
        Ei = [A.alloc("Ei%d" % i, [128, 128], F32) for i in range(4)]
        Mb = [A.alloc("Mb%d" % i, [128, 4, 128], F32) for i in range(2)]
        MTb = [A.alloc("MTb%d" % i, [128, 4, 128], F32) for i in range(2)]
        Xb = [A.alloc("Xb%d" % i, [128, 4, 128], F32) for i in range(2)]
        for b in range(NBLK):
            cs = slice(b * 128, (b + 1) * 128)
            for h in range(4):
                P.ts("pool", Gm[h][:, :], Mgt[:, :], gts[:, G_G, b, h:h + 1], None, ALU.mult)
            b1s = [nb() for _ in range(4)]
            b2s = [nb() for _ in range(4)]
            for h in range(4):
                P.mm(ps[:, b2s[h], 0:128], kT[:, h, cs], kT[:, h, cs], start=True, stop=True)
                P.mm(ps[:, b2s[h], 128:256], kT[:, h, cs], qT[:, h, cs], start=True, stop=True)
            for h in range(4):
                P.mm(ps[:, b1s[h], 0:128], Mincl[:, :], Gm[h][:, :], start=True, stop=False)
                P.mm(ps[:, b1s[h], 0:128], identB[:, :], NEGsB[:, :], start=False, stop=True)
                P.mm(ps[:, b1s[h], 128:256], Gm[h][:, :], Mincl[:, :], start=True, stop=False)
                P.mm(ps[:, b1s[h], 128:256], identB[:, :], NEGiTB[:, :], start=False, stop=True)
            for h in range(4):
                P.act(Es[h][:, :], ps[:, b1s[h], 0:128], AF.Exp)
                P.act(Ei[h][:, :], ps[:, b1s[h], 128:256], AF.Exp)
            for h in range(4):
                P.stt("dve", MTb[0][:, h, :], ps[:, b2s[h], 0:128], gts[:, G_NBETA, b, h:h + 1], Es[h][:, :], ALU.mult, ALU.mult)
                P.tt("dve", QKm[:, b, h, :], ps[:, b2s[h], 128:256], Ei[h][:, :], ALU.mult)
            bank = nb()
            for h in range(4):
                P.transpose(ps[:, bank, h * 128:(h + 1) * 128], MTb[0][:, h, :], identF[:, :])
            P.copy("act", Mb[0][:, :, :], ps[:, bank, :].rr("p (a b) -> p a b", a=4))
            for h in range(4):
                P.tt("pool", Xb[0][:, h, :], Mb[0][:, h, :], identF[:, :], ALU.add)
            cur = 0
            for k in range(1, 6):
                nxt = 1 - cur
                bm, bmt, bx = nb(), nb(), nb()
                for h in range(4):
                    hs = slice(h * 128, (h + 1) * 128)
                    if k < 5:
                        P.mm(ps[:, bm, hs], MTb[cur][:, h, :], Mb[cur][:, h, :], start=True, stop=True)
                    P.mm(ps[:, bmt, hs], Mb[cur][:, h, :], MTb[cur][:, h, :], start=True, stop=True)
                if k < 5:
                    P.copy("act", Mb[nxt][:, :, :], ps[:, bm, :].rr("p (a b) -> p a b", a=4))
                P.copy("dve", MTb[nxt][:, :, :], ps[:, bmt, :].rr("p (a b) -> p a b", a=4))
                for h in range(4):
                    hs = slice(h * 128, (h + 1) * 128)
                    P.mm(ps[:, bx, hs], MTb[nxt][:, h, :], Xb[cur][:, h, :], start=True, stop=True)
                if k < 5:
                    P.tt("dve", Xb[nxt][:, :, :], Xb[cur][:, :, :], ps[:, bx, :].rr("p (a b) -> p a b", a=4), ALU.add)
                else:
                    P.tt("dve", TTm[:, b, :, :], Xb[cur][:, :, :], ps[:, bx, :].rr("p (a b) -> p a b", a=4), ALU.add)
                cur = nxt

        A.release(mprep)
        bT = qT
        R4 = A.alloc("R4", [128, 4, 128], BF16)
        vn4 = A.alloc("vn4", [128, 4, 128], BF16)
        of = [A.alloc("of%d" % i, [128, 512], F32) for i in range(2)]
        osq = [A.alloc("osq%d" % i, [128, 512], BF16) for i in range(2)]
        orr = [A.alloc("orr%d" % i, [128, 512], F32) for i in range(2)]
        obank = {}
        for tt in range(NTT):
            for h in range(4):
                obank[h] = h
                reserved.add(h)
            for cc in range(8):
                c = tt * 8 + cc
                b, par = c // 2, c % 2
                r0 = par * 64
                tok = slice(c * 64, (c + 1) * 64)
                bA, bB, bC = nb(), nb(), nb()
                for h in range(4):
                    P.mm(ps[r0:r0 + 64, bA, h * 128:(h + 1) * 128], kT[:, h, tok], Sbf[:, h, :], start=True, stop=True)
                for h in range(4):
                    P.mm(ps[:, obank[h], cc * 64:(cc + 1) * 64], Sbf[:, h, :], qdT[:, h, tok], start=True, stop=False)
                for h in range(4):
                    P.stt("dve", R4[r0:r0 + 64, h, :], ps[r0:r0 + 64, bA, h * 128:(h + 1) * 128],
                          gts[r0:r0 + 64, G_NBEG, b, h:h + 1], vb[r0:r0 + 64, b, h, :], ALU.mult, ALU.add)
                for h in range(4):
                    P.mm(ps[r0:r0 + 64, bB, h * 128:(h + 1) * 128], TTm[r0:r0 + 64, b, h, r0:r0 + 64], R4[r0:r0 + 64, h, :],
                         start=True, stop=True)
                P.copy("act", vn4[r0:r0 + 64, :, :], ps[r0:r0 + 64, bB, :].rr("p (a b) -> p a b", a=4))
                for h in range(4):
                    P.mm(ps[:, obank[h], cc * 64:(cc + 1) * 64], vn4[r0:r0 + 64, h, :], QKm[r0:r0 + 64, b, h, r0:r0 + 64],
                         start=False, stop=True)
                for h in range(4):
                    P.mm(ps[:, bC, h * 128:(h + 1) * 128], kd[r0:r0 + 64, b, h, :], vn4[r0:r0 + 64, h, :], start=True, stop=True)
                for h in range(4):
                    P.stt("dve", Sst[:, h, :], Sst[:, h, :], eglB[:, b, par, h:h + 1], ps[:, bC, h * 128:(h + 1) * 128],
                          ALU.mult, ALU.add)
                P.copy("act", Sbf[:, :, :], Sst[:, :, :])
            for h in range(4):
                o_ = of[h % 2]
                P.copy("act", o_[:, :], ps[:, obank[h], :])
                sq_ = osq[h % 2]
                P.act(sq_[:, :], o_[:, :], AF.Square)
                bank = nb()
                P.mm(ps[:, bank, :], onesB[:, :], sq_[:, :], start=True, stop=True)
                rr_ = orr[h % 2]
                P.act(rr_[:, :], ps[:, bank, :], AF.Ln, bias=EPSC[:, 0:1], scale=1.0 / 128)
                P.act(rr_[:, :], rr_[:, :], AF.Exp, scale=-0.5)
                P.stt("dve", o_[:, :], o_[:, :], pc(R_DNG), rr_[:, :], ALU.mult, ALU.mult)
                P.stt("dve", bT[:, h, tt * 512:(tt + 1) * 512], o_[:, :], 0.5, zs2[:, h, tt * 512:(tt + 1) * 512], ALU.mult, ALU.mult)
            reserved.clear()
        if last:
            P.dma("sp", o_delta.rearrange("h d e -> d h e"), Sst[:, :, :], sem="st2")
        if STAGE <= 4:
            A.release(m)
            return
        out_proj_add(w_out_even, lambda kc, tt: (aoT if kc < 4 else bT)[:, kc % 4, tt * 512:(tt + 1) * 512], t0)
        A.release(m)

    def odd_mixer(mt):
        t0 = mt * TT
        last = (mt == NMT - 1)
        m = A.mark()
        hT = A.alloc("hT", [128, 8, TT], BF16)
        rmsnorm(hT, t0, R_NM + 8)
        plT = A.alloc("plT", [128, 8, TT], BF16)
        ub = [A.alloc("ub%d" % i, [128, 16 + TT], F32) for i in range(2)]
        sa = A.alloc("sa", [128, 16 + TT], F32)
        sb2 = A.alloc("sb2", [128, 16 + TT], F32)
        wu = None
        for c in range(8):
            if c % 2 == 0:
                wu = wload(w_in_odd, (c // 2) * WG, WG)
            bu = two_banks()
            proj(wu, (c % 2) * 128, hT, bu)
            u = ub[c % 2]
            P.copy("pool", u[:, 0:16], uhalo[:, c, :])
            P.copy("act", u[:, 16:16 + TT].rr("p (a b) -> p a b", a=NTT), psv(bu))
            P.copy("pool", uhalo[:, c, :], u[:, TT:TT + 16])
            gi = c // 2
            win = 2 << gi
            src = u
            sh, lo, bi = 1, 1, 0
            bufs = [sa, sb2]
            while sh < win:
                dst = bufs[bi]
                bi = 1 - bi
                P.tt("dve", dst[:, lo:16 + TT], src[:, lo:16 + TT], src[:, lo - sh:16 + TT - sh], ALU.add)
                src = dst
                sh *= 2
                lo = 2 * lo + 1
            P.stt("dve", plT[:, c, :], src[:, 16:16 + TT], 1.0 / win, u[:, 16:16 + TT], ALU.mult, ALU.subtract)
            if mt == 0:
                fx = sa if src is sb2 else sb2
                P.tt("dve", fx[:, 0:win - 1], src[:, 16:16 + win - 1], invc[:, 0:win - 1], ALU.mult)
                P.tt("dve", plT[:, c, 0:win - 1], fx[:, 0:win - 1], u[:, 16:16 + win - 1], ALU.subtract)
        if last:
            pst = A.alloc("pst", [128, 8, 16], F32)
            psto = A.alloc("psto", [16, D], F32)
            P.copy("dve", pst[:, :, :], uhalo[:, :, :])
            for g2 in range(2):
                bank = nb()
                for j in range(4):
                    P.transpose(ps[0:16, bank, j * 128:(j + 1) * 128], pst[:, g2 * 4 + j, :], identF[:, :])
                P.copy("dve", psto[:, g2 * 512:(g2 + 1) * 512], ps[0:16, bank, :])
            P.dma("sp", o_pool[:, :], psto[1:16, :], sem="st3")
        zT = A.alloc("zT", [128, 8, TT], BF16)
        wp = A.alloc("wp", [128, 4, 2, 256], BF16)
        P.dma("pool", wp[:, :, :, :], w_pool.rearrange("g (c p) d -> p g c d", p=128), sem="wp")
        tz = [A.alloc("tz%d" % i, [128, TT], F32) for i in range(2)]
        tg = [A.alloc("tg%d" % i, [128, TT], F32) for i in range(2)]
        wgt = None
        for n in range(8):
            gi, half = n // 2, n % 2
            bz = two_banks()
            for cc in range(2):
                for tt in range(NTT):
                    P.mm(ps[:, bz[tt], :], wp[:, gi, cc, half * 128:(half + 1) * 128], plT[:, gi * 2 + cc, tt * 512:(tt + 1) * 512],
                         start=(cc == 0), stop=(cc == 1))
            z_ = tz[n % 2]
            P.ts("dve", z_[:, :].rr("p (a b) -> p a b", a=NTT), psv(bz), pc(R_BP + n), pc(R_PSC + n), ALU.add, ALU.mult)
            if n % 2 == 0:
                wgt = wload(w_in_odd, 1024 + (n // 2) * WG, WG)
            bg = two_banks()
            proj(wgt, (n % 2) * 128, hT, bg)
            t = tg[n % 2]
            P.act(t[:, :].rr("p (a b) -> p a b", a=NTT), psv(bg), AF.Tanh, scale=0.5)
            P.stt("dve", t[:, :].rr("p (a b) -> p a b", a=NTT), t[:, :].rr("p (a b) -> p a b", a=NTT), 1.0, psv(bg),
                  ALU.add, ALU.mult)
            P.stt("dve", zT[:, n, :], t[:, :], 0.5, z_[:, :], ALU.mult, ALU.mult)
        out_proj_add(w_out_odd, lambda kc, tt: zT[:, kc, tt * 512:(tt + 1) * 512], t0)
        A.release(m)

    def final_out():
        m = A.mark()
        sq = [A.alloc("fsq%d" % i, [128, 512], BF16) for i in range(2)]
        rs = A.alloc("frs", [128, 512], F32)
        yf = [A.alloc("yf%d" % i, [128, 512], F32) for i in range(2)]
        yst = A.alloc("yst", [128, 4, D], F32)
        for tt in range(T // 512):
            c0 = tt * 512
            bank = nb()
            for kc in range(8):
                s = sq[kc % 2]
                P.act(s[:, :], xT[:, kc, c0:c0 + 512], AF.Square)
                P.mm(ps[:, bank, :], onesB[:, :], s[:, :], start=(kc == 0), stop=(kc == 7))
            P.act(rs[:, :], ps[:, bank, :], AF.Ln, bias=EPSC[:, 0:1], scale=1.0 / D)
            P.act(rs[:, :], rs[:, :], AF.Exp, scale=-0.5)
            for kc in range(8):
                y_ = yf[kc % 2]
                P.stt("dve", y_[:, :], xT[:, kc, c0:c0 + 512], pc(R_NF + kc), rs[:, :], ALU.mult, ALU.mult)
                bank2 = nb()
                for j in range(4):
                    P.transpose(ps[:, bank2, j * 128:(j + 1) * 128], y_[:, j * 128:(j + 1) * 128], identF[:, :])
                P.copy(ev(), yst[:, :, kc * 128:(kc + 1) * 128], ps[:, bank2, :].rr("p (a b) -> p a b", a=4))
            for j in range(4):
                P.dma("sp", o_y[c0 + j * 128:c0 + (j + 1) * 128, :], yst[:, j, :], sem="sty%d" % j)
            pump(PUMP)
        A.release(m)

    x_s = din("x_s", [NS, D])
    st_conv = din("st_conv", [NS, 30, 512])
    st_qkv = din("st_qkv", [NS, 3, 1536])
    st_delta = din("st_delta", [NS, 4, 128, 128])
    st_pool = din("st_pool", [NS, 15, D])
    c_k = din("c_k", [DEPTH, NS, NMEM, D])
    c_v = din("c_v", [DEPTH, NS, NMEM, D])
    o_ys = dout("o_ys", [NS, D])
    o_conv_s = dout("o_conv_s", [NS, 30, 512])
    o_qkv_s = dout("o_qkv_s", [NS, 3, 1536])
    o_delta_s = dout("o_delta_s", [NS, 4, 128, 128])
    o_pool_s = dout("o_pool_s", [NS, 15, D])

    def bcv(v, axis, shape):
        return V(v.ap.unsqueeze(axis).to_broadcast(list(shape)), v.t, v.rect, v.excl)

    def recip(eng_view):
        P.add("dve", lambda e, t=eng_view: e.reciprocal(t.ap, t.ap), reads=[eng_view], writes=[eng_view])

    def reduce_x(out, in_):
        P.add("dve", lambda e: e.tensor_reduce(out.ap, in_.ap, AX.X, ALU.add), reads=[in_], writes=[out])

    def sample_path():
        eye16 = identF[0:NS, 0:NS]
        eye4 = identF[0:4, 0:4]

        def ring():
            swring["bufs"] = [AS.alloc("swb%d" % i, [128, 8, WG], BF16) for i in range(2)]

        def rmsnorm_s(grow, f32out=None):
            m = AS.mark()
            sq = AS.alloc("ssq", [128, 8, NS], BF16)
            rs = AS.alloc("srs", [128, NS], F32)
            P.act(sq[:, :, :], xsT[:, :, :], AF.Square)
            bank = nb()
            for kc in range(8):
                P.mm(ps[:, bank, 0:NS], onesB[:, :], sq[:, kc, :], start=(kc == 0), stop=(kc == 7))
            P.act(rs[:, :], ps[:, bank, 0:NS], AF.Sqrt, bias=EPSC[:, 0:1], scale=1.0 / D)
            recip(rs[:, :])
            dst = hsT if f32out is None else f32out
            for kc in range(8):
                P.stt("dve", dst[:, kc, :], xsT[:, kc, :], pc(grow + kc), rs[:, :], ALU.mult, ALU.mult)
            AS.release(m)

        def s_outproj(wsrc, in_fn):
            bank = nb()
            reserved.add(bank)
            for g in range(D // WG):
                wt = wload_s(wsrc, g * WG, WG)
                for j in range(WG // 128):
                    n = g * (WG // 128) + j
                    for kc in range(8):
                        P.mm(ps[:, bank, n * NS:(n + 1) * NS], wt[:, kc, j * 128:(j + 1) * 128], in_fn(kc),
                             start=(kc == 0), stop=(kc == 7))
                yield
            reserved.discard(bank)
            P.tt("dve", xsT[:, :, :], xsT[:, :, :], ps[:, bank, 0:8 * NS].rr("p (a b) -> p a b", a=8), ALU.add)

        m_seg = AS.mark()
        ring()
        mx_ = AS.mark()
        xst = AS.alloc("xst", [NS, D], F32)
        P.dma("sp", xst[:, :], x_s, sem="sld0")
        bank = nb()
        for kc in range(8):
            P.transpose(ps[:, bank, kc * NS:(kc + 1) * NS], xst[:, kc * 128:(kc + 1) * 128], eye16)
        P.copy("dve", xsT[:, :, :], ps[:, bank, 0:8 * NS].rr("p (a b) -> p a b", a=8))
        AS.release(mx_)
        rmsnorm_s(R_NM + 0)
        yield
        pF = AS.alloc("pF", [128, 16, NS], F32)
        qkvS = AS.alloc("qkvS", [NS, 1536], F32)
        gls = AS.alloc("gls", [NS, 8], F32)
        aoS = AS.alloc("aoS", [128, 4, NS], BF16)
        bS = AS.alloc("bS", [128, 4, NS], BF16)
        w8s = AS.alloc("w8s", [128, 8, 8], BF16)
        BF_, BQ_ = 4, 5
        reserved.update([BF_, BQ_])
        P.dma("pool", w8s[:, :, :], w_in_even[:, 3584:3592].rearrange("(c p) n -> p c n", p=128), sem="sw8")
        for g in range(14):
            col0 = g * WG
            wt = wload_s(w_in_even, col0, WG)
            if col0 < 1536 or col0 >= 3072:
                for j in range(2):
                    ci = (col0 // 128 + j) if col0 < 1536 else (12 + (col0 - 3072) // 128 + j)
                    for kc in range(8):
                        P.mm(ps[:, BF_, ci * NS:(ci + 1) * NS], wt[:, kc, j * 128:(j + 1) * 128], hsT[:, kc, :],
                             start=(kc == 0), stop=(kc == 7))
            else:
                gq = (col0 - 1536) // WG
                for kc in range(8):
                    P.mm(ps[0:NS, BQ_, (gq % 2) * WG:(gq % 2 + 1) * WG], hsT[:, kc, :], wt[:, kc, 0:WG],
                         start=(kc == 0), stop=(kc == 7))
                if gq % 2 == 1:
                    P.copy("act", qkvS[:, (gq // 2) * 512:(gq // 2 + 1) * 512], ps[0:NS, BQ_, :])
            yield
        for kc in range(8):
            P.mm(ps[0:NS, BQ_, 0:8], hsT[:, kc, :], w8s[:, kc, :], start=(kc == 0), stop=(kc == 7))
        P.copy("dve", pF[:, :, :], ps[:, BF_, 0:16 * NS].rr("p (a b) -> p a b", a=16))
        P.copy("dve", gls[:, :], ps[0:NS, BQ_, 0:8])
        reserved.discard(BF_)
        reserved.discard(BQ_)
        P.dma("sp", o_conv_s[:, 0:29, :], st_conv[:, 1:30, :], sem="sst0")
        P.dma("sp", o_qkv_s[:, 0:2, :], st_qkv[:, 1:3, :], sem="sst1")
        P.dma("sp", o_qkv_s[:, 2, :], qkvS[:, :], sem="sst2")
        yield

        m = AS.mark()
        gluS = AS.alloc("gluS", [128, 4, NS], F32)
        tS = AS.alloc("tS", [128, 4, NS], F32)
        P.act(tS[:, :, :], pF[:, 4:8, :], AF.Tanh, scale=0.5)
        P.ts("dve", tS[:, :, :], tS[:, :, :], 0.5, 0.5, ALU.mult, ALU.add)
        P.tt("dve", gluS[:, :, :], tS[:, :, :], pF[:, 0:4, :], ALU.mult)
        gst = AS.alloc("gst", [NS, 512], F32)
        bank = nb()
        for c in range(4):
            P.transpose(ps[0:NS, bank, c * 128:(c + 1) * 128], gluS[:, c, :], identF[:, :])
        P.copy("act", gst[:, :], ps[0:NS, bank, :])
        P.dma("sp", o_conv_s[:, 29, :], gst[:, :], sem="sst3")
        HS = NS // 2
        stc = AS.alloc("stc", [30, HS, 512], F32)
        cst = AS.alloc("cst", [128, NS, 4, 32], F32)
        prod = AS.alloc("prod", [128, NS, 4, 31], F32)
        convS = AS.alloc("convS", [128, NS, 4], F32)
        cS = AS.alloc("cS", [128, 4, NS], F32)
        for hf in range(2):
            P.dma("sp", stc[:, :, :], st_conv[hf * HS:(hf + 1) * HS].rearrange("s k c -> k s c"), sem="sld1")
            for s4 in range(HS // 4):
                bank = nb()
                for si in range(4):
                    for c in range(4):
                        slot = si * 4 + c
                        P.transpose(ps[:, bank, slot * 32:slot * 32 + 30], stc[:, s4 * 4 + si, c * 128:(c + 1) * 128],
                                    identF[0:30, 0:30])
                s0 = hf * HS + s4 * 4
                P.copy(ev(), cst[:, s0:s0 + 4, :, 0:30],
                       ps[:, bank, :].rr("p (a b c) -> p a b c", a=4, b=4).sub(slice(None), slice(None), slice(None), slice(0, 30)))
                yield
        P.copy("dve", cst[:, :, :, 30], gluS[:, :, :].rr("p c s -> p s c"))
        wv = bcv(pcol1[:, 0:124].rr("p (k c) -> p c k", c=4), 1, [128, NS, 4, 31])
        P.tt("dve", prod[:, :, :, :], cst[:, :, :, 0:31], wv, ALU.mult)
        reduce_x(convS[:, :, :], prod[:, :, :, :])
        for c in range(4):
            P.ts("dve", cS[:, c, :], convS[:, :, c], pc(R_DWB + c), None, ALU.add)
        yield
        sqS = AS.alloc("sqS", [128, 4, NS], F32)
        P.act(sqS[:, :, :], cS[:, :, :], AF.Square)
        bank = nb()
        for c in range(4):
            P.mm(ps[:, bank, 0:NS], onesF[:, :], cS[:, c, :], start=(c == 0), stop=(c == 3))
        for c in range(4):
            P.mm(ps[:, bank, NS:2 * NS], onesF[:, :], sqS[:, c, :], start=(c == 0), stop=(c == 3))
        mean = AS.alloc("smean", [128, NS], F32)
        msq = AS.alloc("smsq", [128, NS], F32)
        rstd = AS.alloc("srstd", [128, NS], F32)
        P.act(mean[:, :], ps[:, bank, 0:NS], AF.Copy, scale=1.0 / 512)
        P.tt("dve", msq[:, :], mean[:, :], mean[:, :], ALU.mult)
        P.stt("dve", rstd[:, :], ps[:, bank, NS:2 * NS], 1.0 / 512, msq[:, :], ALU.mult, ALU.subtract)
        P.act(rstd[:, :], rstd[:, :], AF.Sqrt, bias=EPSC[:, 0:1])
        recip(rstd[:, :])
        thS = AS.alloc("thS", [128, 4, NS], F32)
        for c in range(4):
            P.tt("dve", cS[:, c, :], cS[:, c, :], mean[:, :], ALU.subtract)
            P.tt("dve", cS[:, c, :], cS[:, c, :], rstd[:, :], ALU.mult)
            P.act(cS[:, c, :], cS[:, c, :], AF.Identity, bias=pch[:, 128 + c:129 + c], scale=pch[:, 124 + c:125 + c])
        P.act(thS[:, :, :], cS[:, :, :], AF.Tanh)
        P.stt("dve", cS[:, :, :], thS[:, :, :], 1.0, cS[:, :, :], ALU.add, ALU.mult)
        P.act(thS[:, :, :], pF[:, 8:12, :], AF.Tanh, scale=0.5)
        P.stt("dve", thS[:, :, :], thS[:, :, :], 1.0, pF[:, 8:12, :], ALU.add, ALU.mult)
        P.stt("dve", aoS[:, :, :], thS[:, :, :], 0.5, cS[:, :, :], ALU.mult, ALU.mult)
        AS.release(m)
        yield

        m = AS.mark()
        gS = AS.alloc("gS", [NS, 6, 4], F32)
        P.act(gS[:, 1, :], gls[:, 0:4], AF.Tanh, scale=0.5)
        P.ts("dve", gS[:, 0, :], gS[:, 1, :], 0.5, 0.5, ALU.mult, ALU.add)
        P.tt("dve", gS[:, 1, :], gls[:, 4:8], dtB[0:NS, 0, :], ALU.add)
        P.act(gS[:, 1, :], gS[:, 1, :], AF.Exp)
        P.act(gS[:, 1, :], gS[:, 1, :], AF.Ln, bias=EPSC[0:NS, 2:3])
        P.tt("dve", gS[:, 1, :], gS[:, 1, :], nAB[0:NS, 0, :], ALU.mult)
        P.act(gS[:, 2, :], gS[:, 1, :], AF.Exp)
        rhsAB = AS.alloc("rhsAB", [NS, 2, 4, NS], F32)
        P.tt("dve", rhsAB[:, 0, :, :], bcv(gS[:, 2, :], 2, [NS, 4, NS]), bcv(eye16, 1, [NS, 4, NS]), ALU.mult)
        P.tt("dve", rhsAB[:, 1, :, :], bcv(gS[:, 0, :], 2, [NS, 4, NS]), bcv(eye16, 1, [NS, 4, NS]), ALU.mult)
        bank = nb()
        P.mm(ps[:, bank, 0:2 * 4 * NS], onesF[0:NS, :], rhsAB[:, :, :, :].rr("p a b c -> p (a b c)"), start=True, stop=True)
        abB = AS.alloc("abB", [128, 2, 4, NS], F32)
        P.copy("dve", abB[:, :, :, :], ps[:, bank, 0:2 * 4 * NS].rr("p (a b c) -> p a b c", a=2, b=4))
        aB = abB[:, 0, :, :]
        betaB = abB[:, 1, :, :]
        yield
        qkvn = AS.alloc("qkvn", [NS, 12, 128], F32)
        ss = AS.alloc("ss", [NS, 8], F32)
        qkvT = AS.alloc("qkvT", [128, 12, NS], F32)
        macc = AS.mark()
        accf = AS.alloc("accf", [NS, 1536], F32)
        m2 = AS.mark()
        stq = AS.alloc("stq", [NS, 3, 512], F32)
        wB = AS.alloc("wB", [NS, 4, 512], F32)
        tmp = AS.alloc("tmpq", [NS, 512], F32)
        for q3 in range(3):
            cs3 = slice(q3 * 512, (q3 + 1) * 512)
            acc = accf[:, cs3]
            P.dma("sp", stq[:, :, :], st_qkv[:, :, cs3], sem="sld0")
            for k in range(4):
                P.dma("sp", wB[:, k, :], sc_w[k, cs3].partition_broadcast(NS), sem="sld%d" % (1 + k % 2))
            P.tt("dve", acc, qkvS[:, cs3], wB[:, 3, :], ALU.mult)
            for k in range(3):
                P.tt("dve", tmp[:, :], stq[:, k, :], wB[:, k, :], ALU.mult)
                P.tt("dve", acc, acc, tmp[:, :], ALU.add)
            P.act(tmp[:, :], acc, AF.Tanh, scale=0.5)
            P.stt("dve", acc, tmp[:, :], 1.0, acc, ALU.add, ALU.mult)
            yield
        AS.release(m2)
        m2 = AS.mark()
        tmp2 = AS.alloc("tmp2", [NS, 1024], F32)
        P.tt("dve", tmp2[:, :], accf[:, 0:1024], accf[:, 0:1024], ALU.mult)
        reduce_x(ss[:, :], tmp2[:, :].rr("p (a b) -> p a b", a=8))
        P.act(ss[:, :], ss[:, :], AF.Sqrt, bias=EPSC[0:NS, 1:2])
        recip(ss[:, :])
        P.ts("dve", ss[:, 0:4], ss[:, 0:4], 128.0 ** -0.5, None, ALU.mult)
        P.tt("dve", qkvn[:, 0:8, :], accf[:, 0:1024].rr("p (a b) -> p a b", a=8), bcv(ss[:, :], 2, [NS, 8, 128]), ALU.mult)
        P.ts("dve", qkvn[:, 8:12, :], accf[:, 1024:1536].rr("p (a b) -> p a b", a=4), 0.5, None, ALU.mult)
        bank = nb()
        for j in range(12):
            P.transpose(ps[:, bank, j * NS:(j + 1) * NS], qkvn[:, j, :], eye16)
        P.copy("dve", qkvT[:, :, :], ps[:, bank, 0:12 * NS].rr("p (a b) -> p a b", a=12))
        AS.release(macc)
        yield
        oS = AS.alloc("oS", [128, 4, NS], F32)
        Ssm = AS.alloc("Ssm", [128, 2, NS, 128], F32)
        vnT = AS.alloc("vnT", [128, 2, NS], F32)
        vnt = AS.alloc("vnt", [NS, 2, 128], F32)
        vbd = AS.alloc("vbd", [NS, NS, 128], F32)
        stmp = [AS.alloc("stmp%d" % i, [128, 4, 128], F32) for i in range(2)]
        for hp in range(2):
            for hh in range(2):
                h = hp * 2 + hh
                P.dma("sp", Ssm[:, hh, :, :], st_delta[:, h].rearrange("s d e -> d s e"), sem="slds%d" % hh)
            bank = nb()
            for hh in range(2):
                h = hp * 2 + hh
                for s in range(NS):
                    P.mm(ps[:, bank, hh * NS + s:hh * NS + s + 1], Ssm[:, hh, s, :], qkvT[:, 4 + h, s:s + 1], start=True, stop=True)
            P.tt("dve", vnT[:, :, :], ps[:, bank, 0:2 * NS].rr("p (a b) -> p a b", a=2), abB[:, 0, hp * 2:hp * 2 + 2, :], ALU.mult)
            P.tt("dve", vnT[:, :, :], qkvT[:, 8 + hp * 2:10 + hp * 2, :], vnT[:, :, :], ALU.subtract)
            P.tt("dve", vnT[:, :, :], vnT[:, :, :], abB[:, 1, hp * 2:hp * 2 + 2, :], ALU.mult)
            bank = nb()
            for hh in range(2):
                P.transpose(ps[0:NS, bank, hh * 128:(hh + 1) * 128], vnT[:, hh, :], identF[:, :])
            P.copy("act", vnt[:, :, :], ps[0:NS, bank, 0:256].rr("p (a b) -> p a b", a=2))
            yield
            for hh in range(2):
                h = hp * 2 + hh
                P.tt("dve", vbd[:, :, :], bcv(vnt[:, hh, :], 1, [NS, NS, 128]), bcv(eye16, 2, [NS, NS, 128]), ALU.mult)
                for q4 in range(4):
                    bk = nb()
                    P.mm(ps[:, bk, :], qkvn[:, 4 + h, :], vbd[:, q4 * 4:(q4 + 1) * 4, :].rr("p a b -> p (a b)"), start=True, stop=True)
                    st_ = stmp[q4 % 2]
                    sv = Ssm[:, hh, q4 * 4:(q4 + 1) * 4, :]
                    P.tt("dve", st_[:, :, :], sv, bcv(abB[:, 0, h, q4 * 4:(q4 + 1) * 4], 2, [128, 4, 128]), ALU.mult)
                    P.tt("dve", sv, st_[:, :, :], ps[:, bk, :].rr("p (a b) -> p a b", a=4), ALU.add)
                yield
            for hh in range(2):
                h = hp * 2 + hh
                P.dma("sp", o_delta_s[:, h].rearrange("s d e -> d s e"), Ssm[:, hh, :, :], sem="slds%d" % hh)
            bank = nb()
            for hh in range(2):
                h = hp * 2 + hh
                for s in range(NS):
                    P.mm(ps[:, bank, hh * NS + s:hh * NS + s + 1], Ssm[:, hh, s, :], qkvT[:, h, s:s + 1], start=True, stop=True)
            P.copy("act", oS[:, hp * 2:hp * 2 + 2, :], ps[:, bank, 0:2 * NS].rr("p (a b) -> p a b", a=2))
            yield
        osq_ = AS.alloc("osqS", [128, 4, NS], F32)
        P.act(osq_[:, :, :], oS[:, :, :], AF.Square)
        bank = nb()
        P.mm(ps[:, bank, 0:4 * NS], onesF[:, :], osq_[:, :, :].rr("p a b -> p (a b)"), start=True, stop=True)
        orr_ = AS.alloc("orrS", [128, 4, NS], F32)
        P.act(orr_[:, :, :], ps[:, bank, 0:4 * NS].rr("p (a b) -> p a b", a=4), AF.Sqrt, bias=EPSC[:, 0:1], scale=1.0 / 128)
        recip(orr_[:, :, :])
        P.stt("dve", oS[:, :, :], oS[:, :, :], pc(R_DNG), orr_[:, :, :], ALU.mult, ALU.mult)
        P.act(osq_[:, :, :], pF[:, 12:16, :], AF.Tanh, scale=0.5)
        P.stt("dve", osq_[:, :, :], osq_[:, :, :], 1.0, pF[:, 12:16, :], ALU.add, ALU.mult)
        P.stt("dve", bS[:, :, :], oS[:, :, :], 0.5, osq_[:, :, :], ALU.mult, ALU.mult)
        AS.release(m)
        yield
        yield from s_outproj(w_out_even, lambda kc: (aoS if kc < 4 else bS)[:, kc % 4, :])
        AS.release(m_seg)
        yield "B"

        def s_xattn(l):
            m = AS.mark()
            ring()
            selF = AS.alloc("selF", [NS, NS, 128], F32)
            selS = AS.alloc("selS", [NS, NS, 128], BF16)
            mset(selF[:, :, :], 1.0)
            msel(selF[:, :, :], ALU.is_equal, 0.0, [[-1, NS], [0, 128]], 1)
            P.copy("pool", selS[:, :, :], selF[:, :, :])
            rmsnorm_s(R_NX + l * 8)
            qs = AS.alloc("qs", [NS, D], BF16)
            bq2 = pair_banks()
            reserved.update(bq2)
            for g in range(D // WG):
                wt = wload_s(w_xq[l], g * WG, WG)
                for kc in range(8):
                    P.mm(ps[0:NS, bq2[g // 2], (g % 2) * WG:(g % 2 + 1) * WG], hsT[:, kc, :], wt[:, kc, 0:WG],
                         start=(kc == 0), stop=(kc == 7))
                yield
            for hf in range(2):
                P.act(qs[:, hf * 512:(hf + 1) * 512], ps[0:NS, bq2[hf], :], AF.Copy, scale=1.0 / 16.0)
            reserved.difference_update(bq2)
            oxS = AS.alloc("oxS", [128, 8, NS], BF16)
            NKB = 3
            kbuf = [AS.alloc("kbuf%d" % i, [128, 2, D], BF16) for i in range(NKB)]
            vbuf = [AS.alloc("vbuf%d" % i, [128, 2, D], BF16) for i in range(NKB)]
            qBs = [AS.alloc("qBs%d" % i, [128, D], BF16) for i in range(2)]
            prd = AS.alloc("prd", [128, 2, D], F32)
            scb = [AS.alloc("sc%d" % i, [128, 2, 4], F32) for i in range(3)]
            scbb = [AS.alloc("scb%d" % i, [128, 2, 4], BF16) for i in range(3)]
            o4m = AS.alloc("o4m", [4, 4, 256], F32)
            o4 = AS.alloc("o4", [4, 256], F32)
            rden = AS.alloc("rdenS", [4, 1], F32)

            def stage_load(s):
                P.dma("pool", kbuf[s % NKB][:, :, :], c_k[l, s].rearrange("(c p) n -> p c n", p=128), sem="ck%d" % (s % NKB))
                P.dma("pool", vbuf[s % NKB][:, :, :], c_v[l, s].rearrange("(c p) n -> p c n", p=128), sem="cv%d" % (s % NKB))

            def stage_a(s):
                Kc = kbuf[s % NKB]
                sc = scb[s % 3]
                qb = qBs[s % 2]
                bq = pair_banks()
                for hf in range(2):
                    P.mm(ps[:, bq[hf], :], selS[:, s, :], qs[:, hf * 512:(hf + 1) * 512], start=True, stop=True)
                P.copy("act", qb[:, :].rr("p (a b) -> p a b", a=2), ps[:, bq[0]:bq[0] + 2, :])
                P.tt("dve", prd[:, :, :], Kc[:, :, :], bcv(qb[:, :], 1, [128, 2, D]), ALU.mult)
                reduce_x(sc[:, :, :], prd[:, :, :].rr("p a (h d) -> p a h d", h=4))
                P.act(scbb[s % 3][:, :, :], sc[:, :, :], AF.Exp)

            def stage_b(s):
                Vc = vbuf[s % NKB]
                sc = scbb[s % 3]
                bo = pair_banks()
                for hf in range(2):
                    for mc in range(2):
                        P.mm(ps[0:4, bo[hf], :], sc[:, mc, :], Vc[:, mc, hf * 512:(hf + 1) * 512], start=(mc == 0), stop=(mc == 1))
                P.tt("dve", o4m[:, :, :], ps[0:4, bo[0]:bo[0] + 2, :].rr("p a (h d) -> p (a h) d", h=2), bcv(eye4, 2, [4, 4, 256]), ALU.mult)
                bd = nb()
                for mc in range(2):
                    P.mm(ps[0:4, bd, 0:1], sc[:, mc, :], onesB[:, 0:1], start=(mc == 0), stop=(mc == 1))
                P.add("dve", lambda e, bd=bd: e.reciprocal(rden[:, :].ap, ps[0:4, bd, 0:1].ap), reads=[ps[0:4, bd, 0:1]], writes=[rden[:, :]])
                reduce_x(o4[:, :], o4m[:, :, :].rr("p h d -> p d h"))
                P.ts("dve", o4[:, :], o4[:, :], rden[:, 0:1], None, ALU.mult)
                bt = nb()
                for hf in range(2):
                    P.transpose(ps[:, bt, hf * 4:(hf + 1) * 4], o4[:, hf * 128:(hf + 1) * 128], eye4)
                P.copy("act", oxS[:, :, s].rr("p (h f) -> p f h", f=2), ps[:, bt, 0:8].rr("p (f h) -> p f h", f=2))

            stage_load(0)
            stage_load(1)
            stage_a(0)
            yield
            for s in range(NS):
                if s + 1 < NS:
                    stage_a(s + 1)
                stage_b(s)
                if s + 2 < NS:
                    stage_load(s + 2)
                yield
            yield from s_outproj(w_xo[l], lambda kc: oxS[:, kc, :])
            AS.release(m)

        yield from s_xattn(0)
        yield "B"

        m_odd = AS.mark()
        ring()
        rmsnorm_s(R_NM + 8)
        uS = AS.alloc("uS", [NS, D], F32)
        gF = AS.alloc("gF", [128, 8, NS], F32)
        bu2 = pair_banks()
        bgf = nb()
        reserved.update(bu2 + [bgf])
        for g in range(8):
            wt = wload_s(w_in_odd, g * WG, WG)
            if g < 4:
                for kc in range(8):
                    P.mm(ps[0:NS, bu2[g // 2], (g % 2) * WG:(g % 2 + 1) * WG], hsT[:, kc, :], wt[:, kc, 0:WG],
                         start=(kc == 0), stop=(kc == 7))
            else:
                for j in range(2):
                    n = (g - 4) * 2 + j
                    for kc in range(8):
                        P.mm(ps[:, bgf, n * NS:(n + 1) * NS], wt[:, kc, j * 128:(j + 1) * 128], hsT[:, kc, :],
                             start=(kc == 0), stop=(kc == 7))
            yield
        for hf in range(2):
            P.copy("act", uS[:, hf * 512:(hf + 1) * 512], ps[0:NS, bu2[hf], :])
        P.copy("dve", gF[:, :, :], ps[:, bgf, 0:8 * NS].rr("p (a b) -> p a b", a=8))
        reserved.difference_update(bu2 + [bgf])
        P.dma("sp", o_pool_s[:, 0:14, :], st_pool[:, 1:15, :], sem="sst0")
        P.dma("sp", o_pool_s[:, 14, :], uS[:, :], sem="sst1")
        plt = AS.alloc("plt", [NS, D], F32)
        bsum = AS.alloc("bsum", [NS, 256], F32)
        stp = AS.alloc("stp", [NS, 15, 256], F32)
        for gi in range(4):
            win = 2 << gi
            cs = slice(gi * 256, (gi + 1) * 256)
            P.dma("sp", stp[:, :, :], st_pool[:, :, cs], sem="sld0")
            reduce_x(bsum[:, :], stp[:, 16 - win:15, :].rr("p k c -> p c k"))
            P.tt("dve", bsum[:, :], bsum[:, :], uS[:, cs], ALU.add)
            P.stt("dve", plt[:, cs], bsum[:, :], 1.0 / win, uS[:, cs], ALU.mult, ALU.subtract)
            yield
        plS = AS.alloc("plS", [128, 8, NS], BF16)
        bank = nb()
        for c in range(8):
            P.transpose(ps[:, bank, c * NS:(c + 1) * NS], plt[:, c * 128:(c + 1) * 128], eye16)
        P.copy("dve", plS[:, :, :], ps[:, bank, 0:8 * NS].rr("p (a b) -> p a b", a=8))
        wps = AS.alloc("wps", [128, 4, 2, 256], BF16)
        P.dma("pool", wps[:, :, :, :], w_pool.rearrange("g (c p) d -> p g c d", p=128), sem="swp")
        zS = AS.alloc("zS", [128, 8, NS], BF16)
        zf = AS.alloc("zf", [128, 8, NS], F32)
        tgs = AS.alloc("tgs", [128, 8, NS], F32)
        bank = nb()
        for n in range(8):
            gi, half = n // 2, n % 2
            for cc in range(2):
                P.mm(ps[:, bank, n * NS:(n + 1) * NS], wps[:, gi, cc, half * 128:(half + 1) * 128], plS[:, gi * 2 + cc, :],
                     start=(cc == 0), stop=(cc == 1))
        for n in range(8):
            P.ts("dve", zf[:, n, :], ps[:, bank, n * NS:(n + 1) * NS], pc(R_BP + n), pc(R_PSC + n), ALU.add, ALU.mult)
        P.act(tgs[:, :, :], gF[:, :, :], AF.Tanh, scale=0.5)
        P.stt("dve", tgs[:, :, :], tgs[:, :, :], 1.0, gF[:, :, :], ALU.add, ALU.mult)
        P.stt("dve", zS[:, :, :], tgs[:, :, :], 0.5, zf[:, :, :], ALU.mult, ALU.mult)
        yield
        yield from s_outproj(w_out_odd, lambda kc: zS[:, kc, :])
        AS.release(m_odd)
        yield "B"
        yield from s_xattn(1)

        m = AS.mark()
        yT = AS.alloc("yTs", [128, 8, NS], F32)
        rmsnorm_s(R_NF, f32out=yT)
        yst_ = AS.alloc("ysts", [NS, D], F32)
        for g in range(2):
            bank = nb()
            for j in range(4):
                P.transpose(ps[0:NS, bank, j * 128:(j + 1) * 128], yT[:, g * 4 + j, :], identF[:, :])
            P.copy("act", yst_[:, g * 512:(g + 1) * 512], ps[0:NS, bank, :])
        P.dma("sp", o_ys, yst_[:, :], sem="sst2")
        AS.release(m)
        yield "B"

    sgen = sample_path()
    sdone = [False]

    def pump(n=1, to_boundary=False):
        if sdone[0] or (win_done[0] and not to_boundary):
            return
        dom[0] = "s"
        try:
            k = 0
            while True:
                r = next(sgen)
                k += 1
                if r == "B":
                    win_done[0] = True
                    break
                if not to_boundary and k >= n:
                    break
        except StopIteration:
            sdone[0] = True
        dom[0] = "p"

    win_done = [False]
    PUMP = int(_os.environ.get("K_PUMP", "1"))

    def win_begin():
        win_done[0] = False
        A.limit = AS_BASE
        allowed["p"] = {0, 1, 2, 3}

    def win_end():
        if not win_done[0]:
            pump(to_boundary=True)
        A.limit = A.top
        allowed["p"] = set(range(8))

    mem_prep()
    xstate = load_x_issue()
    mem_kv(0)
    load_x_finish(xstate)
    for l in range(DEPTH):
        if l > 0:
            mem_kv(l)
        for mt in range(NMT):
            if l % 2 == 0:
                even_mixer(mt)
            else:
                odd_mixer(mt)
        if STAGE <= 4:
            break
        for mt in range(NMT):
            xattn(l, mt)
        if STAGE <= 5:
            break
    if STAGE >= 7:
        win_begin()
        final_out()
        win_end()
    while not sdone[0]:
        win_done[0] = False
        pump(to_boundary=True)
    for name, t in dbg.items():
        od = nc.dram_tensor("dbg_" + name, t.shape, F32, kind="ExternalOutput").ap()
        m = A.mark()
        st = A.alloc("dbgst", t.shape, F32)
        P.copy("dve", st[tuple(slice(None) for _ in t.shape)], t[tuple(slice(None) for _ in t.shape)])
        P.dma("sp", od, st[tuple(slice(None) for _ in t.shape)], sem="dbg")
        A.release(m)
    print("SBUF peak bytes/partition:", A.peak, "ops:", len(P.ops))
    P.emit()
    return nc


_REPL = ["w_xk", "w_xv", "w_xq", "w_xo", "norm_mix", "norm_xattn"]
_SQ0 = ["w_in_even", "w_out_even", "w_in_odd", "w_out_odd", "w_pool", "dw_w", "sc_w", "b_pool"]
_ROW = ["norm_final", "dw_b", "ln_a_g", "ln_a_b", "a_log", "dt_bias", "dn_norm_g", "pool_scale"]


def kernel(**inputs):
    nc = build_program()
    f = lambda a: np.ascontiguousarray(a, dtype=np.float32)
    shared = {}
    for k in _REPL:
        shared[k] = f(inputs[k])
    for k in _SQ0:
        shared[k] = f(inputs[k][0])
    for k in _ROW:
        shared[k] = f(np.asarray(inputs[k]).reshape(1, -1))
    in_maps = []
    for c in range(NCORES):
        m = dict(shared)
        m["x_p"] = f(inputs["x_prompt"][c])
        m["mem_p"] = f(inputs["mem_prompt"][c])
        sl = slice(c * NS, (c + 1) * NS)
        m["x_s"] = f(inputs["x_sample"][sl, 0])
        m["st_conv"] = f(inputs["state_conv_a"][0, sl])
        m["st_qkv"] = f(inputs["state_qkv_conv"][0, sl])
        m["st_delta"] = f(inputs["state_delta"][0, sl])
        m["st_pool"] = f(inputs["state_pool"][0, sl])
        m["c_k"] = f(inputs["cache_mem_k"][:, sl]).reshape(DEPTH, NS, NMEM, D)
        m["c_v"] = f(inputs["cache_mem_v"][:, sl]).reshape(DEPTH, NS, NMEM, D)
        in_maps.append(m)
    res = run_bass_kernel_spmd(nc, in_maps, core_ids=list(range(NCORES)))
    R = res.results
    B = NCORES
    out = {}
    out["y_prompt"] = np.stack([R[c]["o_y"] for c in range(B)], axis=0)
    out["new_conv_a_p"] = np.stack([R[c]["o_conv"] for c in range(B)], axis=0)[None]
    out["new_qkv_conv_p"] = np.stack([R[c]["o_qkv"] for c in range(B)], axis=0)[None]
    out["new_delta_p"] = np.stack([R[c]["o_delta"] for c in range(B)], axis=0)[None]
    out["new_pool_p"] = np.stack([R[c]["o_pool"] for c in range(B)], axis=0)[None]
    out["new_mem_k_p"] = np.stack([R[c]["o_mem_k"] for c in range(B)], axis=1).reshape(DEPTH, B, NMEM, 4, 256)
    out["new_mem_v_p"] = np.stack([R[c]["o_mem_v"] for c in range(B)], axis=1).reshape(DEPTH, B, NMEM, 4, 256)
    if STAGE >= 8:
        out["y_sample"] = np.concatenate([R[c]["o_ys"] for c in range(B)], axis=0)[:, None, :]
        out["new_conv_a_s"] = np.concatenate([R[c]["o_conv_s"] for c in range(B)], axis=0)[None]
        out["new_qkv_conv_s"] = np.concatenate([R[c]["o_qkv_s"] for c in range(B)], axis=0)[None]
        out["new_delta_s"] = np.concatenate([R[c]["o_delta_s"] for c in range(B)], axis=0)[None]
        out["new_pool_s"] = np.concatenate([R[c]["o_pool_s"] for c in range(B)], axis=0)[None]
    if _os.environ.get("K_DBG"):
        for k in R[0]:
            if k.startswith("dbg_"):
                out[k] = R[0][k]
        return out
    NSM = NS * NCORES
    for k, shp in (("y_sample", (NSM, 1, D)), ("new_conv_a_s", (1, NSM, 30, 512)), ("new_qkv_conv_s", (1, NSM, 3, 1536)),
                   ("new_delta_s", (1, NSM, 4, 128, 128)), ("new_pool_s", (1, NSM, 15, D))):
        if k not in out:
            out[k] = np.zeros(shp, np.float32)
    order = ["y_prompt", "y_sample", "new_conv_a_p", "new_qkv_conv_p", "new_delta_p", "new_pool_p", "new_mem_k_p",
             "new_mem_v_p", "new_conv_a_s", "new_qkv_conv_s", "new_delta_s", "new_pool_s"]
    return tuple(out[k] for k in order)
```
